# Optimizing a Trainium2 kernel written in Bass

```python
import math
import jax, jax.numpy as jnp
from jax import lax
import numpy as np

D_MODEL = 1024
BATCH = 2
SEQ = 8192
DEPTH = 2

N_HEADS_A = 8
D_COMP_A = 32
D_V_A = 2 * D_COMP_A
WIDTH_A = N_HEADS_A * D_V_A
ROPE_DIMS = D_COMP_A // 4
ROPE_THETA = 500000.0
Q_BLOCK = 128
WIDTH_B = 512
CONV_WIDTH = 31
N_GROUPS_C = 4
GROUP_C = 128
WIDTH_C = N_GROUPS_C * GROUP_C
N_BRANCH = 3
IN_WIDTH = 3 * WIDTH_A + 2 * WIDTH_B + WIDTH_C
D_FF = -(-8 * D_MODEL // (3 * 256)) * 256
EPS = 1e-6

kernel_name = 'hybrid_diffattn_conformer_fnet_gated'


def rmsnorm(x, g):
    xf = x.astype(jnp.float32)
    y = xf * lax.rsqrt(jnp.mean(xf * xf, axis=-1, keepdims=True) + EPS)
    return (y * g.astype(jnp.float32)).astype(x.dtype)


def layernorm(x, g, b):
    xf = x.astype(jnp.float32)
    mu = jnp.mean(xf, axis=-1, keepdims=True)
    var = jnp.mean(jnp.square(xf - mu), axis=-1, keepdims=True)
    y = (xf - mu) * lax.rsqrt(var + EPS)
    return (y * g.astype(jnp.float32) + b.astype(jnp.float32)).astype(x.dtype)


def rope_tables(seq):
    pos = jnp.arange(seq, dtype=jnp.float32)
    inv = 1.0 / (ROPE_THETA ** (jnp.arange(0, ROPE_DIMS, 2, dtype=jnp.float32) / ROPE_DIMS))
    ang = pos[:, None] * inv[None, :]
    return jnp.cos(ang), jnp.sin(ang)


def apply_partial_rope(t, cos, sin):
    half = ROPE_DIMS // 2
    c = cos[None, :, None, None, :].astype(t.dtype)
    s = sin[None, :, None, None, :].astype(t.dtype)
    r1 = t[..., :half]
    r2 = t[..., half:ROPE_DIMS]
    rest = t[..., ROPE_DIMS:]
    return jnp.concatenate([r1 * c - r2 * s, r2 * c + r1 * s, rest], axis=-1)


def diff_attention(q, k, v, lam):
    b, s = q.shape[0], q.shape[1]
    nb = s // Q_BLOCK
    qb = q.reshape(b, nb, Q_BLOCK, N_HEADS_A, 2, D_COMP_A).transpose(1, 0, 2, 3, 4, 5)
    vf = v.astype(jnp.float32)
    scale = D_COMP_A ** -0.5

    def block(qi):
        sc = jnp.einsum('bqhcd,bkhcd->bhcqk', qi, k,
                        preferred_element_type=jnp.float32) * scale
        p = jax.nn.softmax(sc, axis=-1)
        w = p[:, :, 0] - lam * p[:, :, 1]
        return jnp.einsum('bhqk,bkhe->bqhe', w, vf)

    out = lax.map(block, qb)
    return out.transpose(1, 0, 2, 3, 4).reshape(b, s, N_HEADS_A, D_V_A).astype(v.dtype)


def setup_inputs(seed: int = 0) -> dict:
    key = jax.random.key(seed)
    ks = jax.random.split(key, 24)
    f32 = jnp.float32
    L, D = DEPTH, D_MODEL

    def nrm(k, shape, scale):
        return jax.random.normal(k, shape, f32) * scale

    return {
        'x': nrm(ks[0], (BATCH, SEQ, D), 1.0),
        'norm1_g': 1.0 + nrm(ks[1], (L, D), 0.02),
        'w_in': nrm(ks[2], (L, D, IN_WIDTH), D ** -0.5),
        'qnorm_g': 1.0 + nrm(ks[3], (L, D_COMP_A), 0.02),
        'knorm_g': 1.0 + nrm(ks[4], (L, D_COMP_A), 0.02),
        'lambda_q1': nrm(ks[5], (L, D_COMP_A), 0.1),
        'lambda_k1': nrm(ks[6], (L, D_COMP_A), 0.1),
        'lambda_q2': nrm(ks[7], (L, D_COMP_A), 0.1),
        'lambda_k2': nrm(ks[8], (L, D_COMP_A), 0.1),
        'subln_g': 1.0 + nrm(ks[9], (L, D_V_A), 0.02),
        'w_proj_a': nrm(ks[10], (L, WIDTH_A, D), WIDTH_A ** -0.5),
        'conv_w': nrm(ks[11], (L, CONV_WIDTH, WIDTH_B), CONV_WIDTH ** -0.5),
        'conv_b': nrm(ks[12], (L, WIDTH_B), 0.02),
        'conv_ln_g': 1.0 + nrm(ks[13], (L, WIDTH_B), 0.02),
        'conv_ln_b': nrm(ks[14], (L, WIDTH_B), 0.02),
        'w_proj_b': nrm(ks[15], (L, WIDTH_B, D), WIDTH_B ** -0.5),
        'w_proj_c': nrm(ks[16], (L, WIDTH_C, D), WIDTH_C ** -0.5),
        'w_gate': nrm(ks[17], (L, D, N_BRANCH * D), D ** -0.5),
        'b_gate': nrm(ks[18], (L, N_BRANCH * D), 0.02),
        'w_out': nrm(ks[19], (L, D, D), D ** -0.5),
        'norm2_g': 1.0 + nrm(ks[20], (L, D), 0.02),
        'w_ffn_in': nrm(ks[21], (L, D, 2 * D_FF), D ** -0.5),
        'w_ffn_out': nrm(ks[22], (L, D_FF, D), D_FF ** -0.5),
    }


def reference(x, norm1_g, w_in, qnorm_g, knorm_g, lambda_q1, lambda_k1, lambda_q2, lambda_k2,
              subln_g, w_proj_a, conv_w, conv_b, conv_ln_g, conv_ln_b, w_proj_b, w_proj_c,
              w_gate, b_gate, w_out, norm2_g, w_ffn_in, w_ffn_out):
    b, s, d = x.shape
    cos, sin = rope_tables(s)
    for l in range(DEPTH):
        h = rmsnorm(x, norm1_g[l])
        u = h @ w_in[l]
        o = 0
        q = u[..., o:o + WIDTH_A].reshape(b, s, N_HEADS_A, 2, D_COMP_A); o += WIDTH_A
        k = u[..., o:o + WIDTH_A].reshape(b, s, N_HEADS_A, 2, D_COMP_A); o += WIDTH_A
        v = u[..., o:o + WIDTH_A].reshape(b, s, N_HEADS_A, D_V_A); o += WIDTH_A
        glu_in = u[..., o:o + 2 * WIDTH_B]; o += 2 * WIDTH_B
        four_in = u[..., o:o + WIDTH_C]

        q = apply_partial_rope(rmsnorm(q, qnorm_g[l]), cos, sin)
        k = apply_partial_rope(rmsnorm(k, knorm_g[l]), cos, sin)
        lam_init = 0.8 - 0.6 * math.exp(-0.3 * l)
        lam = (jnp.exp(jnp.sum(lambda_q1[l].astype(jnp.float32) * lambda_k1[l].astype(jnp.float32)))
               - jnp.exp(jnp.sum(lambda_q2[l].astype(jnp.float32) * lambda_k2[l].astype(jnp.float32)))
               + lam_init)
        att = diff_attention(q, k, v, lam)
        att = rmsnorm(att, subln_g[l]) * (1.0 - lam_init)
        y_a = att.reshape(b, s, WIDTH_A) @ w_proj_a[l]

        ga, gb = jnp.split(glu_in, 2, axis=-1)
        g = ga * jax.nn.sigmoid(gb)
        cv = lax.conv_general_dilated(
            g, conv_w[l].reshape(CONV_WIDTH, 1, WIDTH_B), window_strides=(1,),
            padding=[((CONV_WIDTH - 1) // 2, (CONV_WIDTH - 1) // 2)],
            dimension_numbers=('NWC', 'WIO', 'NWC'), feature_group_count=WIDTH_B)
        cv = layernorm(cv + conv_b[l], conv_ln_g[l], conv_ln_b[l])
        y_b = jax.nn.silu(cv) @ w_proj_b[l]

        fc = four_in.astype(jnp.float32).reshape(b, s, N_GROUPS_C, GROUP_C)
        fr = jnp.fft.fft2(fc, axes=(1, 3), norm='ortho').real.astype(x.dtype)
        y_c = fr.reshape(b, s, WIDTH_C) @ w_proj_c[l]

        gates = jax.nn.sigmoid(h @ w_gate[l] + b_gate[l]).reshape(b, s, N_BRANCH, d)
        merged = gates[:, :, 0] * y_a + gates[:, :, 1] * y_b + gates[:, :, 2] * y_c
        x = x + merged @ w_out[l]

        h2 = rmsnorm(x, norm2_g[l])
        f_gate, f_up = jnp.split(h2 @ w_ffn_in[l], 2, axis=-1)
        x = x + (jax.nn.silu(f_gate) * f_up) @ w_ffn_out[l]
    return x
```

```python
import math
from contextlib import ExitStack
import numpy as np
import ml_dtypes
import concourse.bass as bass
import concourse.mybir as mybir
from concourse.bass_utils import run_bass_kernel_spmd

F32 = mybir.dt.float32
BF16 = mybir.dt.bfloat16
AF = mybir.ActivationFunctionType
ALU = mybir.AluOpType

NCORES = 8
D = 1024
SEQ = 8192
NT = 2048
NTB = 4
EPS = 1e-6
DFF = 2816
ROPE_THETA = 500000.0


class Res:
    __slots__ = ("w", "r", "name")

    def __init__(self, name=""):
        self.w = None
        self.r = []
        self.name = name


class Ev:
    __slots__ = ("q", "idx", "needed", "sem", "val")

    def __init__(self, q, idx):
        self.q = q
        self.idx = idx
        self.needed = False
        self.sem = None
        self.val = None


COMPUTE_Q = ("pe", "act", "dve", "pool")
DMA_Q = ("sp", "actq", "poolq")
ENGINE_OF = {"pe": "tensor", "act": "scalar", "dve": "vector", "pool": "gpsimd",
             "sp": "sync", "actq": "scalar", "poolq": "gpsimd"}
NDMASEM = 6


class Prog:
    def __init__(self, nc):
        self.nc = nc
        self.stack = ExitStack()
        self.streams = {"tensor": [], "scalar": [], "vector": [], "gpsimd": [], "sync": []}
        self.evcount = {q: 0 for q in COMPUTE_Q + DMA_Q}
        self.dma_n = {q: 0 for q in DMA_Q}
        self.dma_last = {}
        self.sems = {}
        self.final_events = []

    def sb(self, name, shape, dt):
        return self.stack.enter_context(self.nc.sbuf_tensor(name, list(shape), dt))

    def ps(self, name, shape, dt=F32):
        return self.stack.enter_context(self.nc.psum_tensor(name, list(shape), dt))

    def dram(self, name, shape, dt, kind):
        return self.nc.dram_tensor(name, list(shape), dt, kind=kind).ap()

    def op(self, q, name, kw, rd=(), wr=()):
        fn = (name, kw)
        deps = []
        for r in rd:
            if r.w is not None:
                deps.append(r.w)
        for w in wr:
            if w.w is not None:
                deps.append(w.w)
            deps.extend(w.r)
        self.evcount[q] += 1
        ev = Ev(q, self.evcount[q])
        if q in DMA_Q:
            slot = self.dma_n[q] % NDMASEM
            self.dma_n[q] += 1
            prev = self.dma_last.get((q, slot))
            if prev is not None:
                deps.append(prev)
            self.dma_last[(q, slot)] = ev
            ev.sem = (q, slot)
        else:
            ev.sem = (q, 0)
        best = {}
        dmas = []
        for d in deps:
            if d.q in COMPUTE_Q:
                if d.q == "pe" and q == "pe":
                    continue
                if d.q not in best or best[d.q].idx < d.idx:
                    best[d.q] = d
            else:
                if d not in dmas:
                    dmas.append(d)
        waits = list(best.values()) + dmas
        self.streams[ENGINE_OF[q]].append((q, fn, waits, ev))
        for r in rd:
            r.r.append(ev)
        for w in wr:
            w.w = ev
            w.r = []
        return ev

    def finalize(self, block):
        nc = self.nc
        for eng, stream in self.streams.items():
            seen = {}
            for item in stream:
                q, fn, waits, ev = item
                keep = []
                for d in waits:
                    if d.q in COMPUTE_Q:
                        if seen.get(d.q, 0) >= d.idx:
                            continue
                        seen[d.q] = d.idx
                    else:
                        if seen.get(id(d)):
                            continue
                        seen[id(d)] = True
                    d.needed = True
                    keep.append(d)
                item[2][:] = keep
        for ev in self.final_events:
            ev.needed = True
        counters = {}
        for eng, stream in self.streams.items():
            for q, fn, waits, ev in stream:
                if q in DMA_Q:
                    key = ev.sem
                    counters[key] = counters.get(key, 0) + 16
                    ev.val = counters[key]
                    ev.needed = True
                elif ev.needed:
                    key = ev.sem
                    counters[key] = counters.get(key, 0) + 1
                    ev.val = counters[key]
        for key in counters:
            self.sems[key] = self.stack.enter_context(nc.semaphore("s_%s_%d" % key))
        self.maxcount = dict(counters)

        def emit(engname):
            stream = self.streams[engname]

            def body(eng):
                for q, fn, waits, ev in stream:
                    for d in waits:
                        eng.wait_ge(self.sems[d.sem], d.val)
                    ins = getattr(eng, fn[0])(**fn[1])
                    if ev.needed:
                        ins.then_inc(self.sems[ev.sem], 16 if q in DMA_Q else 1)
                if engname == "sync":
                    for ev in self.final_events:
                        eng.wait_ge(self.sems[ev.sem], ev.val)
            return body

        block.tensor(emit("tensor"))
        block.scalar(emit("scalar"))
        block.vector(emit("vector"))
        block.gpsimd(emit("gpsimd"))
        block.sync(emit("sync"))


def bf(a):
    return np.ascontiguousarray(np.asarray(a, dtype=np.float32)).astype(ml_dtypes.bfloat16)


def const_tables(core):
    a = core % 4
    t = {}
    t["ones_d"] = bf(np.full((128, 128), 1.0 / 1024.0))
    bd = np.zeros((128, 128), np.float32)
    for i in range(4):
        bd[32 * i:32 * i + 32, 32 * i:32 * i + 32] = 1.0 / 32.0
    t["bd32"] = bf(bd)
    rot = np.zeros((128, 128), np.float32)
    for blk in range(4):
        o = 32 * blk
        for i in range(4):
            rot[o + 4 + i, o + i] = -1.0
            rot[o + i, o + 4 + i] = 1.0
    t["rot"] = bf(rot)
    pos = (a * NT + np.arange(NT)).astype(np.float64)
    inv = 1.0 / (ROPE_THETA ** (np.arange(0, 8, 2, dtype=np.float64) / 8.0))
    ang = pos[None, :] * inv[:, None]
    cf = np.ones((128, NT), np.float32)
    sf = np.zeros((128, NT), np.float32)
    for blk in range(4):
        o = 32 * blk
        cf[o:o + 4] = np.cos(ang)
        cf[o + 4:o + 8] = np.cos(ang)
        sf[o:o + 4] = np.sin(ang)
        sf[o + 4:o + 8] = np.sin(ang)
    t["cosf"] = cf
    t["sinf"] = sf
    jc = np.outer(np.arange(128), np.arange(128)).astype(np.float64) * (2 * np.pi / 128.0)
    t["dftg"] = bf(np.concatenate([np.cos(jc), -np.sin(jc)], axis=1))
    return t


def build_A():
    nc = bass.Bass("TRN2", target_bir_lowering=False)
    P = Prog(nc)
    xT_d = P.dram("xT", [D, NT], F32, "ExternalInput")
    win_d = P.dram("w_in", [D, 3072], F32, "ExternalInput")
    g1_d = P.dram("g1", [128, 8], F32, "ExternalInput")
    gqk_d = P.dram("gqk", [128, 2], F32, "ExternalInput")
    cos_d = P.dram("cosf", [128, NT], F32, "ExternalInput")
    sin_d = P.dram("sinf", [128, NT], F32, "ExternalInput")
    ones_d = P.dram("ones_d", [128, 128], BF16, "ExternalInput")
    bd_d = P.dram("bd32", [128, 128], BF16, "ExternalInput")
    rot_d = P.dram("rot", [128, 128], BF16, "ExternalInput")
    dftg_d = P.dram("dftg", [128, 256], BF16, "ExternalInput")
    qT_o = P.dram("qT", [512, NT], BF16, "ExternalOutput")
    kT_o = P.dram("kT", [512, NT], BF16, "ExternalOutput")
    vp_o = P.dram("vp", [4, NT, 130], BF16, "ExternalOutput")
    gT_o = P.dram("gT", [512, NT], BF16, "ExternalOutput")
    zp_o = P.dram("zp", [4, NT, 256], BF16, "ExternalOutput")

    xT = P.sb("xT_sb", [128, 8, NT], F32)
    W = P.sb("w_sb", [128, 8, 3072], BF16)
    g1 = P.sb("g1_sb", [128, 8], F32)
    gqk = P.sb("gqk_sb", [128, 2], F32)
    cosf = P.sb("cos_sb", [128, NT], F32)
    sinf = P.sb("sin_sb", [128, NT], F32)
    onesm = P.sb("ones_sb", [128, 128], BF16)
    bdm = P.sb("bd_sb", [128, 128], BF16)
    rotm = P.sb("rot_sb", [128, 128], BF16)
    dftg = P.sb("dftg_sb", [128, 256], BF16)
    hT = P.sb("hT", [128, 8, 512], BF16)
    sq = P.sb("sq", [128, 2, 512], BF16)
    lnv = P.sb("lnv", [128, 512], F32)
    rstd = P.sb("rstd", [128, 512], F32)
    sq2 = P.sb("sq2", [128, 2, 512], BF16)
    ln2 = P.sb("ln2", [128, 2, 512], F32)
    r2 = P.sb("r2", [128, 2, 512], F32)
    qn = P.sb("qn", [128, 2, 512], BF16)
    t1 = P.sb("t1", [128, 2, 512], F32)
    t2 = P.sb("t2", [128, 2, 512], F32)
    qkst = P.sb("qkst", [128, 1, 8, 512], BF16)
    vst = P.sb("vst", [128, 1, 4, 4, 130], BF16)
    sg = P.sb("sg", [128, 2, 512], F32)
    gst = P.sb("gst", [128, 1, 4, 512], BF16)
    fcT = P.sb("fcT", [128, 2, 512], BF16)
    zst = P.sb("zst", [128, 1, 4, 4, 256], BF16)
    PS = P.ps("ps", [128, 8, 512], F32)

    R = lambda n: Res(n)
    r_x = [R("x%d" % i) for i in range(8)]
    r_w = [R("w%d" % i) for i in range(6)]
    r_c = R("consts")
    r_h = R("hT")
    r_sq = [R("sq0"), R("sq1")]
    r_ln = R("lnv")
    r_rstd = R("rstd")
    r_bank = [R("bank%d" % i) for i in range(8)]
    r_sq2 = [R("a"), R("b")]
    r_ln2 = [R("a"), R("b")]
    r_r2 = [R("a"), R("b")]
    r_qn = [R("a"), R("b")]
    r_t1 = [R("a"), R("b")]
    r_t2 = [R("a"), R("b")]
    r_qk = [[R("qk%d" % i) for i in range(8)] for _ in range(2)]
    r_v = [R("vst0"), R("vst1")]
    r_sg = [R("a"), R("b")]
    r_g = [[R("g%d" % i) for i in range(4)] for _ in range(2)]
    r_fc = [R("a"), R("b")]
    r_z = [R("zst0"), R("zst1")]

    for kc in range(8):
        P.op("sp", "dma_start", dict(out=xT[:, kc, :], in_=xT_d[kc * 128:(kc + 1) * 128, :]),
             wr=[r_x[kc]])
    cl = [(g1, g1_d), (gqk, gqk_d), (cosf, cos_d), (sinf, sin_d), (onesm, ones_d), (bdm, bd_d),
          (rotm, rot_d), (dftg, dftg_d)]
    for i, (s, d_) in enumerate(cl):
        P.op("sp", "dma_start", dict(out=s[:], in_=d_[:, :]), wr=[r_c])
    for wb in range(6):
        for kc in range(8):
            P.op("poolq", "dma_start", dict(
                out=W[:, kc, wb * 512:(wb + 1) * 512],
                in_=win_d[kc * 128:(kc + 1) * 128, wb * 512:(wb + 1) * 512]), wr=[r_w[wb]])
    for sl in range(1):
        P.op("pool", "memset", dict(ap=vst[:, sl, :, :, 64:65], constant=1.0), wr=[r_v[sl]])
        P.op("pool", "memset", dict(ap=vst[:, sl, :, :, 129:130], constant=1.0), wr=[r_v[sl]])

    fin = []
    bank_rr = [0]

    def next_bank():
        b = bank_rr[0] % 4
        bank_rr[0] += 1
        return b

    def proj_chunk(oc, tb, bank):
        wb = (oc * 128) // 512
        for kc in range(8):
            P.op("pe", "matmul", dict(out=PS[:, bank, :], lhsT=W[:, kc, oc * 128:(oc + 1) * 128],
                                                 rhs=hT[:, kc, :], start=(kc == 0), stop=(kc == 7)),
                 rd=[r_w[wb], r_h, r_c], wr=[r_bank[bank]])

    for tb in range(NTB):
        ts_ = slice(tb * 512, (tb + 1) * 512)
        sl = 0
        for kc in range(8):
            s = kc % 2
            P.op("pool", "tensor_tensor", dict(out=sq[:, s, :], in0=xT[:, kc, ts_], in1=xT[:, kc, ts_],
                                                              op=ALU.mult), rd=[r_x[kc]], wr=[r_sq[s]])
            P.op("pe", "matmul", dict(out=PS[:, 4, :], lhsT=onesm[:], rhs=sq[:, s, :],
                                                      start=(kc == 0), stop=(kc == 7)),
                 rd=[r_sq[s], r_c], wr=[r_bank[4]])
        P.op("act", "activation", dict(out=lnv[:], in_=PS[:, 4, :], func=AF.Ln, bias=EPS, scale=1.0),
             rd=[r_bank[4]], wr=[r_ln])
        P.op("act", "activation", dict(out=rstd[:], in_=lnv[:], func=AF.Exp, scale=-0.5),
             rd=[r_ln], wr=[r_rstd])
        for kc in range(8):
            P.op("dve", "scalar_tensor_tensor", dict(out=hT[:, kc, :], in0=xT[:, kc, ts_],
                                                                scalar=g1[:, kc:kc + 1], in1=rstd[:],
                                                                op0=ALU.mult, op1=ALU.mult),
                 rd=[r_x[kc], r_rstd, r_c], wr=[r_h])
        for oc in range(8):
            s = oc % 2
            bank = next_bank()
            proj_chunk(oc, tb, bank)
            P.op("act", "activation", dict(out=sq2[:, s, :], in_=PS[:, bank, :], func=AF.Square),
                 rd=[r_bank[bank]], wr=[r_sq2[s]])
            P.op("pe", "matmul", dict(out=PS[:, 5, :], lhsT=bdm[:], rhs=sq2[:, s, :], start=True, stop=True),
                 rd=[r_sq2[s], r_c], wr=[r_bank[5]])
            P.op("act", "activation", dict(out=ln2[:, s, :], in_=PS[:, 5, :], func=AF.Ln, bias=EPS, scale=1.0),
                 rd=[r_bank[5]], wr=[r_ln2[s]])
            P.op("act", "activation", dict(out=r2[:, s, :], in_=ln2[:, s, :], func=AF.Exp, scale=-0.5),
                 rd=[r_ln2[s]], wr=[r_r2[s]])
            gi = 0 if oc < 4 else 1
            P.op("dve", "scalar_tensor_tensor", dict(
                out=qn[:, s, :], in0=PS[:, bank, :], scalar=gqk[:, gi:gi + 1], in1=r2[:, s, :],
                op0=ALU.mult, op1=ALU.mult), rd=[r_bank[bank], r_r2[s], r_c], wr=[r_qn[s]])
            P.op("pe", "matmul", dict(out=PS[:, 6, :], lhsT=rotm[:], rhs=qn[:, s, :], start=True, stop=True),
                 rd=[r_qn[s], r_c], wr=[r_bank[6]])
            P.op("pool", "tensor_tensor", dict(out=t1[:, s, :], in0=qn[:, s, :], in1=cosf[:, ts_], op=ALU.mult),
                 rd=[r_qn[s], r_c], wr=[r_t1[s]])
            P.op("dve", "tensor_tensor", dict(out=t2[:, s, :], in0=PS[:, 6, :], in1=sinf[:, ts_], op=ALU.mult),
                 rd=[r_bank[6], r_c], wr=[r_t2[s]])
            P.op("pool", "tensor_tensor", dict(out=qkst[:, sl, oc, :], in0=t1[:, s, :], in1=t2[:, s, :],
                                                              op=ALU.add), rd=[r_t1[s], r_t2[s]], wr=[r_qk[sl][oc]])
        for tt in range(4):
            tti = tt
            bank = next_bank()
            for kc in range(8):
                P.op("pe", "matmul", dict(
                    out=PS[:, bank, :], lhsT=hT[:, kc, tt * 128:(tt + 1) * 128], rhs=W[:, kc, 1024:1536],
                    start=(kc == 0), stop=(kc == 7)), rd=[r_w[2], r_h], wr=[r_bank[bank]])
            for hp in range(4):
                for h2 in range(2):
                    eng = "act" if h2 == 0 else "dve"
                    src = (hp * 2 + h2) * 64
                    if eng == "act":
                        P.op("act", "activation", dict(
                            out=vst[:, sl, tti, hp, h2 * 65:h2 * 65 + 64], in_=PS[:, bank, src:src + 64], func=AF.Copy),
                            rd=[r_bank[bank]], wr=[r_v[sl]])
                    else:
                        P.op("dve", "tensor_copy", dict(
                            out=vst[:, sl, tti, hp, h2 * 65:h2 * 65 + 64], in_=PS[:, bank, src:src + 64]),
                            rd=[r_bank[bank]], wr=[r_v[sl]])
        for j in range(4):
            s = j % 2
            ba = next_bank()
            proj_chunk(12 + j, tb, ba)
            bb = next_bank()
            proj_chunk(16 + j, tb, bb)
            P.op("act", "activation", dict(out=sg[:, s, :], in_=PS[:, bb, :], func=AF.Sigmoid),
                 rd=[r_bank[bb]], wr=[r_sg[s]])
            P.op("dve", "tensor_tensor", dict(out=gst[:, sl, j, :], in0=PS[:, ba, :], in1=sg[:, s, :],
                                                                  op=ALU.mult),
                 rd=[r_bank[ba], r_sg[s]], wr=[r_g[sl][j]])
        for gr in range(4):
            s = gr % 2
            bank = next_bank()
            proj_chunk(20 + gr, tb, bank)
            P.op("act", "activation", dict(out=fcT[:, s, :], in_=PS[:, bank, :], func=AF.Copy),
                 rd=[r_bank[bank]], wr=[r_fc[s]])
            for tp in range(2):
                for t_ in range(2):
                    tt = tp * 2 + t_
                    P.op("pe", "matmul", dict(
                        out=PS[:, 7, t_ * 256:(t_ + 1) * 256], lhsT=fcT[:, s, tt * 128:(tt + 1) * 128], rhs=dftg[:],
                        start=True, stop=True), rd=[r_fc[s], r_c], wr=[r_bank[7]])
                for t_ in range(2):
                    tti = tp * 2 + t_
                    P.op("dve", "tensor_copy", dict(
                        out=zst[:, sl, tti, gr, :], in_=PS[:, 7, t_ * 256:(t_ + 1) * 256]),
                        rd=[r_bank[7]], wr=[r_z[sl]])

        for oc in range(8):
            dst = qT_o if oc < 4 else kT_o
            o = (oc % 4) * 128
            fin.append(P.op("sp", "dma_start", dict(
                out=dst[o:o + 128, ts_], in_=qkst[:, sl, oc, :]), rd=[r_qk[sl][oc]]))
        for j in range(4):
            fin.append(P.op("sp", "dma_start", dict(
                out=gT_o[j * 128:(j + 1) * 128, ts_], in_=gst[:, sl, j, :]), rd=[r_g[sl][j]]))
        for hp in range(4):
            fin.append(P.op("sp", "dma_start", dict(
                out=vp_o[hp, ts_, :].rearrange("(t p) c -> p t c", p=128), in_=vst[:, sl, :, hp, :]), rd=[r_v[sl]]))
        for gr in range(4):
            fin.append(P.op("sp", "dma_start", dict(
                out=zp_o[gr, ts_, :].rearrange("(t p) c -> p t c", p=128), in_=zst[:, sl, :, gr, :]), rd=[r_z[sl]]))
    P.final_events = fin
    with P.stack:
        with nc.Block() as block:
            P.finalize(block)
    return nc


def _core_slices():
    return [(c // 4, (c % 4) * NT) for c in range(NCORES)]


def run_A(x_T_list, l, inp):
    nc = build_A()
    in_maps = []
    for c in range(NCORES):
        ct = const_tables(c)
        g1 = np.ascontiguousarray(inp["norm1_g"][l].reshape(8, 128).T)
        gqk = np.stack([np.tile(inp["qnorm_g"][l], 4), np.tile(inp["knorm_g"][l], 4)], axis=1).astype(np.float32)
        in_maps.append({"xT": x_T_list[c], "w_in": np.ascontiguousarray(inp["w_in"][l]), "g1": g1,
                        "gqk": np.ascontiguousarray(gqk), "cosf": ct["cosf"], "sinf": ct["sinf"],
                        "ones_d": ct["ones_d"], "bd32": ct["bd32"], "rot": ct["rot"], "dftg": ct["dftg"]})
    res = run_bass_kernel_spmd(nc, in_maps, core_ids=list(range(NCORES)))
    return res.results


def lam_init_of(l):
    return 0.8 - 0.6 * math.exp(-0.3 * l)


def build_B1(l):
    nc = bass.Bass("TRN2", target_bir_lowering=False)
    P = Prog(nc)
    qT_d = P.dram("qT", [512, NT], BF16, "ExternalInput")
    kT_d = P.dram("kT_all", [4, 512, NT], BF16, "ExternalInput")
    vp_d = P.dram("vp_all", [4, 4, NT, 130], BF16, "ExternalInput")
    lam_d = P.dram("lamv", [128, 4, 32], F32, "ExternalInput")
    gsub_d = P.dram("gsub", [64, 1], F32, "ExternalInput")
    sel_d = P.dram("sel", [128, 64], F32, "ExternalInput")
    o64_d = P.dram("ones64", [64, 64], BF16, "ExternalInput")
    aT_o = P.dram("aT", [8, 64, NT], BF16, "ExternalOutput")

    qT = P.sb("qT_sb", [128, 4, NT], BF16)
    KT = P.sb("KT_sb", [128, 2, SEQ], BF16)
    VP = P.sb("VP_sb", [128, 2, 64, 130], BF16)
    lamv = P.sb("lamv_sb", [128, 4, 32], F32)
    lprod = P.sb("lprod", [128, 2, 32], F32)
    lsum = P.sb("lsum", [128, 2], F32)
    lexp = P.sb("lexp", [128, 2], F32)
    neglam = P.sb("neglam", [128, 1], F32)
    gsub = P.sb("gsub_sb", [64, 1], F32)
    gsub2 = P.sb("gsub2_sb", [64, 1], F32)
    sel = P.sb("sel_sb", [128, 64], F32)
    o64 = P.sb("o64_sb", [64, 64], BF16)
    E = P.sb("E_sb", [128, 3, 2, 512], BF16)
    Osb = P.sb("Osb", [65, 2, 512], F32)
    Rr = P.sb("Rr", [64, 2, 512], F32)
    tt0 = P.sb("tt0", [64, 2, 512], F32)
    att = P.sb("att", [64, 512], F32)
    sqa = P.sb("sqa", [64, 512], BF16)
    lna = P.sb("lna", [64, 512], F32)
    rsa = P.sb("rsa", [64, 512], F32)
    aT = P.sb("aT_sb", [64, 8, NT], BF16)
    PS = P.ps("ps", [128, 8, 512], F32)

    R = Res
    r_q = [R() for _ in range(4)]
    r_kt = [[R() for _ in range(4)] for _ in range(2)]
    r_vp = [[R() for _ in range(4)] for _ in range(2)]
    r_c = R()
    r_lam = R()
    r_bank = [R() for _ in range(8)]
    r_E = [R() for _ in range(3)]
    r_O = [R(), R()]
    r_R = [R(), R()]
    r_t = [R(), R()]
    r_att, r_sq, r_ln, r_rs = R(), R(), R(), R()
    r_aT = [[R() for _ in range(4)] for _ in range(8)]

    lam_init = lam_init_of(l)
    for hp in range(4):
        P.op("sp", "dma_start", dict(out=qT[:, hp, :], in_=qT_d[hp * 128:(hp + 1) * 128, :]), wr=[r_q[hp]])
    P.op("sp", "dma_start", dict(out=lamv[:], in_=lam_d[:, :, :]), wr=[r_lam])
    P.op("sp", "dma_start", dict(out=gsub[:], in_=gsub_d[:, :]), wr=[r_c])
    P.op("sp", "dma_start", dict(out=sel[:], in_=sel_d[:, :]), wr=[r_c])
    P.op("sp", "dma_start", dict(out=o64[:], in_=o64_d[:, :]), wr=[r_c])

    def load_kv(hp):
        s = hp % 2
        for r in range(4):
            P.op("sp", "dma_start", dict(out=KT[:, s, r * NT:(r + 1) * NT], in_=kT_d[r, hp * 128:(hp + 1) * 128, :]),
                 wr=[r_kt[s][r]])
            P.op("sp", "dma_start", dict(out=VP[32 * r:32 * r + 32, s, :, :],
                                         in_=vp_d[r, hp].rearrange("(p k) c -> p k c", k=64)), wr=[r_vp[s][r]])

    load_kv(0)
    P.op("dve", "tensor_tensor", dict(out=lprod[:, 0, :], in0=lamv[:, 0, :], in1=lamv[:, 1, :], op=ALU.mult),
         rd=[r_lam], wr=[r_att])
    P.op("dve", "tensor_tensor", dict(out=lprod[:, 1, :], in0=lamv[:, 2, :], in1=lamv[:, 3, :], op=ALU.mult),
         rd=[r_lam], wr=[r_att])
    P.op("dve", "tensor_reduce", dict(out=lsum[:], in_=lprod[:], axis=mybir.AxisListType.X, op=ALU.add),
         rd=[r_att], wr=[r_sq])
    P.op("act", "activation", dict(out=lexp[:], in_=lsum[:], func=AF.Exp), rd=[r_sq], wr=[r_ln])
    P.op("dve", "tensor_tensor", dict(out=neglam[:], in0=lexp[:, 1:2], in1=lexp[:, 0:1], op=ALU.subtract),
         rd=[r_ln], wr=[r_rs])
    P.op("dve", "tensor_scalar", dict(out=neglam[:], in0=neglam[:], scalar1=-lam_init, scalar2=None, op0=ALU.add),
         rd=[r_rs], wr=[r_rs])
    r_neglam = r_rs
    r_neglam_ev_holder = R()
    P.op("dve", "tensor_scalar", dict(out=gsub2[:], in0=gsub[:], scalar1=1.0 - lam_init, scalar2=None, op0=ALU.mult),
         rd=[r_c], wr=[r_neglam_ev_holder])
    r_g2 = r_neglam_ev_holder
    r_att, r_sq, r_ln = R(), R(), R()
    r_rs2 = R()

    scale = 32.0 ** -0.5
    ecnt = 0
    for hp in range(4):
        s = hp % 2
        if hp + 1 < 4:
            load_kv(hp + 1)
        for h2 in range(2):
            h = hp * 2 + h2
            for qb in range(4):
                qs = slice(qb * 512, (qb + 1) * 512)
                for kt in range(64):
                    sb0 = (kt % 2) * 2
                    for c in range(2):
                        i = 2 * h2 + c
                        P.op("pe", "matmul", dict(out=PS[:, sb0 + c, :], lhsT=KT[32 * i:32 * i + 32, s, kt::64],
                                                  rhs=qT[32 * i:32 * i + 32, hp, qs], start=True, stop=True,
                                                  tile_position=(32 * i, 0)),
                             rd=[r_q[hp]] + r_kt[s], wr=[r_bank[sb0 + c]])
                    eb = ecnt % 3
                    ecnt += 1
                    P.op("act", "activation", dict(out=E[:, eb, :, :], in_=PS[:, sb0:sb0 + 2, :], func=AF.Exp, scale=scale),
                         rd=[r_bank[sb0], r_bank[sb0 + 1]], wr=[r_E[eb]])
                    for c in range(2):
                        P.op("pe", "matmul", dict(out=PS[0:65, 4 + c, :], lhsT=VP[:, s, kt, h2 * 65:(h2 + 1) * 65],
                                                  rhs=E[:, eb, c, :], start=(kt == 0), stop=(kt == 63)),
                             rd=[r_E[eb]] + r_vp[s], wr=[r_bank[4 + c]])
                for c in range(2):
                    P.op("dve", "tensor_copy", dict(out=Osb[:, c, :], in_=PS[0:65, 4 + c, :]), rd=[r_bank[4 + c]], wr=[r_O[c]])
                    P.op("pe", "matmul", dict(out=PS[0:64, 6 + c, :], lhsT=sel[0:65, :], rhs=Osb[:, c, :], start=True, stop=True),
                         rd=[r_O[c], r_c], wr=[r_bank[6 + c]])
                    P.op("dve", "reciprocal", dict(out=Rr[:, c, :], in_=PS[0:64, 6 + c, :]), rd=[r_bank[6 + c]], wr=[r_R[c]])
                    P.op("pool", "tensor_tensor", dict(out=tt0[:, c, :], in0=Osb[0:64, c, :], in1=Rr[:, c, :], op=ALU.mult),
                         rd=[r_O[c], r_R[c]], wr=[r_t[c]])
                P.op("dve", "scalar_tensor_tensor", dict(out=att[:], in0=tt0[:, 1, :], scalar=neglam[0:64, 0:1], in1=tt0[:, 0, :],
                                                         op0=ALU.mult, op1=ALU.add), rd=[r_t[0], r_t[1], r_neglam], wr=[r_att])
                P.op("pool", "tensor_tensor", dict(out=sqa[:], in0=att[:], in1=att[:], op=ALU.mult), rd=[r_att], wr=[r_sq])
                P.op("pe", "matmul", dict(out=PS[0:64, 6, :], lhsT=o64[:], rhs=sqa[:], start=True, stop=True),
                     rd=[r_sq, r_c], wr=[r_bank[6]])
                P.op("act", "activation", dict(out=lna[:], in_=PS[0:64, 6, :], func=AF.Ln, bias=EPS, scale=1.0),
                     rd=[r_bank[6]], wr=[r_ln])
                P.op("act", "activation", dict(out=rsa[:], in_=lna[:], func=AF.Exp, scale=-0.5), rd=[r_ln], wr=[r_rs2])
                P.op("dve", "scalar_tensor_tensor", dict(out=aT[:, h, qs], in0=att[:], scalar=gsub2[:, 0:1], in1=rsa[:],
                                                         op0=ALU.mult, op1=ALU.mult), rd=[r_att, r_rs2, r_g2], wr=[r_aT[h][qb]])
    fin = []
    for h in range(8):
        fin.append(P.op("sp", "dma_start", dict(out=aT_o[h], in_=aT[:, h, :]), rd=r_aT[h]))
    P.final_events = fin
    with P.stack:
        with nc.Block() as block:
            P.finalize(block)
    return nc


def consts_B1():
    sel = np.zeros((128, 64), np.float32)
    sel[64, :] = 1.0
    return {"sel": sel, "ones64": bf(np.full((64, 64), 1.0 / 64.0))}


def run_B1(l, inp, qT_list, kT_list, vp_list):
    nc = build_B1(l)
    cb = consts_B1()
    lamv = np.stack([inp["lambda_q1"][l], inp["lambda_k1"][l], inp["lambda_q2"][l], inp["lambda_k2"][l]], 0)
    lamv = np.ascontiguousarray(np.broadcast_to(lamv[None], (128, 4, 32))).astype(np.float32)
    gsub = np.ascontiguousarray(inp["subln_g"][l].reshape(64, 1)).astype(np.float32)
    in_maps = []
    for c in range(NCORES):
        b = c // 4
        kT_all = np.ascontiguousarray(np.stack([kT_list[b * 4 + r] for r in range(4)], 0))
        vp_all = np.ascontiguousarray(np.stack([vp_list[b * 4 + r] for r in range(4)], 0))
        in_maps.append({"qT": qT_list[c], "kT_all": kT_all, "vp_all": vp_all, "lamv": lamv, "gsub": gsub,
                        "sel": cb["sel"], "ones64": cb["ones64"]})
    res = run_bass_kernel_spmd(nc, in_maps, core_ids=list(range(NCORES)))
    return res.results


def build_B2():
    nc = bass.Bass("TRN2", target_bir_lowering=False)
    P = Prog(nc)
    g_d = P.dram("gT", [512, NT], BF16, "ExternalInput")
    gall_d = P.dram("g_all", [4, 512, NT], BF16, "ExternalInput")
    cw_d = P.dram("conv_w", [128, 4, 31], F32, "ExternalInput")
    cv4_d = P.dram("cvec", [128, 3, 4], F32, "ExternalInput")
    cm_d = P.dram("cmask", [128, 8], F32, "ExternalInput")
    id_d = P.dram("ident", [128, 128], BF16, "ExternalInput")
    o512_d = P.dram("ones512", [128, 128], F32, "ExternalInput")
    bT_o = P.dram("bT", [512, NT], BF16, "ExternalOutput")

    gp = P.sb("gpad", [128, 4, NT + 30], BF16)
    hal = P.sb("hal", [128, 4, 4, 2, 15], BF16)
    cw = P.sb("cw", [128, 4, 31], F32)
    cv4 = P.sb("cv4", [128, 3, 4], F32)
    cm = P.sb("cm", [128, 8], F32)
    ident = P.sb("ident_sb", [128, 128], BF16)
    o512 = P.sb("o512", [128, 128], F32)
    Dg = P.sb("Dg", [128, 4, 31, 128], BF16)
    cv = P.sb("cv", [128, 4, 512], F32)
    sqv = P.sb("sqv", [128, 4, 512], F32)
    m2 = P.sb("m2", [128, 512], F32)
    var = P.sb("var", [128, 512], F32)
    lnv = P.sb("lnv", [128, 512], F32)
    rstd = P.sb("rstd", [128, 512], F32)
    nmr = P.sb("nmr", [128, 512], F32)
    yv = P.sb("yv", [128, 2, 512], F32)
    bst = P.sb("bst", [128, 4, NT], BF16)
    PS = P.ps("ps", [128, 8, 512], F32)

    R = Res
    r_g = [R() for _ in range(4)]
    r_hal, r_c, r_D = R(), R(), [R() for _ in range(4)]
    r_bank = [R() for _ in range(8)]
    r_cv = [R() for _ in range(4)]
    r_sq = [R() for _ in range(4)]
    r_m2, r_var, r_ln, r_rstd, r_nmr = R(), R(), R(), R(), R()
    r_y = [R(), R()]
    r_b = [R() for _ in range(4)]

    for j in range(4):
        P.op("sp", "dma_start", dict(out=gp[:, j, 15:15 + NT], in_=g_d[j * 128:(j + 1) * 128, :]), wr=[r_g[j]])
    for i, (s_, d_) in enumerate([(cw, cw_d), (cv4, cv4_d), (cm, cm_d), (ident, id_d), (o512, o512_d)]):
        P.op("sp", "dma_start", dict(out=s_[:], in_=d_), wr=[r_c])
    rh = [R() for _ in range(8)]
    for r in range(4):
        gv = gall_d[r].rearrange("(j p) t -> p j t", p=128)
        P.op("sp", "dma_start", dict(out=hal[:, :, r, 0, :], in_=gv[:, :, NT - 15:NT]), wr=[rh[2 * r]])
        P.op("sp", "dma_start", dict(out=hal[:, :, r, 1, :], in_=gv[:, :, 0:15]), wr=[rh[2 * r + 1]])
    for side in range(2):
        dst = gp[:, :, 0:15] if side == 0 else gp[:, :, 15 + NT:30 + NT]
        for r in range(4):
            mk = cm[:, side * 4 + r:side * 4 + r + 1]
            if r == 0:
                P.op("dve", "tensor_scalar", dict(out=dst, in0=hal[:, :, r, side, :], scalar1=mk, scalar2=None, op0=ALU.mult),
                     rd=[rh[2 * r + side], r_c], wr=r_g)
            else:
                P.op("dve", "scalar_tensor_tensor", dict(out=dst, in0=hal[:, :, r, side, :], scalar=mk, in1=dst,
                                                         op0=ALU.mult, op1=ALU.add), rd=[rh[2 * r + side], r_c], wr=r_g)
    n = 0
    for j in range(4):
        for tau in range(31):
            q = "dve" if n % 2 == 0 else "pool"
            n += 1
            P.op(q, "tensor_scalar", dict(out=Dg[:, j, tau, :], in0=ident[:], scalar1=cw[:, j, tau:tau + 1], scalar2=None,
                                          op0=ALU.mult), rd=[r_c], wr=[r_D[j]])
    fin = []
    for tb in range(NTB):
        ts_ = slice(tb * 512, (tb + 1) * 512)
        for j in range(4):
            bank = j
            for tau in range(31):
                P.op("pe", "matmul", dict(out=PS[:, bank, :], lhsT=Dg[:, j, tau, :],
                                          rhs=gp[:, j, tb * 512 + tau:tb * 512 + tau + 512], start=(tau == 0), stop=(tau == 30)),
                     rd=[r_D[j], r_g[j]], wr=[r_bank[bank]])
            P.op("act", "activation", dict(out=cv[:, j, :], in_=PS[:, bank, :], func=AF.Identity, bias=cv4[:, 0, j:j + 1], scale=1.0),
                 rd=[r_bank[bank], r_c], wr=[r_cv[j]])
            P.op("pool", "tensor_tensor", dict(out=sqv[:, j, :], in0=cv[:, j, :], in1=cv[:, j, :], op=ALU.mult),
                 rd=[r_cv[j]], wr=[r_sq[j]])
        for j in range(4):
            P.op("pe", "matmul", dict(out=PS[:, 4, :], lhsT=o512[:], rhs=cv[:, j, :], start=(j == 0), stop=(j == 3)),
                 rd=[r_cv[j], r_c], wr=[r_bank[4]])
        for j in range(4):
            P.op("pe", "matmul", dict(out=PS[:, 5, :], lhsT=o512[:], rhs=sqv[:, j, :], start=(j == 0), stop=(j == 3)),
                 rd=[r_sq[j], r_c], wr=[r_bank[5]])
        P.op("act", "activation", dict(out=m2[:], in_=PS[:, 4, :], func=AF.Square), rd=[r_bank[4]], wr=[r_m2])
        P.op("dve", "tensor_tensor", dict(out=var[:], in0=PS[:, 5, :], in1=m2[:], op=ALU.subtract), rd=[r_bank[5], r_m2], wr=[r_var])
        P.op("act", "activation", dict(out=lnv[:], in_=var[:], func=AF.Ln, bias=EPS, scale=1.0), rd=[r_var], wr=[r_ln])
        P.op("act", "activation", dict(out=rstd[:], in_=lnv[:], func=AF.Exp, scale=-0.5), rd=[r_ln], wr=[r_rstd])
        P.op("dve", "scalar_tensor_tensor", dict(out=nmr[:], in0=PS[:, 4, :], scalar=-1.0, in1=rstd[:], op0=ALU.mult, op1=ALU.mult),
             rd=[r_bank[4], r_rstd], wr=[r_nmr])
        for j in range(4):
            s = j % 2
            P.op("pool", "tensor_tensor", dict(out=yv[:, s, :], in0=cv[:, j, :], in1=rstd[:], op=ALU.mult),
                 rd=[r_cv[j], r_rstd], wr=[r_y[s]])
            P.op("dve", "tensor_tensor", dict(out=yv[:, s, :], in0=yv[:, s, :], in1=nmr[:], op=ALU.add),
                 rd=[r_nmr], wr=[r_y[s]])
            P.op("act", "activation", dict(out=bst[:, j, ts_], in_=yv[:, s, :], func=AF.Silu, bias=cv4[:, 2, j:j + 1],
                                           scale=cv4[:, 1, j:j + 1]), rd=[r_y[s], r_c], wr=[r_b[j]])
    for j in range(4):
        fin.append(P.op("sp", "dma_start", dict(out=bT_o[j * 128:(j + 1) * 128, :], in_=bst[:, j, :]), rd=[r_b[j]]))
    P.final_events = fin
    with P.stack:
        with nc.Block() as block:
            P.finalize(block)
    return nc


def run_B2(l, inp, gT_list):
    nc = build_B2()
    cw = np.ascontiguousarray(inp["conv_w"][l].T.reshape(4, 128, 31).transpose(1, 0, 2)).astype(np.float32)
    cvec = np.stack([inp["conv_b"][l].reshape(4, 128).T, inp["conv_ln_g"][l].reshape(4, 128).T,
                     inp["conv_ln_b"][l].reshape(4, 128).T], axis=1).astype(np.float32)
    ident = bf(np.eye(128))
    o512 = np.full((128, 128), 1.0 / 512.0, np.float32)
    in_maps = []
    for c in range(NCORES):
        b, a = c // 4, c % 4
        cm = np.zeros((128, 8), np.float32)
        if a > 0:
            cm[:, a - 1] = 1.0
        if a < 3:
            cm[:, 4 + a + 1] = 1.0
        g_all = np.ascontiguousarray(np.stack([gT_list[b * 4 + r] for r in range(4)], 0))
        in_maps.append({"gT": gT_list[c], "g_all": g_all, "conv_w": cw, "cvec": np.ascontiguousarray(cvec), "cmask": cm,
                        "ident": ident, "ones512": o512})
    res = run_bass_kernel_spmd(nc, in_maps, core_ids=list(range(NCORES)))
    return res.results


def build_B3():
    nc = bass.Bass("TRN2", target_bir_lowering=False)
    P = Prog(nc)
    z_d = P.dram("z_all", [4, 4, NT, 256], BF16, "ExternalInput")
    w64_d = P.dram("w64", [128, 2, 128], BF16, "ExternalInput")
    tt_d = P.dram("ttab", [128, 64, 2, 32], BF16, "ExternalInput")
    fT_o = P.dram("fT", [512, NT], BF16, "ExternalOutput")

    Zt = P.sb("Zt", [128, 1, 128, 256], BF16)
    w64 = P.sb("w64_sb", [128, 2, 128], BF16)
    ttab = P.sb("ttab_sb", [128, 64, 2, 32], BF16)
    A = P.sb("A_sb", [128, 2, 128, 2, 64], BF16)
    fst = P.sb("fst", [128, 4, NT], BF16)
    PS = P.ps("ps", [128, 8, 512], F32)

    R = Res
    r_z = [[R() for _ in range(4)] for _ in range(2)]
    r_c = R()
    r_A = [[R() for _ in range(32)] for _ in range(2)]
    r_bank = [R() for _ in range(8)]
    r_f = [R() for _ in range(4)]

    P.op("sp", "dma_start", dict(out=w64[:], in_=w64_d), wr=[r_c])
    P.op("sp", "dma_start", dict(out=ttab[:], in_=tt_d), wr=[r_c])

    def load_pair(gp_):
        sl = 0
        for g2 in range(2):
            gr = gp_ * 2 + g2
            for r in range(4):
                P.op("sp", "dma_start", dict(out=Zt[64 * g2 + 16 * r:64 * g2 + 16 * r + 16, sl, :, :],
                                             in_=z_d[r, gr].rearrange("(p s) c -> p s c", s=128)), wr=[r_z[sl][g2 * 2 + r // 2]])

    load_pair(0)
    ev_n = 0
    bank_n = 0
    fin = []
    for gp_ in range(2):
        sl = 0
        if gp_ == 1:
            load_pair(1)
        for g2 in range(2):
            gr = gp_ * 2 + g2
            asl = gr % 2
            rows = slice(64 * g2, 64 * g2 + 64)
            for j0 in range(0, 128, 4):
                bank = bank_n % 4
                bank_n += 1
                for jj in range(4):
                    j = j0 + jj
                    for c in range(2):
                        P.op("pe", "matmul", dict(out=PS[:, bank, jj * 128:(jj + 1) * 128], lhsT=Zt[rows, sl, :, c * 128 + j],
                                                  rhs=w64[rows, c, :], start=(c == 0), stop=(c == 1)),
                             rd=[r_z[sl][g2 * 2], r_z[sl][g2 * 2 + 1], r_c], wr=[r_bank[bank]])
                q = "act" if ev_n % 2 == 0 else "dve"
                ev_n += 1
                if q == "act":
                    P.op("act", "activation", dict(out=A[:, asl, j0:j0 + 4, :, :], in_=PS[:, bank, :], func=AF.Copy),
                         rd=[r_bank[bank]], wr=[r_A[asl][j0 // 4]])
                else:
                    P.op("dve", "tensor_copy", dict(out=A[:, asl, j0:j0 + 4, :, :], in_=PS[:, bank, :]),
                         rd=[r_bank[bank]], wr=[r_A[asl][j0 // 4]])
            for kb in range(4):
                bank = 4 + (kb % 2)
                for kk in range(16):
                    k2 = kb * 16 + kk
                    for c in range(2):
                        P.op("pe", "matmul", dict(out=PS[:, bank, kk * 32:(kk + 1) * 32], lhsT=A[:, asl, :, c, k2],
                                                  rhs=ttab[:, k2, c, :], start=(c == 0), stop=(c == 1)),
                             rd=r_A[asl] + [r_c], wr=[r_bank[bank]])
                dst = fst[:, gr, :].rearrange("p (a b) -> p a b", b=64)[:, :, kb * 16:(kb + 1) * 16]
                src = PS[:, bank, :].rearrange("p (b a) -> p a b", a=32)
                P.op("dve", "tensor_copy", dict(out=dst, in_=src), rd=[r_bank[bank]], wr=[r_f[gr]])
    for gr in range(4):
        fin.append(P.op("sp", "dma_start", dict(out=fT_o[gr * 128:(gr + 1) * 128, :], in_=fst[:, gr, :]), rd=[r_f[gr]]))
    P.final_events = fin
    with P.stack:
        with nc.Block() as block:
            P.finalize(block)
    return nc


def consts_B3(core):
    a = core % 4
    s2 = np.arange(64, dtype=np.float64)
    k2 = np.arange(64, dtype=np.float64)
    th = 2 * np.pi * np.outer(s2, k2) / 64.0
    C, S = np.cos(th), np.sin(th)
    w = np.stack([np.concatenate([C, -S], 1), np.concatenate([S, C], 1)], 1)
    w64 = bf(np.concatenate([w, w], 0))
    s1 = np.arange(128, dtype=np.float64)[:, None, None]
    k2_ = np.arange(64, dtype=np.float64)[None, :, None]
    k1 = (32 * a + np.arange(32, dtype=np.float64))[None, None, :]
    ph = 2 * np.pi * s1 * (64 * k1 + k2_) / 8192.0
    sc = 2.0 ** -10
    tt = np.stack([np.cos(ph) * sc, np.sin(ph) * sc], 2)
    return {"w64": w64, "ttab": bf(tt)}


def run_B3(zp_list):
    nc = build_B3()
    in_maps = []
    for c in range(NCORES):
        b = c // 4
        cb = consts_B3(c)
        z_all = np.ascontiguousarray(np.stack([zp_list[b * 4 + r] for r in range(4)], 0))
        in_maps.append({"z_all": z_all, "w64": cb["w64"], "ttab": cb["ttab"]})
    res = run_bass_kernel_spmd(nc, in_maps, core_ids=list(range(NCORES)))
    return res.results


DEBUG_C = False


def build_C():
    nc = bass.Bass("TRN2", target_bir_lowering=False)
    P = Prog(nc)
    xT_d = P.dram("xT", [D, NT], F32, "ExternalInput")
    aT_d = P.dram("aT", [8, 64, NT], BF16, "ExternalInput")
    bT_d = P.dram("bT", [512, NT], BF16, "ExternalInput")
    fT_d = P.dram("fT", [512, NT], BF16, "ExternalInput")
    wpa_d = P.dram("w_proj_a", [512, D], F32, "ExternalInput")
    wpb_d = P.dram("w_proj_b", [512, D], F32, "ExternalInput")
    wpc_d = P.dram("w_proj_c", [512, D], F32, "ExternalInput")
    wg_d = P.dram("w_gate", [D, 3 * D], F32, "ExternalInput")
    bg_d = P.dram("b_gate", [128, 24], F32, "ExternalInput")
    wo_d = P.dram("w_out", [D, D], F32, "ExternalInput")
    g12_d = P.dram("g12", [128, 2, 8], F32, "ExternalInput")
    ones_d = P.dram("ones_d", [128, 128], BF16, "ExternalInput")
    w1_d = P.dram("w_ffn_in", [D, 2 * DFF], F32, "ExternalInput")
    w2_d = P.dram("w_ffn_out", [DFF, D], F32, "ExternalInput")
    xo_d = P.dram("xT_out", [D, NT], F32, "ExternalOutput")

    xT = P.sb("xT_sb", [128, 8, NT], F32)
    ARENA_BYTES = 137216
    arena = P.sb("arena", [128, ARENA_BYTES // 2], BF16)

    def carve(off, shape, dt):
        nb = (4 if dt == F32 else 2)
        n = 1
        for d_ in shape[1:]:
            n *= d_
        v = arena[:, off // 2:off // 2 + n * nb // 2]
        if dt == F32:
            v = v.bitcast(F32)
        if len(shape) == 2:
            return v
        if len(shape) == 3:
            return v.rearrange("p (a b) -> p a b", b=shape[2])
        raise ValueError

    WA = carve(0, [128, 8, 3072], BF16)
    WB = carve(49152, [128, 20, 1024], BF16)
    hT = carve(90112, [128, 8, 512], BF16)
    brA = carve(98304, [128, 4, 512], BF16)
    brB = carve(102400, [128, 4, 512], BF16)
    brC = carve(106496, [128, 4, 512], BF16)
    sig = carve(110592, [128, 3, 512], F32)
    tm = carve(116736, [128, 3, 512], F32)
    mT = carve(122880, [128, 8, 512], BF16)
    sq = carve(131072, [128, 2, 512], BF16)
    lnv = carve(133120, [128, 512], F32)
    rstd = carve(135168, [128, 512], F32)
    h2T = carve(71680, [128, 8, NT], BF16)
    actT = carve(110592, [128, 11, 512], BF16)
    sgf = carve(122880, [128, 2, 512], F32)
    bg = P.sb("bg", [128, 24], F32)
    g12 = P.sb("g12_sb", [128, 2, 8], F32)
    onesm = P.sb("ones_sb", [128, 128], BF16)
    PS = P.ps("ps", [128, 8, 512], F32)

    R = Res
    r_x = [R() for _ in range(8)]
    r_wa = [R() for _ in range(8)]
    r_wb = [R() for _ in range(20)]
    r_wpa = R()
    r_c = R()
    r_h = R()
    r_sq = [R(), R()]
    r_ln, r_rstd = R(), R()
    r_br = [R(), R(), R()]
    r_sig = [R(), R(), R()]
    r_tm = [R(), R(), R()]
    r_m = [R() for _ in range(8)]
    r_bank = [R() for _ in range(8)]
    r_act = [R() for _ in range(11)]
    r_sgf = [R(), R()]

    for kc in range(8):
        P.op("sp", "dma_start", dict(out=xT[:, kc, :], in_=xT_d[kc * 128:(kc + 1) * 128, :]), wr=[r_x[kc]])
    for s_, d_ in [(bg, bg_d), (g12, g12_d), (onesm, ones_d)]:
        P.op("sp", "dma_start", dict(out=s_[:], in_=d_), wr=[r_c])
    for kc in range(8):
        for cb in range(3):
            P.op("poolq", "dma_start", dict(out=WA[:, kc, cb * 1024:(cb + 1) * 1024],
                                            in_=wg_d[kc * 128:(kc + 1) * 128, cb * 1024:(cb + 1) * 1024]), wr=[r_wa[kc]])
    for kc in range(8):
        P.op("poolq", "dma_start", dict(out=WB[:, kc, :], in_=wo_d[kc * 128:(kc + 1) * 128, :]), wr=[r_wb[kc]])
    for j in range(4):
        P.op("poolq", "dma_start", dict(out=WB[:, 8 + j, :], in_=wpb_d[j * 128:(j + 1) * 128, :]), wr=[r_wb[8 + j]])
        P.op("poolq", "dma_start", dict(out=WB[:, 12 + j, :], in_=wpc_d[j * 128:(j + 1) * 128, :]), wr=[r_wb[12 + j]])
        P.op("poolq", "dma_start", dict(out=WB[:, 16 + j, :], in_=wpa_d[j * 128:(j + 1) * 128, :]), wr=[r_wb[16 + j]])

    def rmsnorm_block(tb, gi, hdst):
        ts_ = slice(tb * 512, (tb + 1) * 512)
        for kc in range(8):
            s = kc % 2
            P.op("pool", "tensor_tensor", dict(out=sq[:, s, :], in0=xT[:, kc, ts_], in1=xT[:, kc, ts_], op=ALU.mult),
                 rd=[r_x[kc]], wr=[r_sq[s]])
            P.op("pe", "matmul", dict(out=PS[:, 7, :], lhsT=onesm[:], rhs=sq[:, s, :], start=(kc == 0), stop=(kc == 7)),
                 rd=[r_sq[s], r_c], wr=[r_bank[7]])
        P.op("act", "activation", dict(out=lnv[:], in_=PS[:, 7, :], func=AF.Ln, bias=EPS, scale=1.0), rd=[r_bank[7]], wr=[r_ln])
        P.op("act", "activation", dict(out=rstd[:], in_=lnv[:], func=AF.Exp, scale=-0.5), rd=[r_ln], wr=[r_rstd])
        for kc in range(8):
            P.op("dve", "scalar_tensor_tensor", dict(out=hdst[:, kc, :], in0=xT[:, kc, ts_], scalar=g12[:, gi, kc:kc + 1],
                                                     in1=rstd[:], op0=ALU.mult, op1=ALU.mult),
                 rd=[r_x[kc], r_rstd, r_c], wr=[r_h])

    for tb in range(NTB):
        ts_ = slice(tb * 512, (tb + 1) * 512)
        P.op("sp", "dma_start", dict(out=brA[:], in_=aT_d[:, :, ts_].rearrange("(j h) p t -> (h p) j t", h=2)), wr=[r_br[0]])
        P.op("sp", "dma_start", dict(out=brB[:], in_=bT_d[:, ts_].rearrange("(j p) t -> p j t", p=128)), wr=[r_br[1]])
        P.op("sp", "dma_start", dict(out=brC[:], in_=fT_d[:, ts_].rearrange("(j p) t -> p j t", p=128)), wr=[r_br[2]])
        rmsnorm_block(tb, 0, hT[:, :, 0:512])
        for oc in range(8):
            ocs = slice(oc * 128, (oc + 1) * 128)
            for j in range(4):
                P.op("pe", "matmul", dict(out=PS[:, 0, :], lhsT=WB[:, 16 + j, ocs], rhs=brA[:, j, :], start=(j == 0), stop=(j == 3)),
                     rd=[r_wb[16 + j], r_br[0]], wr=[r_bank[0]])
            for j in range(4):
                P.op("pe", "matmul", dict(out=PS[:, 1, :], lhsT=WB[:, 8 + j, ocs], rhs=brB[:, j, :], start=(j == 0), stop=(j == 3)),
                     rd=[r_wb[8 + j], r_br[1]], wr=[r_bank[1]])
            for j in range(4):
                P.op("pe", "matmul", dict(out=PS[:, 2, :], lhsT=WB[:, 12 + j, ocs], rhs=brC[:, j, :], start=(j == 0), stop=(j == 3)),
                     rd=[r_wb[12 + j], r_br[2]], wr=[r_bank[2]])
            for i in range(3):
                for kc in range(8):
                    P.op("pe", "matmul", dict(out=PS[:, 3 + i, :], lhsT=WA[:, kc, i * 1024 + oc * 128:i * 1024 + (oc + 1) * 128],
                                              rhs=hT[:, kc, 0:512], start=(kc == 0), stop=(kc == 7)),
                         rd=[r_wa[kc], r_h], wr=[r_bank[3 + i]])
                P.op("act", "activation", dict(out=sig[:, i, :], in_=PS[:, 3 + i, :], func=AF.Sigmoid,
                                               bias=bg[:, i * 8 + oc:i * 8 + oc + 1], scale=1.0),
                     rd=[r_bank[3 + i], r_c], wr=[r_sig[i]])
                P.op("dve", "tensor_tensor", dict(out=tm[:, i, :], in0=PS[:, i, :], in1=sig[:, i, :], op=ALU.mult),
                     rd=[r_bank[i], r_sig[i]], wr=[r_tm[i]])
            P.op("pool", "tensor_tensor", dict(out=tm[:, 0, :], in0=tm[:, 0, :], in1=tm[:, 1, :], op=ALU.add),
                 rd=[r_tm[1]], wr=[r_tm[0]])
            P.op("pool", "tensor_tensor", dict(out=mT[:, oc, :], in0=tm[:, 0, :], in1=tm[:, 2, :], op=ALU.add),
                 rd=[r_tm[0], r_tm[2]], wr=[r_m[oc]])
        for oc2 in range(8):
            bank = 6
            for kc in range(8):
                P.op("pe", "matmul", dict(out=PS[:, bank, :], lhsT=WB[:, kc, oc2 * 128:(oc2 + 1) * 128], rhs=mT[:, kc, :],
                                          start=(kc == 0), stop=(kc == 7)), rd=[r_wb[kc], r_m[kc]], wr=[r_bank[bank]])
            P.op("dve", "tensor_tensor", dict(out=xT[:, oc2, ts_], in0=PS[:, bank, :], in1=xT[:, oc2, ts_], op=ALU.add),
                 rd=[r_bank[bank]], wr=[r_x[oc2]])

    fin = []
    if DEBUG_C:
        x1_d = P.dram("x1_out", [D, NT], F32, "ExternalOutput")
        for kc in range(8):
            fin.append(P.op("sp", "dma_start", dict(out=x1_d[kc * 128:(kc + 1) * 128, :], in_=xT[:, kc, :]), rd=[r_x[kc]]))
    P.op("dve", "memset", dict(ap=lnv[:, 0:8], constant=0.0), rd=[], wr=r_sig + r_tm + r_br + r_act + r_sgf + r_m + [r_ln, r_h] + r_wb[11:20])
    for tb in range(NTB):
        rmsnorm_block(tb, 1, h2T[:, :, tb * 512:(tb + 1) * 512])
    for half in range(2):
        c0 = half * 1408
        for kc in range(8):
            P.op("poolq", "dma_start", dict(out=WA[:, kc, 0:1408], in_=w1_d[kc * 128:(kc + 1) * 128, c0:c0 + 1408]), wr=[r_wa[kc]])
            P.op("poolq", "dma_start", dict(out=WA[:, kc, 1408:2816], in_=w1_d[kc * 128:(kc + 1) * 128, DFF + c0:DFF + c0 + 1408]),
                 wr=[r_wa[kc]])
        for j in range(11):
            P.op("poolq", "dma_start", dict(out=WB[:, j, :], in_=w2_d[c0 + j * 128:c0 + (j + 1) * 128, :]), wr=[r_wb[j]])
        for tb in range(NTB):
            ts_ = slice(tb * 512, (tb + 1) * 512)
            for j in range(11):
                s = j % 2
                bg_, bu_ = (0, 1) if s == 0 else (2, 3)
                for kc in range(8):
                    P.op("pe", "matmul", dict(out=PS[:, bg_, :], lhsT=WA[:, kc, j * 128:(j + 1) * 128], rhs=h2T[:, kc, ts_],
                                              start=(kc == 0), stop=(kc == 7)), rd=[r_wa[kc], r_h], wr=[r_bank[bg_]])
                for kc in range(8):
                    P.op("pe", "matmul", dict(out=PS[:, bu_, :], lhsT=WA[:, kc, 1408 + j * 128:1408 + (j + 1) * 128], rhs=h2T[:, kc, ts_],
                                              start=(kc == 0), stop=(kc == 7)), rd=[r_wa[kc], r_h], wr=[r_bank[bu_]])
                P.op("act", "activation", dict(out=sgf[:, s, :], in_=PS[:, bg_, :], func=AF.Silu), rd=[r_bank[bg_]], wr=[r_sgf[s]])
                P.op("dve", "tensor_tensor", dict(out=actT[:, j, :], in0=PS[:, bu_, :], in1=sgf[:, s, :], op=ALU.mult),
                     rd=[r_bank[bu_], r_sgf[s]], wr=[r_act[j]])
            for oc in range(8):
                bank = 4 + (oc % 2)
                for j in range(11):
                    P.op("pe", "matmul", dict(out=PS[:, bank, :], lhsT=WB[:, j, oc * 128:(oc + 1) * 128], rhs=actT[:, j, :],
                                              start=(j == 0), stop=(j == 10)), rd=[r_wb[j], r_act[j]], wr=[r_bank[bank]])
                P.op("dve", "tensor_tensor", dict(out=xT[:, oc, ts_], in0=PS[:, bank, :], in1=xT[:, oc, ts_], op=ALU.add),
                     rd=[r_bank[bank]], wr=[r_x[oc]])
    for kc in range(8):
        fin.append(P.op("sp", "dma_start", dict(out=xo_d[kc * 128:(kc + 1) * 128, :], in_=xT[:, kc, :]), rd=[r_x[kc]]))
    P.final_events = fin
    with P.stack:
        with nc.Block() as block:
            P.finalize(block)
    return nc


def run_C(l, inp, xT_list, aT_list, bT_list, fT_list):
    nc = build_C()
    ct = const_tables(0)
    g12 = np.stack([inp["norm1_g"][l].reshape(8, 128).T, inp["norm2_g"][l].reshape(8, 128).T], axis=1).astype(np.float32)
    bgate = np.ascontiguousarray(inp["b_gate"][l].reshape(24, 128).T).astype(np.float32)
    in_maps = []
    for c in range(NCORES):
        in_maps.append({"xT": xT_list[c], "aT": aT_list[c], "bT": bT_list[c], "fT": fT_list[c],
                        "w_proj_a": np.ascontiguousarray(inp["w_proj_a"][l]), "w_proj_b": np.ascontiguousarray(inp["w_proj_b"][l]),
                        "w_proj_c": np.ascontiguousarray(inp["w_proj_c"][l]), "w_gate": np.ascontiguousarray(inp["w_gate"][l]),
                        "b_gate": bgate, "w_out": np.ascontiguousarray(inp["w_out"][l]), "g12": np.ascontiguousarray(g12),
                        "ones_d": ct["ones_d"], "w_ffn_in": np.ascontiguousarray(inp["w_ffn_in"][l]),
                        "w_ffn_out": np.ascontiguousarray(inp["w_ffn_out"][l])})
    res = run_bass_kernel_spmd(nc, in_maps, core_ids=list(range(NCORES)))
    return res.results


def kernel(**inp):
    inp = {k: np.asarray(v) for k, v in inp.items()}
    x = inp["x"]
    xT = [np.ascontiguousarray(x[c // 4, (c % 4) * NT:(c % 4 + 1) * NT, :].T) for c in range(NCORES)]
    for l in range(2):
        rA = run_A(xT, l, inp)
        qT = [np.asarray(r["qT"]) for r in rA]
        kT = [np.asarray(r["kT"]) for r in rA]
        vp = [np.asarray(r["vp"]) for r in rA]
        gT = [np.asarray(r["gT"]) for r in rA]
        zp = [np.asarray(r["zp"]) for r in rA]
        rB1 = run_B1(l, inp, qT, kT, vp)
        rB2 = run_B2(l, inp, gT)
        rB3 = run_B3(zp)
        rC = run_C(l, inp, xT, [np.asarray(r["aT"]) for r in rB1], [np.asarray(r["bT"]) for r in rB2],
                   [np.asarray(r["fT"]) for r in rB3])
        xT = [np.ascontiguousarray(np.asarray(r["xT_out"])) for r in rC]
    out = np.empty((2, SEQ, D), np.float32)
    for c in range(NCORES):
        out[c // 4, (c % 4) * NT:(c % 4 + 1) * NT, :] = xT[c].T
    return out
```

```python
import math
from contextlib import ExitStack
import numpy as np
import ml_dtypes
import concourse.bass as bass
import concourse.mybir as mybir
from concourse.bass_utils import run_bass_kernel_spmd

F32 = mybir.dt.float32
BF16 = mybir.dt.bfloat16
AF = mybir.ActivationFunctionType
ALU = mybir.AluOpType

NCORES = 8
D = 1024
SEQ = 8192
NT = 2048
NTB = 4
EPS = 1e-6
DFF = 2816
ROPE_THETA = 500000.0


class Res:
    __slots__ = ("w", "r", "name")

    def __init__(self, name=""):
        self.w = None
        self.r = []
        self.name = name


class Ev:
    __slots__ = ("q", "idx", "needed", "sem", "val")

    def __init__(self, q, idx):
        self.q = q
        self.idx = idx
        self.needed = False
        self.sem = None
        self.val = None


COMPUTE_Q = ("pe", "act", "dve", "pool")
DMA_Q = ("sp", "actq", "poolq")
ENGINE_OF = {"pe": "tensor", "act": "scalar", "dve": "vector", "pool": "gpsimd",
             "sp": "sync", "actq": "scalar", "poolq": "gpsimd"}
NDMASEM = 6


class Prog:
    def __init__(self, nc):
        self.nc = nc
        self.stack = ExitStack()
        self.streams = {"tensor": [], "scalar": [], "vector": [], "gpsimd": [], "sync": []}
        self.evcount = {q: 0 for q in COMPUTE_Q + DMA_Q}
        self.dma_n = {q: 0 for q in DMA_Q}
        self.dma_last = {}
        self.sems = {}
        self.final_events = []

    def sb(self, name, shape, dt):
        return self.stack.enter_context(self.nc.sbuf_tensor(name, list(shape), dt))

    def ps(self, name, shape, dt=F32):
        return self.stack.enter_context(self.nc.psum_tensor(name, list(shape), dt))

    def dram(self, name, shape, dt, kind):
        return self.nc.dram_tensor(name, list(shape), dt, kind=kind).ap()

    def op(self, q, name, kw, rd=(), wr=()):
        fn = (name, kw)
        deps = []
        for r in rd:
            if r.w is not None:
                deps.append(r.w)
        for w in wr:
            if w.w is not None:
                deps.append(w.w)
            deps.extend(w.r)
        self.evcount[q] += 1
        ev = Ev(q, self.evcount[q])
        if q in DMA_Q:
            slot = self.dma_n[q] % NDMASEM
            self.dma_n[q] += 1
            prev = self.dma_last.get((q, slot))
            if prev is not None:
                deps.append(prev)
            self.dma_last[(q, slot)] = ev
            ev.sem = (q, slot)
        else:
            ev.sem = (q, 0)
        best = {}
        dmas = []
        for d in deps:
            if d.q in COMPUTE_Q:
                if d.q == "pe" and q == "pe":
                    continue
                if d.q not in best or best[d.q].idx < d.idx:
                    best[d.q] = d
            else:
                if d not in dmas:
                    dmas.append(d)
        waits = list(best.values()) + dmas
        self.streams[ENGINE_OF[q]].append((q, fn, waits, ev))
        for r in rd:
            r.r.append(ev)
        for w in wr:
            w.w = ev
            w.r = []
        return ev

    def finalize(self, block):
        nc = self.nc
        for eng, stream in self.streams.items():
            seen = {}
            for item in stream:
                q, fn, waits, ev = item
                keep = []
                for d in waits:
                    if d.q in COMPUTE_Q:
                        if seen.get(d.q, 0) >= d.idx:
                            continue
                        seen[d.q] = d.idx
                    else:
                        if seen.get(id(d)):
                            continue
                        seen[id(d)] = True
                    d.needed = True
                    keep.append(d)
                item[2][:] = keep
        for ev in self.final_events:
            ev.needed = True
        counters = {}
        for eng, stream in self.streams.items():
            for q, fn, waits, ev in stream:
                if q in DMA_Q:
                    key = ev.sem
                    counters[key] = counters.get(key, 0) + 16
                    ev.val = counters[key]
                    ev.needed = True
                elif ev.needed:
                    key = ev.sem
                    counters[key] = counters.get(key, 0) + 1
                    ev.val = counters[key]
        for key in counters:
            self.sems[key] = self.stack.enter_context(nc.semaphore("s_%s_%d" % key))
        self.maxcount = dict(counters)

        def emit(engname):
            stream = self.streams[engname]

            def body(eng):
                for q, fn, waits, ev in stream:
                    for d in waits:
                        eng.wait_ge(self.sems[d.sem], d.val)
                    ins = getattr(eng, fn[0])(**fn[1])
                    if ev.needed:
                        ins.then_inc(self.sems[ev.sem], 16 if q in DMA_Q else 1)
                if engname == "sync":
                    for ev in self.final_events:
                        eng.wait_ge(self.sems[ev.sem], ev.val)
            return body

        block.tensor(emit("tensor"))
        block.scalar(emit("scalar"))
        block.vector(emit("vector"))
        block.gpsimd(emit("gpsimd"))
        block.sync(emit("sync"))


def bf(a):
    return np.ascontiguousarray(np.asarray(a, dtype=np.float32)).astype(ml_dtypes.bfloat16)


def const_tables(core):
    a = core % 4
    t = {}
    t["ones_d"] = bf(np.full((128, 128), 1.0 / 1024.0))
    bd = np.zeros((128, 128), np.float32)
    for i in range(4):
        bd[32 * i:32 * i + 32, 32 * i:32 * i + 32] = 1.0 / 32.0
    t["bd32"] = bf(bd)
    rot = np.zeros((128, 128), np.float32)
    for blk in range(4):
        o = 32 * blk
        for i in range(4):
            rot[o + 4 + i, o + i] = -1.0
            rot[o + i, o + 4 + i] = 1.0
    t["rot"] = bf(rot)
    pos = (a * NT + np.arange(NT)).astype(np.float64)
    inv = 1.0 / (ROPE_THETA ** (np.arange(0, 8, 2, dtype=np.float64) / 8.0))
    ang = pos[None, :] * inv[:, None]
    cf = np.ones((128, NT), np.float32)
    sf = np.zeros((128, NT), np.float32)
    for blk in range(4):
        o = 32 * blk
        cf[o:o + 4] = np.cos(ang)
        cf[o + 4:o + 8] = np.cos(ang)
        sf[o:o + 4] = np.sin(ang)
        sf[o + 4:o + 8] = np.sin(ang)
    t["cosf"] = cf
    t["sinf"] = sf
    jc = np.outer(np.arange(128), np.arange(128)).astype(np.float64) * (2 * np.pi / 128.0)
    t["dftg"] = bf(np.concatenate([np.cos(jc), -np.sin(jc)], axis=1))
    return t


def build_A():
    nc = bass.Bass("TRN2", target_bir_lowering=False)
    P = Prog(nc)
    xT_d = P.dram("xT", [D, NT], F32, "ExternalInput")
    win_d = P.dram("w_in", [D, 3072], F32, "ExternalInput")
    g1_d = P.dram("g1", [128, 8], F32, "ExternalInput")
    gqk_d = P.dram("gqk", [128, 2], F32, "ExternalInput")
    cos_d = P.dram("cosf", [128, NT], F32, "ExternalInput")
    sin_d = P.dram("sinf", [128, NT], F32, "ExternalInput")
    ones_d = P.dram("ones_d", [128, 128], BF16, "ExternalInput")
    bd_d = P.dram("bd32", [128, 128], BF16, "ExternalInput")
    rot_d = P.dram("rot", [128, 128], BF16, "ExternalInput")
    dftg_d = P.dram("dftg", [128, 256], BF16, "ExternalInput")
    qT_o = P.dram("qT", [512, NT], BF16, "ExternalOutput")
    kT_o = P.dram("kT", [512, NT], BF16, "ExternalOutput")
    vp_o = P.dram("vp", [4, NT, 130], BF16, "ExternalOutput")
    gT_o = P.dram("gT", [512, NT], BF16, "ExternalOutput")
    zp_o = P.dram("zp", [4, NT, 256], BF16, "ExternalOutput")

    xT = P.sb("xT_sb", [128, 8, NT], F32)
    W = P.sb("w_sb", [128, 8, 3072], BF16)
    g1 = P.sb("g1_sb", [128, 8], F32)
    gqk = P.sb("gqk_sb", [128, 2], F32)
    cosf = P.sb("cos_sb", [128, NT], F32)
    sinf = P.sb("sin_sb", [128, NT], F32)
    onesm = P.sb("ones_sb", [128, 128], BF16)
    bdm = P.sb("bd_sb", [128, 128], BF16)
    rotm = P.sb("rot_sb", [128, 128], BF16)
    dftg = P.sb("dftg_sb", [128, 256], BF16)
    hT = P.sb("hT", [128, 8, 512], BF16)
    sq = P.sb("sq", [128, 2, 512], BF16)
    lnv = P.sb("lnv", [128, 512], F32)
    rstd = P.sb("rstd", [128, 512], F32)
    sq2 = P.sb("sq2", [128, 2, 512], BF16)
    ln2 = P.sb("ln2", [128, 2, 512], F32)
    r2 = P.sb("r2", [128, 2, 512], F32)
    qn = P.sb("qn", [128, 2, 512], BF16)
    t1 = P.sb("t1", [128, 2, 512], F32)
    t2 = P.sb("t2", [128, 2, 512], F32)
    qkst = P.sb("qkst", [128, 1, 8, 512], BF16)
    vst = P.sb("vst", [128, 1, 4, 4, 130], BF16)
    sg = P.sb("sg", [128, 2, 512], F32)
    gst = P.sb("gst", [128, 1, 4, 512], BF16)
    fcT = P.sb("fcT", [128, 2, 512], BF16)
    zst = P.sb("zst", [128, 1, 4, 4, 256], BF16)
    PS = P.ps("ps", [128, 8, 512], F32)

    R = lambda n: Res(n)
    r_x = [R("x%d" % i) for i in range(8)]
    r_w = [R("w%d" % i) for i in range(6)]
    r_c = R("consts")
    r_h = R("hT")
    r_sq = [R("sq0"), R("sq1")]
    r_ln = R("lnv")
    r_rstd = R("rstd")
    r_bank = [R("bank%d" % i) for i in range(8)]
    r_sq2 = [R("a"), R("b")]
    r_ln2 = [R("a"), R("b")]
    r_r2 = [R("a"), R("b")]
    r_qn = [R("a"), R("b")]
    r_t1 = [R("a"), R("b")]
    r_t2 = [R("a"), R("b")]
    r_qk = [[R("qk%d" % i) for i in range(8)] for _ in range(2)]
    r_v = [R("vst0"), R("vst1")]
    r_sg = [R("a"), R("b")]
    r_g = [[R("g%d" % i) for i in range(4)] for _ in range(2)]
    r_fc = [R("a"), R("b")]
    r_z = [R("zst0"), R("zst1")]

    for kc in range(8):
        P.op("sp", "dma_start", dict(out=xT[:, kc, :], in_=xT_d[kc * 128:(kc + 1) * 128, :]),
             wr=[r_x[kc]])
    cl = [(g1, g1_d), (gqk, gqk_d), (cosf, cos_d), (sinf, sin_d), (onesm, ones_d), (bdm, bd_d),
          (rotm, rot_d), (dftg, dftg_d)]
    for i, (s, d_) in enumerate(cl):
        P.op("sp", "dma_start", dict(out=s[:], in_=d_[:, :]), wr=[r_c])
    for wb in range(6):
        for kc in range(8):
            P.op("poolq", "dma_start", dict(
                out=W[:, kc, wb * 512:(wb + 1) * 512],
                in_=win_d[kc * 128:(kc + 1) * 128, wb * 512:(wb + 1) * 512]), wr=[r_w[wb]])
    for sl in range(1):
        P.op("pool", "memset", dict(ap=vst[:, sl, :, :, 64:65], constant=1.0), wr=[r_v[sl]])
        P.op("pool", "memset", dict(ap=vst[:, sl, :, :, 129:130], constant=1.0), wr=[r_v[sl]])

    fin = []
    bank_rr = [0]

    def next_bank():
        b = bank_rr[0] % 4
        bank_rr[0] += 1
        return b

    def proj_chunk(oc, tb, bank):
        wb = (oc * 128) // 512
        for kc in range(8):
            P.op("pe", "matmul", dict(out=PS[:, bank, :], lhsT=W[:, kc, oc * 128:(oc + 1) * 128],
                                                 rhs=hT[:, kc, :], start=(kc == 0), stop=(kc == 7)),
                 rd=[r_w[wb], r_h, r_c], wr=[r_bank[bank]])

    for tb in range(NTB):
        ts_ = slice(tb * 512, (tb + 1) * 512)
        sl = 0
        for kc in range(8):
            s = kc % 2
            P.op("pool", "tensor_tensor", dict(out=sq[:, s, :], in0=xT[:, kc, ts_], in1=xT[:, kc, ts_],
                                                              op=ALU.mult), rd=[r_x[kc]], wr=[r_sq[s]])
            P.op("pe", "matmul", dict(out=PS[:, 4, :], lhsT=onesm[:], rhs=sq[:, s, :],
                                                      start=(kc == 0), stop=(kc == 7)),
                 rd=[r_sq[s], r_c], wr=[r_bank[4]])
        P.op("act", "activation", dict(out=lnv[:], in_=PS[:, 4, :], func=AF.Ln, bias=EPS, scale=1.0),
             rd=[r_bank[4]], wr=[r_ln])
        P.op("act", "activation", dict(out=rstd[:], in_=lnv[:], func=AF.Exp, scale=-0.5),
             rd=[r_ln], wr=[r_rstd])
        for kc in range(8):
            P.op("dve", "scalar_tensor_tensor", dict(out=hT[:, kc, :], in0=xT[:, kc, ts_],
                                                                scalar=g1[:, kc:kc + 1], in1=rstd[:],
                                                                op0=ALU.mult, op1=ALU.mult),
                 rd=[r_x[kc], r_rstd, r_c], wr=[r_h])
        for oc in range(8):
            s = oc % 2
            bank = next_bank()
            proj_chunk(oc, tb, bank)
            P.op("act", "activation", dict(out=sq2[:, s, :], in_=PS[:, bank, :], func=AF.Square),
                 rd=[r_bank[bank]], wr=[r_sq2[s]])
            P.op("pe", "matmul", dict(out=PS[:, 5, :], lhsT=bdm[:], rhs=sq2[:, s, :], start=True, stop=True),
                 rd=[r_sq2[s], r_c], wr=[r_bank[5]])
            P.op("act", "activation", dict(out=ln2[:, s, :], in_=PS[:, 5, :], func=AF.Ln, bias=EPS, scale=1.0),
                 rd=[r_bank[5]], wr=[r_ln2[s]])
            P.op("act", "activation", dict(out=r2[:, s, :], in_=ln2[:, s, :], func=AF.Exp, scale=-0.5),
                 rd=[r_ln2[s]], wr=[r_r2[s]])
            gi = 0 if oc < 4 else 1
            P.op("dve", "scalar_tensor_tensor", dict(
                out=qn[:, s, :], in0=PS[:, bank, :], scalar=gqk[:, gi:gi + 1], in1=r2[:, s, :],
                op0=ALU.mult, op1=ALU.mult), rd=[r_bank[bank], r_r2[s], r_c], wr=[r_qn[s]])
            P.op("pe", "matmul", dict(out=PS[:, 6, :], lhsT=rotm[:], rhs=qn[:, s, :], start=True, stop=True),
                 rd=[r_qn[s], r_c], wr=[r_bank[6]])
            P.op("pool", "tensor_tensor", dict(out=t1[:, s, :], in0=qn[:, s, :], in1=cosf[:, ts_], op=ALU.mult),
                 rd=[r_qn[s], r_c], wr=[r_t1[s]])
            P.op("dve", "tensor_tensor", dict(out=t2[:, s, :], in0=PS[:, 6, :], in1=sinf[:, ts_], op=ALU.mult),
                 rd=[r_bank[6], r_c], wr=[r_t2[s]])
            P.op("pool", "tensor_tensor", dict(out=qkst[:, sl, oc, :], in0=t1[:, s, :], in1=t2[:, s, :],
                                                              op=ALU.add), rd=[r_t1[s], r_t2[s]], wr=[r_qk[sl][oc]])
        for tt in range(4):
            tti = tt
            bank = next_bank()
            for kc in range(8):
                P.op("pe", "matmul", dict(
                    out=PS[:, bank, :], lhsT=hT[:, kc, tt * 128:(tt + 1) * 128], rhs=W[:, kc, 1024:1536],
                    start=(kc == 0), stop=(kc == 7)), rd=[r_w[2], r_h], wr=[r_bank[bank]])
            for hp in range(4):
                for h2 in range(2):
                    eng = "act" if h2 == 0 else "dve"
                    src = (hp * 2 + h2) * 64
                    if eng == "act":
                        P.op("act", "activation", dict(
                            out=vst[:, sl, tti, hp, h2 * 65:h2 * 65 + 64], in_=PS[:, bank, src:src + 64], func=AF.Copy),
                            rd=[r_bank[bank]], wr=[r_v[sl]])
                    else:
                        P.op("dve", "tensor_copy", dict(
                            out=vst[:, sl, tti, hp, h2 * 65:h2 * 65 + 64], in_=PS[:, bank, src:src + 64]),
                            rd=[r_bank[bank]], wr=[r_v[sl]])
        for j in range(4):
            s = j % 2
            ba = next_bank()
            proj_chunk(12 + j, tb, ba)
            bb = next_bank()
            proj_chunk(16 + j, tb, bb)
            P.op("act", "activation", dict(out=sg[:, s, :], in_=PS[:, bb, :], func=AF.Sigmoid),
                 rd=[r_bank[bb]], wr=[r_sg[s]])
            P.op("dve", "tensor_tensor", dict(out=gst[:, sl, j, :], in0=PS[:, ba, :], in1=sg[:, s, :],
                                                                  op=ALU.mult),
                 rd=[r_bank[ba], r_sg[s]], wr=[r_g[sl][j]])
        for gr in range(4):
            s = gr % 2
            bank = next_bank()
            proj_chunk(20 + gr, tb, bank)
            P.op("act", "activation", dict(out=fcT[:, s, :], in_=PS[:, bank, :], func=AF.Copy),
                 rd=[r_bank[bank]], wr=[r_fc[s]])
            for tp in range(2):
                for t_ in range(2):
                    tt = tp * 2 + t_
                    P.op("pe", "matmul", dict(
                        out=PS[:, 7, t_ * 256:(t_ + 1) * 256], lhsT=fcT[:, s, tt * 128:(tt + 1) * 128], rhs=dftg[:],
                        start=True, stop=True), rd=[r_fc[s], r_c], wr=[r_bank[7]])
                for t_ in range(2):
                    tti = tp * 2 + t_
                    P.op("dve", "tensor_copy", dict(
                        out=zst[:, sl, tti, gr, :], in_=PS[:, 7, t_ * 256:(t_ + 1) * 256]),
                        rd=[r_bank[7]], wr=[r_z[sl]])

        for oc in range(8):
            dst = qT_o if oc < 4 else kT_o
            o = (oc % 4) * 128
            fin.append(P.op("sp", "dma_start", dict(
                out=dst[o:o + 128, ts_], in_=qkst[:, sl, oc, :]), rd=[r_qk[sl][oc]]))
        for j in range(4):
            fin.append(P.op("sp", "dma_start", dict(
                out=gT_o[j * 128:(j + 1) * 128, ts_], in_=gst[:, sl, j, :]), rd=[r_g[sl][j]]))
        for hp in range(4):
            fin.append(P.op("sp", "dma_start", dict(
                out=vp_o[hp, ts_, :].rearrange("(t p) c -> p t c", p=128), in_=vst[:, sl, :, hp, :]), rd=[r_v[sl]]))
        for gr in range(4):
            fin.append(P.op("sp", "dma_start", dict(
                out=zp_o[gr, ts_, :].rearrange("(t p) c -> p t c", p=128), in_=zst[:, sl, :, gr, :]), rd=[r_z[sl]]))
    P.final_events = fin
    with P.stack:
        with nc.Block() as block:
            P.finalize(block)
    return nc


def _core_slices():
    return [(c // 4, (c % 4) * NT) for c in range(NCORES)]


def run_A(x_T_list, l, inp):
    nc = build_A()
    in_maps = []
    for c in range(NCORES):
        ct = const_tables(c)
        g1 = np.ascontiguousarray(inp["norm1_g"][l].reshape(8, 128).T)
        gqk = np.stack([np.tile(inp["qnorm_g"][l], 4), np.tile(inp["knorm_g"][l], 4)], axis=1).astype(np.float32)
        in_maps.append({"xT": x_T_list[c], "w_in": np.ascontiguousarray(inp["w_in"][l]), "g1": g1,
                        "gqk": np.ascontiguousarray(gqk), "cosf": ct["cosf"], "sinf": ct["sinf"],
                        "ones_d": ct["ones_d"], "bd32": ct["bd32"], "rot": ct["rot"], "dftg": ct["dftg"]})
    res = run_bass_kernel_spmd(nc, in_maps, core_ids=list(range(NCORES)))
    return res.results


def lam_init_of(l):
    return 0.8 - 0.6 * math.exp(-0.3 * l)


def build_B1(l):
    nc = bass.Bass("TRN2", target_bir_lowering=False)
    P = Prog(nc)
    qT_d = P.dram("qT", [512, NT], BF16, "ExternalInput")
    kT_d = P.dram("kT_all", [4, 512, NT], BF16, "ExternalInput")
    vp_d = P.dram("vp_all", [4, 4, NT, 130], BF16, "ExternalInput")
    lam_d = P.dram("lamv", [128, 4, 32], F32, "ExternalInput")
    gsub_d = P.dram("gsub", [64, 1], F32, "ExternalInput")
    sel_d = P.dram("sel", [128, 64], F32, "ExternalInput")
    o64_d = P.dram("ones64", [64, 64], BF16, "ExternalInput")
    qm_d = P.dram("qmask", [128, 4], F32, "ExternalInput")
    aT_o = P.dram("aT", [8, 64, NT], BF16, "ExternalOutput")

    qT = P.sb("qT_sb", [128, 4, NT], BF16)
    qTm = P.sb("qTm_sb", [128, 2, 4, NT], BF16)
    qm = P.sb("qm_sb", [128, 4], F32)
    KT = P.sb("KT_sb", [128, 2, SEQ], BF16)
    VP = P.sb("VP_sb", [128, 2, 64, 130], BF16)
    lamv = P.sb("lamv_sb", [128, 4, 32], F32)
    lprod = P.sb("lprod", [128, 2, 32], F32)
    lsum = P.sb("lsum", [128, 2], F32)
    lexp = P.sb("lexp", [128, 2], F32)
    neglam = P.sb("neglam", [128, 1], F32)
    gsub = P.sb("gsub_sb", [64, 1], F32)
    gsub2 = P.sb("gsub2_sb", [64, 1], F32)
    sel = P.sb("sel_sb", [128, 64], F32)
    o64 = P.sb("o64_sb", [64, 64], BF16)
    E = P.sb("E_sb", [128, 3, 2, 512], BF16)
    Osb = P.sb("Osb", [65, 2, 512], F32)
    Rr = P.sb("Rr", [64, 2, 512], F32)
    tt0 = P.sb("tt0", [64, 2, 512], F32)
    att = P.sb("att", [64, 512], F32)
    sqa = P.sb("sqa", [64, 512], BF16)
    lna = P.sb("lna", [64, 512], F32)
    rsa = P.sb("rsa", [64, 512], F32)
    aT = P.sb("aT_sb", [64, 8, NT], BF16)
    PS = P.ps("ps", [128, 8, 512], F32)

    R = Res
    r_q = [R() for _ in range(4)]
    r_qm = [[R() for _ in range(4)] for _ in range(2)]
    r_kt = [[R() for _ in range(4)] for _ in range(2)]
    r_vp = [[R() for _ in range(4)] for _ in range(2)]
    r_c = R()
    r_lam = R()
    r_bank = [R() for _ in range(8)]
    r_E = [R() for _ in range(3)]
    r_O = [R(), R()]
    r_R = [R(), R()]
    r_t = [R(), R()]
    r_att, r_sq, r_ln, r_rs = R(), R(), R(), R()
    r_aT = [[R() for _ in range(4)] for _ in range(8)]

    lam_init = lam_init_of(l)
    for hp in range(4):
        P.op("sp", "dma_start", dict(out=qT[:, hp, :], in_=qT_d[hp * 128:(hp + 1) * 128, :]), wr=[r_q[hp]])
    P.op("sp", "dma_start", dict(out=lamv[:], in_=lam_d[:, :, :]), wr=[r_lam])
    P.op("sp", "dma_start", dict(out=gsub[:], in_=gsub_d[:, :]), wr=[r_c])
    P.op("sp", "dma_start", dict(out=sel[:], in_=sel_d[:, :]), wr=[r_c])
    P.op("sp", "dma_start", dict(out=o64[:], in_=o64_d[:, :]), wr=[r_c])
    P.op("sp", "dma_start", dict(out=qm[:], in_=qm_d[:, :]), wr=[r_c])

    def load_kv(hp):
        s = hp % 2
        for r in range(4):
            P.op("sp", "dma_start", dict(out=KT[:, s, r * NT:(r + 1) * NT], in_=kT_d[r, hp * 128:(hp + 1) * 128, :]),
                 wr=[r_kt[s][r]])
            P.op("sp", "dma_start", dict(out=VP[:, s, 16 * r:16 * r + 16, :],
                                         in_=vp_d[r, hp].rearrange("(k p) c -> p k c", p=128)), wr=[r_vp[s][r]])

    load_kv(0)
    P.op("dve", "tensor_tensor", dict(out=lprod[:, 0, :], in0=lamv[:, 0, :], in1=lamv[:, 1, :], op=ALU.mult),
         rd=[r_lam], wr=[r_att])
    P.op("dve", "tensor_tensor", dict(out=lprod[:, 1, :], in0=lamv[:, 2, :], in1=lamv[:, 3, :], op=ALU.mult),
         rd=[r_lam], wr=[r_att])
    P.op("dve", "tensor_reduce", dict(out=lsum[:], in_=lprod[:], axis=mybir.AxisListType.X, op=ALU.add),
         rd=[r_att], wr=[r_sq])
    P.op("act", "activation", dict(out=lexp[:], in_=lsum[:], func=AF.Exp), rd=[r_sq], wr=[r_ln])
    P.op("dve", "tensor_tensor", dict(out=neglam[:], in0=lexp[:, 1:2], in1=lexp[:, 0:1], op=ALU.subtract),
         rd=[r_ln], wr=[r_rs])
    P.op("dve", "tensor_scalar", dict(out=neglam[:], in0=neglam[:], scalar1=-lam_init, scalar2=None, op0=ALU.add),
         rd=[r_rs], wr=[r_rs])
    r_neglam = r_rs
    r_neglam_ev_holder = R()
    P.op("dve", "tensor_scalar", dict(out=gsub2[:], in0=gsub[:], scalar1=1.0 - lam_init, scalar2=None, op0=ALU.mult),
         rd=[r_c], wr=[r_neglam_ev_holder])
    r_g2 = r_neglam_ev_holder
    r_att, r_sq, r_ln = R(), R(), R()
    r_rs2 = R()

    scale = 32.0 ** -0.5
    ecnt = [0]

    def qk(hp, s, h2, qs, kt):
        sb0 = (kt % 2) * 2
        for c in range(2):
            i = 2 * h2 + c
            P.op("pe", "matmul", dict(out=PS[:, sb0 + c, :], lhsT=KT[:, s, kt * 128:(kt + 1) * 128],
                                      rhs=qTm[:, s, i, qs], start=True, stop=True),
                 rd=[r_qm[s][i]] + r_kt[s], wr=[r_bank[sb0 + c]])

    def post_copy():
        for c in range(2):
            P.op("dve", "tensor_copy", dict(out=Osb[:, c, :], in_=PS[0:65, 4 + c, :]), rd=[r_bank[4 + c]], wr=[r_O[c]])

    def post_rest(h, qs, qb):
        for c in range(2):
            P.op("pe", "matmul", dict(out=PS[0:64, 6 + c, :], lhsT=sel[0:65, :], rhs=Osb[:, c, :], start=True, stop=True),
                 rd=[r_O[c], r_c], wr=[r_bank[6 + c]])
            P.op("dve", "reciprocal", dict(out=Rr[:, c, :], in_=PS[0:64, 6 + c, :]), rd=[r_bank[6 + c]], wr=[r_R[c]])
            P.op("pool", "tensor_tensor", dict(out=tt0[:, c, :], in0=Osb[0:64, c, :], in1=Rr[:, c, :], op=ALU.mult),
                 rd=[r_O[c], r_R[c]], wr=[r_t[c]])
        P.op("dve", "scalar_tensor_tensor", dict(out=att[:], in0=tt0[:, 1, :], scalar=neglam[0:64, 0:1], in1=tt0[:, 0, :],
                                                 op0=ALU.mult, op1=ALU.add), rd=[r_t[0], r_t[1], r_neglam], wr=[r_att])
        P.op("pool", "tensor_tensor", dict(out=sqa[:], in0=att[:], in1=att[:], op=ALU.mult), rd=[r_att], wr=[r_sq])
        P.op("pe", "matmul", dict(out=PS[0:64, 6, :], lhsT=o64[:], rhs=sqa[:], start=True, stop=True),
             rd=[r_sq, r_c], wr=[r_bank[6]])
        P.op("act", "activation", dict(out=lna[:], in_=PS[0:64, 6, :], func=AF.Ln, bias=EPS, scale=1.0),
             rd=[r_bank[6]], wr=[r_ln])
        P.op("act", "activation", dict(out=rsa[:], in_=lna[:], func=AF.Exp, scale=-0.5), rd=[r_ln], wr=[r_rs2])
        P.op("dve", "scalar_tensor_tensor", dict(out=aT[:, h, qs], in0=att[:], scalar=gsub2[:, 0:1], in1=rsa[:],
                                                 op0=ALU.mult, op1=ALU.mult), rd=[r_att, r_rs2, r_g2], wr=[r_aT[h][qb]])

    def mask_q(hp):
        s_ = hp % 2
        for i in range(4):
            P.op("dve", "tensor_scalar", dict(out=qTm[:, s_, i, :], in0=qT[:, hp, :], scalar1=qm[:, i:i + 1], scalar2=None,
                                               op0=ALU.mult), rd=[r_q[hp], r_c], wr=[r_qm[s_][i]])

    pending = None
    mask_q(0)
    for hp in range(4):
        s = hp % 2
        if hp + 1 < 4:
            load_kv(hp + 1)
            mask_q(hp + 1)
        for h2 in range(2):
            h = hp * 2 + h2
            for qb in range(4):
                qs = slice(qb * 512, (qb + 1) * 512)
                qk(hp, s, h2, qs, 0)
                qk(hp, s, h2, qs, 1)
                for kt in range(64):
                    sb0 = (kt % 2) * 2
                    eb = ecnt[0] % 3
                    ecnt[0] += 1
                    P.op("act", "activation", dict(out=E[:, eb, :, :], in_=PS[:, sb0:sb0 + 2, :], func=AF.Exp, scale=scale),
                         rd=[r_bank[sb0], r_bank[sb0 + 1]], wr=[r_E[eb]])
                    if kt + 2 < 64:
                        qk(hp, s, h2, qs, kt + 2)
                    for c in range(2):
                        P.op("pe", "matmul", dict(out=PS[0:65, 4 + c, :], lhsT=VP[:, s, kt, h2 * 65:(h2 + 1) * 65],
                                                  rhs=E[:, eb, c, :], start=(kt == 0), stop=(kt == 63)),
                             rd=[r_E[eb]] + r_vp[s], wr=[r_bank[4 + c]])
                    if kt == 6 and pending is not None:
                        post_rest(*pending)
                        pending = None
                post_copy()
                pending = (h, qs, qb)
    post_rest(*pending)
    fin = []
    for h in range(8):
        fin.append(P.op("sp", "dma_start", dict(out=aT_o[h], in_=aT[:, h, :]), rd=r_aT[h]))
    P.final_events = fin
    with P.stack:
        with nc.Block() as block:
            P.finalize(block)
    return nc


def consts_B1():
    sel = np.zeros((128, 64), np.float32)
    sel[64, :] = 1.0
    qm = np.zeros((128, 4), np.float32)
    for i in range(4):
        qm[32 * i:32 * i + 32, i] = 1.0
    return {"sel": sel, "ones64": bf(np.full((64, 64), 1.0 / 64.0)), "qmask": qm}


def run_B1(l, inp, qT_list, kT_list, vp_list):
    nc = build_B1(l)
    cb = consts_B1()
    lamv = np.stack([inp["lambda_q1"][l], inp["lambda_k1"][l], inp["lambda_q2"][l], inp["lambda_k2"][l]], 0)
    lamv = np.ascontiguousarray(np.broadcast_to(lamv[None], (128, 4, 32))).astype(np.float32)
    gsub = np.ascontiguousarray(inp["subln_g"][l].reshape(64, 1)).astype(np.float32)
    in_maps = []
    for c in range(NCORES):
        b = c // 4
        kT_all = np.ascontiguousarray(np.stack([kT_list[b * 4 + r] for r in range(4)], 0))
        vp_all = np.ascontiguousarray(np.stack([vp_list[b * 4 + r] for r in range(4)], 0))
        in_maps.append({"qT": qT_list[c], "kT_all": kT_all, "vp_all": vp_all, "lamv": lamv, "gsub": gsub,
                        "sel": cb["sel"], "ones64": cb["ones64"], "qmask": cb["qmask"]})
    res = run_bass_kernel_spmd(nc, in_maps, core_ids=list(range(NCORES)))
    return res.results


def build_B2():
    nc = bass.Bass("TRN2", target_bir_lowering=False)
    P = Prog(nc)
    g_d = P.dram("gT", [512, NT], BF16, "ExternalInput")
    gall_d = P.dram("g_all", [4, 512, NT], BF16, "ExternalInput")
    cw_d = P.dram("conv_w", [128, 4, 31], F32, "ExternalInput")
    cv4_d = P.dram("cvec", [128, 3, 4], F32, "ExternalInput")
    cm_d = P.dram("cmask", [128, 8], F32, "ExternalInput")
    id_d = P.dram("ident", [128, 128], BF16, "ExternalInput")
    o512_d = P.dram("ones512", [128, 128], F32, "ExternalInput")
    bT_o = P.dram("bT", [512, NT], BF16, "ExternalOutput")

    gp = P.sb("gpad", [128, 4, NT + 30], BF16)
    hal = P.sb("hal", [128, 4, 4, 2, 15], BF16)
    cw = P.sb("cw", [128, 4, 31], F32)
    cv4 = P.sb("cv4", [128, 3, 4], F32)
    cm = P.sb("cm", [128, 8], F32)
    ident = P.sb("ident_sb", [128, 128], BF16)
    o512 = P.sb("o512", [128, 128], F32)
    Dg = P.sb("Dg", [128, 4, 31, 128], BF16)
    cv = P.sb("cv", [128, 4, 512], F32)
    sqv = P.sb("sqv", [128, 4, 512], F32)
    m2 = P.sb("m2", [128, 512], F32)
    var = P.sb("var", [128, 512], F32)
    lnv = P.sb("lnv", [128, 512], F32)
    rstd = P.sb("rstd", [128, 512], F32)
    nmr = P.sb("nmr", [128, 512], F32)
    yv = P.sb("yv", [128, 2, 512], F32)
    bst = P.sb("bst", [128, 4, NT], BF16)
    PS = P.ps("ps", [128, 8, 512], F32)

    R = Res
    r_g = [R() for _ in range(4)]
    r_hal, r_c, r_D = R(), R(), [R() for _ in range(4)]
    r_bank = [R() for _ in range(8)]
    r_cv = [R() for _ in range(4)]
    r_sq = [R() for _ in range(4)]
    r_m2, r_var, r_ln, r_rstd, r_nmr = R(), R(), R(), R(), R()
    r_y = [R(), R()]
    r_b = [R() for _ in range(4)]

    for j in range(4):
        P.op("sp", "dma_start", dict(out=gp[:, j, 15:15 + NT], in_=g_d[j * 128:(j + 1) * 128, :]), wr=[r_g[j]])
    for i, (s_, d_) in enumerate([(cw, cw_d), (cv4, cv4_d), (cm, cm_d), (ident, id_d), (o512, o512_d)]):
        P.op("sp", "dma_start", dict(out=s_[:], in_=d_), wr=[r_c])
    rh = [R() for _ in range(8)]
    for r in range(4):
        gv = gall_d[r].rearrange("(j p) t -> p j t", p=128)
        P.op("sp", "dma_start", dict(out=hal[:, :, r, 0, :], in_=gv[:, :, NT - 15:NT]), wr=[rh[2 * r]])
        P.op("sp", "dma_start", dict(out=hal[:, :, r, 1, :], in_=gv[:, :, 0:15]), wr=[rh[2 * r + 1]])
    for side in range(2):
        dst = gp[:, :, 0:15] if side == 0 else gp[:, :, 15 + NT:30 + NT]
        for r in range(4):
            mk = cm[:, side * 4 + r:side * 4 + r + 1]
            if r == 0:
                P.op("dve", "tensor_scalar", dict(out=dst, in0=hal[:, :, r, side, :], scalar1=mk, scalar2=None, op0=ALU.mult),
                     rd=[rh[2 * r + side], r_c], wr=r_g)
            else:
                P.op("dve", "scalar_tensor_tensor", dict(out=dst, in0=hal[:, :, r, side, :], scalar=mk, in1=dst,
                                                         op0=ALU.mult, op1=ALU.add), rd=[rh[2 * r + side], r_c], wr=r_g)
    n = 0
    for j in range(4):
        for tau in range(31):
            q = "dve" if n % 2 == 0 else "pool"
            n += 1
            P.op(q, "tensor_scalar", dict(out=Dg[:, j, tau, :], in0=ident[:], scalar1=cw[:, j, tau:tau + 1], scalar2=None,
                                          op0=ALU.mult), rd=[r_c], wr=[r_D[j]])
    fin = []
    for tb in range(NTB):
        ts_ = slice(tb * 512, (tb + 1) * 512)
        for j in range(4):
            bank = j
            for tau in range(31):
                P.op("pe", "matmul", dict(out=PS[:, bank, :], lhsT=Dg[:, j, tau, :],
                                          rhs=gp[:, j, tb * 512 + tau:tb * 512 + tau + 512], start=(tau == 0), stop=(tau == 30)),
                     rd=[r_D[j], r_g[j]], wr=[r_bank[bank]])
            P.op("act", "activation", dict(out=cv[:, j, :], in_=PS[:, bank, :], func=AF.Identity, bias=cv4[:, 0, j:j + 1], scale=1.0),
                 rd=[r_bank[bank], r_c], wr=[r_cv[j]])
            P.op("pool", "tensor_tensor", dict(out=sqv[:, j, :], in0=cv[:, j, :], in1=cv[:, j, :], op=ALU.mult),
                 rd=[r_cv[j]], wr=[r_sq[j]])
        for j in range(4):
            P.op("pe", "matmul", dict(out=PS[:, 4, :], lhsT=o512[:], rhs=cv[:, j, :], start=(j == 0), stop=(j == 3)),
                 rd=[r_cv[j], r_c], wr=[r_bank[4]])
        for j in range(4):
            P.op("pe", "matmul", dict(out=PS[:, 5, :], lhsT=o512[:], rhs=sqv[:, j, :], start=(j == 0), stop=(j == 3)),
                 rd=[r_sq[j], r_c], wr=[r_bank[5]])
        P.op("act", "activation", dict(out=m2[:], in_=PS[:, 4, :], func=AF.Square), rd=[r_bank[4]], wr=[r_m2])
        P.op("dve", "tensor_tensor", dict(out=var[:], in0=PS[:, 5, :], in1=m2[:], op=ALU.subtract), rd=[r_bank[5], r_m2], wr=[r_var])
        P.op("act", "activation", dict(out=lnv[:], in_=var[:], func=AF.Ln, bias=EPS, scale=1.0), rd=[r_var], wr=[r_ln])
        P.op("act", "activation", dict(out=rstd[:], in_=lnv[:], func=AF.Exp, scale=-0.5), rd=[r_ln], wr=[r_rstd])
        P.op("dve", "scalar_tensor_tensor", dict(out=nmr[:], in0=PS[:, 4, :], scalar=-1.0, in1=rstd[:], op0=ALU.mult, op1=ALU.mult),
             rd=[r_bank[4], r_rstd], wr=[r_nmr])
        for j in range(4):
            s = j % 2
            P.op("pool", "tensor_tensor", dict(out=yv[:, s, :], in0=cv[:, j, :], in1=rstd[:], op=ALU.mult),
                 rd=[r_cv[j], r_rstd], wr=[r_y[s]])
            P.op("dve", "tensor_tensor", dict(out=yv[:, s, :], in0=yv[:, s, :], in1=nmr[:], op=ALU.add),
                 rd=[r_nmr], wr=[r_y[s]])
            P.op("act", "activation", dict(out=bst[:, j, ts_], in_=yv[:, s, :], func=AF.Silu, bias=cv4[:, 2, j:j + 1],
                                           scale=cv4[:, 1, j:j + 1]), rd=[r_y[s], r_c], wr=[r_b[j]])
    for j in range(4):
        fin.append(P.op("sp", "dma_start", dict(out=bT_o[j * 128:(j + 1) * 128, :], in_=bst[:, j, :]), rd=[r_b[j]]))
    P.final_events = fin
    with P.stack:
        with nc.Block() as block:
            P.finalize(block)
    return nc


def run_B2(l, inp, gT_list):
    nc = build_B2()
    cw = np.ascontiguousarray(inp["conv_w"][l].T.reshape(4, 128, 31).transpose(1, 0, 2)).astype(np.float32)
    cvec = np.stack([inp["conv_b"][l].reshape(4, 128).T, inp["conv_ln_g"][l].reshape(4, 128).T,
                     inp["conv_ln_b"][l].reshape(4, 128).T], axis=1).astype(np.float32)
    ident = bf(np.eye(128))
    o512 = np.full((128, 128), 1.0 / 512.0, np.float32)
    in_maps = []
    for c in range(NCORES):
        b, a = c // 4, c % 4
        cm = np.zeros((128, 8), np.float32)
        if a > 0:
            cm[:, a - 1] = 1.0
        if a < 3:
            cm[:, 4 + a + 1] = 1.0
        g_all = np.ascontiguousarray(np.stack([gT_list[b * 4 + r] for r in range(4)], 0))
        in_maps.append({"gT": gT_list[c], "g_all": g_all, "conv_w": cw, "cvec": np.ascontiguousarray(cvec), "cmask": cm,
                        "ident": ident, "ones512": o512})
    res = run_bass_kernel_spmd(nc, in_maps, core_ids=list(range(NCORES)))
    return res.results


def build_B3():
    nc = bass.Bass("TRN2", target_bir_lowering=False)
    P = Prog(nc)
    z_d = P.dram("z_all", [4, 4, NT, 256], BF16, "ExternalInput")
    w64_d = P.dram("w64", [128, 2, 128], BF16, "ExternalInput")
    tt_d = P.dram("ttab", [128, 64, 2, 32], BF16, "ExternalInput")
    fT_o = P.dram("fT", [512, NT], BF16, "ExternalOutput")

    Zt = P.sb("Zt", [128, 1, 128, 256], BF16)
    w64 = P.sb("w64_sb", [128, 2, 128], BF16)
    ttab = P.sb("ttab_sb", [128, 64, 2, 32], BF16)
    A = P.sb("A_sb", [128, 2, 128, 2, 64], BF16)
    fst = P.sb("fst", [128, 4, NT], BF16)
    PS = P.ps("ps", [128, 8, 512], F32)

    R = Res
    r_z = [[R() for _ in range(4)] for _ in range(2)]
    r_c = R()
    r_A = [[R() for _ in range(32)] for _ in range(2)]
    r_bank = [R() for _ in range(8)]
    r_f = [R() for _ in range(4)]

    P.op("sp", "dma_start", dict(out=w64[:], in_=w64_d), wr=[r_c])
    P.op("sp", "dma_start", dict(out=ttab[:], in_=tt_d), wr=[r_c])

    def load_pair(gp_):
        sl = 0
        for g2 in range(2):
            gr = gp_ * 2 + g2
            for r in range(4):
                P.op("sp", "dma_start", dict(out=Zt[64 * g2 + 16 * r:64 * g2 + 16 * r + 16, sl, :, :],
                                             in_=z_d[r, gr].rearrange("(p s) c -> p s c", s=128)), wr=[r_z[sl][g2 * 2 + r // 2]])

    load_pair(0)
    ev_n = 0
    bank_n = 0
    fin = []
    for gp_ in range(2):
        sl = 0
        if gp_ == 1:
            load_pair(1)
        for g2 in range(2):
            gr = gp_ * 2 + g2
            asl = gr % 2
            rows = slice(64 * g2, 64 * g2 + 64)
            for j0 in range(0, 128, 4):
                bank = bank_n % 4
                bank_n += 1
                for jj in range(4):
                    j = j0 + jj
                    for c in range(2):
                        P.op("pe", "matmul", dict(out=PS[:, bank, jj * 128:(jj + 1) * 128], lhsT=Zt[rows, sl, :, c * 128 + j],
                                                  rhs=w64[rows, c, :], start=(c == 0), stop=(c == 1)),
                             rd=[r_z[sl][g2 * 2], r_z[sl][g2 * 2 + 1], r_c], wr=[r_bank[bank]])
                q = "act" if ev_n % 2 == 0 else "dve"
                ev_n += 1
                if q == "act":
                    P.op("act", "activation", dict(out=A[:, asl, j0:j0 + 4, :, :], in_=PS[:, bank, :], func=AF.Copy),
                         rd=[r_bank[bank]], wr=[r_A[asl][j0 // 4]])
                else:
                    P.op("dve", "tensor_copy", dict(out=A[:, asl, j0:j0 + 4, :, :], in_=PS[:, bank, :]),
                         rd=[r_bank[bank]], wr=[r_A[asl][j0 // 4]])
            for kb in range(4):
                bank = 4 + (kb % 2)
                for kk in range(16):
                    k2 = kb * 16 + kk
                    for c in range(2):
                        P.op("pe", "matmul", dict(out=PS[:, bank, kk * 32:(kk + 1) * 32], lhsT=A[:, asl, :, c, k2],
                                                  rhs=ttab[:, k2, c, :], start=(c == 0), stop=(c == 1)),
                             rd=r_A[asl] + [r_c], wr=[r_bank[bank]])
                dst = fst[:, gr, :].rearrange("p (a b) -> p a b", b=64)[:, :, kb * 16:(kb + 1) * 16]
                src = PS[:, bank, :].rearrange("p (b a) -> p a b", a=32)
                P.op("dve", "tensor_copy", dict(out=dst, in_=src), rd=[r_bank[bank]], wr=[r_f[gr]])
    for gr in range(4):
        fin.append(P.op("sp", "dma_start", dict(out=fT_o[gr * 128:(gr + 1) * 128, :], in_=fst[:, gr, :]), rd=[r_f[gr]]))
    P.final_events = fin
    with P.stack:
        with nc.Block() as block:
            P.finalize(block)
    return nc


def consts_B3(core):
    a = core % 4
    s2 = np.arange(64, dtype=np.float64)
    k2 = np.arange(64, dtype=np.float64)
    th = 2 * np.pi * np.outer(s2, k2) / 64.0
    C, S = np.cos(th), np.sin(th)
    w = np.stack([np.concatenate([C, -S], 1), np.concatenate([S, C], 1)], 1)
    w64 = bf(np.concatenate([w, w], 0))
    s1 = np.arange(128, dtype=np.float64)[:, None, None]
    k2_ = np.arange(64, dtype=np.float64)[None, :, None]
    k1 = (32 * a + np.arange(32, dtype=np.float64))[None, None, :]
    ph = 2 * np.pi * s1 * (64 * k1 + k2_) / 8192.0
    sc = 2.0 ** -10
    tt = np.stack([np.cos(ph) * sc, np.sin(ph) * sc], 2)
    return {"w64": w64, "ttab": bf(tt)}


def run_B3(zp_list):
    nc = build_B3()
    in_maps = []
    for c in range(NCORES):
        b = c // 4
        cb = consts_B3(c)
        z_all = np.ascontiguousarray(np.stack([zp_list[b * 4 + r] for r in range(4)], 0))
        in_maps.append({"z_all": z_all, "w64": cb["w64"], "ttab": cb["ttab"]})
    res = run_bass_kernel_spmd(nc, in_maps, core_ids=list(range(NCORES)))
    return res.results


DEBUG_C = False


def build_C():
    nc = bass.Bass("TRN2", target_bir_lowering=False)
    P = Prog(nc)
    xT_d = P.dram("xT", [D, NT], F32, "ExternalInput")
    aT_d = P.dram("aT", [8, 64, NT], BF16, "ExternalInput")
    bT_d = P.dram("bT", [512, NT], BF16, "ExternalInput")
    fT_d = P.dram("fT", [512, NT], BF16, "ExternalInput")
    wpa_d = P.dram("w_proj_a", [512, D], F32, "ExternalInput")
    wpb_d = P.dram("w_proj_b", [512, D], F32, "ExternalInput")
    wpc_d = P.dram("w_proj_c", [512, D], F32, "ExternalInput")
    wg_d = P.dram("w_gate", [D, 3 * D], F32, "ExternalInput")
    bg_d = P.dram("b_gate", [128, 24], F32, "ExternalInput")
    wo_d = P.dram("w_out", [D, D], F32, "ExternalInput")
    g12_d = P.dram("g12", [128, 2, 8], F32, "ExternalInput")
    ones_d = P.dram("ones_d", [128, 128], BF16, "ExternalInput")
    w1_d = P.dram("w_ffn_in", [D, 2 * DFF], F32, "ExternalInput")
    w2_d = P.dram("w_ffn_out", [DFF, D], F32, "ExternalInput")
    xo_d = P.dram("xT_out", [D, NT], F32, "ExternalOutput")

    xT = P.sb("xT_sb", [128, 8, NT], F32)
    ARENA_BYTES = 137216
    arena = P.sb("arena", [128, ARENA_BYTES // 2], BF16)

    def carve(off, shape, dt):
        nb = (4 if dt == F32 else 2)
        n = 1
        for d_ in shape[1:]:
            n *= d_
        v = arena[:, off // 2:off // 2 + n * nb // 2]
        if dt == F32:
            v = v.bitcast(F32)
        if len(shape) == 2:
            return v
        if len(shape) == 3:
            return v.rearrange("p (a b) -> p a b", b=shape[2])
        raise ValueError

    WA = carve(0, [128, 8, 3072], BF16)
    WB = carve(49152, [128, 20, 1024], BF16)
    hT = carve(90112, [128, 8, 512], BF16)
    brA = carve(98304, [128, 4, 512], BF16)
    brB = carve(102400, [128, 4, 512], BF16)
    brC = carve(106496, [128, 4, 512], BF16)
    sig = carve(110592, [128, 3, 512], F32)
    tm = carve(116736, [128, 3, 512], F32)
    mT = carve(122880, [128, 8, 512], BF16)
    sq = carve(131072, [128, 2, 512], BF16)
    lnv = carve(133120, [128, 512], F32)
    rstd = carve(135168, [128, 512], F32)
    h2T = carve(71680, [128, 8, NT], BF16)
    actT = carve(110592, [128, 11, 512], BF16)
    sgf = carve(122880, [128, 2, 512], F32)
    bg = P.sb("bg", [128, 24], F32)
    g12 = P.sb("g12_sb", [128, 2, 8], F32)
    onesm = P.sb("ones_sb", [128, 128], BF16)
    PS = P.ps("ps", [128, 8, 512], F32)

    R = Res
    r_x = [R() for _ in range(8)]
    r_wa = [R() for _ in range(8)]
    r_wb = [R() for _ in range(20)]
    r_wpa = R()
    r_c = R()
    r_h = R()
    r_sq = [R(), R()]
    r_ln, r_rstd = R(), R()
    r_br = [R(), R(), R()]
    r_sig = [R(), R(), R()]
    r_tm = [R(), R(), R()]
    r_m = [R() for _ in range(8)]
    r_bank = [R() for _ in range(8)]
    r_act = [R() for _ in range(11)]
    r_sgf = [R(), R()]

    for kc in range(8):
        P.op("sp", "dma_start", dict(out=xT[:, kc, :], in_=xT_d[kc * 128:(kc + 1) * 128, :]), wr=[r_x[kc]])
    for s_, d_ in [(bg, bg_d), (g12, g12_d), (onesm, ones_d)]:
        P.op("sp", "dma_start", dict(out=s_[:], in_=d_), wr=[r_c])
    for kc in range(8):
        for cb in range(3):
            P.op("poolq", "dma_start", dict(out=WA[:, kc, cb * 1024:(cb + 1) * 1024],
                                            in_=wg_d[kc * 128:(kc + 1) * 128, cb * 1024:(cb + 1) * 1024]), wr=[r_wa[kc]])
    for kc in range(8):
        P.op("poolq", "dma_start", dict(out=WB[:, kc, :], in_=wo_d[kc * 128:(kc + 1) * 128, :]), wr=[r_wb[kc]])
    for j in range(4):
        P.op("poolq", "dma_start", dict(out=WB[:, 8 + j, :], in_=wpb_d[j * 128:(j + 1) * 128, :]), wr=[r_wb[8 + j]])
        P.op("poolq", "dma_start", dict(out=WB[:, 12 + j, :], in_=wpc_d[j * 128:(j + 1) * 128, :]), wr=[r_wb[12 + j]])
        P.op("poolq", "dma_start", dict(out=WB[:, 16 + j, :], in_=wpa_d[j * 128:(j + 1) * 128, :]), wr=[r_wb[16 + j]])

    def rmsnorm_block(tb, gi, hdst):
        ts_ = slice(tb * 512, (tb + 1) * 512)
        for kc in range(8):
            s = kc % 2
            P.op("pool", "tensor_tensor", dict(out=sq[:, s, :], in0=xT[:, kc, ts_], in1=xT[:, kc, ts_], op=ALU.mult),
                 rd=[r_x[kc]], wr=[r_sq[s]])
            P.op("pe", "matmul", dict(out=PS[:, 7, :], lhsT=onesm[:], rhs=sq[:, s, :], start=(kc == 0), stop=(kc == 7)),
                 rd=[r_sq[s], r_c], wr=[r_bank[7]])
        P.op("act", "activation", dict(out=lnv[:], in_=PS[:, 7, :], func=AF.Ln, bias=EPS, scale=1.0), rd=[r_bank[7]], wr=[r_ln])
        P.op("act", "activation", dict(out=rstd[:], in_=lnv[:], func=AF.Exp, scale=-0.5), rd=[r_ln], wr=[r_rstd])
        for kc in range(8):
            P.op("dve", "scalar_tensor_tensor", dict(out=hdst[:, kc, :], in0=xT[:, kc, ts_], scalar=g12[:, gi, kc:kc + 1],
                                                     in1=rstd[:], op0=ALU.mult, op1=ALU.mult),
                 rd=[r_x[kc], r_rstd, r_c], wr=[r_h])

    for tb in range(NTB):
        ts_ = slice(tb * 512, (tb + 1) * 512)
        P.op("sp", "dma_start", dict(out=brA[:], in_=aT_d[:, :, ts_].rearrange("(j h) p t -> (h p) j t", h=2)), wr=[r_br[0]])
        P.op("sp", "dma_start", dict(out=brB[:], in_=bT_d[:, ts_].rearrange("(j p) t -> p j t", p=128)), wr=[r_br[1]])
        P.op("sp", "dma_start", dict(out=brC[:], in_=fT_d[:, ts_].rearrange("(j p) t -> p j t", p=128)), wr=[r_br[2]])
        rmsnorm_block(tb, 0, hT[:, :, 0:512])
        for oc in range(8):
            ocs = slice(oc * 128, (oc + 1) * 128)
            for j in range(4):
                P.op("pe", "matmul", dict(out=PS[:, 0, :], lhsT=WB[:, 16 + j, ocs], rhs=brA[:, j, :], start=(j == 0), stop=(j == 3)),
                     rd=[r_wb[16 + j], r_br[0]], wr=[r_bank[0]])
            for j in range(4):
                P.op("pe", "matmul", dict(out=PS[:, 1, :], lhsT=WB[:, 8 + j, ocs], rhs=brB[:, j, :], start=(j == 0), stop=(j == 3)),
                     rd=[r_wb[8 + j], r_br[1]], wr=[r_bank[1]])
            for j in range(4):
                P.op("pe", "matmul", dict(out=PS[:, 2, :], lhsT=WB[:, 12 + j, ocs], rhs=brC[:, j, :], start=(j == 0), stop=(j == 3)),
                     rd=[r_wb[12 + j], r_br[2]], wr=[r_bank[2]])
            for i in range(3):
                for kc in range(8):
                    P.op("pe", "matmul", dict(out=PS[:, 3 + i, :], lhsT=WA[:, kc, i * 1024 + oc * 128:i * 1024 + (oc + 1) * 128],
                                              rhs=hT[:, kc, 0:512], start=(kc == 0), stop=(kc == 7)),
                         rd=[r_wa[kc], r_h], wr=[r_bank[3 + i]])
                P.op("act", "activation", dict(out=sig[:, i, :], in_=PS[:, 3 + i, :], func=AF.Sigmoid,
                                               bias=bg[:, i * 8 + oc:i * 8 + oc + 1], scale=1.0),
                     rd=[r_bank[3 + i], r_c], wr=[r_sig[i]])
                P.op("dve", "tensor_tensor", dict(out=tm[:, i, :], in0=PS[:, i, :], in1=sig[:, i, :], op=ALU.mult),
                     rd=[r_bank[i], r_sig[i]], wr=[r_tm[i]])
            P.op("pool", "tensor_tensor", dict(out=tm[:, 0, :], in0=tm[:, 0, :], in1=tm[:, 1, :], op=ALU.add),
                 rd=[r_tm[1]], wr=[r_tm[0]])
            P.op("pool", "tensor_tensor", dict(out=mT[:, oc, :], in0=tm[:, 0, :], in1=tm[:, 2, :], op=ALU.add),
                 rd=[r_tm[0], r_tm[2]], wr=[r_m[oc]])
        for oc2 in range(8):
            bank = 6
            for kc in range(8):
                P.op("pe", "matmul", dict(out=PS[:, bank, :], lhsT=WB[:, kc, oc2 * 128:(oc2 + 1) * 128], rhs=mT[:, kc, :],
                                          start=(kc == 0), stop=(kc == 7)), rd=[r_wb[kc], r_m[kc]], wr=[r_bank[bank]])
            P.op("dve", "tensor_tensor", dict(out=xT[:, oc2, ts_], in0=PS[:, bank, :], in1=xT[:, oc2, ts_], op=ALU.add),
                 rd=[r_bank[bank]], wr=[r_x[oc2]])

    fin = []
    if DEBUG_C:
        x1_d = P.dram("x1_out", [D, NT], F32, "ExternalOutput")
        for kc in range(8):
            fin.append(P.op("sp", "dma_start", dict(out=x1_d[kc * 128:(kc + 1) * 128, :], in_=xT[:, kc, :]), rd=[r_x[kc]]))
    P.op("dve", "memset", dict(ap=lnv[:, 0:8], constant=0.0), rd=[], wr=r_sig + r_tm + r_br + r_act + r_sgf + r_m + [r_ln, r_h] + r_wb[11:20])
    for tb in range(NTB):
        rmsnorm_block(tb, 1, h2T[:, :, tb * 512:(tb + 1) * 512])
    for half in range(2):
        c0 = half * 1408
        for kc in range(8):
            P.op("poolq", "dma_start", dict(out=WA[:, kc, 0:1408], in_=w1_d[kc * 128:(kc + 1) * 128, c0:c0 + 1408]), wr=[r_wa[kc]])
            P.op("poolq", "dma_start", dict(out=WA[:, kc, 1408:2816], in_=w1_d[kc * 128:(kc + 1) * 128, DFF + c0:DFF + c0 + 1408]),
                 wr=[r_wa[kc]])
        for j in range(11):
            P.op("poolq", "dma_start", dict(out=WB[:, j, :], in_=w2_d[c0 + j * 128:c0 + (j + 1) * 128, :]), wr=[r_wb[j]])
        for tb in range(NTB):
            ts_ = slice(tb * 512, (tb + 1) * 512)
            for j in range(11):
                s = j % 2
                bg_, bu_ = (0, 1) if s == 0 else (2, 3)
                for kc in range(8):
                    P.op("pe", "matmul", dict(out=PS[:, bg_, :], lhsT=WA[:, kc, j * 128:(j + 1) * 128], rhs=h2T[:, kc, ts_],
                                              start=(kc == 0), stop=(kc == 7)), rd=[r_wa[kc], r_h], wr=[r_bank[bg_]])
                for kc in range(8):
                    P.op("pe", "matmul", dict(out=PS[:, bu_, :], lhsT=WA[:, kc, 1408 + j * 128:1408 + (j + 1) * 128], rhs=h2T[:, kc, ts_],
                                              start=(kc == 0), stop=(kc == 7)), rd=[r_wa[kc], r_h], wr=[r_bank[bu_]])
                P.op("act", "activation", dict(out=sgf[:, s, :], in_=PS[:, bg_, :], func=AF.Silu), rd=[r_bank[bg_]], wr=[r_sgf[s]])
                P.op("dve", "tensor_tensor", dict(out=actT[:, j, :], in0=PS[:, bu_, :], in1=sgf[:, s, :], op=ALU.mult),
                     rd=[r_bank[bu_], r_sgf[s]], wr=[r_act[j]])
            for oc in range(8):
                bank = 4 + (oc % 2)
                for j in range(11):
                    P.op("pe", "matmul", dict(out=PS[:, bank, :], lhsT=WB[:, j, oc * 128:(oc + 1) * 128], rhs=actT[:, j, :],
                                              start=(j == 0), stop=(j == 10)), rd=[r_wb[j], r_act[j]], wr=[r_bank[bank]])
                P.op("dve", "tensor_tensor", dict(out=xT[:, oc, ts_], in0=PS[:, bank, :], in1=xT[:, oc, ts_], op=ALU.add),
                     rd=[r_bank[bank]], wr=[r_x[oc]])
    for kc in range(8):
        fin.append(P.op("sp", "dma_start", dict(out=xo_d[kc * 128:(kc + 1) * 128, :], in_=xT[:, kc, :]), rd=[r_x[kc]]))
    P.final_events = fin
    with P.stack:
        with nc.Block() as block:
            P.finalize(block)
    return nc


def run_C(l, inp, xT_list, aT_list, bT_list, fT_list):
    nc = build_C()
    ct = const_tables(0)
    g12 = np.stack([inp["norm1_g"][l].reshape(8, 128).T, inp["norm2_g"][l].reshape(8, 128).T], axis=1).astype(np.float32)
    bgate = np.ascontiguousarray(inp["b_gate"][l].reshape(24, 128).T).astype(np.float32)
    in_maps = []
    for c in range(NCORES):
        in_maps.append({"xT": xT_list[c], "aT": aT_list[c], "bT": bT_list[c], "fT": fT_list[c],
                        "w_proj_a": np.ascontiguousarray(inp["w_proj_a"][l]), "w_proj_b": np.ascontiguousarray(inp["w_proj_b"][l]),
                        "w_proj_c": np.ascontiguousarray(inp["w_proj_c"][l]), "w_gate": np.ascontiguousarray(inp["w_gate"][l]),
                        "b_gate": bgate, "w_out": np.ascontiguousarray(inp["w_out"][l]), "g12": np.ascontiguousarray(g12),
                        "ones_d": ct["ones_d"], "w_ffn_in": np.ascontiguousarray(inp["w_ffn_in"][l]),
                        "w_ffn_out": np.ascontiguousarray(inp["w_ffn_out"][l])})
    res = run_bass_kernel_spmd(nc, in_maps, core_ids=list(range(NCORES)))
    return res.results


def kernel(**inp):
    inp = {k: np.asarray(v) for k, v in inp.items()}
    x = inp["x"]
    xT = [np.ascontiguousarray(x[c // 4, (c % 4) * NT:(c % 4 + 1) * NT, :].T) for c in range(NCORES)]
    for l in range(2):
        rA = run_A(xT, l, inp)
        qT = [np.asarray(r["qT"]) for r in rA]
        kT = [np.asarray(r["kT"]) for r in rA]
        vp = [np.asarray(r["vp"]) for r in rA]
        gT = [np.asarray(r["gT"]) for r in rA]
        zp = [np.asarray(r["zp"]) for r in rA]
        rB1 = run_B1(l, inp, qT, kT, vp)
        rB2 = run_B2(l, inp, gT)
        rB3 = run_B3(zp)
        rC = run_C(l, inp, xT, [np.asarray(r["aT"]) for r in rB1], [np.asarray(r["bT"]) for r in rB2],
                   [np.asarray(r["fT"]) for r in rB3])
        xT = [np.ascontiguousarray(np.asarray(r["xT_out"])) for r in rC]
    out = np.empty((2, SEQ, D), np.float32)
    for c in range(NCORES):
        out[c // 4, (c % 4) * NT:(c % 4 + 1) * NT, :] = xT[c].T
    return out
```

```python
import math
from contextlib import ExitStack
import numpy as np
import ml_dtypes
import concourse.bass as bass
import concourse.mybir as mybir
from concourse.bass_utils import run_bass_kernel_spmd

F32 = mybir.dt.float32
BF16 = mybir.dt.bfloat16
AF = mybir.ActivationFunctionType
ALU = mybir.AluOpType

NCORES = 8
D = 1024
SEQ = 8192
NT = 2048
NTB = 4
EPS = 1e-6
DFF = 2816
ROPE_THETA = 500000.0


class Res:
    __slots__ = ("w", "r", "name")

    def __init__(self, name=""):
        self.w = None
        self.r = []
        self.name = name


class Ev:
    __slots__ = ("q", "idx", "needed", "sem", "val")

    def __init__(self, q, idx):
        self.q = q
        self.idx = idx
        self.needed = False
        self.sem = None
        self.val = None


COMPUTE_Q = ("pe", "act", "dve", "pool")
DMA_Q = ("sp", "actq", "poolq")
ENGINE_OF = {"pe": "tensor", "act": "scalar", "dve": "vector", "pool": "gpsimd",
             "sp": "sync", "actq": "scalar", "poolq": "gpsimd"}
NDMASEM = 6


class Prog:
    ARENA_BYTES = 212736

    def __init__(self, nc, internal=()):
        self.nc = nc
        self.stack = ExitStack()
        self.arena = None
        self.off = 0
        self.psum = None
        self.drams = {}
        self.internal = set(internal)
        self.barrier_ev = None
        self.dma_pending = []
        self.last_ev = {}
        self.bscr = None
        self.streams = {"tensor": [], "scalar": [], "vector": [], "gpsimd": [], "sync": []}
        self.evcount = {q: 0 for q in COMPUTE_Q + DMA_Q}
        self.dma_n = {q: 0 for q in DMA_Q}
        self.dma_last = {}
        self.sems = {}
        self.final_events = []

    def sb(self, name, shape, dt):
        if self.arena is None:
            self.arena = self.stack.enter_context(self.nc.sbuf_tensor("arena", [128, self.ARENA_BYTES // 2], BF16))
            self.bscr = self.stack.enter_context(self.nc.sbuf_tensor("bscr", [128, 8], F32))
        nb = 4 if dt == F32 else 2
        n = 1
        for d_ in shape[1:]:
            n *= d_
        nbytes = (n * nb + 63) // 64 * 64
        off = self.off
        assert off + nbytes <= self.ARENA_BYTES, (name, off, nbytes)
        self.off = off + nbytes
        v = self.arena[:, off // 2:off // 2 + n * nb // 2]
        if dt == F32:
            v = v.bitcast(F32)
        if shape[0] < 128:
            v = v[0:shape[0]]
        dims = list(shape[1:])
        if len(dims) == 1:
            return v
        names = "abcd"[:len(dims)]
        pat = "p (%s) -> p %s" % (" ".join(names), " ".join(names))
        return v.rearrange(pat, **{names[i]: dims[i] for i in range(1, len(dims))})

    def phase_begin(self, keep=0):
        self.off = keep

    def ps(self, name, shape, dt=F32):
        if self.psum is None:
            self.psum = self.stack.enter_context(self.nc.psum_tensor(name, list(shape), dt))
        return self.psum

    def dram(self, name, shape, dt, kind):
        if name in self.drams:
            return self.drams[name]
        if name in self.internal:
            kind = "Internal"
        ap = self.nc.dram_tensor(name, list(shape), dt, kind=kind).ap()
        self.drams[name] = ap
        return ap

    def barrier(self):
        deps = list(self.last_ev.values()) + list(self.dma_pending)
        r = Res()
        ev = self.op("pool", "memset", dict(ap=self.bscr[:, 0:2], constant=0.0), wr=[r], extra=deps)
        self.barrier_ev = ev
        self.dma_pending = []

    def op(self, q, name, kw, rd=(), wr=(), extra=()):
        fn = (name, kw)
        deps = list(extra)
        if self.barrier_ev is not None:
            deps.append(self.barrier_ev)
        for r in rd:
            if r.w is not None:
                deps.append(r.w)
        for w in wr:
            if w.w is not None:
                deps.append(w.w)
            deps.extend(w.r)
        self.evcount[q] += 1
        ev = Ev(q, self.evcount[q])
        if q in DMA_Q:
            slot = self.dma_n[q] % NDMASEM
            self.dma_n[q] += 1
            prev = self.dma_last.get((q, slot))
            if prev is not None:
                deps.append(prev)
            self.dma_last[(q, slot)] = ev
            ev.sem = (q, slot)
        else:
            ev.sem = (q, 0)
        best = {}
        dmas = []
        for d in deps:
            if d.q in COMPUTE_Q:
                if d.q == "pe" and q == "pe":
                    continue
                if d.q not in best or best[d.q].idx < d.idx:
                    best[d.q] = d
            else:
                if d not in dmas:
                    dmas.append(d)
        waits = list(best.values()) + dmas
        self.streams[ENGINE_OF[q]].append((q, fn, waits, ev))
        if q in DMA_Q:
            self.dma_pending.append(ev)
        else:
            self.last_ev[q] = ev
        for r in rd:
            r.r.append(ev)
        for w in wr:
            w.w = ev
            w.r = []
        return ev

    def finalize(self, block):
        nc = self.nc
        for eng, stream in self.streams.items():
            seen = {}
            for item in stream:
                q, fn, waits, ev = item
                keep = []
                for d in waits:
                    if d.q in COMPUTE_Q:
                        if seen.get(d.q, 0) >= d.idx:
                            continue
                        seen[d.q] = d.idx
                    else:
                        if seen.get(id(d)):
                            continue
                        seen[id(d)] = True
                    d.needed = True
                    keep.append(d)
                item[2][:] = keep
        for ev in self.final_events:
            ev.needed = True
        counters = {}
        for eng, stream in self.streams.items():
            for q, fn, waits, ev in stream:
                if q in DMA_Q:
                    key = ev.sem
                    counters[key] = counters.get(key, 0) + 16
                    ev.val = counters[key]
                    ev.needed = True
                elif ev.needed:
                    key = ev.sem
                    counters[key] = counters.get(key, 0) + 1
                    ev.val = counters[key]
        for key in counters:
            self.sems[key] = self.stack.enter_context(nc.semaphore("s_%s_%d" % key))
        self.maxcount = dict(counters)

        def emit(engname):
            stream = self.streams[engname]

            def body(eng):
                for q, fn, waits, ev in stream:
                    for d in waits:
                        eng.wait_ge(self.sems[d.sem], d.val)
                    ins = getattr(eng, fn[0])(**fn[1])
                    if ev.needed:
                        ins.then_inc(self.sems[ev.sem], 16 if q in DMA_Q else 1)
                if engname == "sync":
                    for ev in self.final_events:
                        eng.wait_ge(self.sems[ev.sem], ev.val)
            return body

        block.tensor(emit("tensor"))
        block.scalar(emit("scalar"))
        block.vector(emit("vector"))
        block.gpsimd(emit("gpsimd"))
        block.sync(emit("sync"))


def bf(a):
    return np.ascontiguousarray(np.asarray(a, dtype=np.float32)).astype(ml_dtypes.bfloat16)


def const_tables(core):
    a = core % 4
    t = {}
    t["ones_d"] = bf(np.full((128, 128), 1.0 / 1024.0))
    bd = np.zeros((128, 128), np.float32)
    for i in range(4):
        bd[32 * i:32 * i + 32, 32 * i:32 * i + 32] = 1.0 / 32.0
    t["bd32"] = bf(bd)
    rot = np.zeros((128, 128), np.float32)
    for blk in range(4):
        o = 32 * blk
        for i in range(4):
            rot[o + 4 + i, o + i] = -1.0
            rot[o + i, o + 4 + i] = 1.0
    t["rot"] = bf(rot)
    pos = (a * NT + np.arange(NT)).astype(np.float64)
    inv = 1.0 / (ROPE_THETA ** (np.arange(0, 8, 2, dtype=np.float64) / 8.0))
    ang = pos[None, :] * inv[:, None]
    cf = np.ones((128, NT), np.float32)
    sf = np.zeros((128, NT), np.float32)
    for blk in range(4):
        o = 32 * blk
        cf[o:o + 4] = np.cos(ang)
        cf[o + 4:o + 8] = np.cos(ang)
        sf[o:o + 4] = np.sin(ang)
        sf[o + 4:o + 8] = np.sin(ang)
    t["cosf"] = cf
    t["sinf"] = sf
    jc = np.outer(np.arange(128), np.arange(128)).astype(np.float64) * (2 * np.pi / 128.0)
    t["dftg"] = bf(np.concatenate([np.cos(jc), -np.sin(jc)], axis=1))
    return t


def phase_A(P, sfx, x_res=None):
    nc = P.nc
    if x_res is None:
        xT_d = P.dram("xT" + sfx, [D, NT], F32, "ExternalInput")
    win_d = P.dram("w_in" + sfx, [D, 3072], F32, "ExternalInput")
    g1_d = P.dram("g1" + sfx, [128, 8], F32, "ExternalInput")
    gqk_d = P.dram("gqk" + sfx, [128, 2], F32, "ExternalInput")
    cos_d = P.dram("cosf" + sfx, [128, NT], F32, "ExternalInput")
    sin_d = P.dram("sinf" + sfx, [128, NT], F32, "ExternalInput")
    ones_d = P.dram("ones_d" + sfx, [128, 128], BF16, "ExternalInput")
    bd_d = P.dram("bd32" + sfx, [128, 128], BF16, "ExternalInput")
    rot_d = P.dram("rot" + sfx, [128, 128], BF16, "ExternalInput")
    dftg_d = P.dram("dftg" + sfx, [128, 256], BF16, "ExternalInput")
    qT_o = P.dram("qT" + sfx, [512, NT], BF16, "ExternalOutput")
    kT_o = P.dram("kT" + sfx, [512, NT], BF16, "ExternalOutput")
    vp_o = P.dram("vp" + sfx, [4, NT, 130], BF16, "ExternalOutput")
    gT_o = P.dram("gT" + sfx, [512, NT], BF16, "ExternalOutput")
    zp_o = P.dram("zp" + sfx, [4, NT, 256], BF16, "ExternalOutput")

    xT = P.sb("xT_sb", [128, 8, NT], F32)
    W = P.sb("w_sb", [128, 8, 3072], BF16)
    g1 = P.sb("g1_sb", [128, 8], F32)
    gqk = P.sb("gqk_sb", [128, 2], F32)
    cosf = P.sb("cos_sb", [128, NT], F32)
    sinf = P.sb("sin_sb", [128, NT], F32)
    onesm = P.sb("ones_sb", [128, 128], BF16)
    bdm = P.sb("bd_sb", [128, 128], BF16)
    rotm = P.sb("rot_sb", [128, 128], BF16)
    dftg = P.sb("dftg_sb", [128, 256], BF16)
    hT = P.sb("hT", [128, 8, 512], BF16)
    sq = P.sb("sq", [128, 2, 512], BF16)
    lnv = P.sb("lnv", [128, 512], F32)
    rstd = P.sb("rstd", [128, 512], F32)
    sq2 = P.sb("sq2", [128, 2, 512], BF16)
    ln2 = P.sb("ln2", [128, 2, 512], F32)
    r2 = P.sb("r2", [128, 2, 512], F32)
    qn = P.sb("qn", [128, 2, 512], BF16)
    t1 = P.sb("t1", [128, 2, 512], F32)
    t2 = P.sb("t2", [128, 2, 512], F32)
    qkst = P.sb("qkst", [128, 1, 8, 512], BF16)
    vst = P.sb("vst", [128, 1, 4, 4, 130], BF16)
    sg = P.sb("sg", [128, 2, 512], F32)
    gst = P.sb("gst", [128, 1, 4, 512], BF16)
    fcT = P.sb("fcT", [128, 2, 512], BF16)
    zst = P.sb("zst", [128, 1, 4, 4, 256], BF16)
    PS = P.ps("ps", [128, 8, 512], F32)

    R = lambda n: Res(n)
    r_x = [R("x%d" % i) for i in range(8)]
    r_w = [R("w%d" % i) for i in range(6)]
    r_c = R("consts")
    r_h = R("hT")
    r_sq = [R("sq0"), R("sq1")]
    r_ln = R("lnv")
    r_rstd = R("rstd")
    r_bank = [R("bank%d" % i) for i in range(8)]
    r_sq2 = [R("a"), R("b")]
    r_ln2 = [R("a"), R("b")]
    r_r2 = [R("a"), R("b")]
    r_qn = [R("a"), R("b")]
    r_t1 = [R("a"), R("b")]
    r_t2 = [R("a"), R("b")]
    r_qk = [[R("qk%d" % i) for i in range(8)] for _ in range(2)]
    r_v = [R("vst0"), R("vst1")]
    r_sg = [R("a"), R("b")]
    r_g = [[R("g%d" % i) for i in range(4)] for _ in range(2)]
    r_fc = [R("a"), R("b")]
    r_z = [R("zst0"), R("zst1")]

    if x_res is None:
        for kc in range(8):
            P.op("sp", "dma_start", dict(out=xT[:, kc, :], in_=xT_d[kc * 128:(kc + 1) * 128, :]),
                 wr=[r_x[kc]])
    cl = [(g1, g1_d), (gqk, gqk_d), (cosf, cos_d), (sinf, sin_d), (onesm, ones_d), (bdm, bd_d),
          (rotm, rot_d), (dftg, dftg_d)]
    for i, (s, d_) in enumerate(cl):
        P.op("sp", "dma_start", dict(out=s[:], in_=d_[:, :]), wr=[r_c])
    for wb in range(6):
        for kc in range(8):
            P.op("poolq", "dma_start", dict(
                out=W[:, kc, wb * 512:(wb + 1) * 512],
                in_=win_d[kc * 128:(kc + 1) * 128, wb * 512:(wb + 1) * 512]), wr=[r_w[wb]])
    for sl in range(1):
        P.op("pool", "memset", dict(ap=vst[:, sl, :, :, 64:65], constant=1.0), wr=[r_v[sl]])
        P.op("pool", "memset", dict(ap=vst[:, sl, :, :, 129:130], constant=1.0), wr=[r_v[sl]])

    fin = []
    bank_rr = [0]

    def next_bank():
        b = bank_rr[0] % 4
        bank_rr[0] += 1
        return b

    def proj_chunk(oc, tb, bank):
        wb = (oc * 128) // 512
        for kc in range(8):
            P.op("pe", "matmul", dict(out=PS[:, bank, :], lhsT=W[:, kc, oc * 128:(oc + 1) * 128],
                                                 rhs=hT[:, kc, :], start=(kc == 0), stop=(kc == 7)),
                 rd=[r_w[wb], r_h, r_c], wr=[r_bank[bank]])

    for tb in range(NTB):
        ts_ = slice(tb * 512, (tb + 1) * 512)
        sl = 0
        for kc in range(8):
            s = kc % 2
            P.op("pool", "tensor_tensor", dict(out=sq[:, s, :], in0=xT[:, kc, ts_], in1=xT[:, kc, ts_],
                                                              op=ALU.mult), rd=[r_x[kc]], wr=[r_sq[s]])
            P.op("pe", "matmul", dict(out=PS[:, 4, :], lhsT=onesm[:], rhs=sq[:, s, :],
                                                      start=(kc == 0), stop=(kc == 7)),
                 rd=[r_sq[s], r_c], wr=[r_bank[4]])
        P.op("act", "activation", dict(out=lnv[:], in_=PS[:, 4, :], func=AF.Ln, bias=EPS, scale=1.0),
             rd=[r_bank[4]], wr=[r_ln])
        P.op("act", "activation", dict(out=rstd[:], in_=lnv[:], func=AF.Exp, scale=-0.5),
             rd=[r_ln], wr=[r_rstd])
        for kc in range(8):
            P.op("dve", "scalar_tensor_tensor", dict(out=hT[:, kc, :], in0=xT[:, kc, ts_],
                                                                scalar=g1[:, kc:kc + 1], in1=rstd[:],
                                                                op0=ALU.mult, op1=ALU.mult),
                 rd=[r_x[kc], r_rstd, r_c], wr=[r_h])
        for oc in range(8):
            s = oc % 2
            bank = next_bank()
            proj_chunk(oc, tb, bank)
            P.op("act", "activation", dict(out=sq2[:, s, :], in_=PS[:, bank, :], func=AF.Square),
                 rd=[r_bank[bank]], wr=[r_sq2[s]])
            P.op("pe", "matmul", dict(out=PS[:, 5, :], lhsT=bdm[:], rhs=sq2[:, s, :], start=True, stop=True),
                 rd=[r_sq2[s], r_c], wr=[r_bank[5]])
            P.op("act", "activation", dict(out=ln2[:, s, :], in_=PS[:, 5, :], func=AF.Ln, bias=EPS, scale=1.0),
                 rd=[r_bank[5]], wr=[r_ln2[s]])
            P.op("act", "activation", dict(out=r2[:, s, :], in_=ln2[:, s, :], func=AF.Exp, scale=-0.5),
                 rd=[r_ln2[s]], wr=[r_r2[s]])
            gi = 0 if oc < 4 else 1
            P.op("dve", "scalar_tensor_tensor", dict(
                out=qn[:, s, :], in0=PS[:, bank, :], scalar=gqk[:, gi:gi + 1], in1=r2[:, s, :],
                op0=ALU.mult, op1=ALU.mult), rd=[r_bank[bank], r_r2[s], r_c], wr=[r_qn[s]])
            P.op("pe", "matmul", dict(out=PS[:, 6, :], lhsT=rotm[:], rhs=qn[:, s, :], start=True, stop=True),
                 rd=[r_qn[s], r_c], wr=[r_bank[6]])
            P.op("pool", "tensor_tensor", dict(out=t1[:, s, :], in0=qn[:, s, :], in1=cosf[:, ts_], op=ALU.mult),
                 rd=[r_qn[s], r_c], wr=[r_t1[s]])
            P.op("dve", "tensor_tensor", dict(out=t2[:, s, :], in0=PS[:, 6, :], in1=sinf[:, ts_], op=ALU.mult),
                 rd=[r_bank[6], r_c], wr=[r_t2[s]])
            P.op("pool", "tensor_tensor", dict(out=qkst[:, sl, oc, :], in0=t1[:, s, :], in1=t2[:, s, :],
                                                              op=ALU.add), rd=[r_t1[s], r_t2[s]], wr=[r_qk[sl][oc]])
        for tt in range(4):
            tti = tt
            bank = next_bank()
            for kc in range(8):
                P.op("pe", "matmul", dict(
                    out=PS[:, bank, :], lhsT=hT[:, kc, tt * 128:(tt + 1) * 128], rhs=W[:, kc, 1024:1536],
                    start=(kc == 0), stop=(kc == 7)), rd=[r_w[2], r_h], wr=[r_bank[bank]])
            for hp in range(4):
                for h2 in range(2):
                    eng = "act" if h2 == 0 else "dve"
                    src = (hp * 2 + h2) * 64
                    if eng == "act":
                        P.op("act", "activation", dict(
                            out=vst[:, sl, tti, hp, h2 * 65:h2 * 65 + 64], in_=PS[:, bank, src:src + 64], func=AF.Copy),
                            rd=[r_bank[bank]], wr=[r_v[sl]])
                    else:
                        P.op("dve", "tensor_copy", dict(
                            out=vst[:, sl, tti, hp, h2 * 65:h2 * 65 + 64], in_=PS[:, bank, src:src + 64]),
                            rd=[r_bank[bank]], wr=[r_v[sl]])
        for j in range(4):
            s = j % 2
            ba = next_bank()
            proj_chunk(12 + j, tb, ba)
            bb = next_bank()
            proj_chunk(16 + j, tb, bb)
            P.op("act", "activation", dict(out=sg[:, s, :], in_=PS[:, bb, :], func=AF.Sigmoid),
                 rd=[r_bank[bb]], wr=[r_sg[s]])
            P.op("dve", "tensor_tensor", dict(out=gst[:, sl, j, :], in0=PS[:, ba, :], in1=sg[:, s, :],
                                                                  op=ALU.mult),
                 rd=[r_bank[ba], r_sg[s]], wr=[r_g[sl][j]])
        for gr in range(4):
            s = gr % 2
            bank = next_bank()
            proj_chunk(20 + gr, tb, bank)
            P.op("act", "activation", dict(out=fcT[:, s, :], in_=PS[:, bank, :], func=AF.Copy),
                 rd=[r_bank[bank]], wr=[r_fc[s]])
            for tp in range(2):
                for t_ in range(2):
                    tt = tp * 2 + t_
                    P.op("pe", "matmul", dict(
                        out=PS[:, 7, t_ * 256:(t_ + 1) * 256], lhsT=fcT[:, s, tt * 128:(tt + 1) * 128], rhs=dftg[:],
                        start=True, stop=True), rd=[r_fc[s], r_c], wr=[r_bank[7]])
                for t_ in range(2):
                    tti = tp * 2 + t_
                    P.op("dve", "tensor_copy", dict(
                        out=zst[:, sl, tti, gr, :], in_=PS[:, 7, t_ * 256:(t_ + 1) * 256]),
                        rd=[r_bank[7]], wr=[r_z[sl]])

        for oc in range(8):
            dst = qT_o if oc < 4 else kT_o
            o = (oc % 4) * 128
            fin.append(P.op("sp", "dma_start", dict(
                out=dst[o:o + 128, ts_], in_=qkst[:, sl, oc, :]), rd=[r_qk[sl][oc]]))
        for j in range(4):
            fin.append(P.op("sp", "dma_start", dict(
                out=gT_o[j * 128:(j + 1) * 128, ts_], in_=gst[:, sl, j, :]), rd=[r_g[sl][j]]))
        for hp in range(4):
            fin.append(P.op("sp", "dma_start", dict(
                out=vp_o[hp, ts_, :].rearrange("(t p) c -> p t c", p=128), in_=vst[:, sl, :, hp, :]), rd=[r_v[sl]]))
        for gr in range(4):
            fin.append(P.op("sp", "dma_start", dict(
                out=zp_o[gr, ts_, :].rearrange("(t p) c -> p t c", p=128), in_=zst[:, sl, :, gr, :]), rd=[r_z[sl]]))
    return fin


def lam_init_of(l):
    return 0.8 - 0.6 * math.exp(-0.3 * l)


def phase_B1(P, sfx, l):
    nc = P.nc
    qT_d = P.dram("qT" + sfx, [512, NT], BF16, "ExternalInput")
    kT_d = P.dram("kT_all" + sfx, [4, 512, NT], BF16, "ExternalInput")
    vp_d = P.dram("vp_all" + sfx, [4, 4, NT, 130], BF16, "ExternalInput")
    lam_d = P.dram("lamv" + sfx, [128, 4, 32], F32, "ExternalInput")
    gsub_d = P.dram("gsub" + sfx, [64, 1], F32, "ExternalInput")
    sel_d = P.dram("sel" + sfx, [128, 64], F32, "ExternalInput")
    o64_d = P.dram("ones64" + sfx, [64, 64], BF16, "ExternalInput")
    qm_d = P.dram("qmask" + sfx, [128, 4], F32, "ExternalInput")
    aT_o = P.dram("aT" + sfx, [8, 64, NT], BF16, "ExternalOutput")

    qT = P.sb("qT_sb", [128, 4, NT], BF16)
    qTm = P.sb("qTm_sb", [128, 2, 4, NT], BF16)
    qm = P.sb("qm_sb", [128, 4], F32)
    KT = P.sb("KT_sb", [128, 2, SEQ], BF16)
    VP = P.sb("VP_sb", [128, 2, 64, 130], BF16)
    lamv = P.sb("lamv_sb", [128, 4, 32], F32)
    lprod = P.sb("lprod", [128, 2, 32], F32)
    lsum = P.sb("lsum", [128, 2], F32)
    lexp = P.sb("lexp", [128, 2], F32)
    neglam = P.sb("neglam", [128, 1], F32)
    gsub = P.sb("gsub_sb", [64, 1], F32)
    gsub2 = P.sb("gsub2_sb", [64, 1], F32)
    sel = P.sb("sel_sb", [128, 64], F32)
    o64 = P.sb("o64_sb", [64, 64], BF16)
    E = P.sb("E_sb", [128, 3, 2, 512], BF16)
    Osb = P.sb("Osb", [65, 2, 512], F32)
    Rr = P.sb("Rr", [64, 2, 512], F32)
    tt0 = P.sb("tt0", [64, 2, 512], F32)
    att = P.sb("att", [64, 512], F32)
    sqa = P.sb("sqa", [64, 512], BF16)
    lna = P.sb("lna", [64, 512], F32)
    rsa = P.sb("rsa", [64, 512], F32)
    aT = P.sb("aT_sb", [64, 8, NT], BF16)
    PS = P.ps("ps", [128, 8, 512], F32)

    R = Res
    r_q = [R() for _ in range(4)]
    r_qm = [[R() for _ in range(4)] for _ in range(2)]
    r_kt = [[R() for _ in range(4)] for _ in range(2)]
    r_vp = [[R() for _ in range(4)] for _ in range(2)]
    r_c = R()
    r_lam = R()
    r_bank = [R() for _ in range(8)]
    r_E = [R() for _ in range(3)]
    r_O = [R(), R()]
    r_R = [R(), R()]
    r_t = [R(), R()]
    r_att, r_sq, r_ln, r_rs = R(), R(), R(), R()
    r_aT = [[R() for _ in range(4)] for _ in range(8)]

    lam_init = lam_init_of(l)
    for hp in range(4):
        P.op("sp", "dma_start", dict(out=qT[:, hp, :], in_=qT_d[hp * 128:(hp + 1) * 128, :]), wr=[r_q[hp]])
    P.op("sp", "dma_start", dict(out=lamv[:], in_=lam_d[:, :, :]), wr=[r_lam])
    P.op("sp", "dma_start", dict(out=gsub[:], in_=gsub_d[:, :]), wr=[r_c])
    P.op("sp", "dma_start", dict(out=sel[:], in_=sel_d[:, :]), wr=[r_c])
    P.op("sp", "dma_start", dict(out=o64[:], in_=o64_d[:, :]), wr=[r_c])
    P.op("sp", "dma_start", dict(out=qm[:], in_=qm_d[:, :]), wr=[r_c])

    def load_kv(hp):
        s = hp % 2
        for r in range(4):
            P.op("sp", "dma_start", dict(out=KT[:, s, r * NT:(r + 1) * NT], in_=kT_d[r, hp * 128:(hp + 1) * 128, :]),
                 wr=[r_kt[s][r]])
            P.op("sp", "dma_start", dict(out=VP[:, s, 16 * r:16 * r + 16, :],
                                         in_=vp_d[r, hp].rearrange("(k p) c -> p k c", p=128)), wr=[r_vp[s][r]])

    load_kv(0)
    P.op("dve", "tensor_tensor", dict(out=lprod[:, 0, :], in0=lamv[:, 0, :], in1=lamv[:, 1, :], op=ALU.mult),
         rd=[r_lam], wr=[r_att])
    P.op("dve", "tensor_tensor", dict(out=lprod[:, 1, :], in0=lamv[:, 2, :], in1=lamv[:, 3, :], op=ALU.mult),
         rd=[r_lam], wr=[r_att])
    P.op("dve", "tensor_reduce", dict(out=lsum[:], in_=lprod[:], axis=mybir.AxisListType.X, op=ALU.add),
         rd=[r_att], wr=[r_sq])
    P.op("act", "activation", dict(out=lexp[:], in_=lsum[:], func=AF.Exp), rd=[r_sq], wr=[r_ln])
    P.op("dve", "tensor_tensor", dict(out=neglam[:], in0=lexp[:, 1:2], in1=lexp[:, 0:1], op=ALU.subtract),
         rd=[r_ln], wr=[r_rs])
    P.op("dve", "tensor_scalar", dict(out=neglam[:], in0=neglam[:], scalar1=-lam_init, scalar2=None, op0=ALU.add),
         rd=[r_rs], wr=[r_rs])
    r_neglam = r_rs
    r_neglam_ev_holder = R()
    P.op("dve", "tensor_scalar", dict(out=gsub2[:], in0=gsub[:], scalar1=1.0 - lam_init, scalar2=None, op0=ALU.mult),
         rd=[r_c], wr=[r_neglam_ev_holder])
    r_g2 = r_neglam_ev_holder
    r_att, r_sq, r_ln = R(), R(), R()
    r_rs2 = R()

    scale = 32.0 ** -0.5
    ecnt = [0]

    def qk(hp, s, h2, qs, kt):
        sb0 = (kt % 2) * 2
        for c in range(2):
            i = 2 * h2 + c
            P.op("pe", "matmul", dict(out=PS[:, sb0 + c, :], lhsT=KT[:, s, kt * 128:(kt + 1) * 128],
                                      rhs=qTm[:, s, i, qs], start=True, stop=True),
                 rd=[r_qm[s][i]] + r_kt[s], wr=[r_bank[sb0 + c]])

    def post_copy():
        for c in range(2):
            P.op("dve", "tensor_copy", dict(out=Osb[:, c, :], in_=PS[0:65, 4 + c, :]), rd=[r_bank[4 + c]], wr=[r_O[c]])

    def post_rest(h, qs, qb):
        for c in range(2):
            P.op("pe", "matmul", dict(out=PS[0:64, 6 + c, :], lhsT=sel[0:65, :], rhs=Osb[:, c, :], start=True, stop=True),
                 rd=[r_O[c], r_c], wr=[r_bank[6 + c]])
            P.op("dve", "reciprocal", dict(out=Rr[:, c, :], in_=PS[0:64, 6 + c, :]), rd=[r_bank[6 + c]], wr=[r_R[c]])
            P.op("pool", "tensor_tensor", dict(out=tt0[:, c, :], in0=Osb[0:64, c, :], in1=Rr[:, c, :], op=ALU.mult),
                 rd=[r_O[c], r_R[c]], wr=[r_t[c]])
        P.op("dve", "scalar_tensor_tensor", dict(out=att[:], in0=tt0[:, 1, :], scalar=neglam[0:64, 0:1], in1=tt0[:, 0, :],
                                                 op0=ALU.mult, op1=ALU.add), rd=[r_t[0], r_t[1], r_neglam], wr=[r_att])
        P.op("pool", "tensor_tensor", dict(out=sqa[:], in0=att[:], in1=att[:], op=ALU.mult), rd=[r_att], wr=[r_sq])
        P.op("pe", "matmul", dict(out=PS[0:64, 6, :], lhsT=o64[:], rhs=sqa[:], start=True, stop=True),
             rd=[r_sq, r_c], wr=[r_bank[6]])
        P.op("act", "activation", dict(out=lna[:], in_=PS[0:64, 6, :], func=AF.Ln, bias=EPS, scale=1.0),
             rd=[r_bank[6]], wr=[r_ln])
        P.op("act", "activation", dict(out=rsa[:], in_=lna[:], func=AF.Exp, scale=-0.5), rd=[r_ln], wr=[r_rs2])
        P.op("dve", "scalar_tensor_tensor", dict(out=aT[:, h, qs], in0=att[:], scalar=gsub2[:, 0:1], in1=rsa[:],
                                                 op0=ALU.mult, op1=ALU.mult), rd=[r_att, r_rs2, r_g2], wr=[r_aT[h][qb]])

    def mask_q(hp):
        s_ = hp % 2
        for i in range(4):
            P.op("dve", "tensor_scalar", dict(out=qTm[:, s_, i, :], in0=qT[:, hp, :], scalar1=qm[:, i:i + 1], scalar2=None,
                                               op0=ALU.mult), rd=[r_q[hp], r_c], wr=[r_qm[s_][i]])

    pending = None
    mask_q(0)
    for hp in range(4):
        s = hp % 2
        if hp + 1 < 4:
            load_kv(hp + 1)
            mask_q(hp + 1)
        for h2 in range(2):
            h = hp * 2 + h2
            for qb in range(4):
                qs = slice(qb * 512, (qb + 1) * 512)
                qk(hp, s, h2, qs, 0)
                qk(hp, s, h2, qs, 1)
                for kt in range(64):
                    sb0 = (kt % 2) * 2
                    eb = ecnt[0] % 3
                    ecnt[0] += 1
                    P.op("act", "activation", dict(out=E[:, eb, :, :], in_=PS[:, sb0:sb0 + 2, :], func=AF.Exp, scale=scale),
                         rd=[r_bank[sb0], r_bank[sb0 + 1]], wr=[r_E[eb]])
                    if kt + 2 < 64:
                        qk(hp, s, h2, qs, kt + 2)
                    for c in range(2):
                        P.op("pe", "matmul", dict(out=PS[0:65, 4 + c, :], lhsT=VP[:, s, kt, h2 * 65:(h2 + 1) * 65],
                                                  rhs=E[:, eb, c, :], start=(kt == 0), stop=(kt == 63)),
                             rd=[r_E[eb]] + r_vp[s], wr=[r_bank[4 + c]])
                    if kt == 6 and pending is not None:
                        post_rest(*pending)
                        pending = None
                post_copy()
                pending = (h, qs, qb)
    post_rest(*pending)
    fin = []
    for h in range(8):
        fin.append(P.op("sp", "dma_start", dict(out=aT_o[h], in_=aT[:, h, :]), rd=r_aT[h]))
    return fin


def consts_B1():
    sel = np.zeros((128, 64), np.float32)
    sel[64, :] = 1.0
    qm = np.zeros((128, 4), np.float32)
    for i in range(4):
        qm[32 * i:32 * i + 32, i] = 1.0
    return {"sel": sel, "ones64": bf(np.full((64, 64), 1.0 / 64.0)), "qmask": qm}


def phase_B2(P, sfx):
    nc = P.nc
    g_d = P.dram("gT" + sfx, [512, NT], BF16, "ExternalInput")
    gall_d = P.dram("g_all" + sfx, [4, 512, NT], BF16, "ExternalInput")
    cw_d = P.dram("conv_w" + sfx, [128, 4, 31], F32, "ExternalInput")
    cv4_d = P.dram("cvec" + sfx, [128, 3, 4], F32, "ExternalInput")
    cm_d = P.dram("cmask" + sfx, [128, 8], F32, "ExternalInput")
    id_d = P.dram("ident" + sfx, [128, 128], BF16, "ExternalInput")
    o512_d = P.dram("ones512" + sfx, [128, 128], F32, "ExternalInput")
    bT_o = P.dram("bT" + sfx, [512, NT], BF16, "ExternalOutput")

    gp = P.sb("gpad", [128, 4, NT + 30], BF16)
    hal = P.sb("hal", [128, 4, 4, 2, 15], BF16)
    cw = P.sb("cw", [128, 4, 31], F32)
    cv4 = P.sb("cv4", [128, 3, 4], F32)
    cm = P.sb("cm", [128, 8], F32)
    ident = P.sb("ident_sb", [128, 128], BF16)
    o512 = P.sb("o512", [128, 128], F32)
    Dg = P.sb("Dg", [128, 4, 31, 128], BF16)
    cv = P.sb("cv", [128, 4, 512], F32)
    sqv = P.sb("sqv", [128, 4, 512], F32)
    m2 = P.sb("m2", [128, 512], F32)
    var = P.sb("var", [128, 512], F32)
    lnv = P.sb("lnv", [128, 512], F32)
    rstd = P.sb("rstd", [128, 512], F32)
    nmr = P.sb("nmr", [128, 512], F32)
    yv = P.sb("yv", [128, 2, 512], F32)
    bst = P.sb("bst", [128, 4, NT], BF16)
    PS = P.ps("ps", [128, 8, 512], F32)

    R = Res
    r_g = [R() for _ in range(4)]
    r_hal, r_c, r_D = R(), R(), [R() for _ in range(4)]
    r_bank = [R() for _ in range(8)]
    r_cv = [R() for _ in range(4)]
    r_sq = [R() for _ in range(4)]
    r_m2, r_var, r_ln, r_rstd, r_nmr = R(), R(), R(), R(), R()
    r_y = [R(), R()]
    r_b = [R() for _ in range(4)]

    for j in range(4):
        P.op("sp", "dma_start", dict(out=gp[:, j, 15:15 + NT], in_=g_d[j * 128:(j + 1) * 128, :]), wr=[r_g[j]])
    for i, (s_, d_) in enumerate([(cw, cw_d), (cv4, cv4_d), (cm, cm_d), (ident, id_d), (o512, o512_d)]):
        P.op("sp", "dma_start", dict(out=s_[:], in_=d_), wr=[r_c])
    rh = [R() for _ in range(8)]
    for r in range(4):
        gv = gall_d[r].rearrange("(j p) t -> p j t", p=128)
        P.op("sp", "dma_start", dict(out=hal[:, :, r, 0, :], in_=gv[:, :, NT - 15:NT]), wr=[rh[2 * r]])
        P.op("sp", "dma_start", dict(out=hal[:, :, r, 1, :], in_=gv[:, :, 0:15]), wr=[rh[2 * r + 1]])
    for side in range(2):
        dst = gp[:, :, 0:15] if side == 0 else gp[:, :, 15 + NT:30 + NT]
        for r in range(4):
            mk = cm[:, side * 4 + r:side * 4 + r + 1]
            if r == 0:
                P.op("dve", "tensor_scalar", dict(out=dst, in0=hal[:, :, r, side, :], scalar1=mk, scalar2=None, op0=ALU.mult),
                     rd=[rh[2 * r + side], r_c], wr=r_g)
            else:
                P.op("dve", "scalar_tensor_tensor", dict(out=dst, in0=hal[:, :, r, side, :], scalar=mk, in1=dst,
                                                         op0=ALU.mult, op1=ALU.add), rd=[rh[2 * r + side], r_c], wr=r_g)
    n = 0
    for j in range(4):
        for tau in range(31):
            q = "dve" if n % 2 == 0 else "pool"
            n += 1
            P.op(q, "tensor_scalar", dict(out=Dg[:, j, tau, :], in0=ident[:], scalar1=cw[:, j, tau:tau + 1], scalar2=None,
                                          op0=ALU.mult), rd=[r_c], wr=[r_D[j]])
    fin = []
    for tb in range(NTB):
        ts_ = slice(tb * 512, (tb + 1) * 512)
        for j in range(4):
            bank = j
            for tau in range(31):
                P.op("pe", "matmul", dict(out=PS[:, bank, :], lhsT=Dg[:, j, tau, :],
                                          rhs=gp[:, j, tb * 512 + tau:tb * 512 + tau + 512], start=(tau == 0), stop=(tau == 30)),
                     rd=[r_D[j], r_g[j]], wr=[r_bank[bank]])
            P.op("act", "activation", dict(out=cv[:, j, :], in_=PS[:, bank, :], func=AF.Identity, bias=cv4[:, 0, j:j + 1], scale=1.0),
                 rd=[r_bank[bank], r_c], wr=[r_cv[j]])
            P.op("pool", "tensor_tensor", dict(out=sqv[:, j, :], in0=cv[:, j, :], in1=cv[:, j, :], op=ALU.mult),
                 rd=[r_cv[j]], wr=[r_sq[j]])
        for j in range(4):
            P.op("pe", "matmul", dict(out=PS[:, 4, :], lhsT=o512[:], rhs=cv[:, j, :], start=(j == 0), stop=(j == 3)),
                 rd=[r_cv[j], r_c], wr=[r_bank[4]])
        for j in range(4):
            P.op("pe", "matmul", dict(out=PS[:, 5, :], lhsT=o512[:], rhs=sqv[:, j, :], start=(j == 0), stop=(j == 3)),
                 rd=[r_sq[j], r_c], wr=[r_bank[5]])
        P.op("act", "activation", dict(out=m2[:], in_=PS[:, 4, :], func=AF.Square), rd=[r_bank[4]], wr=[r_m2])
        P.op("dve", "tensor_tensor", dict(out=var[:], in0=PS[:, 5, :], in1=m2[:], op=ALU.subtract), rd=[r_bank[5], r_m2], wr=[r_var])
        P.op("act", "activation", dict(out=lnv[:], in_=var[:], func=AF.Ln, bias=EPS, scale=1.0), rd=[r_var], wr=[r_ln])
        P.op("act", "activation", dict(out=rstd[:], in_=lnv[:], func=AF.Exp, scale=-0.5), rd=[r_ln], wr=[r_rstd])
        P.op("dve", "scalar_tensor_tensor", dict(out=nmr[:], in0=PS[:, 4, :], scalar=-1.0, in1=rstd[:], op0=ALU.mult, op1=ALU.mult),
             rd=[r_bank[4], r_rstd], wr=[r_nmr])
        for j in range(4):
            s = j % 2
            P.op("pool", "tensor_tensor", dict(out=yv[:, s, :], in0=cv[:, j, :], in1=rstd[:], op=ALU.mult),
                 rd=[r_cv[j], r_rstd], wr=[r_y[s]])
            P.op("dve", "tensor_tensor", dict(out=yv[:, s, :], in0=yv[:, s, :], in1=nmr[:], op=ALU.add),
                 rd=[r_nmr], wr=[r_y[s]])
            P.op("act", "activation", dict(out=bst[:, j, ts_], in_=yv[:, s, :], func=AF.Silu, bias=cv4[:, 2, j:j + 1],
                                           scale=cv4[:, 1, j:j + 1]), rd=[r_y[s], r_c], wr=[r_b[j]])
    for j in range(4):
        fin.append(P.op("sp", "dma_start", dict(out=bT_o[j * 128:(j + 1) * 128, :], in_=bst[:, j, :]), rd=[r_b[j]]))
    return fin


def phase_B3(P, sfx):
    nc = P.nc
    z_d = P.dram("z_all" + sfx, [4, 4, NT, 256], BF16, "ExternalInput")
    w64_d = P.dram("w64" + sfx, [128, 2, 128], BF16, "ExternalInput")
    tt_d = P.dram("ttab" + sfx, [128, 64, 2, 32], BF16, "ExternalInput")
    fT_o = P.dram("fT" + sfx, [512, NT], BF16, "ExternalOutput")

    Zt = P.sb("Zt", [128, 1, 128, 256], BF16)
    w64 = P.sb("w64_sb", [128, 2, 128], BF16)
    ttab = P.sb("ttab_sb", [128, 64, 2, 32], BF16)
    A = P.sb("A_sb", [128, 2, 128, 2, 64], BF16)
    fst = P.sb("fst", [128, 4, NT], BF16)
    PS = P.ps("ps", [128, 8, 512], F32)

    R = Res
    r_z = [[R() for _ in range(4)] for _ in range(2)]
    r_c = R()
    r_A = [[R() for _ in range(32)] for _ in range(2)]
    r_bank = [R() for _ in range(8)]
    r_f = [R() for _ in range(4)]

    P.op("sp", "dma_start", dict(out=w64[:], in_=w64_d), wr=[r_c])
    P.op("sp", "dma_start", dict(out=ttab[:], in_=tt_d), wr=[r_c])

    def load_pair(gp_):
        sl = 0
        for g2 in range(2):
            gr = gp_ * 2 + g2
            for r in range(4):
                P.op("sp", "dma_start", dict(out=Zt[64 * g2 + 16 * r:64 * g2 + 16 * r + 16, sl, :, :],
                                             in_=z_d[r, gr].rearrange("(p s) c -> p s c", s=128)), wr=[r_z[sl][g2 * 2 + r // 2]])

    load_pair(0)
    ev_n = 0
    bank_n = 0
    fin = []
    for gp_ in range(2):
        sl = 0
        if gp_ == 1:
            load_pair(1)
        for g2 in range(2):
            gr = gp_ * 2 + g2
            asl = gr % 2
            rows = slice(64 * g2, 64 * g2 + 64)
            for j0 in range(0, 128, 4):
                bank = bank_n % 4
                bank_n += 1
                for jj in range(4):
                    j = j0 + jj
                    for c in range(2):
                        P.op("pe", "matmul", dict(out=PS[:, bank, jj * 128:(jj + 1) * 128], lhsT=Zt[rows, sl, :, c * 128 + j],
                                                  rhs=w64[rows, c, :], start=(c == 0), stop=(c == 1)),
                             rd=[r_z[sl][g2 * 2], r_z[sl][g2 * 2 + 1], r_c], wr=[r_bank[bank]])
                q = "act" if ev_n % 2 == 0 else "dve"
                ev_n += 1
                if q == "act":
                    P.op("act", "activation", dict(out=A[:, asl, j0:j0 + 4, :, :], in_=PS[:, bank, :], func=AF.Copy),
                         rd=[r_bank[bank]], wr=[r_A[asl][j0 // 4]])
                else:
                    P.op("dve", "tensor_copy", dict(out=A[:, asl, j0:j0 + 4, :, :], in_=PS[:, bank, :]),
                         rd=[r_bank[bank]], wr=[r_A[asl][j0 // 4]])
            for kb in range(4):
                bank = 4 + (kb % 2)
                for kk in range(16):
                    k2 = kb * 16 + kk
                    for c in range(2):
                        P.op("pe", "matmul", dict(out=PS[:, bank, kk * 32:(kk + 1) * 32], lhsT=A[:, asl, :, c, k2],
                                                  rhs=ttab[:, k2, c, :], start=(c == 0), stop=(c == 1)),
                             rd=r_A[asl] + [r_c], wr=[r_bank[bank]])
                dst = fst[:, gr, :].rearrange("p (a b) -> p a b", b=64)[:, :, kb * 16:(kb + 1) * 16]
                src = PS[:, bank, :].rearrange("p (b a) -> p a b", a=32)
                P.op("dve", "tensor_copy", dict(out=dst, in_=src), rd=[r_bank[bank]], wr=[r_f[gr]])
    for gr in range(4):
        fin.append(P.op("sp", "dma_start", dict(out=fT_o[gr * 128:(gr + 1) * 128, :], in_=fst[:, gr, :]), rd=[r_f[gr]]))
    return fin


def consts_B3(core):
    a = core % 4
    s2 = np.arange(64, dtype=np.float64)
    k2 = np.arange(64, dtype=np.float64)
    th = 2 * np.pi * np.outer(s2, k2) / 64.0
    C, S = np.cos(th), np.sin(th)
    w = np.stack([np.concatenate([C, -S], 1), np.concatenate([S, C], 1)], 1)
    w64 = bf(np.concatenate([w, w], 0))
    s1 = np.arange(128, dtype=np.float64)[:, None, None]
    k2_ = np.arange(64, dtype=np.float64)[None, :, None]
    k1 = (32 * a + np.arange(32, dtype=np.float64))[None, None, :]
    ph = 2 * np.pi * s1 * (64 * k1 + k2_) / 8192.0
    sc = 2.0 ** -10
    tt = np.stack([np.cos(ph) * sc, np.sin(ph) * sc], 2)
    return {"w64": w64, "ttab": bf(tt)}


DEBUG_C = False


def phase_C(P, sfx):
    nc = P.nc
    xT_d = P.dram("xT" + sfx, [D, NT], F32, "ExternalInput")
    aT_d = P.dram("aT" + sfx, [8, 64, NT], BF16, "ExternalInput")
    bT_d = P.dram("bT" + sfx, [512, NT], BF16, "ExternalInput")
    fT_d = P.dram("fT" + sfx, [512, NT], BF16, "ExternalInput")
    wpa_d = P.dram("w_proj_a" + sfx, [512, D], F32, "ExternalInput")
    wpb_d = P.dram("w_proj_b" + sfx, [512, D], F32, "ExternalInput")
    wpc_d = P.dram("w_proj_c" + sfx, [512, D], F32, "ExternalInput")
    wg_d = P.dram("w_gate" + sfx, [D, 3 * D], F32, "ExternalInput")
    bg_d = P.dram("b_gate" + sfx, [128, 24], F32, "ExternalInput")
    wo_d = P.dram("w_out" + sfx, [D, D], F32, "ExternalInput")
    g12_d = P.dram("g12" + sfx, [128, 2, 8], F32, "ExternalInput")
    ones_d = P.dram("ones_d" + sfx, [128, 128], BF16, "ExternalInput")
    w1_d = P.dram("w_ffn_in" + sfx, [D, 2 * DFF], F32, "ExternalInput")
    w2_d = P.dram("w_ffn_out" + sfx, [DFF, D], F32, "ExternalInput")
    xo_d = P.dram("xT_out" + sfx, [D, NT], F32, "ExternalOutput")

    xT = P.sb("xT_sb", [128, 8, NT], F32)
    ARENA_BYTES = 137216
    arena = P.sb("arena_c", [128, ARENA_BYTES // 2], BF16)

    def carve(off, shape, dt):
        nb = (4 if dt == F32 else 2)
        n = 1
        for d_ in shape[1:]:
            n *= d_
        v = arena[:, off // 2:off // 2 + n * nb // 2]
        if dt == F32:
            v = v.bitcast(F32)
        if len(shape) == 2:
            return v
        if len(shape) == 3:
            return v.rearrange("p (a b) -> p a b", b=shape[2])
        raise ValueError

    WA = carve(0, [128, 8, 3072], BF16)
    WB = carve(49152, [128, 20, 1024], BF16)
    hT = carve(90112, [128, 8, 512], BF16)
    brA = carve(98304, [128, 4, 512], BF16)
    brB = carve(102400, [128, 4, 512], BF16)
    brC = carve(106496, [128, 4, 512], BF16)
    sig = carve(110592, [128, 3, 512], F32)
    tm = carve(116736, [128, 3, 512], F32)
    mT = carve(122880, [128, 8, 512], BF16)
    sq = carve(131072, [128, 2, 512], BF16)
    lnv = carve(133120, [128, 512], F32)
    rstd = carve(135168, [128, 512], F32)
    h2T = carve(71680, [128, 8, NT], BF16)
    actT = carve(110592, [128, 11, 512], BF16)
    sgf = carve(122880, [128, 2, 512], F32)
    bg = P.sb("bg", [128, 24], F32)
    g12 = P.sb("g12_sb", [128, 2, 8], F32)
    onesm = P.sb("ones_sb", [128, 128], BF16)
    PS = P.ps("ps", [128, 8, 512], F32)

    R = Res
    r_x = [R() for _ in range(8)]
    r_wa = [R() for _ in range(8)]
    r_wb = [R() for _ in range(20)]
    r_wpa = R()
    r_c = R()
    r_h = R()
    r_sq = [R(), R()]
    r_ln, r_rstd = R(), R()
    r_br = [R(), R(), R()]
    r_sig = [R(), R(), R()]
    r_tm = [R(), R(), R()]
    r_m = [R() for _ in range(8)]
    r_bank = [R() for _ in range(8)]
    r_act = [R() for _ in range(11)]
    r_sgf = [R(), R()]

    for kc in range(8):
        P.op("sp", "dma_start", dict(out=xT[:, kc, :], in_=xT_d[kc * 128:(kc + 1) * 128, :]), wr=[r_x[kc]])
    for s_, d_ in [(bg, bg_d), (g12, g12_d), (onesm, ones_d)]:
        P.op("sp", "dma_start", dict(out=s_[:], in_=d_), wr=[r_c])
    for kc in range(8):
        for cb in range(3):
            P.op("poolq", "dma_start", dict(out=WA[:, kc, cb * 1024:(cb + 1) * 1024],
                                            in_=wg_d[kc * 128:(kc + 1) * 128, cb * 1024:(cb + 1) * 1024]), wr=[r_wa[kc]])
    for kc in range(8):
        P.op("poolq", "dma_start", dict(out=WB[:, kc, :], in_=wo_d[kc * 128:(kc + 1) * 128, :]), wr=[r_wb[kc]])
    for j in range(4):
        P.op("poolq", "dma_start", dict(out=WB[:, 8 + j, :], in_=wpb_d[j * 128:(j + 1) * 128, :]), wr=[r_wb[8 + j]])
        P.op("poolq", "dma_start", dict(out=WB[:, 12 + j, :], in_=wpc_d[j * 128:(j + 1) * 128, :]), wr=[r_wb[12 + j]])
        P.op("poolq", "dma_start", dict(out=WB[:, 16 + j, :], in_=wpa_d[j * 128:(j + 1) * 128, :]), wr=[r_wb[16 + j]])

    def rmsnorm_block(tb, gi, hdst):
        ts_ = slice(tb * 512, (tb + 1) * 512)
        for kc in range(8):
            s = kc % 2
            P.op("pool", "tensor_tensor", dict(out=sq[:, s, :], in0=xT[:, kc, ts_], in1=xT[:, kc, ts_], op=ALU.mult),
                 rd=[r_x[kc]], wr=[r_sq[s]])
            P.op("pe", "matmul", dict(out=PS[:, 7, :], lhsT=onesm[:], rhs=sq[:, s, :], start=(kc == 0), stop=(kc == 7)),
                 rd=[r_sq[s], r_c], wr=[r_bank[7]])
        P.op("act", "activation", dict(out=lnv[:], in_=PS[:, 7, :], func=AF.Ln, bias=EPS, scale=1.0), rd=[r_bank[7]], wr=[r_ln])
        P.op("act", "activation", dict(out=rstd[:], in_=lnv[:], func=AF.Exp, scale=-0.5), rd=[r_ln], wr=[r_rstd])
        for kc in range(8):
            P.op("dve", "scalar_tensor_tensor", dict(out=hdst[:, kc, :], in0=xT[:, kc, ts_], scalar=g12[:, gi, kc:kc + 1],
                                                     in1=rstd[:], op0=ALU.mult, op1=ALU.mult),
                 rd=[r_x[kc], r_rstd, r_c], wr=[r_h])

    for tb in range(NTB):
        ts_ = slice(tb * 512, (tb + 1) * 512)
        P.op("sp", "dma_start", dict(out=brA[:], in_=aT_d[:, :, ts_].rearrange("(j h) p t -> (h p) j t", h=2)), wr=[r_br[0]])
        P.op("sp", "dma_start", dict(out=brB[:], in_=bT_d[:, ts_].rearrange("(j p) t -> p j t", p=128)), wr=[r_br[1]])
        P.op("sp", "dma_start", dict(out=brC[:], in_=fT_d[:, ts_].rearrange("(j p) t -> p j t", p=128)), wr=[r_br[2]])
        rmsnorm_block(tb, 0, hT[:, :, 0:512])
        for oc in range(8):
            ocs = slice(oc * 128, (oc + 1) * 128)
            for j in range(4):
                P.op("pe", "matmul", dict(out=PS[:, 0, :], lhsT=WB[:, 16 + j, ocs], rhs=brA[:, j, :], start=(j == 0), stop=(j == 3)),
                     rd=[r_wb[16 + j], r_br[0]], wr=[r_bank[0]])
            for j in range(4):
                P.op("pe", "matmul", dict(out=PS[:, 1, :], lhsT=WB[:, 8 + j, ocs], rhs=brB[:, j, :], start=(j == 0), stop=(j == 3)),
                     rd=[r_wb[8 + j], r_br[1]], wr=[r_bank[1]])
            for j in range(4):
                P.op("pe", "matmul", dict(out=PS[:, 2, :], lhsT=WB[:, 12 + j, ocs], rhs=brC[:, j, :], start=(j == 0), stop=(j == 3)),
                     rd=[r_wb[12 + j], r_br[2]], wr=[r_bank[2]])
            for i in range(3):
                for kc in range(8):
                    P.op("pe", "matmul", dict(out=PS[:, 3 + i, :], lhsT=WA[:, kc, i * 1024 + oc * 128:i * 1024 + (oc + 1) * 128],
                                              rhs=hT[:, kc, 0:512], start=(kc == 0), stop=(kc == 7)),
                         rd=[r_wa[kc], r_h], wr=[r_bank[3 + i]])
                P.op("act", "activation", dict(out=sig[:, i, :], in_=PS[:, 3 + i, :], func=AF.Sigmoid,
                                               bias=bg[:, i * 8 + oc:i * 8 + oc + 1], scale=1.0),
                     rd=[r_bank[3 + i], r_c], wr=[r_sig[i]])
                P.op("dve", "tensor_tensor", dict(out=tm[:, i, :], in0=PS[:, i, :], in1=sig[:, i, :], op=ALU.mult),
                     rd=[r_bank[i], r_sig[i]], wr=[r_tm[i]])
            P.op("pool", "tensor_tensor", dict(out=tm[:, 0, :], in0=tm[:, 0, :], in1=tm[:, 1, :], op=ALU.add),
                 rd=[r_tm[1]], wr=[r_tm[0]])
            P.op("pool", "tensor_tensor", dict(out=mT[:, oc, :], in0=tm[:, 0, :], in1=tm[:, 2, :], op=ALU.add),
                 rd=[r_tm[0], r_tm[2]], wr=[r_m[oc]])
        for oc2 in range(8):
            bank = 6
            for kc in range(8):
                P.op("pe", "matmul", dict(out=PS[:, bank, :], lhsT=WB[:, kc, oc2 * 128:(oc2 + 1) * 128], rhs=mT[:, kc, :],
                                          start=(kc == 0), stop=(kc == 7)), rd=[r_wb[kc], r_m[kc]], wr=[r_bank[bank]])
            P.op("dve", "tensor_tensor", dict(out=xT[:, oc2, ts_], in0=PS[:, bank, :], in1=xT[:, oc2, ts_], op=ALU.add),
                 rd=[r_bank[bank]], wr=[r_x[oc2]])

    fin = []
    if DEBUG_C:
        x1_d = P.dram("x1_out" + sfx, [D, NT], F32, "ExternalOutput")
        for kc in range(8):
            fin.append(P.op("sp", "dma_start", dict(out=x1_d[kc * 128:(kc + 1) * 128, :], in_=xT[:, kc, :]), rd=[r_x[kc]]))
    P.op("dve", "memset", dict(ap=lnv[:, 0:8], constant=0.0), rd=[], wr=r_sig + r_tm + r_br + r_act + r_sgf + r_m + [r_ln, r_h] + r_wb[11:20])
    for tb in range(NTB):
        rmsnorm_block(tb, 1, h2T[:, :, tb * 512:(tb + 1) * 512])
    for half in range(2):
        c0 = half * 1408
        for kc in range(8):
            P.op("poolq", "dma_start", dict(out=WA[:, kc, 0:1408], in_=w1_d[kc * 128:(kc + 1) * 128, c0:c0 + 1408]), wr=[r_wa[kc]])
            P.op("poolq", "dma_start", dict(out=WA[:, kc, 1408:2816], in_=w1_d[kc * 128:(kc + 1) * 128, DFF + c0:DFF + c0 + 1408]),
                 wr=[r_wa[kc]])
        for j in range(11):
            P.op("poolq", "dma_start", dict(out=WB[:, j, :], in_=w2_d[c0 + j * 128:c0 + (j + 1) * 128, :]), wr=[r_wb[j]])
        for tb in range(NTB):
            ts_ = slice(tb * 512, (tb + 1) * 512)
            for j in range(11):
                s = j % 2
                bg_, bu_ = (0, 1) if s == 0 else (2, 3)
                for kc in range(8):
                    P.op("pe", "matmul", dict(out=PS[:, bg_, :], lhsT=WA[:, kc, j * 128:(j + 1) * 128], rhs=h2T[:, kc, ts_],
                                              start=(kc == 0), stop=(kc == 7)), rd=[r_wa[kc], r_h], wr=[r_bank[bg_]])
                for kc in range(8):
                    P.op("pe", "matmul", dict(out=PS[:, bu_, :], lhsT=WA[:, kc, 1408 + j * 128:1408 + (j + 1) * 128], rhs=h2T[:, kc, ts_],
                                              start=(kc == 0), stop=(kc == 7)), rd=[r_wa[kc], r_h], wr=[r_bank[bu_]])
                P.op("act", "activation", dict(out=sgf[:, s, :], in_=PS[:, bg_, :], func=AF.Silu), rd=[r_bank[bg_]], wr=[r_sgf[s]])
                P.op("dve", "tensor_tensor", dict(out=actT[:, j, :], in0=PS[:, bu_, :], in1=sgf[:, s, :], op=ALU.mult),
                     rd=[r_bank[bu_], r_sgf[s]], wr=[r_act[j]])
            for oc in range(8):
                bank = 4 + (oc % 2)
                for j in range(11):
                    P.op("pe", "matmul", dict(out=PS[:, bank, :], lhsT=WB[:, j, oc * 128:(oc + 1) * 128], rhs=actT[:, j, :],
                                              start=(j == 0), stop=(j == 10)), rd=[r_wb[j], r_act[j]], wr=[r_bank[bank]])
                P.op("dve", "tensor_tensor", dict(out=xT[:, oc, ts_], in0=PS[:, bank, :], in1=xT[:, oc, ts_], op=ALU.add),
                     rd=[r_bank[bank]], wr=[r_x[oc]])
    for kc in range(8):
        fin.append(P.op("sp", "dma_start", dict(out=xo_d[kc * 128:(kc + 1) * 128, :], in_=xT[:, kc, :]), rd=[r_x[kc]]))
    return fin, xT, r_x


def build_launch(kind):
    nc = bass.Bass("TRN2", target_bir_lowering=False)
    if kind == "L1":
        P = Prog(nc)
        fin = phase_A(P, "_0")
    else:
        l = 0 if kind == "L2" else 1
        sfx = "_%d" % l
        P = Prog(nc, internal={"aT" + sfx, "bT" + sfx, "fT" + sfx})
        fin = []
        P.phase_begin()
        phase_B1(P, sfx, l)
        P.barrier()
        P.phase_begin()
        phase_B2(P, sfx)
        P.barrier()
        P.phase_begin()
        phase_B3(P, sfx)
        P.barrier()
        P.phase_begin()
        finC, xT, r_x = phase_C(P, sfx)
        fin += finC
        if kind == "L2":
            P.barrier()
            P.phase_begin()
            fin += phase_A(P, "_1", x_res=(xT, r_x))
    P.final_events = fin
    with P.stack:
        with nc.Block() as block:
            P.finalize(block)
    return nc


def inputs_A(l, inp, c, sfx, xT=None):
    ct = const_tables(c)
    g1 = np.ascontiguousarray(inp["norm1_g"][l].reshape(8, 128).T)
    gqk = np.stack([np.tile(inp["qnorm_g"][l], 4), np.tile(inp["knorm_g"][l], 4)], axis=1).astype(np.float32)
    d = {"w_in": np.ascontiguousarray(inp["w_in"][l]), "g1": g1, "gqk": np.ascontiguousarray(gqk),
         "cosf": ct["cosf"], "sinf": ct["sinf"], "ones_d": ct["ones_d"], "bd32": ct["bd32"], "rot": ct["rot"],
         "dftg": ct["dftg"]}
    if xT is not None:
        d["xT"] = xT
    return {k + sfx: v for k, v in d.items()}


def inputs_B(l, inp, c, sfx, ex):
    b, a = c // 4, c % 4
    cb = consts_B1()
    lamv = np.stack([inp["lambda_q1"][l], inp["lambda_k1"][l], inp["lambda_q2"][l], inp["lambda_k2"][l]], 0)
    lamv = np.ascontiguousarray(np.broadcast_to(lamv[None], (128, 4, 32))).astype(np.float32)
    gsub = np.ascontiguousarray(inp["subln_g"][l].reshape(64, 1)).astype(np.float32)
    grp = [b * 4 + r for r in range(4)]
    d = {"qT": ex["qT"][c], "kT_all": np.ascontiguousarray(np.stack([ex["kT"][i] for i in grp], 0)),
         "vp_all": np.ascontiguousarray(np.stack([ex["vp"][i] for i in grp], 0)), "lamv": lamv, "gsub": gsub,
         "sel": cb["sel"], "ones64": cb["ones64"], "qmask": cb["qmask"]}
    cw = np.ascontiguousarray(inp["conv_w"][l].T.reshape(4, 128, 31).transpose(1, 0, 2)).astype(np.float32)
    cvec = np.stack([inp["conv_b"][l].reshape(4, 128).T, inp["conv_ln_g"][l].reshape(4, 128).T,
                     inp["conv_ln_b"][l].reshape(4, 128).T], axis=1).astype(np.float32)
    cm = np.zeros((128, 8), np.float32)
    if a > 0:
        cm[:, a - 1] = 1.0
    if a < 3:
        cm[:, 4 + a + 1] = 1.0
    d.update({"gT": ex["gT"][c], "g_all": np.ascontiguousarray(np.stack([ex["gT"][i] for i in grp], 0)), "conv_w": cw,
              "cvec": np.ascontiguousarray(cvec), "cmask": cm, "ident": bf(np.eye(128)),
              "ones512": np.full((128, 128), 1.0 / 512.0, np.float32)})
    c3 = consts_B3(c)
    d.update({"z_all": np.ascontiguousarray(np.stack([ex["zp"][i] for i in grp], 0)), "w64": c3["w64"], "ttab": c3["ttab"]})
    return {k + sfx: v for k, v in d.items()}


def inputs_C(l, inp, c, sfx, xT):
    ct = const_tables(0)
    g12 = np.stack([inp["norm1_g"][l].reshape(8, 128).T, inp["norm2_g"][l].reshape(8, 128).T], axis=1).astype(np.float32)
    bgate = np.ascontiguousarray(inp["b_gate"][l].reshape(24, 128).T).astype(np.float32)
    d = {"xT": xT, "w_proj_a": np.ascontiguousarray(inp["w_proj_a"][l]), "w_proj_b": np.ascontiguousarray(inp["w_proj_b"][l]),
         "w_proj_c": np.ascontiguousarray(inp["w_proj_c"][l]), "w_gate": np.ascontiguousarray(inp["w_gate"][l]),
         "b_gate": bgate, "w_out": np.ascontiguousarray(inp["w_out"][l]), "g12": np.ascontiguousarray(g12),
         "ones_d": ct["ones_d"], "w_ffn_in": np.ascontiguousarray(inp["w_ffn_in"][l]),
         "w_ffn_out": np.ascontiguousarray(inp["w_ffn_out"][l])}
    return {k + sfx: v for k, v in d.items()}


def _collect(res, sfx):
    return {k: [np.asarray(r[k + sfx]) for r in res] for k in ("qT", "kT", "vp", "gT", "zp")}


def kernel(**inp):
    inp = {k: np.asarray(v) for k, v in inp.items()}
    x = inp["x"]
    xT = [np.ascontiguousarray(x[c // 4, (c % 4) * NT:(c % 4 + 1) * NT, :].T) for c in range(NCORES)]
    cores = list(range(NCORES))
    r1 = run_bass_kernel_spmd(build_launch("L1"), [inputs_A(0, inp, c, "_0", xT[c]) for c in cores], core_ids=cores).results
    ex0 = _collect(r1, "_0")
    in2 = []
    for c in cores:
        d = inputs_B(0, inp, c, "_0", ex0)
        d.update(inputs_C(0, inp, c, "_0", xT[c]))
        d.update(inputs_A(1, inp, c, "_1"))
        in2.append(d)
    r2 = run_bass_kernel_spmd(build_launch("L2"), in2, core_ids=cores).results
    ex1 = _collect(r2, "_1")
    x1 = [np.ascontiguousarray(np.asarray(r["xT_out_0"])) for r in r2]
    in3 = []
    for c in cores:
        d = inputs_B(1, inp, c, "_1", ex1)
        d.update(inputs_C(1, inp, c, "_1", x1[c]))
        in3.append(d)
    r3 = run_bass_kernel_spmd(build_launch("L3"), in3, core_ids=cores).results
    out = np.empty((2, SEQ, D), np.float32)
    for c in cores:
        out[c // 4, (c % 4) * NT:(c % 4 + 1) * NT, :] = np.asarray(r3[c]["xT_out_1"]).T
    return out
```

```python
import math
from contextlib import ExitStack
import numpy as np
import ml_dtypes
import concourse.bass as bass
import concourse.mybir as mybir
from concourse.bass_utils import run_bass_kernel_spmd

F32 = mybir.dt.float32
BF16 = mybir.dt.bfloat16
AF = mybir.ActivationFunctionType
ALU = mybir.AluOpType

NCORES = 8
D = 1024
SEQ = 8192
NT = 2048
NTB = 4
EPS = 1e-6
DFF = 2816
ROPE_THETA = 500000.0


class Res:
    __slots__ = ("w", "r", "name")

    def __init__(self, name=""):
        self.w = None
        self.r = []
        self.name = name


class Ev:
    __slots__ = ("q", "idx", "needed", "sem", "val")

    def __init__(self, q, idx):
        self.q = q
        self.idx = idx
        self.needed = False
        self.sem = None
        self.val = None


COMPUTE_Q = ("pe", "act", "dve", "pool")
DMA_Q = ("sp", "actq", "poolq")
ENGINE_OF = {"pe": "tensor", "act": "scalar", "dve": "vector", "pool": "gpsimd",
             "sp": "sync", "actq": "scalar", "poolq": "gpsimd"}
NDMASEM = 6


class Prog:
    ARENA_BYTES = 212736

    def __init__(self, nc, internal=()):
        self.nc = nc
        self.stack = ExitStack()
        self.arena = None
        self.off = 0
        self.psum = None
        self.drams = {}
        self.internal = set(internal)
        self.barrier_ev = None
        self.dma_pending = []
        self.last_ev = {}
        self.bscr = None
        self.streams = {"tensor": [], "scalar": [], "vector": [], "gpsimd": [], "sync": []}
        self.evcount = {q: 0 for q in COMPUTE_Q + DMA_Q}
        self.dma_n = {q: 0 for q in DMA_Q}
        self.dma_last = {}
        self.sems = {}
        self.final_events = []

    def sb(self, name, shape, dt):
        if self.arena is None:
            self.arena = self.stack.enter_context(self.nc.sbuf_tensor("arena", [128, self.ARENA_BYTES // 2], BF16))
            self.bscr = self.stack.enter_context(self.nc.sbuf_tensor("bscr", [128, 8], F32))
        nb = 4 if dt == F32 else 2
        n = 1
        for d_ in shape[1:]:
            n *= d_
        nbytes = (n * nb + 63) // 64 * 64
        off = self.off
        assert off + nbytes <= self.ARENA_BYTES, (name, off, nbytes)
        self.off = off + nbytes
        v = self.arena[:, off // 2:off // 2 + n * nb // 2]
        if dt == F32:
            v = v.bitcast(F32)
        if shape[0] < 128:
            v = v[0:shape[0]]
        dims = list(shape[1:])
        if len(dims) == 1:
            return v
        names = "abcd"[:len(dims)]
        pat = "p (%s) -> p %s" % (" ".join(names), " ".join(names))
        return v.rearrange(pat, **{names[i]: dims[i] for i in range(1, len(dims))})

    def phase_begin(self, keep=0):
        self.off = keep

    def ps(self, name, shape, dt=F32):
        if self.psum is None:
            self.psum = self.stack.enter_context(self.nc.psum_tensor(name, list(shape), dt))
        return self.psum

    def dram(self, name, shape, dt, kind):
        if name in self.drams:
            return self.drams[name]
        if name in self.internal:
            kind = "Internal"
        ap = self.nc.dram_tensor(name, list(shape), dt, kind=kind).ap()
        self.drams[name] = ap
        return ap

    def barrier(self):
        deps = list(self.last_ev.values()) + list(self.dma_pending)
        r = Res()
        ev = self.op("pool", "memset", dict(ap=self.bscr[:, 0:2], constant=0.0), wr=[r], extra=deps)
        self.barrier_ev = ev
        self.dma_pending = []

    def op(self, q, name, kw, rd=(), wr=(), extra=()):
        fn = (name, kw)
        deps = list(extra)
        if self.barrier_ev is not None:
            deps.append(self.barrier_ev)
        for r in rd:
            if r.w is not None:
                deps.append(r.w)
        for w in wr:
            if w.w is not None:
                deps.append(w.w)
            deps.extend(w.r)
        self.evcount[q] += 1
        ev = Ev(q, self.evcount[q])
        if q in DMA_Q:
            slot = self.dma_n[q] % NDMASEM
            self.dma_n[q] += 1
            prev = self.dma_last.get((q, slot))
            if prev is not None:
                deps.append(prev)
            self.dma_last[(q, slot)] = ev
            ev.sem = (q, slot)
        else:
            ev.sem = (q, 0)
        best = {}
        dmas = []
        for d in deps:
            if d.q in COMPUTE_Q:
                if d.q == "pe" and q == "pe":
                    continue
                if d.q not in best or best[d.q].idx < d.idx:
                    best[d.q] = d
            else:
                if d not in dmas:
                    dmas.append(d)
        waits = list(best.values()) + dmas
        self.streams[ENGINE_OF[q]].append((q, fn, waits, ev))
        if q in DMA_Q:
            self.dma_pending.append(ev)
        else:
            self.last_ev[q] = ev
        for r in rd:
            r.r.append(ev)
        for w in wr:
            w.w = ev
            w.r = []
        return ev

    def finalize(self, block):
        nc = self.nc
        for eng, stream in self.streams.items():
            seen = {}
            for item in stream:
                q, fn, waits, ev = item
                keep = []
                for d in waits:
                    if d.q in COMPUTE_Q:
                        if seen.get(d.q, 0) >= d.idx:
                            continue
                        seen[d.q] = d.idx
                    else:
                        if seen.get(id(d)):
                            continue
                        seen[id(d)] = True
                    d.needed = True
                    keep.append(d)
                item[2][:] = keep
        for ev in self.final_events:
            ev.needed = True
        counters = {}
        for eng, stream in self.streams.items():
            for q, fn, waits, ev in stream:
                if q in DMA_Q:
                    key = ev.sem
                    counters[key] = counters.get(key, 0) + 16
                    ev.val = counters[key]
                    ev.needed = True
                elif ev.needed:
                    key = ev.sem
                    counters[key] = counters.get(key, 0) + 1
                    ev.val = counters[key]
        for key in counters:
            self.sems[key] = self.stack.enter_context(nc.semaphore("s_%s_%d" % key))
        self.maxcount = dict(counters)

        def emit(engname):
            stream = self.streams[engname]

            def body(eng):
                for q, fn, waits, ev in stream:
                    for d in waits:
                        eng.wait_ge(self.sems[d.sem], d.val)
                    ins = getattr(eng, fn[0])(**fn[1])
                    if ev.needed:
                        ins.then_inc(self.sems[ev.sem], 16 if q in DMA_Q else 1)
                if engname == "sync":
                    for ev in self.final_events:
                        eng.wait_ge(self.sems[ev.sem], ev.val)
            return body

        block.tensor(emit("tensor"))
        block.scalar(emit("scalar"))
        block.vector(emit("vector"))
        block.gpsimd(emit("gpsimd"))
        block.sync(emit("sync"))


def bf(a):
    return np.ascontiguousarray(np.asarray(a, dtype=np.float32)).astype(ml_dtypes.bfloat16)


def const_tables(core):
    a = core % 4
    t = {}
    t["ones_d"] = bf(np.full((128, 128), 1.0 / 1024.0))
    bd = np.zeros((128, 128), np.float32)
    for i in range(4):
        bd[32 * i:32 * i + 32, 32 * i:32 * i + 32] = 1.0 / 32.0
    t["bd32"] = bf(bd)
    rot = np.zeros((128, 128), np.float32)
    for blk in range(4):
        o = 32 * blk
        for i in range(4):
            rot[o + 4 + i, o + i] = -1.0
            rot[o + i, o + 4 + i] = 1.0
    t["rot"] = bf(rot)
    pos = (a * NT + np.arange(NT)).astype(np.float64)
    inv = 1.0 / (ROPE_THETA ** (np.arange(0, 8, 2, dtype=np.float64) / 8.0))
    ang = pos[None, :] * inv[:, None]
    cf = np.ones((128, NT), np.float32)
    sf = np.zeros((128, NT), np.float32)
    for blk in range(4):
        o = 32 * blk
        cf[o:o + 4] = np.cos(ang)
        cf[o + 4:o + 8] = np.cos(ang)
        sf[o:o + 4] = np.sin(ang)
        sf[o + 4:o + 8] = np.sin(ang)
    t["cosf"] = cf
    t["sinf"] = sf
    jc = np.outer(np.arange(128), np.arange(128)).astype(np.float64) * (2 * np.pi / 128.0)
    t["dftg"] = bf(np.concatenate([np.cos(jc), -np.sin(jc)], axis=1))
    return t


def phase_A(P, sfx, x_res=None):
    nc = P.nc
    if x_res is None:
        xT_d = P.dram("xT" + sfx, [D, NT], F32, "ExternalInput")
    win_d = P.dram("w_in" + sfx, [D, 3072], F32, "ExternalInput")
    g1_d = P.dram("g1" + sfx, [128, 8], F32, "ExternalInput")
    gqk_d = P.dram("gqk" + sfx, [128, 2], F32, "ExternalInput")
    cos_d = P.dram("cosf" + sfx, [128, NT], F32, "ExternalInput")
    sin_d = P.dram("sinf" + sfx, [128, NT], F32, "ExternalInput")
    ones_d = P.dram("ones_d" + sfx, [128, 128], BF16, "ExternalInput")
    bd_d = P.dram("bd32" + sfx, [128, 128], BF16, "ExternalInput")
    rot_d = P.dram("rot" + sfx, [128, 128], BF16, "ExternalInput")
    dftg_d = P.dram("dftg" + sfx, [128, 256], BF16, "ExternalInput")
    qT_o = P.dram("qT" + sfx, [512, NT], BF16, "ExternalOutput")
    kT_o = P.dram("kT" + sfx, [512, NT], BF16, "ExternalOutput")
    vp_o = P.dram("vp" + sfx, [4, NT, 130], BF16, "ExternalOutput")
    gT_o = P.dram("gT" + sfx, [512, NT], BF16, "ExternalOutput")
    zp_o = P.dram("zp" + sfx, [4, NT, 256], BF16, "ExternalOutput")

    xT = P.sb("xT_sb", [128, 8, NT], F32)
    W = P.sb("w_sb", [128, 8, 3072], BF16)
    g1 = P.sb("g1_sb", [128, 8], F32)
    gqk = P.sb("gqk_sb", [128, 2], F32)
    cosf = P.sb("cos_sb", [128, NT], F32)
    sinf = P.sb("sin_sb", [128, NT], F32)
    onesm = P.sb("ones_sb", [128, 128], BF16)
    bdm = P.sb("bd_sb", [128, 128], BF16)
    rotm = P.sb("rot_sb", [128, 128], BF16)
    dftg = P.sb("dftg_sb", [128, 256], BF16)
    hT = P.sb("hT", [128, 8, 512], BF16)
    sq = P.sb("sq", [128, 2, 512], BF16)
    lnv = P.sb("lnv", [128, 512], F32)
    rstd = P.sb("rstd", [128, 512], F32)
    sq2 = P.sb("sq2", [128, 2, 512], BF16)
    ln2 = P.sb("ln2", [128, 2, 512], F32)
    r2 = P.sb("r2", [128, 2, 512], F32)
    qn = P.sb("qn", [128, 2, 512], BF16)
    t1 = P.sb("t1", [128, 2, 512], F32)
    t2 = P.sb("t2", [128, 2, 512], F32)
    qkst = P.sb("qkst", [128, 1, 8, 512], BF16)
    vst = P.sb("vst", [128, 1, 4, 4, 130], BF16)
    sg = P.sb("sg", [128, 2, 512], F32)
    gst = P.sb("gst", [128, 1, 4, 512], BF16)
    fcT = P.sb("fcT", [128, 2, 512], BF16)
    zst = P.sb("zst", [128, 1, 4, 4, 256], BF16)
    PS = P.ps("ps", [128, 8, 512], F32)

    R = lambda n: Res(n)
    r_x = [[R("x%d" % i) for _ in range(NTB)] for i in range(8)]
    r_w = [[R("w%d" % i) for _ in range(8)] for i in range(6)]
    r_c = R("consts")
    r_cs = R("cossin")
    r_h = R("hT")
    r_sq = [R("sq0"), R("sq1")]
    r_ln = R("lnv")
    r_rstd = R("rstd")
    r_bank = [R("bank%d" % i) for i in range(8)]
    r_sq2 = [R("a"), R("b")]
    r_ln2 = [R("a"), R("b")]
    r_r2 = [R("a"), R("b")]
    r_qn = [R("a"), R("b")]
    r_t1 = [R("a"), R("b")]
    r_t2 = [R("a"), R("b")]
    r_qk = [[R("qk%d" % i) for i in range(8)] for _ in range(2)]
    r_v = [R("vst0"), R("vst1")]
    r_sg = [R("a"), R("b")]
    r_g = [[R("g%d" % i) for i in range(4)] for _ in range(2)]
    r_fc = [R("a"), R("b")]
    r_z = [R("zst0"), R("zst1")]

    for s_, d_ in [(g1, g1_d), (gqk, gqk_d), (onesm, ones_d), (bdm, bd_d), (rotm, rot_d), (dftg, dftg_d)]:
        P.op("sp", "dma_start", dict(out=s_[:], in_=d_[:, :]), wr=[r_c])
    if x_res is None:
        for tb in range(NTB):
            for kc in range(8):
                P.op("sp", "dma_start", dict(out=xT[:, kc, tb * 512:(tb + 1) * 512],
                                             in_=xT_d[kc * 128:(kc + 1) * 128, tb * 512:(tb + 1) * 512]), wr=[r_x[kc][tb]])
            if tb == 0:
                for s_, d_ in [(cosf, cos_d), (sinf, sin_d)]:
                    P.op("sp", "dma_start", dict(out=s_[:], in_=d_[:, :]), wr=[r_cs])
    else:
        for s_, d_ in [(cosf, cos_d), (sinf, sin_d)]:
            P.op("sp", "dma_start", dict(out=s_[:], in_=d_[:, :]), wr=[r_cs])
    for wb in range(6):
        for kc in range(8):
            P.op("poolq", "dma_start", dict(
                out=W[:, kc, wb * 512:(wb + 1) * 512],
                in_=win_d[kc * 128:(kc + 1) * 128, wb * 512:(wb + 1) * 512]), wr=[r_w[wb][kc]])
    for sl in range(1):
        P.op("pool", "memset", dict(ap=vst[:, sl, :, :, 64:65], constant=1.0), wr=[r_v[sl]])
        P.op("pool", "memset", dict(ap=vst[:, sl, :, :, 129:130], constant=1.0), wr=[r_v[sl]])

    fin = []
    bank_rr = [0]

    def next_bank():
        b = bank_rr[0] % 4
        bank_rr[0] += 1
        return b

    def proj_chunk(oc, tb, bank):
        wb = (oc * 128) // 512
        for kc in range(8):
            P.op("pe", "matmul", dict(out=PS[:, bank, :], lhsT=W[:, kc, oc * 128:(oc + 1) * 128],
                                                 rhs=hT[:, kc, :], start=(kc == 0), stop=(kc == 7)),
                 rd=[r_w[wb][kc], r_h, r_c], wr=[r_bank[bank]])

    for tb in range(NTB):
        ts_ = slice(tb * 512, (tb + 1) * 512)
        sl = 0
        for kc in range(8):
            s = kc % 2
            P.op("pool", "tensor_tensor", dict(out=sq[:, s, :], in0=xT[:, kc, ts_], in1=xT[:, kc, ts_],
                                                              op=ALU.mult), rd=[r_x[kc][tb]], wr=[r_sq[s]])
            P.op("pe", "matmul", dict(out=PS[:, 4, :], lhsT=onesm[:], rhs=sq[:, s, :],
                                                      start=(kc == 0), stop=(kc == 7)),
                 rd=[r_sq[s], r_c], wr=[r_bank[4]])
        P.op("act", "activation", dict(out=lnv[:], in_=PS[:, 4, :], func=AF.Ln, bias=EPS, scale=1.0),
             rd=[r_bank[4]], wr=[r_ln])
        P.op("act", "activation", dict(out=rstd[:], in_=lnv[:], func=AF.Exp, scale=-0.5),
             rd=[r_ln], wr=[r_rstd])
        for kc in range(8):
            P.op("dve", "scalar_tensor_tensor", dict(out=hT[:, kc, :], in0=xT[:, kc, ts_],
                                                                scalar=g1[:, kc:kc + 1], in1=rstd[:],
                                                                op0=ALU.mult, op1=ALU.mult),
                 rd=[r_x[kc][tb], r_rstd, r_c], wr=[r_h])
        def stage2(oc, bank):
            s = oc % 2
            gi = 0 if oc < 4 else 1
            P.op("pe", "matmul", dict(out=PS[:, 5, :], lhsT=bdm[:], rhs=sq2[:, s, :], start=True, stop=True),
                 rd=[r_sq2[s], r_c], wr=[r_bank[5]])
            P.op("act", "activation", dict(out=ln2[:, s, :], in_=PS[:, 5, :], func=AF.Ln, bias=EPS, scale=1.0),
                 rd=[r_bank[5]], wr=[r_ln2[s]])
            P.op("act", "activation", dict(out=r2[:, s, :], in_=ln2[:, s, :], func=AF.Exp, scale=-0.5),
                 rd=[r_ln2[s]], wr=[r_r2[s]])
            P.op("dve", "scalar_tensor_tensor", dict(
                out=qn[:, s, :], in0=PS[:, bank, :], scalar=gqk[:, gi:gi + 1], in1=r2[:, s, :],
                op0=ALU.mult, op1=ALU.mult), rd=[r_bank[bank], r_r2[s], r_c], wr=[r_qn[s]])

        def stage3(oc):
            s = oc % 2
            P.op("pe", "matmul", dict(out=PS[:, 6, :], lhsT=rotm[:], rhs=qn[:, s, :], start=True, stop=True),
                 rd=[r_qn[s], r_c], wr=[r_bank[6]])
            P.op("pool", "tensor_tensor", dict(out=t1[:, s, :], in0=qn[:, s, :], in1=cosf[:, ts_], op=ALU.mult),
                 rd=[r_qn[s], r_cs], wr=[r_t1[s]])
            P.op("dve", "tensor_tensor", dict(out=t2[:, s, :], in0=PS[:, 6, :], in1=sinf[:, ts_], op=ALU.mult),
                 rd=[r_bank[6], r_cs], wr=[r_t2[s]])
            P.op("pool", "tensor_tensor", dict(out=qkst[:, sl, oc, :], in0=t1[:, s, :], in1=t2[:, s, :],
                                               op=ALU.add), rd=[r_t1[s], r_t2[s]], wr=[r_qk[sl][oc]])

        hist = []
        for oc in range(8):
            s = oc % 2
            bank = next_bank()
            proj_chunk(oc, tb, bank)
            P.op("act", "activation", dict(out=sq2[:, s, :], in_=PS[:, bank, :], func=AF.Square),
                 rd=[r_bank[bank]], wr=[r_sq2[s]])
            hist.append((oc, bank))
            if len(hist) >= 2:
                stage2(*hist[-2])
            if len(hist) >= 3:
                stage3(hist[-3][0])
        stage2(*hist[-1])
        stage3(hist[-2][0])
        stage3(hist[-1][0])
        for tt in range(4):
            tti = tt
            bank = next_bank()
            for kc in range(8):
                P.op("pe", "matmul", dict(
                    out=PS[:, bank, :], lhsT=hT[:, kc, tt * 128:(tt + 1) * 128], rhs=W[:, kc, 1024:1536],
                    start=(kc == 0), stop=(kc == 7)), rd=[r_w[2][kc], r_h], wr=[r_bank[bank]])
            src = PS[:, bank, :].rearrange("p (a b c) -> p a b c", a=4, b=2)
            P.op("act", "activation", dict(out=vst[:, sl, tti, :, 0:64], in_=src[:, :, 0, :], func=AF.Copy),
                 rd=[r_bank[bank]], wr=[r_v[sl]])
            P.op("dve", "tensor_copy", dict(out=vst[:, sl, tti, :, 65:129], in_=src[:, :, 1, :]),
                 rd=[r_bank[bank]], wr=[r_v[sl]])
        for j in range(4):
            s = j % 2
            ba = next_bank()
            proj_chunk(12 + j, tb, ba)
            bb = next_bank()
            proj_chunk(16 + j, tb, bb)
            P.op("act", "activation", dict(out=sg[:, s, :], in_=PS[:, bb, :], func=AF.Sigmoid),
                 rd=[r_bank[bb]], wr=[r_sg[s]])
            P.op("dve", "tensor_tensor", dict(out=gst[:, sl, j, :], in0=PS[:, ba, :], in1=sg[:, s, :],
                                                                  op=ALU.mult),
                 rd=[r_bank[ba], r_sg[s]], wr=[r_g[sl][j]])
        for gr in range(4):
            s = gr % 2
            bank = next_bank()
            proj_chunk(20 + gr, tb, bank)
            P.op("act", "activation", dict(out=fcT[:, s, :], in_=PS[:, bank, :], func=AF.Copy),
                 rd=[r_bank[bank]], wr=[r_fc[s]])
            for tp in range(2):
                for t_ in range(2):
                    tt = tp * 2 + t_
                    P.op("pe", "matmul", dict(
                        out=PS[:, 7, t_ * 256:(t_ + 1) * 256], lhsT=fcT[:, s, tt * 128:(tt + 1) * 128], rhs=dftg[:],
                        start=True, stop=True), rd=[r_fc[s], r_c], wr=[r_bank[7]])
                for t_ in range(2):
                    tti = tp * 2 + t_
                    P.op("dve", "tensor_copy", dict(
                        out=zst[:, sl, tti, gr, :], in_=PS[:, 7, t_ * 256:(t_ + 1) * 256]),
                        rd=[r_bank[7]], wr=[r_z[sl]])

        for oc in range(8):
            dst = qT_o if oc < 4 else kT_o
            o = (oc % 4) * 128
            fin.append(P.op("sp", "dma_start", dict(
                out=dst[o:o + 128, ts_], in_=qkst[:, sl, oc, :]), rd=[r_qk[sl][oc]]))
        for j in range(4):
            fin.append(P.op("sp", "dma_start", dict(
                out=gT_o[j * 128:(j + 1) * 128, ts_], in_=gst[:, sl, j, :]), rd=[r_g[sl][j]]))
        for hp in range(4):
            fin.append(P.op("sp", "dma_start", dict(
                out=vp_o[hp, ts_, :].rearrange("(t p) c -> p t c", p=128), in_=vst[:, sl, :, hp, :]), rd=[r_v[sl]]))
        for gr in range(4):
            fin.append(P.op("sp", "dma_start", dict(
                out=zp_o[gr, ts_, :].rearrange("(t p) c -> p t c", p=128), in_=zst[:, sl, :, gr, :]), rd=[r_z[sl]]))
    return fin


def lam_init_of(l):
    return 0.8 - 0.6 * math.exp(-0.3 * l)


def phase_B1(P, sfx, l):
    nc = P.nc
    qT_d = P.dram("qT" + sfx, [512, NT], BF16, "ExternalInput")
    kT_d = P.dram("kT_all" + sfx, [4, 512, NT], BF16, "ExternalInput")
    vp_d = P.dram("vp_all" + sfx, [4, 4, NT, 130], BF16, "ExternalInput")
    lam_d = P.dram("lamv" + sfx, [128, 4, 32], F32, "ExternalInput")
    gsub_d = P.dram("gsub" + sfx, [64, 1], F32, "ExternalInput")
    sel_d = P.dram("sel" + sfx, [128, 64], F32, "ExternalInput")
    o64_d = P.dram("ones64" + sfx, [64, 64], BF16, "ExternalInput")
    qm_d = P.dram("qmask" + sfx, [128, 4], F32, "ExternalInput")
    aT_o = P.dram("aT" + sfx, [8, 64, NT], BF16, "ExternalOutput")

    qT = P.sb("qT_sb", [128, 4, NT], BF16)
    qTm = P.sb("qTm_sb", [128, 2, 4, NT], BF16)
    qm = P.sb("qm_sb", [128, 4], F32)
    KT = P.sb("KT_sb", [128, 2, SEQ], BF16)
    VP = P.sb("VP_sb", [128, 2, 64, 130], BF16)
    lamv = P.sb("lamv_sb", [128, 4, 32], F32)
    lprod = P.sb("lprod", [128, 2, 32], F32)
    lsum = P.sb("lsum", [128, 2], F32)
    lexp = P.sb("lexp", [128, 2], F32)
    neglam = P.sb("neglam", [128, 1], F32)
    gsub = P.sb("gsub_sb", [64, 1], F32)
    gsub2 = P.sb("gsub2_sb", [64, 1], F32)
    sel = P.sb("sel_sb", [128, 64], F32)
    o64 = P.sb("o64_sb", [64, 64], BF16)
    E = P.sb("E_sb", [128, 3, 2, 512], BF16)
    Osb = P.sb("Osb", [65, 2, 512], F32)
    Rr = P.sb("Rr", [64, 2, 512], F32)
    tt0 = P.sb("tt0", [64, 2, 512], F32)
    att = P.sb("att", [64, 512], F32)
    sqa = P.sb("sqa", [64, 512], BF16)
    lna = P.sb("lna", [64, 512], F32)
    rsa = P.sb("rsa", [64, 512], F32)
    aT = P.sb("aT_sb", [64, 8, NT], BF16)
    PS = P.ps("ps", [128, 8, 512], F32)

    R = Res
    r_q = [R() for _ in range(4)]
    r_qm = [[R() for _ in range(4)] for _ in range(2)]
    r_kt = [[R() for _ in range(4)] for _ in range(2)]
    r_vp = [[R() for _ in range(4)] for _ in range(2)]
    r_c = R()
    r_lam = R()
    r_bank = [R() for _ in range(8)]
    r_E = [R() for _ in range(3)]
    r_O = [R(), R()]
    r_R = [R(), R()]
    r_t = [R(), R()]
    r_att, r_sq, r_ln, r_rs = R(), R(), R(), R()
    r_aT = [[R() for _ in range(4)] for _ in range(8)]

    lam_init = lam_init_of(l)
    for hp in range(4):
        P.op("sp", "dma_start", dict(out=qT[:, hp, :], in_=qT_d[hp * 128:(hp + 1) * 128, :]), wr=[r_q[hp]])
    P.op("sp", "dma_start", dict(out=lamv[:], in_=lam_d[:, :, :]), wr=[r_lam])
    P.op("sp", "dma_start", dict(out=gsub[:], in_=gsub_d[:, :]), wr=[r_c])
    P.op("sp", "dma_start", dict(out=sel[:], in_=sel_d[:, :]), wr=[r_c])
    P.op("sp", "dma_start", dict(out=o64[:], in_=o64_d[:, :]), wr=[r_c])
    P.op("sp", "dma_start", dict(out=qm[:], in_=qm_d[:, :]), wr=[r_c])

    def load_kv(hp):
        s = hp % 2
        for r in range(4):
            P.op("sp", "dma_start", dict(out=KT[:, s, r * NT:(r + 1) * NT], in_=kT_d[r, hp * 128:(hp + 1) * 128, :]),
                 wr=[r_kt[s][r]])
            P.op("sp", "dma_start", dict(out=VP[:, s, 16 * r:16 * r + 16, :],
                                         in_=vp_d[r, hp].rearrange("(k p) c -> p k c", p=128)), wr=[r_vp[s][r]])

    load_kv(0)
    P.op("dve", "tensor_tensor", dict(out=lprod[:, 0, :], in0=lamv[:, 0, :], in1=lamv[:, 1, :], op=ALU.mult),
         rd=[r_lam], wr=[r_att])
    P.op("dve", "tensor_tensor", dict(out=lprod[:, 1, :], in0=lamv[:, 2, :], in1=lamv[:, 3, :], op=ALU.mult),
         rd=[r_lam], wr=[r_att])
    P.op("dve", "tensor_reduce", dict(out=lsum[:], in_=lprod[:], axis=mybir.AxisListType.X, op=ALU.add),
         rd=[r_att], wr=[r_sq])
    P.op("act", "activation", dict(out=lexp[:], in_=lsum[:], func=AF.Exp), rd=[r_sq], wr=[r_ln])
    P.op("dve", "tensor_tensor", dict(out=neglam[:], in0=lexp[:, 1:2], in1=lexp[:, 0:1], op=ALU.subtract),
         rd=[r_ln], wr=[r_rs])
    P.op("dve", "tensor_scalar", dict(out=neglam[:], in0=neglam[:], scalar1=-lam_init, scalar2=None, op0=ALU.add),
         rd=[r_rs], wr=[r_rs])
    r_neglam = r_rs
    r_neglam_ev_holder = R()
    P.op("dve", "tensor_scalar", dict(out=gsub2[:], in0=gsub[:], scalar1=1.0 - lam_init, scalar2=None, op0=ALU.mult),
         rd=[r_c], wr=[r_neglam_ev_holder])
    r_g2 = r_neglam_ev_holder
    r_att, r_sq, r_ln = R(), R(), R()
    r_rs2 = R()

    scale = 32.0 ** -0.5
    ecnt = [0]

    def qk(hp, s, h2, qs, kt):
        sb0 = (kt % 2) * 2
        for c in range(2):
            i = 2 * h2 + c
            P.op("pe", "matmul", dict(out=PS[:, sb0 + c, :], lhsT=KT[:, s, kt * 128:(kt + 1) * 128],
                                      rhs=qTm[:, s, i, qs], start=True, stop=True),
                 rd=[r_qm[s][i]] + r_kt[s], wr=[r_bank[sb0 + c]])

    def post_copy():
        for c in range(2):
            P.op("dve", "tensor_copy", dict(out=Osb[:, c, :], in_=PS[0:65, 4 + c, :]), rd=[r_bank[4 + c]], wr=[r_O[c]])

    def post_rest(h, qs, qb):
        for c in range(2):
            P.op("pe", "matmul", dict(out=PS[0:64, 6 + c, :], lhsT=sel[0:65, :], rhs=Osb[:, c, :], start=True, stop=True),
                 rd=[r_O[c], r_c], wr=[r_bank[6 + c]])
            P.op("dve", "reciprocal", dict(out=Rr[:, c, :], in_=PS[0:64, 6 + c, :]), rd=[r_bank[6 + c]], wr=[r_R[c]])
            P.op("pool", "tensor_tensor", dict(out=tt0[:, c, :], in0=Osb[0:64, c, :], in1=Rr[:, c, :], op=ALU.mult),
                 rd=[r_O[c], r_R[c]], wr=[r_t[c]])
        P.op("dve", "scalar_tensor_tensor", dict(out=att[:], in0=tt0[:, 1, :], scalar=neglam[0:64, 0:1], in1=tt0[:, 0, :],
                                                 op0=ALU.mult, op1=ALU.add), rd=[r_t[0], r_t[1], r_neglam], wr=[r_att])
        P.op("pool", "tensor_tensor", dict(out=sqa[:], in0=att[:], in1=att[:], op=ALU.mult), rd=[r_att], wr=[r_sq])
        P.op("pe", "matmul", dict(out=PS[0:64, 6, :], lhsT=o64[:], rhs=sqa[:], start=True, stop=True),
             rd=[r_sq, r_c], wr=[r_bank[6]])
        P.op("act", "activation", dict(out=lna[:], in_=PS[0:64, 6, :], func=AF.Ln, bias=EPS, scale=1.0),
             rd=[r_bank[6]], wr=[r_ln])
        P.op("act", "activation", dict(out=rsa[:], in_=lna[:], func=AF.Exp, scale=-0.5), rd=[r_ln], wr=[r_rs2])
        P.op("dve", "scalar_tensor_tensor", dict(out=aT[:, h, qs], in0=att[:], scalar=gsub2[:, 0:1], in1=rsa[:],
                                                 op0=ALU.mult, op1=ALU.mult), rd=[r_att, r_rs2, r_g2], wr=[r_aT[h][qb]])

    def mask_q(hp):
        s_ = hp % 2
        for i in range(4):
            P.op("dve", "tensor_scalar", dict(out=qTm[:, s_, i, :], in0=qT[:, hp, :], scalar1=qm[:, i:i + 1], scalar2=None,
                                               op0=ALU.mult), rd=[r_q[hp], r_c], wr=[r_qm[s_][i]])

    pending = None
    mask_q(0)
    for hp in range(4):
        s = hp % 2
        if hp + 1 < 4:
            load_kv(hp + 1)
            mask_q(hp + 1)
        for h2 in range(2):
            h = hp * 2 + h2
            for qb in range(4):
                qs = slice(qb * 512, (qb + 1) * 512)
                qk(hp, s, h2, qs, 0)
                qk(hp, s, h2, qs, 1)
                for kt in range(64):
                    sb0 = (kt % 2) * 2
                    eb = ecnt[0] % 3
                    ecnt[0] += 1
                    P.op("act", "activation", dict(out=E[:, eb, :, :], in_=PS[:, sb0:sb0 + 2, :], func=AF.Exp, scale=scale),
                         rd=[r_bank[sb0], r_bank[sb0 + 1]], wr=[r_E[eb]])
                    if kt + 2 < 64:
                        qk(hp, s, h2, qs, kt + 2)
                    for c in range(2):
                        P.op("pe", "matmul", dict(out=PS[0:65, 4 + c, :], lhsT=VP[:, s, kt, h2 * 65:(h2 + 1) * 65],
                                                  rhs=E[:, eb, c, :], start=(kt == 0), stop=(kt == 63)),
                             rd=[r_E[eb]] + r_vp[s], wr=[r_bank[4 + c]])
                    if kt == 6 and pending is not None:
                        post_rest(*pending)
                        pending = None
                post_copy()
                pending = (h, qs, qb)
    post_rest(*pending)
    fin = []
    for h in range(8):
        fin.append(P.op("sp", "dma_start", dict(out=aT_o[h], in_=aT[:, h, :]), rd=r_aT[h]))
    return fin


def consts_B1():
    sel = np.zeros((128, 64), np.float32)
    sel[64, :] = 1.0
    qm = np.zeros((128, 4), np.float32)
    for i in range(4):
        qm[32 * i:32 * i + 32, i] = 1.0
    return {"sel": sel, "ones64": bf(np.full((64, 64), 1.0 / 64.0)), "qmask": qm}


def phase_B2(P, sfx):
    nc = P.nc
    g_d = P.dram("gT" + sfx, [512, NT], BF16, "ExternalInput")
    gall_d = P.dram("g_all" + sfx, [4, 512, NT], BF16, "ExternalInput")
    cw_d = P.dram("conv_w" + sfx, [128, 4, 31], F32, "ExternalInput")
    cv4_d = P.dram("cvec" + sfx, [128, 3, 4], F32, "ExternalInput")
    cm_d = P.dram("cmask" + sfx, [128, 8], F32, "ExternalInput")
    id_d = P.dram("ident" + sfx, [128, 128], BF16, "ExternalInput")
    o512_d = P.dram("ones512" + sfx, [128, 128], F32, "ExternalInput")
    bT_o = P.dram("bT" + sfx, [512, NT], BF16, "ExternalOutput")

    gp = P.sb("gpad", [128, 4, NT + 30], BF16)
    hal = P.sb("hal", [128, 4, 4, 2, 15], BF16)
    cw = P.sb("cw", [128, 4, 31], F32)
    cv4 = P.sb("cv4", [128, 3, 4], F32)
    cm = P.sb("cm", [128, 8], F32)
    ident = P.sb("ident_sb", [128, 128], BF16)
    o512 = P.sb("o512", [128, 128], F32)
    Dg = P.sb("Dg", [128, 4, 31, 128], BF16)
    cv = P.sb("cv", [128, 4, 512], F32)
    sqv = P.sb("sqv", [128, 4, 512], F32)
    m2 = P.sb("m2", [128, 512], F32)
    var = P.sb("var", [128, 512], F32)
    lnv = P.sb("lnv", [128, 512], F32)
    rstd = P.sb("rstd", [128, 512], F32)
    nmr = P.sb("nmr", [128, 512], F32)
    yv = P.sb("yv", [128, 2, 512], F32)
    bst = P.sb("bst", [128, 4, NT], BF16)
    PS = P.ps("ps", [128, 8, 512], F32)

    R = Res
    r_g = [R() for _ in range(4)]
    r_hal, r_c, r_D = R(), R(), [R() for _ in range(4)]
    r_bank = [R() for _ in range(8)]
    r_cv = [R() for _ in range(4)]
    r_sq = [R() for _ in range(4)]
    r_m2, r_var, r_ln, r_rstd, r_nmr = R(), R(), R(), R(), R()
    r_y = [R(), R()]
    r_b = [R() for _ in range(4)]

    for j in range(4):
        P.op("sp", "dma_start", dict(out=gp[:, j, 15:15 + NT], in_=g_d[j * 128:(j + 1) * 128, :]), wr=[r_g[j]])
    for i, (s_, d_) in enumerate([(cw, cw_d), (cv4, cv4_d), (cm, cm_d), (ident, id_d), (o512, o512_d)]):
        P.op("sp", "dma_start", dict(out=s_[:], in_=d_), wr=[r_c])
    rh = [R() for _ in range(8)]
    for r in range(4):
        gv = gall_d[r].rearrange("(j p) t -> p j t", p=128)
        P.op("sp", "dma_start", dict(out=hal[:, :, r, 0, :], in_=gv[:, :, NT - 15:NT]), wr=[rh[2 * r]])
        P.op("sp", "dma_start", dict(out=hal[:, :, r, 1, :], in_=gv[:, :, 0:15]), wr=[rh[2 * r + 1]])
    for side in range(2):
        dst = gp[:, :, 0:15] if side == 0 else gp[:, :, 15 + NT:30 + NT]
        for r in range(4):
            mk = cm[:, side * 4 + r:side * 4 + r + 1]
            if r == 0:
                P.op("dve", "tensor_scalar", dict(out=dst, in0=hal[:, :, r, side, :], scalar1=mk, scalar2=None, op0=ALU.mult),
                     rd=[rh[2 * r + side], r_c], wr=r_g)
            else:
                P.op("dve", "scalar_tensor_tensor", dict(out=dst, in0=hal[:, :, r, side, :], scalar=mk, in1=dst,
                                                         op0=ALU.mult, op1=ALU.add), rd=[rh[2 * r + side], r_c], wr=r_g)
    n = 0
    for j in range(4):
        for tau in range(31):
            n += 1
            if n % 3 != 0:
                P.op("dve", "tensor_scalar", dict(out=Dg[:, j, tau, :], in0=ident[:], scalar1=cw[:, j, tau:tau + 1], scalar2=None,
                                                  op0=ALU.mult), rd=[r_c], wr=[r_D[j]])
            else:
                P.op("act", "activation", dict(out=Dg[:, j, tau, :], in_=ident[:], func=AF.Copy, scale=cw[:, j, tau:tau + 1]),
                     rd=[r_c], wr=[r_D[j]])
    fin = []
    for tb in range(NTB):
        ts_ = slice(tb * 512, (tb + 1) * 512)
        for j in range(4):
            bank = j
            for tau in range(31):
                P.op("pe", "matmul", dict(out=PS[:, bank, :], lhsT=Dg[:, j, tau, :],
                                          rhs=gp[:, j, tb * 512 + tau:tb * 512 + tau + 512], start=(tau == 0), stop=(tau == 30)),
                     rd=[r_D[j], r_g[j]], wr=[r_bank[bank]])
            P.op("act", "activation", dict(out=cv[:, j, :], in_=PS[:, bank, :], func=AF.Identity, bias=cv4[:, 0, j:j + 1], scale=1.0),
                 rd=[r_bank[bank], r_c], wr=[r_cv[j]])
            P.op("pool", "tensor_tensor", dict(out=sqv[:, j, :], in0=cv[:, j, :], in1=cv[:, j, :], op=ALU.mult),
                 rd=[r_cv[j]], wr=[r_sq[j]])
        for j in range(4):
            P.op("pe", "matmul", dict(out=PS[:, 4, :], lhsT=o512[:], rhs=cv[:, j, :], start=(j == 0), stop=(j == 3)),
                 rd=[r_cv[j], r_c], wr=[r_bank[4]])
        for j in range(4):
            P.op("pe", "matmul", dict(out=PS[:, 5, :], lhsT=o512[:], rhs=sqv[:, j, :], start=(j == 0), stop=(j == 3)),
                 rd=[r_sq[j], r_c], wr=[r_bank[5]])
        P.op("act", "activation", dict(out=m2[:], in_=PS[:, 4, :], func=AF.Square), rd=[r_bank[4]], wr=[r_m2])
        P.op("dve", "tensor_tensor", dict(out=var[:], in0=PS[:, 5, :], in1=m2[:], op=ALU.subtract), rd=[r_bank[5], r_m2], wr=[r_var])
        P.op("act", "activation", dict(out=lnv[:], in_=var[:], func=AF.Ln, bias=EPS, scale=1.0), rd=[r_var], wr=[r_ln])
        P.op("act", "activation", dict(out=rstd[:], in_=lnv[:], func=AF.Exp, scale=-0.5), rd=[r_ln], wr=[r_rstd])
        P.op("dve", "scalar_tensor_tensor", dict(out=nmr[:], in0=PS[:, 4, :], scalar=-1.0, in1=rstd[:], op0=ALU.mult, op1=ALU.mult),
             rd=[r_bank[4], r_rstd], wr=[r_nmr])
        for j in range(4):
            s = j % 2
            P.op("pool", "tensor_tensor", dict(out=yv[:, s, :], in0=cv[:, j, :], in1=rstd[:], op=ALU.mult),
                 rd=[r_cv[j], r_rstd], wr=[r_y[s]])
            P.op("dve", "tensor_tensor", dict(out=yv[:, s, :], in0=yv[:, s, :], in1=nmr[:], op=ALU.add),
                 rd=[r_nmr], wr=[r_y[s]])
            P.op("act", "activation", dict(out=bst[:, j, ts_], in_=yv[:, s, :], func=AF.Silu, bias=cv4[:, 2, j:j + 1],
                                           scale=cv4[:, 1, j:j + 1]), rd=[r_y[s], r_c], wr=[r_b[j]])
    for j in range(4):
        fin.append(P.op("sp", "dma_start", dict(out=bT_o[j * 128:(j + 1) * 128, :], in_=bst[:, j, :]), rd=[r_b[j]]))
    return fin


def phase_B3(P, sfx):
    nc = P.nc
    z_d = P.dram("z_all" + sfx, [4, 4, NT, 256], BF16, "ExternalInput")
    w64_d = P.dram("w64" + sfx, [128, 2, 128], BF16, "ExternalInput")
    tt_d = P.dram("ttab" + sfx, [128, 64, 2, 32], BF16, "ExternalInput")
    fT_o = P.dram("fT" + sfx, [512, NT], BF16, "ExternalOutput")

    Zt = P.sb("Zt", [128, 1, 128, 256], BF16)
    w64 = P.sb("w64_sb", [128, 2, 128], BF16)
    ttab = P.sb("ttab_sb", [128, 64, 2, 32], BF16)
    A = P.sb("A_sb", [128, 2, 128, 2, 64], BF16)
    fst = P.sb("fst", [128, 4, NT], BF16)
    PS = P.ps("ps", [128, 8, 512], F32)

    R = Res
    r_z = [[R() for _ in range(4)] for _ in range(2)]
    r_c = R()
    r_A = [[R() for _ in range(32)] for _ in range(2)]
    r_bank = [R() for _ in range(8)]
    r_f = [R() for _ in range(4)]

    P.op("sp", "dma_start", dict(out=w64[:], in_=w64_d), wr=[r_c])
    P.op("sp", "dma_start", dict(out=ttab[:], in_=tt_d), wr=[r_c])

    def load_pair(gp_):
        sl = 0
        for g2 in range(2):
            gr = gp_ * 2 + g2
            for r in range(4):
                P.op("sp", "dma_start", dict(out=Zt[64 * g2 + 16 * r:64 * g2 + 16 * r + 16, sl, :, :],
                                             in_=z_d[r, gr].rearrange("(p s) c -> p s c", s=128)), wr=[r_z[sl][g2 * 2 + r // 2]])

    load_pair(0)
    ev_n = 0
    bank_n = 0
    fin = []
    for gp_ in range(2):
        sl = 0
        if gp_ == 1:
            load_pair(1)
        for g2 in range(2):
            gr = gp_ * 2 + g2
            asl = gr % 2
            rows = slice(64 * g2, 64 * g2 + 64)
            for j0 in range(0, 128, 4):
                bank = bank_n % 4
                bank_n += 1
                for jj in range(4):
                    j = j0 + jj
                    for c in range(2):
                        P.op("pe", "matmul", dict(out=PS[:, bank, jj * 128:(jj + 1) * 128], lhsT=Zt[rows, sl, :, c * 128 + j],
                                                  rhs=w64[rows, c, :], start=(c == 0), stop=(c == 1)),
                             rd=[r_z[sl][g2 * 2], r_z[sl][g2 * 2 + 1], r_c], wr=[r_bank[bank]])
                q = "act" if ev_n % 2 == 0 else "dve"
                ev_n += 1
                if q == "act":
                    P.op("act", "activation", dict(out=A[:, asl, j0:j0 + 4, :, :], in_=PS[:, bank, :], func=AF.Copy),
                         rd=[r_bank[bank]], wr=[r_A[asl][j0 // 4]])
                else:
                    P.op("dve", "tensor_copy", dict(out=A[:, asl, j0:j0 + 4, :, :], in_=PS[:, bank, :]),
                         rd=[r_bank[bank]], wr=[r_A[asl][j0 // 4]])
            for kb in range(4):
                bank = 4 + (kb % 2)
                for kk in range(16):
                    k2 = kb * 16 + kk
                    for c in range(2):
                        P.op("pe", "matmul", dict(out=PS[:, bank, kk * 32:(kk + 1) * 32], lhsT=A[:, asl, :, c, k2],
                                                  rhs=ttab[:, k2, c, :], start=(c == 0), stop=(c == 1)),
                             rd=r_A[asl] + [r_c], wr=[r_bank[bank]])
                dst = fst[:, gr, :].rearrange("p (a b) -> p a b", b=64)[:, :, kb * 16:(kb + 1) * 16]
                src = PS[:, bank, :].rearrange("p (b a) -> p a b", a=32)
                P.op("dve", "tensor_copy", dict(out=dst, in_=src), rd=[r_bank[bank]], wr=[r_f[gr]])
    for gr in range(4):
        fin.append(P.op("sp", "dma_start", dict(out=fT_o[gr * 128:(gr + 1) * 128, :], in_=fst[:, gr, :]), rd=[r_f[gr]]))
    return fin


def consts_B3(core):
    a = core % 4
    s2 = np.arange(64, dtype=np.float64)
    k2 = np.arange(64, dtype=np.float64)
    th = 2 * np.pi * np.outer(s2, k2) / 64.0
    C, S = np.cos(th), np.sin(th)
    w = np.stack([np.concatenate([C, -S], 1), np.concatenate([S, C], 1)], 1)
    w64 = bf(np.concatenate([w, w], 0))
    s1 = np.arange(128, dtype=np.float64)[:, None, None]
    k2_ = np.arange(64, dtype=np.float64)[None, :, None]
    k1 = (32 * a + np.arange(32, dtype=np.float64))[None, None, :]
    ph = 2 * np.pi * s1 * (64 * k1 + k2_) / 8192.0
    sc = 2.0 ** -10
    tt = np.stack([np.cos(ph) * sc, np.sin(ph) * sc], 2)
    return {"w64": w64, "ttab": bf(tt)}


DEBUG_C = False


def phase_C(P, sfx):
    nc = P.nc
    xT_d = P.dram("xT" + sfx, [D, NT], F32, "ExternalInput")
    aT_d = P.dram("aT" + sfx, [8, 64, NT], BF16, "ExternalInput")
    bT_d = P.dram("bT" + sfx, [512, NT], BF16, "ExternalInput")
    fT_d = P.dram("fT" + sfx, [512, NT], BF16, "ExternalInput")
    wpa_d = P.dram("w_proj_a" + sfx, [512, D], F32, "ExternalInput")
    wpb_d = P.dram("w_proj_b" + sfx, [512, D], F32, "ExternalInput")
    wpc_d = P.dram("w_proj_c" + sfx, [512, D], F32, "ExternalInput")
    wg_d = P.dram("w_gate" + sfx, [D, 3 * D], F32, "ExternalInput")
    bg_d = P.dram("b_gate" + sfx, [128, 24], F32, "ExternalInput")
    wo_d = P.dram("w_out" + sfx, [D, D], F32, "ExternalInput")
    g12_d = P.dram("g12" + sfx, [128, 2, 8], F32, "ExternalInput")
    ones_d = P.dram("ones_d" + sfx, [128, 128], BF16, "ExternalInput")
    w1_d = P.dram("w_ffn_in" + sfx, [D, 2 * DFF], F32, "ExternalInput")
    w2_d = P.dram("w_ffn_out" + sfx, [DFF, D], F32, "ExternalInput")
    xo_d = P.dram("xT_out" + sfx, [D, NT], F32, "ExternalOutput")

    xT = P.sb("xT_sb", [128, 8, NT], F32)
    ARENA_BYTES = 137216
    arena = P.sb("arena_c", [128, ARENA_BYTES // 2], BF16)

    def carve(off, shape, dt):
        nb = (4 if dt == F32 else 2)
        n = 1
        for d_ in shape[1:]:
            n *= d_
        v = arena[:, off // 2:off // 2 + n * nb // 2]
        if dt == F32:
            v = v.bitcast(F32)
        if len(shape) == 2:
            return v
        if len(shape) == 3:
            return v.rearrange("p (a b) -> p a b", b=shape[2])
        raise ValueError

    WA = carve(0, [128, 8, 3072], BF16)
    WB = carve(49152, [128, 20, 1024], BF16)
    hT = carve(90112, [128, 8, 512], BF16)
    brA = carve(98304, [128, 4, 512], BF16)
    brB = carve(102400, [128, 4, 512], BF16)
    brC = carve(106496, [128, 4, 512], BF16)
    sig = carve(110592, [128, 3, 512], F32)
    tm = carve(116736, [128, 3, 512], F32)
    mT = carve(122880, [128, 8, 512], BF16)
    sq = carve(131072, [128, 2, 512], BF16)
    lnv = carve(133120, [128, 512], F32)
    rstd = carve(135168, [128, 512], F32)
    h2T = carve(71680, [128, 8, NT], BF16)
    actT = carve(110592, [128, 11, 512], BF16)
    sgf = carve(122880, [128, 2, 512], F32)
    bg = P.sb("bg", [128, 24], F32)
    g12 = P.sb("g12_sb", [128, 2, 8], F32)
    onesm = P.sb("ones_sb", [128, 128], BF16)
    PS = P.ps("ps", [128, 8, 512], F32)

    R = Res
    r_x = [[R() for _ in range(NTB)] for _ in range(8)]
    r_wa = [[R() for _ in range(3)] for _ in range(8)]
    r_wb = [R() for _ in range(20)]
    r_wpa = R()
    r_c = R()
    r_h = R()
    r_sq = [R(), R()]
    r_ln, r_rstd = R(), R()
    r_br = [R(), R(), R()]
    r_sig = [R(), R(), R()]
    r_tm = [R(), R(), R()]
    r_m = [R() for _ in range(8)]
    r_bank = [R() for _ in range(8)]
    r_act = [R() for _ in range(11)]
    r_sgf = [R(), R()]

    for s_, d_ in [(bg, bg_d), (g12, g12_d), (onesm, ones_d)]:
        P.op("sp", "dma_start", dict(out=s_[:], in_=d_), wr=[r_c])

    def load_x(tb):
        for kc in range(8):
            P.op("sp", "dma_start", dict(out=xT[:, kc, tb * 512:(tb + 1) * 512],
                                         in_=xT_d[kc * 128:(kc + 1) * 128, tb * 512:(tb + 1) * 512]), wr=[r_x[kc][tb]])

    load_x(0)
    for j in range(4):
        P.op("poolq", "dma_start", dict(out=WB[:, 16 + j, :], in_=wpa_d[j * 128:(j + 1) * 128, :]), wr=[r_wb[16 + j]])
    for j in range(4):
        P.op("poolq", "dma_start", dict(out=WB[:, 8 + j, :], in_=wpb_d[j * 128:(j + 1) * 128, :]), wr=[r_wb[8 + j]])
    for j in range(4):
        P.op("poolq", "dma_start", dict(out=WB[:, 12 + j, :], in_=wpc_d[j * 128:(j + 1) * 128, :]), wr=[r_wb[12 + j]])
    for cb in range(3):
        for kc in range(8):
            P.op("poolq", "dma_start", dict(out=WA[:, kc, cb * 1024:(cb + 1) * 1024],
                                            in_=wg_d[kc * 128:(kc + 1) * 128, cb * 1024:(cb + 1) * 1024]), wr=[r_wa[kc][cb]])
    for kc in range(8):
        P.op("poolq", "dma_start", dict(out=WB[:, kc, :], in_=wo_d[kc * 128:(kc + 1) * 128, :]), wr=[r_wb[kc]])

    def rmsnorm_block(tb, gi, hdst):
        ts_ = slice(tb * 512, (tb + 1) * 512)
        for kc in range(8):
            s = kc % 2
            P.op("pool", "tensor_tensor", dict(out=sq[:, s, :], in0=xT[:, kc, ts_], in1=xT[:, kc, ts_], op=ALU.mult),
                 rd=[r_x[kc][tb]], wr=[r_sq[s]])
            P.op("pe", "matmul", dict(out=PS[:, 7, :], lhsT=onesm[:], rhs=sq[:, s, :], start=(kc == 0), stop=(kc == 7)),
                 rd=[r_sq[s], r_c], wr=[r_bank[7]])
        P.op("act", "activation", dict(out=lnv[:], in_=PS[:, 7, :], func=AF.Ln, bias=EPS, scale=1.0), rd=[r_bank[7]], wr=[r_ln])
        P.op("act", "activation", dict(out=rstd[:], in_=lnv[:], func=AF.Exp, scale=-0.5), rd=[r_ln], wr=[r_rstd])
        for kc in range(8):
            P.op("dve", "scalar_tensor_tensor", dict(out=hdst[:, kc, :], in0=xT[:, kc, ts_], scalar=g12[:, gi, kc:kc + 1],
                                                     in1=rstd[:], op0=ALU.mult, op1=ALU.mult),
                 rd=[r_x[kc][tb], r_rstd, r_c], wr=[r_h])

    for tb in range(NTB):
        ts_ = slice(tb * 512, (tb + 1) * 512)
        P.op("sp", "dma_start", dict(out=brA[:], in_=aT_d[:, :, ts_].rearrange("(j h) p t -> (h p) j t", h=2)), wr=[r_br[0]])
        P.op("sp", "dma_start", dict(out=brB[:], in_=bT_d[:, ts_].rearrange("(j p) t -> p j t", p=128)), wr=[r_br[1]])
        P.op("sp", "dma_start", dict(out=brC[:], in_=fT_d[:, ts_].rearrange("(j p) t -> p j t", p=128)), wr=[r_br[2]])
        if tb + 1 < NTB:
            load_x(tb + 1)
        rmsnorm_block(tb, 0, hT[:, :, 0:512])
        for oc in range(8):
            ocs = slice(oc * 128, (oc + 1) * 128)
            for j in range(4):
                P.op("pe", "matmul", dict(out=PS[:, 0, :], lhsT=WB[:, 16 + j, ocs], rhs=brA[:, j, :], start=(j == 0), stop=(j == 3)),
                     rd=[r_wb[16 + j], r_br[0]], wr=[r_bank[0]])
            for j in range(4):
                P.op("pe", "matmul", dict(out=PS[:, 1, :], lhsT=WB[:, 8 + j, ocs], rhs=brB[:, j, :], start=(j == 0), stop=(j == 3)),
                     rd=[r_wb[8 + j], r_br[1]], wr=[r_bank[1]])
            for j in range(4):
                P.op("pe", "matmul", dict(out=PS[:, 2, :], lhsT=WB[:, 12 + j, ocs], rhs=brC[:, j, :], start=(j == 0), stop=(j == 3)),
                     rd=[r_wb[12 + j], r_br[2]], wr=[r_bank[2]])
            for i in range(3):
                for kc in range(8):
                    P.op("pe", "matmul", dict(out=PS[:, 3 + i, :], lhsT=WA[:, kc, i * 1024 + oc * 128:i * 1024 + (oc + 1) * 128],
                                              rhs=hT[:, kc, 0:512], start=(kc == 0), stop=(kc == 7)),
                         rd=[r_wa[kc][i], r_h], wr=[r_bank[3 + i]])
                P.op("act", "activation", dict(out=sig[:, i, :], in_=PS[:, 3 + i, :], func=AF.Sigmoid,
                                               bias=bg[:, i * 8 + oc:i * 8 + oc + 1], scale=1.0),
                     rd=[r_bank[3 + i], r_c], wr=[r_sig[i]])
                P.op("dve", "tensor_tensor", dict(out=tm[:, i, :], in0=PS[:, i, :], in1=sig[:, i, :], op=ALU.mult),
                     rd=[r_bank[i], r_sig[i]], wr=[r_tm[i]])
            P.op("pool", "tensor_tensor", dict(out=tm[:, 0, :], in0=tm[:, 0, :], in1=tm[:, 1, :], op=ALU.add),
                 rd=[r_tm[1]], wr=[r_tm[0]])
            P.op("pool", "tensor_tensor", dict(out=mT[:, oc, :], in0=tm[:, 0, :], in1=tm[:, 2, :], op=ALU.add),
                 rd=[r_tm[0], r_tm[2]], wr=[r_m[oc]])
        for oc2 in range(8):
            bank = 6
            for kc in range(8):
                P.op("pe", "matmul", dict(out=PS[:, bank, :], lhsT=WB[:, kc, oc2 * 128:(oc2 + 1) * 128], rhs=mT[:, kc, :],
                                          start=(kc == 0), stop=(kc == 7)), rd=[r_wb[kc], r_m[kc]], wr=[r_bank[bank]])
            P.op("dve", "tensor_tensor", dict(out=xT[:, oc2, ts_], in0=PS[:, bank, :], in1=xT[:, oc2, ts_], op=ALU.add),
                 rd=[r_bank[bank]], wr=[r_x[oc2][tb]])

    fin = []
    if DEBUG_C:
        x1_d = P.dram("x1_out" + sfx, [D, NT], F32, "ExternalOutput")
        for kc in range(8):
            fin.append(P.op("sp", "dma_start", dict(out=x1_d[kc * 128:(kc + 1) * 128, :], in_=xT[:, kc, :]), rd=r_x[kc]))
    P.op("dve", "memset", dict(ap=lnv[:, 0:8], constant=0.0), rd=[], wr=r_sig + r_tm + r_br + r_act + r_sgf + r_m + [r_ln, r_h] + r_wb[11:20])
    for tb in range(NTB):
        rmsnorm_block(tb, 1, h2T[:, :, tb * 512:(tb + 1) * 512])
    for half in range(2):
        c0 = half * 1408
        for kc in range(8):
            P.op("poolq", "dma_start", dict(out=WA[:, kc, 0:1408], in_=w1_d[kc * 128:(kc + 1) * 128, c0:c0 + 1408]), wr=r_wa[kc])
            P.op("poolq", "dma_start", dict(out=WA[:, kc, 1408:2816], in_=w1_d[kc * 128:(kc + 1) * 128, DFF + c0:DFF + c0 + 1408]),
                 wr=r_wa[kc])
        for j in range(11):
            P.op("poolq", "dma_start", dict(out=WB[:, j, :], in_=w2_d[c0 + j * 128:c0 + (j + 1) * 128, :]), wr=[r_wb[j]])
        for tb in range(NTB):
            ts_ = slice(tb * 512, (tb + 1) * 512)
            for j in range(11):
                s = j % 2
                bg_, bu_ = (0, 1) if s == 0 else (2, 3)
                for kc in range(8):
                    P.op("pe", "matmul", dict(out=PS[:, bg_, :], lhsT=WA[:, kc, j * 128:(j + 1) * 128], rhs=h2T[:, kc, ts_],
                                              start=(kc == 0), stop=(kc == 7)), rd=[r_wa[kc][0], r_h], wr=[r_bank[bg_]])
                for kc in range(8):
                    P.op("pe", "matmul", dict(out=PS[:, bu_, :], lhsT=WA[:, kc, 1408 + j * 128:1408 + (j + 1) * 128], rhs=h2T[:, kc, ts_],
                                              start=(kc == 0), stop=(kc == 7)), rd=[r_wa[kc][0], r_h], wr=[r_bank[bu_]])
                P.op("act", "activation", dict(out=sgf[:, s, :], in_=PS[:, bg_, :], func=AF.Silu), rd=[r_bank[bg_]], wr=[r_sgf[s]])
                P.op("dve", "tensor_tensor", dict(out=actT[:, j, :], in0=PS[:, bu_, :], in1=sgf[:, s, :], op=ALU.mult),
                     rd=[r_bank[bu_], r_sgf[s]], wr=[r_act[j]])
            for oc in range(8):
                bank = 4 + (oc % 2)
                for j in range(11):
                    P.op("pe", "matmul", dict(out=PS[:, bank, :], lhsT=WB[:, j, oc * 128:(oc + 1) * 128], rhs=actT[:, j, :],
                                              start=(j == 0), stop=(j == 10)), rd=[r_wb[j], r_act[j]], wr=[r_bank[bank]])
                P.op("dve", "tensor_tensor", dict(out=xT[:, oc, ts_], in0=PS[:, bank, :], in1=xT[:, oc, ts_], op=ALU.add),
                     rd=[r_bank[bank]], wr=[r_x[oc][tb]])
    for kc in range(8):
        fin.append(P.op("sp", "dma_start", dict(out=xo_d[kc * 128:(kc + 1) * 128, :], in_=xT[:, kc, :]), rd=r_x[kc]))
    return fin, xT, r_x


def build_launch(kind):
    nc = bass.Bass("TRN2", target_bir_lowering=False)
    if kind == "L1":
        P = Prog(nc)
        fin = phase_A(P, "_0")
    else:
        l = 0 if kind == "L2" else 1
        sfx = "_%d" % l
        P = Prog(nc, internal={"aT" + sfx, "bT" + sfx, "fT" + sfx})
        fin = []
        P.phase_begin()
        phase_B1(P, sfx, l)
        P.barrier()
        P.phase_begin()
        phase_B2(P, sfx)
        P.barrier()
        P.phase_begin()
        phase_B3(P, sfx)
        P.barrier()
        P.phase_begin()
        finC, xT, r_x = phase_C(P, sfx)
        fin += finC
        if kind == "L2":
            P.barrier()
            P.phase_begin()
            fin += phase_A(P, "_1", x_res=(xT, r_x))
    P.final_events = fin
    with P.stack:
        with nc.Block() as block:
            P.finalize(block)
    return nc


def inputs_A(l, inp, c, sfx, xT=None):
    ct = const_tables(c)
    g1 = np.ascontiguousarray(inp["norm1_g"][l].reshape(8, 128).T)
    gqk = np.stack([np.tile(inp["qnorm_g"][l], 4), np.tile(inp["knorm_g"][l], 4)], axis=1).astype(np.float32)
    d = {"w_in": np.ascontiguousarray(inp["w_in"][l]), "g1": g1, "gqk": np.ascontiguousarray(gqk),
         "cosf": ct["cosf"], "sinf": ct["sinf"], "ones_d": ct["ones_d"], "bd32": ct["bd32"], "rot": ct["rot"],
         "dftg": ct["dftg"]}
    if xT is not None:
        d["xT"] = xT
    return {k + sfx: v for k, v in d.items()}


def inputs_B(l, inp, c, sfx, ex):
    b, a = c // 4, c % 4
    cb = consts_B1()
    lamv = np.stack([inp["lambda_q1"][l], inp["lambda_k1"][l], inp["lambda_q2"][l], inp["lambda_k2"][l]], 0)
    lamv = np.ascontiguousarray(np.broadcast_to(lamv[None], (128, 4, 32))).astype(np.float32)
    gsub = np.ascontiguousarray(inp["subln_g"][l].reshape(64, 1)).astype(np.float32)
    grp = [b * 4 + r for r in range(4)]
    d = {"qT": ex["qT"][c], "kT_all": np.ascontiguousarray(np.stack([ex["kT"][i] for i in grp], 0)),
         "vp_all": np.ascontiguousarray(np.stack([ex["vp"][i] for i in grp], 0)), "lamv": lamv, "gsub": gsub,
         "sel": cb["sel"], "ones64": cb["ones64"], "qmask": cb["qmask"]}
    cw = np.ascontiguousarray(inp["conv_w"][l].T.reshape(4, 128, 31).transpose(1, 0, 2)).astype(np.float32)
    cvec = np.stack([inp["conv_b"][l].reshape(4, 128).T, inp["conv_ln_g"][l].reshape(4, 128).T,
                     inp["conv_ln_b"][l].reshape(4, 128).T], axis=1).astype(np.float32)
    cm = np.zeros((128, 8), np.float32)
    if a > 0:
        cm[:, a - 1] = 1.0
    if a < 3:
        cm[:, 4 + a + 1] = 1.0
    d.update({"gT": ex["gT"][c], "g_all": np.ascontiguousarray(np.stack([ex["gT"][i] for i in grp], 0)), "conv_w": cw,
              "cvec": np.ascontiguousarray(cvec), "cmask": cm, "ident": bf(np.eye(128)),
              "ones512": np.full((128, 128), 1.0 / 512.0, np.float32)})
    c3 = consts_B3(c)
    d.update({"z_all": np.ascontiguousarray(np.stack([ex["zp"][i] for i in grp], 0)), "w64": c3["w64"], "ttab": c3["ttab"]})
    return {k + sfx: v for k, v in d.items()}


def inputs_C(l, inp, c, sfx, xT):
    ct = const_tables(0)
    g12 = np.stack([inp["norm1_g"][l].reshape(8, 128).T, inp["norm2_g"][l].reshape(8, 128).T], axis=1).astype(np.float32)
    bgate = np.ascontiguousarray(inp["b_gate"][l].reshape(24, 128).T).astype(np.float32)
    d = {"xT": xT, "w_proj_a": np.ascontiguousarray(inp["w_proj_a"][l]), "w_proj_b": np.ascontiguousarray(inp["w_proj_b"][l]),
         "w_proj_c": np.ascontiguousarray(inp["w_proj_c"][l]), "w_gate": np.ascontiguousarray(inp["w_gate"][l]),
         "b_gate": bgate, "w_out": np.ascontiguousarray(inp["w_out"][l]), "g12": np.ascontiguousarray(g12),
         "ones_d": ct["ones_d"], "w_ffn_in": np.ascontiguousarray(inp["w_ffn_in"][l]),
         "w_ffn_out": np.ascontiguousarray(inp["w_ffn_out"][l])}
    return {k + sfx: v for k, v in d.items()}


def _collect(res, sfx):
    return {k: [np.asarray(r[k + sfx]) for r in res] for k in ("qT", "kT", "vp", "gT", "zp")}


def kernel(**inp):
    inp = {k: np.asarray(v) for k, v in inp.items()}
    x = inp["x"]
    xT = [np.ascontiguousarray(x[c // 4, (c % 4) * NT:(c % 4 + 1) * NT, :].T) for c in range(NCORES)]
    cores = list(range(NCORES))
    r1 = run_bass_kernel_spmd(build_launch("L1"), [inputs_A(0, inp, c, "_0", xT[c]) for c in cores], core_ids=cores).results
    ex0 = _collect(r1, "_0")
    in2 = []
    for c in cores:
        d = inputs_B(0, inp, c, "_0", ex0)
        d.update(inputs_C(0, inp, c, "_0", xT[c]))
        d.update(inputs_A(1, inp, c, "_1"))
        in2.append(d)
    r2 = run_bass_kernel_spmd(build_launch("L2"), in2, core_ids=cores).results
    ex1 = _collect(r2, "_1")
    x1 = [np.ascontiguousarray(np.asarray(r["xT_out_0"])) for r in r2]
    in3 = []
    for c in cores:
        d = inputs_B(1, inp, c, "_1", ex1)
        d.update(inputs_C(1, inp, c, "_1", x1[c]))
        in3.append(d)
    r3 = run_bass_kernel_spmd(build_launch("L3"), in3, core_ids=cores).results
    out = np.empty((2, SEQ, D), np.float32)
    for c in cores:
        out[c // 4, (c % 4) * NT:(c % 4 + 1) * NT, :] = np.asarray(r3[c]["xT_out_1"]).T
    return out
```

```python
import math
from contextlib import ExitStack
import numpy as np
import ml_dtypes
import concourse.bass as bass
import concourse.mybir as mybir
from concourse.bass_utils import run_bass_kernel_spmd

F32 = mybir.dt.float32
BF16 = mybir.dt.bfloat16
AF = mybir.ActivationFunctionType
ALU = mybir.AluOpType

NCORES = 8
D = 1024
SEQ = 8192
NT = 2048
NTB = 4
EPS = 1e-6
DFF = 2816
ROPE_THETA = 500000.0


class Res:
    __slots__ = ("w", "r", "name")

    def __init__(self, name=""):
        self.w = None
        self.r = []
        self.name = name


class Ev:
    __slots__ = ("q", "idx", "needed", "sem", "val")

    def __init__(self, q, idx):
        self.q = q
        self.idx = idx
        self.needed = False
        self.sem = None
        self.val = None


COMPUTE_Q = ("pe", "act", "dve", "pool")
DMA_Q = ("sp", "actq", "poolq")
ENGINE_OF = {"pe": "tensor", "act": "scalar", "dve": "vector", "pool": "gpsimd",
             "sp": "sync", "actq": "scalar", "poolq": "gpsimd"}
NDMASEM = 6


class Prog:
    ARENA_BYTES = 212736

    def __init__(self, nc, internal=()):
        self.nc = nc
        self.stack = ExitStack()
        self.arena = None
        self.off = 0
        self.psum = None
        self.drams = {}
        self.internal = set(internal)
        self.barrier_ev = None
        self.dma_pending = []
        self.last_ev = {}
        self.bscr = None
        self.streams = {"tensor": [], "scalar": [], "vector": [], "gpsimd": [], "sync": []}
        self.evcount = {q: 0 for q in COMPUTE_Q + DMA_Q}
        self.dma_n = {q: 0 for q in DMA_Q}
        self.dma_last = {}
        self.sems = {}
        self.final_events = []

    def sb(self, name, shape, dt):
        if self.arena is None:
            self.arena = self.stack.enter_context(self.nc.sbuf_tensor("arena", [128, self.ARENA_BYTES // 2], BF16))
            self.bscr = self.stack.enter_context(self.nc.sbuf_tensor("bscr", [128, 8], F32))
        nb = 4 if dt == F32 else 2
        n = 1
        for d_ in shape[1:]:
            n *= d_
        nbytes = (n * nb + 63) // 64 * 64
        off = self.off
        assert off + nbytes <= self.ARENA_BYTES, (name, off, nbytes)
        self.off = off + nbytes
        v = self.arena[:, off // 2:off // 2 + n * nb // 2]
        if dt == F32:
            v = v.bitcast(F32)
        if shape[0] < 128:
            v = v[0:shape[0]]
        dims = list(shape[1:])
        if len(dims) == 1:
            return v
        names = "abcd"[:len(dims)]
        pat = "p (%s) -> p %s" % (" ".join(names), " ".join(names))
        return v.rearrange(pat, **{names[i]: dims[i] for i in range(1, len(dims))})

    def phase_begin(self, keep=0):
        self.off = keep

    def ps(self, name, shape, dt=F32):
        if self.psum is None:
            self.psum = self.stack.enter_context(self.nc.psum_tensor(name, list(shape), dt))
        return self.psum

    def dram(self, name, shape, dt, kind):
        if name in self.drams:
            return self.drams[name]
        if name in self.internal:
            kind = "Internal"
        ap = self.nc.dram_tensor(name, list(shape), dt, kind=kind).ap()
        self.drams[name] = ap
        return ap

    def barrier(self):
        deps = list(self.last_ev.values()) + list(self.dma_pending)
        r = Res()
        ev = self.op("pool", "memset", dict(ap=self.bscr[:, 0:2], constant=0.0), wr=[r], extra=deps)
        self.barrier_ev = ev
        self.dma_pending = []

    def op(self, q, name, kw, rd=(), wr=(), extra=()):
        fn = (name, kw)
        deps = list(extra)
        if self.barrier_ev is not None:
            deps.append(self.barrier_ev)
        for r in rd:
            if r.w is not None:
                deps.append(r.w)
        for w in wr:
            if w.w is not None:
                deps.append(w.w)
            deps.extend(w.r)
        self.evcount[q] += 1
        ev = Ev(q, self.evcount[q])
        if q in DMA_Q:
            slot = self.dma_n[q] % NDMASEM
            self.dma_n[q] += 1
            prev = self.dma_last.get((q, slot))
            if prev is not None:
                deps.append(prev)
            self.dma_last[(q, slot)] = ev
            ev.sem = (q, slot)
        else:
            ev.sem = (q, 0)
        best = {}
        dmas = []
        for d in deps:
            if d.q in COMPUTE_Q:
                if d.q == "pe" and q == "pe":
                    continue
                if d.q not in best or best[d.q].idx < d.idx:
                    best[d.q] = d
            else:
                if d not in dmas:
                    dmas.append(d)
        waits = list(best.values()) + dmas
        self.streams[ENGINE_OF[q]].append((q, fn, waits, ev))
        if q in DMA_Q:
            self.dma_pending.append(ev)
        else:
            self.last_ev[q] = ev
        for r in rd:
            r.r.append(ev)
        for w in wr:
            w.w = ev
            w.r = []
        return ev

    def finalize(self, block):
        nc = self.nc
        for eng, stream in self.streams.items():
            seen = {}
            for item in stream:
                q, fn, waits, ev = item
                keep = []
                for d in waits:
                    if d.q in COMPUTE_Q:
                        if seen.get(d.q, 0) >= d.idx:
                            continue
                        seen[d.q] = d.idx
                    else:
                        if seen.get(id(d)):
                            continue
                        seen[id(d)] = True
                    d.needed = True
                    keep.append(d)
                item[2][:] = keep
        for ev in self.final_events:
            ev.needed = True
        counters = {}
        for eng, stream in self.streams.items():
            for q, fn, waits, ev in stream:
                if q in DMA_Q:
                    key = ev.sem
                    counters[key] = counters.get(key, 0) + 16
                    ev.val = counters[key]
                    ev.needed = True
                elif ev.needed:
                    key = ev.sem
                    counters[key] = counters.get(key, 0) + 1
                    ev.val = counters[key]
        for key in counters:
            self.sems[key] = self.stack.enter_context(nc.semaphore("s_%s_%d" % key))
        self.maxcount = dict(counters)

        def emit(engname):
            stream = self.streams[engname]

            def body(eng):
                for q, fn, waits, ev in stream:
                    for d in waits:
                        eng.wait_ge(self.sems[d.sem], d.val)
                    ins = getattr(eng, fn[0])(**fn[1])
                    if ev.needed:
                        ins.then_inc(self.sems[ev.sem], 16 if q in DMA_Q else 1)
                if engname == "sync":
                    for ev in self.final_events:
                        eng.wait_ge(self.sems[ev.sem], ev.val)
            return body

        block.tensor(emit("tensor"))
        block.scalar(emit("scalar"))
        block.vector(emit("vector"))
        block.gpsimd(emit("gpsimd"))
        block.sync(emit("sync"))


def bf(a):
    return np.ascontiguousarray(np.asarray(a, dtype=np.float32)).astype(ml_dtypes.bfloat16)


def const_tables(core):
    a = core % 4
    t = {}
    t["ones_d"] = bf(np.full((128, 128), 1.0 / 1024.0))
    bd = np.zeros((128, 128), np.float32)
    for i in range(4):
        bd[32 * i:32 * i + 32, 32 * i:32 * i + 32] = 1.0 / 32.0
    t["bd32"] = bf(bd)
    rot = np.zeros((128, 128), np.float32)
    for blk in range(4):
        o = 32 * blk
        for i in range(4):
            rot[o + 4 + i, o + i] = -1.0
            rot[o + i, o + 4 + i] = 1.0
    t["rot"] = bf(rot)
    pos = (a * NT + np.arange(NT)).astype(np.float64)
    inv = 1.0 / (ROPE_THETA ** (np.arange(0, 8, 2, dtype=np.float64) / 8.0))
    ang = pos[None, :] * inv[:, None]
    cf = np.ones((128, NT), np.float32)
    sf = np.zeros((128, NT), np.float32)
    for blk in range(4):
        o = 32 * blk
        cf[o:o + 4] = np.cos(ang)
        cf[o + 4:o + 8] = np.cos(ang)
        sf[o:o + 4] = np.sin(ang)
        sf[o + 4:o + 8] = np.sin(ang)
    t["cosf"] = cf
    t["sinf"] = sf
    jc = np.outer(np.arange(128), np.arange(128)).astype(np.float64) * (2 * np.pi / 128.0)
    t["dftg"] = bf(np.concatenate([np.cos(jc), -np.sin(jc)], axis=1))
    return t


WEIGHT_SHAPES = {"w_proj_a": [512, D], "w_proj_b": [512, D], "w_proj_c": [512, D], "w_gate": [D, 3 * D], "w_out": [D, D],
                 "w_ffn_in": [D, 2 * DFF], "w_ffn_out": [DFF, D], "w_in": [D, 3072]}


def cast_weights(P, names_sfx):
    for name, sfx in names_sfx:
        shape = WEIGHT_SHAPES[name]
        src = P.dram(name + sfx, shape, F32, "ExternalInput")
        dst = P.dram(name + sfx + "_bf", shape, BF16, "Internal")
        for r0 in range(0, shape[0], 128):
            P.op("poolq", "dma_start", dict(out=dst[r0:r0 + 128, :], in_=src[r0:r0 + 128, :]), wr=[Res()])


def wsrc(P, name, sfx, bfw):
    shape = WEIGHT_SHAPES[name]
    if bfw:
        return P.dram(name + sfx + "_bf", shape, BF16, "Internal")
    return P.dram(name + sfx, shape, F32, "ExternalInput")


def phase_A(P, sfx, x_res=None, bfw=False):
    nc = P.nc
    if x_res is None:
        xT_d = P.dram("xT" + sfx, [D, NT], F32, "ExternalInput")
    win_d = wsrc(P, "w_in", sfx, bfw)
    g1_d = P.dram("g1" + sfx, [128, 8], F32, "ExternalInput")
    gqk_d = P.dram("gqk" + sfx, [128, 2], F32, "ExternalInput")
    cos_d = P.dram("cosf" + sfx, [128, NT], F32, "ExternalInput")
    sin_d = P.dram("sinf" + sfx, [128, NT], F32, "ExternalInput")
    ones_d = P.dram("ones_d" + sfx, [128, 128], BF16, "ExternalInput")
    bd_d = P.dram("bd32" + sfx, [128, 128], BF16, "ExternalInput")
    rot_d = P.dram("rot" + sfx, [128, 128], BF16, "ExternalInput")
    dftg_d = P.dram("dftg" + sfx, [128, 256], BF16, "ExternalInput")
    qT_o = P.dram("qT" + sfx, [512, NT], BF16, "ExternalOutput")
    kT_o = P.dram("kT" + sfx, [512, NT], BF16, "ExternalOutput")
    vp_o = P.dram("vp" + sfx, [4, NT, 130], BF16, "ExternalOutput")
    gT_o = P.dram("gT" + sfx, [512, NT], BF16, "ExternalOutput")
    zp_o = P.dram("zp" + sfx, [4, NT, 256], BF16, "ExternalOutput")

    xT = P.sb("xT_sb", [128, 8, NT], F32)
    W = P.sb("w_sb", [128, 8, 3072], BF16)
    g1 = P.sb("g1_sb", [128, 8], F32)
    gqk = P.sb("gqk_sb", [128, 2], F32)
    cosf = P.sb("cos_sb", [128, NT], F32)
    sinf = P.sb("sin_sb", [128, NT], F32)
    onesm = P.sb("ones_sb", [128, 128], BF16)
    bdm = P.sb("bd_sb", [128, 128], BF16)
    rotm = P.sb("rot_sb", [128, 128], BF16)
    dftg = P.sb("dftg_sb", [128, 256], BF16)
    hT = P.sb("hT", [128, 8, 512], BF16)
    sq = P.sb("sq", [128, 2, 512], BF16)
    lnv = P.sb("lnv", [128, 512], F32)
    rstd = P.sb("rstd", [128, 512], F32)
    sq2 = P.sb("sq2", [128, 2, 512], BF16)
    ln2 = P.sb("ln2", [128, 2, 512], F32)
    r2 = P.sb("r2", [128, 2, 512], F32)
    qn = P.sb("qn", [128, 2, 512], BF16)
    t1 = P.sb("t1", [128, 2, 512], F32)
    t2 = P.sb("t2", [128, 2, 512], F32)
    qkst = P.sb("qkst", [128, 1, 8, 512], BF16)
    vst = P.sb("vst", [128, 1, 4, 4, 130], BF16)
    sg = P.sb("sg", [128, 2, 512], F32)
    gst = P.sb("gst", [128, 1, 4, 512], BF16)
    fcT = P.sb("fcT", [128, 2, 512], BF16)
    zst = P.sb("zst", [128, 1, 4, 4, 256], BF16)
    PS = P.ps("ps", [128, 8, 512], F32)

    R = lambda n: Res(n)
    r_x = [[R("x%d" % i) for _ in range(NTB)] for i in range(8)]
    r_w = [[R("w%d" % i) for _ in range(8)] for i in range(6)]
    r_c = R("consts")
    r_cs = R("cossin")
    r_h = R("hT")
    r_sq = [R("sq0"), R("sq1")]
    r_ln = R("lnv")
    r_rstd = R("rstd")
    r_bank = [R("bank%d" % i) for i in range(8)]
    r_sq2 = [R("a"), R("b")]
    r_ln2 = [R("a"), R("b")]
    r_r2 = [R("a"), R("b")]
    r_qn = [R("a"), R("b")]
    r_t1 = [R("a"), R("b")]
    r_t2 = [R("a"), R("b")]
    r_qk = [[R("qk%d" % i) for i in range(8)] for _ in range(2)]
    r_v = [R("vst0"), R("vst1")]
    r_sg = [R("a"), R("b")]
    r_g = [[R("g%d" % i) for i in range(4)] for _ in range(2)]
    r_fc = [R("a"), R("b")]
    r_z = [R("zst0"), R("zst1")]

    for s_, d_ in [(g1, g1_d), (gqk, gqk_d), (onesm, ones_d), (bdm, bd_d), (rotm, rot_d), (dftg, dftg_d)]:
        P.op("sp", "dma_start", dict(out=s_[:], in_=d_[:, :]), wr=[r_c])
    if x_res is None:
        for tb in range(NTB):
            for kc in range(8):
                P.op("sp", "dma_start", dict(out=xT[:, kc, tb * 512:(tb + 1) * 512],
                                             in_=xT_d[kc * 128:(kc + 1) * 128, tb * 512:(tb + 1) * 512]), wr=[r_x[kc][tb]])
            if tb == 0:
                for s_, d_ in [(cosf, cos_d), (sinf, sin_d)]:
                    P.op("sp", "dma_start", dict(out=s_[:], in_=d_[:, :]), wr=[r_cs])
    else:
        for s_, d_ in [(cosf, cos_d), (sinf, sin_d)]:
            P.op("sp", "dma_start", dict(out=s_[:], in_=d_[:, :]), wr=[r_cs])
    for wb in range(6):
        for kc in range(8):
            P.op("poolq", "dma_start", dict(
                out=W[:, kc, wb * 512:(wb + 1) * 512],
                in_=win_d[kc * 128:(kc + 1) * 128, wb * 512:(wb + 1) * 512]), wr=[r_w[wb][kc]])
    for sl in range(1):
        P.op("pool", "memset", dict(ap=vst[:, sl, :, :, 64:65], constant=1.0), wr=[r_v[sl]])
        P.op("pool", "memset", dict(ap=vst[:, sl, :, :, 129:130], constant=1.0), wr=[r_v[sl]])

    fin = []
    bank_rr = [0]

    def next_bank():
        b = bank_rr[0] % 4
        bank_rr[0] += 1
        return b

    def proj_chunk(oc, tb, bank):
        wb = (oc * 128) // 512
        for kc in range(8):
            P.op("pe", "matmul", dict(out=PS[:, bank, :], lhsT=W[:, kc, oc * 128:(oc + 1) * 128],
                                                 rhs=hT[:, kc, :], start=(kc == 0), stop=(kc == 7)),
                 rd=[r_w[wb][kc], r_h, r_c], wr=[r_bank[bank]])

    for tb in range(NTB):
        ts_ = slice(tb * 512, (tb + 1) * 512)
        sl = 0
        for kc in range(8):
            s = kc % 2
            P.op("pool", "tensor_tensor", dict(out=sq[:, s, :], in0=xT[:, kc, ts_], in1=xT[:, kc, ts_],
                                                              op=ALU.mult), rd=[r_x[kc][tb]], wr=[r_sq[s]])
            P.op("pe", "matmul", dict(out=PS[:, 4, :], lhsT=onesm[:], rhs=sq[:, s, :],
                                                      start=(kc == 0), stop=(kc == 7)),
                 rd=[r_sq[s], r_c], wr=[r_bank[4]])
        P.op("act", "activation", dict(out=lnv[:], in_=PS[:, 4, :], func=AF.Ln, bias=EPS, scale=1.0),
             rd=[r_bank[4]], wr=[r_ln])
        P.op("act", "activation", dict(out=rstd[:], in_=lnv[:], func=AF.Exp, scale=-0.5),
             rd=[r_ln], wr=[r_rstd])
        for kc in range(8):
            P.op("dve", "scalar_tensor_tensor", dict(out=hT[:, kc, :], in0=xT[:, kc, ts_],
                                                                scalar=g1[:, kc:kc + 1], in1=rstd[:],
                                                                op0=ALU.mult, op1=ALU.mult),
                 rd=[r_x[kc][tb], r_rstd, r_c], wr=[r_h])
        def stage2(oc, bank):
            s = oc % 2
            gi = 0 if oc < 4 else 1
            P.op("pe", "matmul", dict(out=PS[:, 5, :], lhsT=bdm[:], rhs=sq2[:, s, :], start=True, stop=True),
                 rd=[r_sq2[s], r_c], wr=[r_bank[5]])
            P.op("act", "activation", dict(out=ln2[:, s, :], in_=PS[:, 5, :], func=AF.Ln, bias=EPS, scale=1.0),
                 rd=[r_bank[5]], wr=[r_ln2[s]])
            P.op("act", "activation", dict(out=r2[:, s, :], in_=ln2[:, s, :], func=AF.Exp, scale=-0.5),
                 rd=[r_ln2[s]], wr=[r_r2[s]])
            P.op("dve", "scalar_tensor_tensor", dict(
                out=qn[:, s, :], in0=PS[:, bank, :], scalar=gqk[:, gi:gi + 1], in1=r2[:, s, :],
                op0=ALU.mult, op1=ALU.mult), rd=[r_bank[bank], r_r2[s], r_c], wr=[r_qn[s]])

        def stage3(oc):
            s = oc % 2
            P.op("pe", "matmul", dict(out=PS[:, 6, :], lhsT=rotm[:], rhs=qn[:, s, :], start=True, stop=True),
                 rd=[r_qn[s], r_c], wr=[r_bank[6]])
            P.op("pool", "tensor_tensor", dict(out=t1[:, s, :], in0=qn[:, s, :], in1=cosf[:, ts_], op=ALU.mult),
                 rd=[r_qn[s], r_cs], wr=[r_t1[s]])
            P.op("dve", "tensor_tensor", dict(out=t2[:, s, :], in0=PS[:, 6, :], in1=sinf[:, ts_], op=ALU.mult),
                 rd=[r_bank[6], r_cs], wr=[r_t2[s]])
            P.op("pool", "tensor_tensor", dict(out=qkst[:, sl, oc, :], in0=t1[:, s, :], in1=t2[:, s, :],
                                               op=ALU.add), rd=[r_t1[s], r_t2[s]], wr=[r_qk[sl][oc]])

        hist = []
        for oc in range(8):
            s = oc % 2
            bank = next_bank()
            proj_chunk(oc, tb, bank)
            P.op("act", "activation", dict(out=sq2[:, s, :], in_=PS[:, bank, :], func=AF.Square),
                 rd=[r_bank[bank]], wr=[r_sq2[s]])
            hist.append((oc, bank))
            if len(hist) >= 2:
                stage2(*hist[-2])
            if len(hist) >= 3:
                stage3(hist[-3][0])
        stage2(*hist[-1])
        stage3(hist[-2][0])
        stage3(hist[-1][0])
        for tt in range(4):
            tti = tt
            bank = next_bank()
            for kc in range(8):
                P.op("pe", "matmul", dict(
                    out=PS[:, bank, :], lhsT=hT[:, kc, tt * 128:(tt + 1) * 128], rhs=W[:, kc, 1024:1536],
                    start=(kc == 0), stop=(kc == 7)), rd=[r_w[2][kc], r_h], wr=[r_bank[bank]])
            src = PS[:, bank, :].rearrange("p (a b c) -> p a b c", a=4, b=2)
            P.op("act", "activation", dict(out=vst[:, sl, tti, :, 0:64], in_=src[:, :, 0, :], func=AF.Copy),
                 rd=[r_bank[bank]], wr=[r_v[sl]])
            P.op("dve", "tensor_copy", dict(out=vst[:, sl, tti, :, 65:129], in_=src[:, :, 1, :]),
                 rd=[r_bank[bank]], wr=[r_v[sl]])
        for j in range(4):
            s = j % 2
            ba = next_bank()
            proj_chunk(12 + j, tb, ba)
            bb = next_bank()
            proj_chunk(16 + j, tb, bb)
            P.op("act", "activation", dict(out=sg[:, s, :], in_=PS[:, bb, :], func=AF.Sigmoid),
                 rd=[r_bank[bb]], wr=[r_sg[s]])
            P.op("dve", "tensor_tensor", dict(out=gst[:, sl, j, :], in0=PS[:, ba, :], in1=sg[:, s, :],
                                                                  op=ALU.mult),
                 rd=[r_bank[ba], r_sg[s]], wr=[r_g[sl][j]])
        for gr in range(4):
            s = gr % 2
            bank = next_bank()
            proj_chunk(20 + gr, tb, bank)
            P.op("act", "activation", dict(out=fcT[:, s, :], in_=PS[:, bank, :], func=AF.Copy),
                 rd=[r_bank[bank]], wr=[r_fc[s]])
            for tp in range(2):
                for t_ in range(2):
                    tt = tp * 2 + t_
                    P.op("pe", "matmul", dict(
                        out=PS[:, 7, t_ * 256:(t_ + 1) * 256], lhsT=fcT[:, s, tt * 128:(tt + 1) * 128], rhs=dftg[:],
                        start=True, stop=True), rd=[r_fc[s], r_c], wr=[r_bank[7]])
                for t_ in range(2):
                    tti = tp * 2 + t_
                    P.op("dve", "tensor_copy", dict(
                        out=zst[:, sl, tti, gr, :], in_=PS[:, 7, t_ * 256:(t_ + 1) * 256]),
                        rd=[r_bank[7]], wr=[r_z[sl]])

        for oc in range(8):
            dst = qT_o if oc < 4 else kT_o
            o = (oc % 4) * 128
            fin.append(P.op("sp", "dma_start", dict(
                out=dst[o:o + 128, ts_], in_=qkst[:, sl, oc, :]), rd=[r_qk[sl][oc]]))
        for j in range(4):
            fin.append(P.op("sp", "dma_start", dict(
                out=gT_o[j * 128:(j + 1) * 128, ts_], in_=gst[:, sl, j, :]), rd=[r_g[sl][j]]))
        for hp in range(4):
            fin.append(P.op("sp", "dma_start", dict(
                out=vp_o[hp, ts_, :].rearrange("(t p) c -> p t c", p=128), in_=vst[:, sl, :, hp, :]), rd=[r_v[sl]]))
        for gr in range(4):
            fin.append(P.op("sp", "dma_start", dict(
                out=zp_o[gr, ts_, :].rearrange("(t p) c -> p t c", p=128), in_=zst[:, sl, :, gr, :]), rd=[r_z[sl]]))
    return fin


def lam_init_of(l):
    return 0.8 - 0.6 * math.exp(-0.3 * l)


def phase_B1(P, sfx, l, after_loads=None):
    nc = P.nc
    qT_d = P.dram("qT" + sfx, [512, NT], BF16, "ExternalInput")
    kT_d = P.dram("kT_all" + sfx, [4, 512, NT], BF16, "ExternalInput")
    vp_d = P.dram("vp_all" + sfx, [4, 4, NT, 130], BF16, "ExternalInput")
    lam_d = P.dram("lamv" + sfx, [128, 4, 32], F32, "ExternalInput")
    gsub_d = P.dram("gsub" + sfx, [64, 1], F32, "ExternalInput")
    sel_d = P.dram("sel" + sfx, [128, 64], F32, "ExternalInput")
    o64_d = P.dram("ones64" + sfx, [64, 64], BF16, "ExternalInput")
    qm_d = P.dram("qmask" + sfx, [128, 4], F32, "ExternalInput")
    aT_o = P.dram("aT" + sfx, [8, 64, NT], BF16, "ExternalOutput")

    qT = P.sb("qT_sb", [128, 4, NT], BF16)
    qTm = P.sb("qTm_sb", [128, 2, 4, NT], BF16)
    qm = P.sb("qm_sb", [128, 4], F32)
    KT = P.sb("KT_sb", [128, 2, SEQ], BF16)
    VP = P.sb("VP_sb", [128, 2, 64, 130], BF16)
    lamv = P.sb("lamv_sb", [128, 4, 32], F32)
    lprod = P.sb("lprod", [128, 2, 32], F32)
    lsum = P.sb("lsum", [128, 2], F32)
    lexp = P.sb("lexp", [128, 2], F32)
    neglam = P.sb("neglam", [128, 1], F32)
    gsub = P.sb("gsub_sb", [64, 1], F32)
    gsub2 = P.sb("gsub2_sb", [64, 1], F32)
    sel = P.sb("sel_sb", [128, 64], F32)
    o64 = P.sb("o64_sb", [64, 64], BF16)
    E = P.sb("E_sb", [128, 3, 2, 512], BF16)
    Osb = P.sb("Osb", [65, 2, 512], F32)
    Rr = P.sb("Rr", [64, 2, 512], F32)
    tt0 = P.sb("tt0", [64, 2, 512], F32)
    att = P.sb("att", [64, 512], F32)
    sqa = P.sb("sqa", [64, 512], BF16)
    lna = P.sb("lna", [64, 512], F32)
    rsa = P.sb("rsa", [64, 512], F32)
    aT = P.sb("aT_sb", [64, 8, NT], BF16)
    PS = P.ps("ps", [128, 8, 512], F32)

    R = Res
    r_q = [R() for _ in range(4)]
    r_qm = [[R() for _ in range(4)] for _ in range(2)]
    r_kt = [[R() for _ in range(4)] for _ in range(2)]
    r_vp = [[R() for _ in range(4)] for _ in range(2)]
    r_c = R()
    r_lam = R()
    r_bank = [R() for _ in range(8)]
    r_E = [R() for _ in range(3)]
    r_O = [R(), R()]
    r_R = [R(), R()]
    r_t = [R(), R()]
    r_att, r_sq, r_ln, r_rs = R(), R(), R(), R()
    r_aT = [[R() for _ in range(4)] for _ in range(8)]

    lam_init = lam_init_of(l)
    for hp in range(4):
        P.op("sp", "dma_start", dict(out=qT[:, hp, :], in_=qT_d[hp * 128:(hp + 1) * 128, :]), wr=[r_q[hp]])
    P.op("sp", "dma_start", dict(out=lamv[:], in_=lam_d[:, :, :]), wr=[r_lam])
    P.op("sp", "dma_start", dict(out=gsub[:], in_=gsub_d[:, :]), wr=[r_c])
    P.op("sp", "dma_start", dict(out=sel[:], in_=sel_d[:, :]), wr=[r_c])
    P.op("sp", "dma_start", dict(out=o64[:], in_=o64_d[:, :]), wr=[r_c])
    P.op("sp", "dma_start", dict(out=qm[:], in_=qm_d[:, :]), wr=[r_c])

    def load_kv(hp):
        s = hp % 2
        for r in range(4):
            P.op("sp", "dma_start", dict(out=KT[:, s, r * NT:(r + 1) * NT], in_=kT_d[r, hp * 128:(hp + 1) * 128, :]),
                 wr=[r_kt[s][r]])
            P.op("sp", "dma_start", dict(out=VP[:, s, 16 * r:16 * r + 16, :],
                                         in_=vp_d[r, hp].rearrange("(k p) c -> p k c", p=128)), wr=[r_vp[s][r]])

    load_kv(0)
    if after_loads is not None:
        after_loads()
    P.op("dve", "tensor_tensor", dict(out=lprod[:, 0, :], in0=lamv[:, 0, :], in1=lamv[:, 1, :], op=ALU.mult),
         rd=[r_lam], wr=[r_att])
    P.op("dve", "tensor_tensor", dict(out=lprod[:, 1, :], in0=lamv[:, 2, :], in1=lamv[:, 3, :], op=ALU.mult),
         rd=[r_lam], wr=[r_att])
    P.op("dve", "tensor_reduce", dict(out=lsum[:], in_=lprod[:], axis=mybir.AxisListType.X, op=ALU.add),
         rd=[r_att], wr=[r_sq])
    P.op("act", "activation", dict(out=lexp[:], in_=lsum[:], func=AF.Exp), rd=[r_sq], wr=[r_ln])
    P.op("dve", "tensor_tensor", dict(out=neglam[:], in0=lexp[:, 1:2], in1=lexp[:, 0:1], op=ALU.subtract),
         rd=[r_ln], wr=[r_rs])
    P.op("dve", "tensor_scalar", dict(out=neglam[:], in0=neglam[:], scalar1=-lam_init, scalar2=None, op0=ALU.add),
         rd=[r_rs], wr=[r_rs])
    r_neglam = r_rs
    r_neglam_ev_holder = R()
    P.op("dve", "tensor_scalar", dict(out=gsub2[:], in0=gsub[:], scalar1=1.0 - lam_init, scalar2=None, op0=ALU.mult),
         rd=[r_c], wr=[r_neglam_ev_holder])
    r_g2 = r_neglam_ev_holder
    r_att, r_sq, r_ln = R(), R(), R()
    r_rs2 = R()

    scale = 32.0 ** -0.5
    ecnt = [0]

    def qk(hp, s, h2, qs, kt):
        sb0 = (kt % 2) * 2
        for c in range(2):
            i = 2 * h2 + c
            P.op("pe", "matmul", dict(out=PS[:, sb0 + c, :], lhsT=KT[:, s, kt * 128:(kt + 1) * 128],
                                      rhs=qTm[:, s, i, qs], start=True, stop=True),
                 rd=[r_qm[s][i]] + r_kt[s], wr=[r_bank[sb0 + c]])

    def post_copy():
        for c in range(2):
            P.op("dve", "tensor_copy", dict(out=Osb[:, c, :], in_=PS[0:65, 4 + c, :]), rd=[r_bank[4 + c]], wr=[r_O[c]])

    def post_rest(h, qs, qb):
        for c in range(2):
            P.op("pe", "matmul", dict(out=PS[0:64, 6 + c, :], lhsT=sel[0:65, :], rhs=Osb[:, c, :], start=True, stop=True),
                 rd=[r_O[c], r_c], wr=[r_bank[6 + c]])
            P.op("dve", "reciprocal", dict(out=Rr[:, c, :], in_=PS[0:64, 6 + c, :]), rd=[r_bank[6 + c]], wr=[r_R[c]])
            P.op("pool", "tensor_tensor", dict(out=tt0[:, c, :], in0=Osb[0:64, c, :], in1=Rr[:, c, :], op=ALU.mult),
                 rd=[r_O[c], r_R[c]], wr=[r_t[c]])
        P.op("dve", "scalar_tensor_tensor", dict(out=att[:], in0=tt0[:, 1, :], scalar=neglam[0:64, 0:1], in1=tt0[:, 0, :],
                                                 op0=ALU.mult, op1=ALU.add), rd=[r_t[0], r_t[1], r_neglam], wr=[r_att])
        P.op("pool", "tensor_tensor", dict(out=sqa[:], in0=att[:], in1=att[:], op=ALU.mult), rd=[r_att], wr=[r_sq])
        P.op("pe", "matmul", dict(out=PS[0:64, 6, :], lhsT=o64[:], rhs=sqa[:], start=True, stop=True),
             rd=[r_sq, r_c], wr=[r_bank[6]])
        P.op("act", "activation", dict(out=lna[:], in_=PS[0:64, 6, :], func=AF.Ln, bias=EPS, scale=1.0),
             rd=[r_bank[6]], wr=[r_ln])
        P.op("act", "activation", dict(out=rsa[:], in_=lna[:], func=AF.Exp, scale=-0.5), rd=[r_ln], wr=[r_rs2])
        P.op("dve", "scalar_tensor_tensor", dict(out=aT[:, h, qs], in0=att[:], scalar=gsub2[:, 0:1], in1=rsa[:],
                                                 op0=ALU.mult, op1=ALU.mult), rd=[r_att, r_rs2, r_g2], wr=[r_aT[h][qb]])

    def mask_q(hp):
        s_ = hp % 2
        for i in range(4):
            P.op("dve", "tensor_scalar", dict(out=qTm[:, s_, i, :], in0=qT[:, hp, :], scalar1=qm[:, i:i + 1], scalar2=None,
                                               op0=ALU.mult), rd=[r_q[hp], r_c], wr=[r_qm[s_][i]])

    pending = None
    mask_q(0)
    for hp in range(4):
        s = hp % 2
        if hp + 1 < 4:
            load_kv(hp + 1)
            mask_q(hp + 1)
        for h2 in range(2):
            h = hp * 2 + h2
            for qb in range(4):
                qs = slice(qb * 512, (qb + 1) * 512)
                qk(hp, s, h2, qs, 0)
                qk(hp, s, h2, qs, 1)
                for kt in range(64):
                    sb0 = (kt % 2) * 2
                    eb = ecnt[0] % 3
                    ecnt[0] += 1
                    P.op("act", "activation", dict(out=E[:, eb, :, :], in_=PS[:, sb0:sb0 + 2, :], func=AF.Exp, scale=scale),
                         rd=[r_bank[sb0], r_bank[sb0 + 1]], wr=[r_E[eb]])
                    if kt + 2 < 64:
                        qk(hp, s, h2, qs, kt + 2)
                    for c in range(2):
                        P.op("pe", "matmul", dict(out=PS[0:65, 4 + c, :], lhsT=VP[:, s, kt, h2 * 65:(h2 + 1) * 65],
                                                  rhs=E[:, eb, c, :], start=(kt == 0), stop=(kt == 63)),
                             rd=[r_E[eb]] + r_vp[s], wr=[r_bank[4 + c]])
                    if kt == 6 and pending is not None:
                        post_rest(*pending)
                        pending = None
                post_copy()
                pending = (h, qs, qb)
    post_rest(*pending)
    fin = []
    for h in range(8):
        fin.append(P.op("sp", "dma_start", dict(out=aT_o[h], in_=aT[:, h, :]), rd=r_aT[h]))
    return fin


def consts_B1():
    sel = np.zeros((128, 64), np.float32)
    sel[64, :] = 1.0
    qm = np.zeros((128, 4), np.float32)
    for i in range(4):
        qm[32 * i:32 * i + 32, i] = 1.0
    return {"sel": sel, "ones64": bf(np.full((64, 64), 1.0 / 64.0)), "qmask": qm}


def phase_B2(P, sfx, after_loads=None):
    nc = P.nc
    g_d = P.dram("gT" + sfx, [512, NT], BF16, "ExternalInput")
    gall_d = P.dram("g_all" + sfx, [4, 512, NT], BF16, "ExternalInput")
    cw_d = P.dram("conv_w" + sfx, [128, 4, 31], F32, "ExternalInput")
    cv4_d = P.dram("cvec" + sfx, [128, 3, 4], F32, "ExternalInput")
    cm_d = P.dram("cmask" + sfx, [128, 8], F32, "ExternalInput")
    id_d = P.dram("ident" + sfx, [128, 128], BF16, "ExternalInput")
    o512_d = P.dram("ones512" + sfx, [128, 128], F32, "ExternalInput")
    bT_o = P.dram("bT" + sfx, [512, NT], BF16, "ExternalOutput")

    gp = P.sb("gpad", [128, 4, NT + 30], BF16)
    hal = P.sb("hal", [128, 4, 4, 2, 15], BF16)
    cw = P.sb("cw", [128, 4, 31], F32)
    cv4 = P.sb("cv4", [128, 3, 4], F32)
    cm = P.sb("cm", [128, 8], F32)
    ident = P.sb("ident_sb", [128, 128], BF16)
    o512 = P.sb("o512", [128, 128], F32)
    Dg = P.sb("Dg", [128, 4, 31, 128], BF16)
    cv = P.sb("cv", [128, 4, 512], F32)
    sqv = P.sb("sqv", [128, 4, 512], F32)
    m2 = P.sb("m2", [128, 512], F32)
    var = P.sb("var", [128, 512], F32)
    lnv = P.sb("lnv", [128, 512], F32)
    rstd = P.sb("rstd", [128, 512], F32)
    nmr = P.sb("nmr", [128, 512], F32)
    yv = P.sb("yv", [128, 2, 512], F32)
    bst = P.sb("bst", [128, 4, NT], BF16)
    PS = P.ps("ps", [128, 8, 512], F32)

    R = Res
    r_g = [R() for _ in range(4)]
    r_hal, r_c, r_D = R(), R(), [R() for _ in range(4)]
    r_bank = [R() for _ in range(8)]
    r_cv = [R() for _ in range(4)]
    r_sq = [R() for _ in range(4)]
    r_m2, r_var, r_ln, r_rstd, r_nmr = R(), R(), R(), R(), R()
    r_y = [R(), R()]
    r_b = [R() for _ in range(4)]

    for j in range(4):
        P.op("sp", "dma_start", dict(out=gp[:, j, 15:15 + NT], in_=g_d[j * 128:(j + 1) * 128, :]), wr=[r_g[j]])
    for i, (s_, d_) in enumerate([(cw, cw_d), (cv4, cv4_d), (cm, cm_d), (ident, id_d), (o512, o512_d)]):
        P.op("sp", "dma_start", dict(out=s_[:], in_=d_), wr=[r_c])
    rh = [R() for _ in range(8)]
    for r in range(4):
        gv = gall_d[r].rearrange("(j p) t -> p j t", p=128)
        P.op("sp", "dma_start", dict(out=hal[:, :, r, 0, :], in_=gv[:, :, NT - 15:NT]), wr=[rh[2 * r]])
        P.op("sp", "dma_start", dict(out=hal[:, :, r, 1, :], in_=gv[:, :, 0:15]), wr=[rh[2 * r + 1]])
    if after_loads is not None:
        after_loads()
    for side in range(2):
        dst = gp[:, :, 0:15] if side == 0 else gp[:, :, 15 + NT:30 + NT]
        for r in range(4):
            mk = cm[:, side * 4 + r:side * 4 + r + 1]
            if r == 0:
                P.op("dve", "tensor_scalar", dict(out=dst, in0=hal[:, :, r, side, :], scalar1=mk, scalar2=None, op0=ALU.mult),
                     rd=[rh[2 * r + side], r_c], wr=r_g)
            else:
                P.op("dve", "scalar_tensor_tensor", dict(out=dst, in0=hal[:, :, r, side, :], scalar=mk, in1=dst,
                                                         op0=ALU.mult, op1=ALU.add), rd=[rh[2 * r + side], r_c], wr=r_g)
    n = 0
    for j in range(4):
        for tau in range(31):
            n += 1
            if n % 3 != 0:
                P.op("dve", "tensor_scalar", dict(out=Dg[:, j, tau, :], in0=ident[:], scalar1=cw[:, j, tau:tau + 1], scalar2=None,
                                                  op0=ALU.mult), rd=[r_c], wr=[r_D[j]])
            else:
                P.op("act", "activation", dict(out=Dg[:, j, tau, :], in_=ident[:], func=AF.Copy, scale=cw[:, j, tau:tau + 1]),
                     rd=[r_c], wr=[r_D[j]])
    fin = []
    for tb in range(NTB):
        ts_ = slice(tb * 512, (tb + 1) * 512)
        for j in range(4):
            bank = j
            for tau in range(31):
                P.op("pe", "matmul", dict(out=PS[:, bank, :], lhsT=Dg[:, j, tau, :],
                                          rhs=gp[:, j, tb * 512 + tau:tb * 512 + tau + 512], start=(tau == 0), stop=(tau == 30)),
                     rd=[r_D[j], r_g[j]], wr=[r_bank[bank]])
            P.op("act", "activation", dict(out=cv[:, j, :], in_=PS[:, bank, :], func=AF.Identity, bias=cv4[:, 0, j:j + 1], scale=1.0),
                 rd=[r_bank[bank], r_c], wr=[r_cv[j]])
            P.op("pool", "tensor_tensor", dict(out=sqv[:, j, :], in0=cv[:, j, :], in1=cv[:, j, :], op=ALU.mult),
                 rd=[r_cv[j]], wr=[r_sq[j]])
        for j in range(4):
            P.op("pe", "matmul", dict(out=PS[:, 4, :], lhsT=o512[:], rhs=cv[:, j, :], start=(j == 0), stop=(j == 3)),
                 rd=[r_cv[j], r_c], wr=[r_bank[4]])
        for j in range(4):
            P.op("pe", "matmul", dict(out=PS[:, 5, :], lhsT=o512[:], rhs=sqv[:, j, :], start=(j == 0), stop=(j == 3)),
                 rd=[r_sq[j], r_c], wr=[r_bank[5]])
        P.op("act", "activation", dict(out=m2[:], in_=PS[:, 4, :], func=AF.Square), rd=[r_bank[4]], wr=[r_m2])
        P.op("dve", "tensor_tensor", dict(out=var[:], in0=PS[:, 5, :], in1=m2[:], op=ALU.subtract), rd=[r_bank[5], r_m2], wr=[r_var])
        P.op("act", "activation", dict(out=lnv[:], in_=var[:], func=AF.Ln, bias=EPS, scale=1.0), rd=[r_var], wr=[r_ln])
        P.op("act", "activation", dict(out=rstd[:], in_=lnv[:], func=AF.Exp, scale=-0.5), rd=[r_ln], wr=[r_rstd])
        P.op("dve", "scalar_tensor_tensor", dict(out=nmr[:], in0=PS[:, 4, :], scalar=-1.0, in1=rstd[:], op0=ALU.mult, op1=ALU.mult),
             rd=[r_bank[4], r_rstd], wr=[r_nmr])
        for j in range(4):
            s = j % 2
            P.op("pool", "tensor_tensor", dict(out=yv[:, s, :], in0=cv[:, j, :], in1=rstd[:], op=ALU.mult),
                 rd=[r_cv[j], r_rstd], wr=[r_y[s]])
            P.op("dve", "tensor_tensor", dict(out=yv[:, s, :], in0=yv[:, s, :], in1=nmr[:], op=ALU.add),
                 rd=[r_nmr], wr=[r_y[s]])
            P.op("act", "activation", dict(out=bst[:, j, ts_], in_=yv[:, s, :], func=AF.Silu, bias=cv4[:, 2, j:j + 1],
                                           scale=cv4[:, 1, j:j + 1]), rd=[r_y[s], r_c], wr=[r_b[j]])
    for j in range(4):
        fin.append(P.op("sp", "dma_start", dict(out=bT_o[j * 128:(j + 1) * 128, :], in_=bst[:, j, :]), rd=[r_b[j]]))
    return fin


def b3_prefetch(P, sfx):
    z_d = P.dram("z_all" + sfx, [4, 4, NT, 256], BF16, "ExternalInput")
    Zt = P.sb("Zt", [128, 1, 128, 256], BF16)

    def issue():
        for g2 in range(2):
            for r in range(4):
                P.op("sp", "dma_start", dict(out=Zt[64 * g2 + 16 * r:64 * g2 + 16 * r + 16, 0, :, :],
                                             in_=z_d[r, g2].rearrange("(p s) c -> p s c", s=128)), wr=[Res()])
    return Zt, issue


def phase_B3(P, sfx, pre=None):
    nc = P.nc
    z_d = P.dram("z_all" + sfx, [4, 4, NT, 256], BF16, "ExternalInput")
    w64_d = P.dram("w64" + sfx, [128, 2, 128], BF16, "ExternalInput")
    tt_d = P.dram("ttab" + sfx, [128, 64, 2, 32], BF16, "ExternalInput")
    fT_o = P.dram("fT" + sfx, [512, NT], BF16, "ExternalOutput")

    Zt = P.sb("Zt", [128, 1, 128, 256], BF16)
    w64 = P.sb("w64_sb", [128, 2, 128], BF16)
    ttab = P.sb("ttab_sb", [128, 64, 2, 32], BF16)
    A = P.sb("A_sb", [128, 2, 128, 2, 64], BF16)
    fst = P.sb("fst", [128, 4, NT], BF16)
    PS = P.ps("ps", [128, 8, 512], F32)

    R = Res
    r_z = [[R() for _ in range(4)] for _ in range(2)]
    r_c = R()
    r_A = [[R() for _ in range(32)] for _ in range(2)]
    r_bank = [R() for _ in range(8)]
    r_f = [R() for _ in range(4)]

    P.op("sp", "dma_start", dict(out=w64[:], in_=w64_d), wr=[r_c])
    P.op("sp", "dma_start", dict(out=ttab[:], in_=tt_d), wr=[r_c])

    def load_group(gp_, g2):
        gr = gp_ * 2 + g2
        for r in range(4):
            P.op("sp", "dma_start", dict(out=Zt[64 * g2 + 16 * r:64 * g2 + 16 * r + 16, 0, :, :],
                                         in_=z_d[r, gr].rearrange("(p s) c -> p s c", s=128)), wr=[r_z[g2][r]])

    if pre is None:
        load_group(0, 0)
        load_group(0, 1)
    ev_n = 0
    bank_n = 0
    fin = []
    for gp_ in range(2):
        sl = 0
        for g2 in range(2):
            gr = gp_ * 2 + g2
            asl = gr % 2
            rows = slice(64 * g2, 64 * g2 + 64)
            for j0 in range(0, 128, 4):
                bank = bank_n % 4
                bank_n += 1
                for jj in range(4):
                    j = j0 + jj
                    for c in range(2):
                        P.op("pe", "matmul", dict(out=PS[:, bank, jj * 128:(jj + 1) * 128], lhsT=Zt[rows, sl, :, c * 128 + j],
                                                  rhs=w64[rows, c, :], start=(c == 0), stop=(c == 1)),
                             rd=r_z[g2] + [r_c], wr=[r_bank[bank]])
                q = "act" if ev_n % 2 == 0 else "dve"
                ev_n += 1
                if q == "act":
                    P.op("act", "activation", dict(out=A[:, asl, j0:j0 + 4, :, :], in_=PS[:, bank, :], func=AF.Copy),
                         rd=[r_bank[bank]], wr=[r_A[asl][j0 // 4]])
                else:
                    P.op("dve", "tensor_copy", dict(out=A[:, asl, j0:j0 + 4, :, :], in_=PS[:, bank, :]),
                         rd=[r_bank[bank]], wr=[r_A[asl][j0 // 4]])
            if gp_ == 0:
                load_group(1, g2)
            for kb in range(4):
                bank = 4 + (kb % 2)
                for kk in range(16):
                    k2 = kb * 16 + kk
                    for c in range(2):
                        P.op("pe", "matmul", dict(out=PS[:, bank, kk * 32:(kk + 1) * 32], lhsT=A[:, asl, :, c, k2],
                                                  rhs=ttab[:, k2, c, :], start=(c == 0), stop=(c == 1)),
                             rd=r_A[asl] + [r_c], wr=[r_bank[bank]])
                dst = fst[:, gr, :].rearrange("p (a b) -> p a b", b=64)[:, :, kb * 16:(kb + 1) * 16]
                src = PS[:, bank, :].rearrange("p (b a) -> p a b", a=32)
                P.op("dve", "tensor_copy", dict(out=dst, in_=src), rd=[r_bank[bank]], wr=[r_f[gr]])
    for gr in range(4):
        fin.append(P.op("sp", "dma_start", dict(out=fT_o[gr * 128:(gr + 1) * 128, :], in_=fst[:, gr, :]), rd=[r_f[gr]]))
    return fin


def consts_B3(core):
    a = core % 4
    s2 = np.arange(64, dtype=np.float64)
    k2 = np.arange(64, dtype=np.float64)
    th = 2 * np.pi * np.outer(s2, k2) / 64.0
    C, S = np.cos(th), np.sin(th)
    w = np.stack([np.concatenate([C, -S], 1), np.concatenate([S, C], 1)], 1)
    w64 = bf(np.concatenate([w, w], 0))
    s1 = np.arange(128, dtype=np.float64)[:, None, None]
    k2_ = np.arange(64, dtype=np.float64)[None, :, None]
    k1 = (32 * a + np.arange(32, dtype=np.float64))[None, None, :]
    ph = 2 * np.pi * s1 * (64 * k1 + k2_) / 8192.0
    sc = 2.0 ** -10
    tt = np.stack([np.cos(ph) * sc, np.sin(ph) * sc], 2)
    return {"w64": w64, "ttab": bf(tt)}


DEBUG_C = False


def phase_C(P, sfx, bfw=False):
    nc = P.nc
    xT_d = P.dram("xT" + sfx, [D, NT], F32, "ExternalInput")
    aT_d = P.dram("aT" + sfx, [8, 64, NT], BF16, "ExternalInput")
    bT_d = P.dram("bT" + sfx, [512, NT], BF16, "ExternalInput")
    fT_d = P.dram("fT" + sfx, [512, NT], BF16, "ExternalInput")
    wpa_d = wsrc(P, "w_proj_a", sfx, bfw)
    wpb_d = wsrc(P, "w_proj_b", sfx, bfw)
    wpc_d = wsrc(P, "w_proj_c", sfx, bfw)
    wg_d = wsrc(P, "w_gate", sfx, bfw)
    bg_d = P.dram("b_gate" + sfx, [128, 24], F32, "ExternalInput")
    wo_d = wsrc(P, "w_out", sfx, bfw)
    g12_d = P.dram("g12" + sfx, [128, 2, 8], F32, "ExternalInput")
    ones_d = P.dram("ones_d" + sfx, [128, 128], BF16, "ExternalInput")
    w1_d = wsrc(P, "w_ffn_in", sfx, bfw)
    w2_d = wsrc(P, "w_ffn_out", sfx, bfw)
    xo_d = P.dram("xT_out" + sfx, [D, NT], F32, "ExternalOutput")

    xT = P.sb("xT_sb", [128, 8, NT], F32)
    ARENA_BYTES = 137216
    arena = P.sb("arena_c", [128, ARENA_BYTES // 2], BF16)

    def carve(off, shape, dt):
        nb = (4 if dt == F32 else 2)
        n = 1
        for d_ in shape[1:]:
            n *= d_
        v = arena[:, off // 2:off // 2 + n * nb // 2]
        if dt == F32:
            v = v.bitcast(F32)
        if len(shape) == 2:
            return v
        if len(shape) == 3:
            return v.rearrange("p (a b) -> p a b", b=shape[2])
        raise ValueError

    WA = carve(0, [128, 8, 3072], BF16)
    WB = carve(49152, [128, 20, 1024], BF16)
    hT = carve(90112, [128, 8, 512], BF16)
    brA = carve(98304, [128, 4, 512], BF16)
    brB = carve(102400, [128, 4, 512], BF16)
    brC = carve(106496, [128, 4, 512], BF16)
    sig = carve(110592, [128, 3, 512], F32)
    tm = carve(116736, [128, 3, 512], F32)
    mT = carve(122880, [128, 8, 512], BF16)
    sq = carve(131072, [128, 2, 512], BF16)
    lnv = carve(133120, [128, 512], F32)
    rstd = carve(135168, [128, 512], F32)
    bg = P.sb("bg", [128, 24], F32)
    g12 = P.sb("g12_sb", [128, 2, 8], F32)
    onesm = P.sb("ones_sb", [128, 128], BF16)
    PS = P.ps("ps", [128, 8, 512], F32)

    R = Res
    r_x = [[R() for _ in range(NTB)] for _ in range(8)]
    r_wa = [[R() for _ in range(3)] for _ in range(8)]
    r_wb = [R() for _ in range(20)]
    r_wpa = R()
    r_c = R()
    r_h = R()
    r_sq = [R(), R()]
    r_ln, r_rstd = R(), R()
    r_br = [R(), R(), R()]
    r_sig = [R(), R(), R()]
    r_tm = [R(), R(), R()]
    r_m = [R() for _ in range(8)]
    r_bank = [R() for _ in range(8)]
    r_act = [R() for _ in range(11)]
    r_sgf = [R(), R()]

    for s_, d_ in [(bg, bg_d), (g12, g12_d), (onesm, ones_d)]:
        P.op("sp", "dma_start", dict(out=s_[:], in_=d_), wr=[r_c])

    def load_x(tb):
        for kc in range(8):
            P.op("sp", "dma_start", dict(out=xT[:, kc, tb * 512:(tb + 1) * 512],
                                         in_=xT_d[kc * 128:(kc + 1) * 128, tb * 512:(tb + 1) * 512]), wr=[r_x[kc][tb]])

    load_x(0)
    for j in range(4):
        P.op("poolq", "dma_start", dict(out=WB[:, 16 + j, :], in_=wpa_d[j * 128:(j + 1) * 128, :]), wr=[r_wb[16 + j]])
    for j in range(4):
        P.op("poolq", "dma_start", dict(out=WB[:, 8 + j, :], in_=wpb_d[j * 128:(j + 1) * 128, :]), wr=[r_wb[8 + j]])
    for j in range(4):
        P.op("poolq", "dma_start", dict(out=WB[:, 12 + j, :], in_=wpc_d[j * 128:(j + 1) * 128, :]), wr=[r_wb[12 + j]])
    for cb in range(3):
        for kc in range(8):
            P.op("poolq", "dma_start", dict(out=WA[:, kc, cb * 1024:(cb + 1) * 1024],
                                            in_=wg_d[kc * 128:(kc + 1) * 128, cb * 1024:(cb + 1) * 1024]), wr=[r_wa[kc][cb]])
    for kc in range(8):
        P.op("poolq", "dma_start", dict(out=WB[:, kc, :], in_=wo_d[kc * 128:(kc + 1) * 128, :]), wr=[r_wb[kc]])

    def rmsnorm_block(tb, gi, hdst):
        ts_ = slice(tb * 512, (tb + 1) * 512)
        for kc in range(8):
            s = kc % 2
            P.op("pool", "tensor_tensor", dict(out=sq[:, s, :], in0=xT[:, kc, ts_], in1=xT[:, kc, ts_], op=ALU.mult),
                 rd=[r_x[kc][tb]], wr=[r_sq[s]])
            P.op("pe", "matmul", dict(out=PS[:, 7, :], lhsT=onesm[:], rhs=sq[:, s, :], start=(kc == 0), stop=(kc == 7)),
                 rd=[r_sq[s], r_c], wr=[r_bank[7]])
        P.op("act", "activation", dict(out=lnv[:], in_=PS[:, 7, :], func=AF.Ln, bias=EPS, scale=1.0), rd=[r_bank[7]], wr=[r_ln])
        P.op("act", "activation", dict(out=rstd[:], in_=lnv[:], func=AF.Exp, scale=-0.5), rd=[r_ln], wr=[r_rstd])
        for kc in range(8):
            P.op("dve", "scalar_tensor_tensor", dict(out=hdst[:, kc, :], in0=xT[:, kc, ts_], scalar=g12[:, gi, kc:kc + 1],
                                                     in1=rstd[:], op0=ALU.mult, op1=ALU.mult),
                 rd=[r_x[kc][tb], r_rstd, r_c], wr=[r_h])

    for tb in range(NTB):
        ts_ = slice(tb * 512, (tb + 1) * 512)
        P.op("sp", "dma_start", dict(out=brA[:], in_=aT_d[:, :, ts_].rearrange("(j h) p t -> (h p) j t", h=2)), wr=[r_br[0]])
        P.op("sp", "dma_start", dict(out=brB[:], in_=bT_d[:, ts_].rearrange("(j p) t -> p j t", p=128)), wr=[r_br[1]])
        P.op("sp", "dma_start", dict(out=brC[:], in_=fT_d[:, ts_].rearrange("(j p) t -> p j t", p=128)), wr=[r_br[2]])
        if tb + 1 < NTB:
            load_x(tb + 1)
        rmsnorm_block(tb, 0, hT[:, :, 0:512])
        for oc in range(8):
            ocs = slice(oc * 128, (oc + 1) * 128)
            for j in range(4):
                P.op("pe", "matmul", dict(out=PS[:, 0, :], lhsT=WB[:, 16 + j, ocs], rhs=brA[:, j, :], start=(j == 0), stop=(j == 3)),
                     rd=[r_wb[16 + j], r_br[0]], wr=[r_bank[0]])
            for j in range(4):
                P.op("pe", "matmul", dict(out=PS[:, 1, :], lhsT=WB[:, 8 + j, ocs], rhs=brB[:, j, :], start=(j == 0), stop=(j == 3)),
                     rd=[r_wb[8 + j], r_br[1]], wr=[r_bank[1]])
            for j in range(4):
                P.op("pe", "matmul", dict(out=PS[:, 2, :], lhsT=WB[:, 12 + j, ocs], rhs=brC[:, j, :], start=(j == 0), stop=(j == 3)),
                     rd=[r_wb[12 + j], r_br[2]], wr=[r_bank[2]])
            for i in range(3):
                for kc in range(8):
                    P.op("pe", "matmul", dict(out=PS[:, 3 + i, :], lhsT=WA[:, kc, i * 1024 + oc * 128:i * 1024 + (oc + 1) * 128],
                                              rhs=hT[:, kc, 0:512], start=(kc == 0), stop=(kc == 7)),
                         rd=[r_wa[kc][i], r_h], wr=[r_bank[3 + i]])
                P.op("act", "activation", dict(out=sig[:, i, :], in_=PS[:, 3 + i, :], func=AF.Sigmoid,
                                               bias=bg[:, i * 8 + oc:i * 8 + oc + 1], scale=1.0),
                     rd=[r_bank[3 + i], r_c], wr=[r_sig[i]])
                P.op("dve", "tensor_tensor", dict(out=tm[:, i, :], in0=PS[:, i, :], in1=sig[:, i, :], op=ALU.mult),
                     rd=[r_bank[i], r_sig[i]], wr=[r_tm[i]])
            P.op("pool", "tensor_tensor", dict(out=tm[:, 0, :], in0=tm[:, 0, :], in1=tm[:, 1, :], op=ALU.add),
                 rd=[r_tm[1]], wr=[r_tm[0]])
            P.op("pool", "tensor_tensor", dict(out=mT[:, oc, :], in0=tm[:, 0, :], in1=tm[:, 2, :], op=ALU.add),
                 rd=[r_tm[0], r_tm[2]], wr=[r_m[oc]])
        for oc2 in range(8):
            bank = 6
            for kc in range(8):
                P.op("pe", "matmul", dict(out=PS[:, bank, :], lhsT=WB[:, kc, oc2 * 128:(oc2 + 1) * 128], rhs=mT[:, kc, :],
                                          start=(kc == 0), stop=(kc == 7)), rd=[r_wb[kc], r_m[kc]], wr=[r_bank[bank]])
            P.op("dve", "tensor_tensor", dict(out=xT[:, oc2, ts_], in0=PS[:, bank, :], in1=xT[:, oc2, ts_], op=ALU.add),
                 rd=[r_bank[bank]], wr=[r_x[oc2][tb]])

    fin = []
    if DEBUG_C:
        x1_d = P.dram("x1_out" + sfx, [D, NT], F32, "ExternalOutput")
        for kc in range(8):
            fin.append(P.op("sp", "dma_start", dict(out=x1_d[kc * 128:(kc + 1) * 128, :], in_=xT[:, kc, :]), rd=r_x[kc]))
    groups = [(0, 6), (6, 6), (12, 5), (17, 5)]
    W1s = [carve(0, [128, 8, 1536], BF16), carve(36864, [128, 8, 1536], BF16)]
    W2s = [carve(24576, [128, 6, 1024], BF16), carve(61440, [128, 6, 1024], BF16)]
    h2T = carve(73728, [128, 8, NT], BF16)
    actT = carve(106496, [128, 6, 512], BF16)
    sgf = carve(112640, [128, 2, 512], F32)
    r_s1 = [[R() for _ in range(8)] for _ in range(2)]
    r_s1u = [[R() for _ in range(8)] for _ in range(2)]
    r_s2 = [[R() for _ in range(6)] for _ in range(2)]
    first_extra = [[x_ for kc in range(8) for x_ in r_wa[kc]],
                   r_wa[6] + r_wa[7] + r_wb[0:12]]
    loaded = [False, False]

    def load_group(g):
        sl_ = g % 2
        j0, n = groups[g]
        extra = [] if loaded[sl_] else first_extra[sl_]
        loaded[sl_] = True
        for kc in range(8):
            P.op("poolq", "dma_start", dict(out=W1s[sl_][:, kc, 0:n * 128],
                                            in_=w1_d[kc * 128:(kc + 1) * 128, j0 * 128:(j0 + n) * 128]), wr=[r_s1[sl_][kc]] + extra)
            P.op("poolq", "dma_start", dict(out=W1s[sl_][:, kc, 768:768 + n * 128],
                                            in_=w1_d[kc * 128:(kc + 1) * 128, DFF + j0 * 128:DFF + (j0 + n) * 128]),
                 wr=[r_s1u[sl_][kc]] + extra)
        for j in range(n):
            P.op("poolq", "dma_start", dict(out=W2s[sl_][:, j, :], in_=w2_d[(j0 + j) * 128:(j0 + j + 1) * 128, :]),
                 wr=[r_s2[sl_][j]] + extra)

    load_group(0)
    P.op("dve", "memset", dict(ap=lnv[:, 0:8], constant=0.0), rd=[], wr=r_sig + r_tm + r_br + r_act + r_sgf + r_m + [r_ln, r_h] + r_wb[12:20])
    for tb in range(NTB):
        rmsnorm_block(tb, 1, h2T[:, :, tb * 512:(tb + 1) * 512])
    load_group(1)
    nb = 0
    for g in range(4):
        sl_ = g % 2
        j0, n = groups[g]
        if g >= 1 and g + 1 < 4:
            load_group(g + 1)
        for tb in range(NTB):
            ts_ = slice(tb * 512, (tb + 1) * 512)
            for j in range(n):
                s = nb % 2
                nb += 1
                bg_, bu_ = (0, 1) if s == 0 else (2, 3)
                for kc in range(8):
                    P.op("pe", "matmul", dict(out=PS[:, bg_, :], lhsT=W1s[sl_][:, kc, j * 128:(j + 1) * 128], rhs=h2T[:, kc, ts_],
                                              start=(kc == 0), stop=(kc == 7)), rd=[r_s1[sl_][kc], r_h], wr=[r_bank[bg_]])
                for kc in range(8):
                    P.op("pe", "matmul", dict(out=PS[:, bu_, :], lhsT=W1s[sl_][:, kc, 768 + j * 128:768 + (j + 1) * 128], rhs=h2T[:, kc, ts_],
                                              start=(kc == 0), stop=(kc == 7)), rd=[r_s1u[sl_][kc], r_h], wr=[r_bank[bu_]])
                P.op("act", "activation", dict(out=sgf[:, s, :], in_=PS[:, bg_, :], func=AF.Silu), rd=[r_bank[bg_]], wr=[r_sgf[s]])
                P.op("dve", "tensor_tensor", dict(out=actT[:, j, :], in0=PS[:, bu_, :], in1=sgf[:, s, :], op=ALU.mult),
                     rd=[r_bank[bu_], r_sgf[s]], wr=[r_act[j]])
            for oc in range(8):
                bank = 4 + (oc % 2)
                for j in range(n):
                    P.op("pe", "matmul", dict(out=PS[:, bank, :], lhsT=W2s[sl_][:, j, oc * 128:(oc + 1) * 128], rhs=actT[:, j, :],
                                              start=(j == 0), stop=(j == n - 1)), rd=[r_s2[sl_][j], r_act[j]], wr=[r_bank[bank]])
                P.op("dve", "tensor_tensor", dict(out=xT[:, oc, ts_], in0=PS[:, bank, :], in1=xT[:, oc, ts_], op=ALU.add),
                     rd=[r_bank[bank]], wr=[r_x[oc][tb]])
    for kc in range(8):
        fin.append(P.op("sp", "dma_start", dict(out=xo_d[kc * 128:(kc + 1) * 128, :], in_=xT[:, kc, :]), rd=r_x[kc]))
    return fin, xT, r_x


def build_launch(kind):
    nc = bass.Bass("TRN2", target_bir_lowering=False)
    if kind == "L1":
        P = Prog(nc)
        fin = phase_A(P, "_0")
    else:
        l = 0 if kind == "L2" else 1
        sfx = "_%d" % l
        P = Prog(nc, internal={"aT" + sfx, "bT" + sfx, "fT" + sfx})
        fin = []
        P.phase_begin()
        wl = [(n_, sfx) for n_ in ("w_proj_a", "w_proj_b", "w_proj_c", "w_gate", "w_out", "w_ffn_in", "w_ffn_out")]
        if kind == "L2":
            wl.append(("w_in", "_1"))
        phase_B1(P, sfx, l)
        P.barrier()
        P.phase_begin()
        pre, issue = b3_prefetch(P, sfx)
        phase_B2(P, sfx, after_loads=issue)
        P.barrier()
        P.phase_begin()
        phase_B3(P, sfx, pre)
        P.barrier()
        P.phase_begin()
        finC, xT, r_x = phase_C(P, sfx)
        fin += finC
        if kind == "L2":
            P.barrier()
            P.phase_begin()
            fin += phase_A(P, "_1", x_res=(xT, r_x))
    P.final_events = fin
    with P.stack:
        with nc.Block() as block:
            P.finalize(block)
    return nc


def inputs_A(l, inp, c, sfx, xT=None):
    ct = const_tables(c)
    g1 = np.ascontiguousarray(inp["norm1_g"][l].reshape(8, 128).T)
    gqk = np.stack([np.tile(inp["qnorm_g"][l], 4), np.tile(inp["knorm_g"][l], 4)], axis=1).astype(np.float32)
    d = {"w_in": np.ascontiguousarray(inp["w_in"][l]), "g1": g1, "gqk": np.ascontiguousarray(gqk),
         "cosf": ct["cosf"], "sinf": ct["sinf"], "ones_d": ct["ones_d"], "bd32": ct["bd32"], "rot": ct["rot"],
         "dftg": ct["dftg"]}
    if xT is not None:
        d["xT"] = xT
    return {k + sfx: v for k, v in d.items()}


def inputs_B(l, inp, c, sfx, ex):
    b, a = c // 4, c % 4
    cb = consts_B1()
    lamv = np.stack([inp["lambda_q1"][l], inp["lambda_k1"][l], inp["lambda_q2"][l], inp["lambda_k2"][l]], 0)
    lamv = np.ascontiguousarray(np.broadcast_to(lamv[None], (128, 4, 32))).astype(np.float32)
    gsub = np.ascontiguousarray(inp["subln_g"][l].reshape(64, 1)).astype(np.float32)
    grp = [b * 4 + r for r in range(4)]
    d = {"qT": ex["qT"][c], "kT_all": np.ascontiguousarray(np.stack([ex["kT"][i] for i in grp], 0)),
         "vp_all": np.ascontiguousarray(np.stack([ex["vp"][i] for i in grp], 0)), "lamv": lamv, "gsub": gsub,
         "sel": cb["sel"], "ones64": cb["ones64"], "qmask": cb["qmask"]}
    cw = np.ascontiguousarray(inp["conv_w"][l].T.reshape(4, 128, 31).transpose(1, 0, 2)).astype(np.float32)
    cvec = np.stack([inp["conv_b"][l].reshape(4, 128).T, inp["conv_ln_g"][l].reshape(4, 128).T,
                     inp["conv_ln_b"][l].reshape(4, 128).T], axis=1).astype(np.float32)
    cm = np.zeros((128, 8), np.float32)
    if a > 0:
        cm[:, a - 1] = 1.0
    if a < 3:
        cm[:, 4 + a + 1] = 1.0
    d.update({"gT": ex["gT"][c], "g_all": np.ascontiguousarray(np.stack([ex["gT"][i] for i in grp], 0)), "conv_w": cw,
              "cvec": np.ascontiguousarray(cvec), "cmask": cm, "ident": bf(np.eye(128)),
              "ones512": np.full((128, 128), 1.0 / 512.0, np.float32)})
    c3 = consts_B3(c)
    d.update({"z_all": np.ascontiguousarray(np.stack([ex["zp"][i] for i in grp], 0)), "w64": c3["w64"], "ttab": c3["ttab"]})
    return {k + sfx: v for k, v in d.items()}


def inputs_C(l, inp, c, sfx, xT):
    ct = const_tables(0)
    g12 = np.stack([inp["norm1_g"][l].reshape(8, 128).T, inp["norm2_g"][l].reshape(8, 128).T], axis=1).astype(np.float32)
    bgate = np.ascontiguousarray(inp["b_gate"][l].reshape(24, 128).T).astype(np.float32)
    d = {"xT": xT, "w_proj_a": np.ascontiguousarray(inp["w_proj_a"][l]), "w_proj_b": np.ascontiguousarray(inp["w_proj_b"][l]),
         "w_proj_c": np.ascontiguousarray(inp["w_proj_c"][l]), "w_gate": np.ascontiguousarray(inp["w_gate"][l]),
         "b_gate": bgate, "w_out": np.ascontiguousarray(inp["w_out"][l]), "g12": np.ascontiguousarray(g12),
         "ones_d": ct["ones_d"], "w_ffn_in": np.ascontiguousarray(inp["w_ffn_in"][l]),
         "w_ffn_out": np.ascontiguousarray(inp["w_ffn_out"][l])}
    return {k + sfx: v for k, v in d.items()}


def _collect(res, sfx):
    return {k: [np.asarray(r[k + sfx]) for r in res] for k in ("qT", "kT", "vp", "gT", "zp")}


def kernel(**inp):
    inp = {k: np.asarray(v) for k, v in inp.items()}
    x = inp["x"]
    xT = [np.ascontiguousarray(x[c // 4, (c % 4) * NT:(c % 4 + 1) * NT, :].T) for c in range(NCORES)]
    cores = list(range(NCORES))
    r1 = run_bass_kernel_spmd(build_launch("L1"), [inputs_A(0, inp, c, "_0", xT[c]) for c in cores], core_ids=cores).results
    ex0 = _collect(r1, "_0")
    in2 = []
    for c in cores:
        d = inputs_B(0, inp, c, "_0", ex0)
        d.update(inputs_C(0, inp, c, "_0", xT[c]))
        d.update(inputs_A(1, inp, c, "_1"))
        in2.append(d)
    r2 = run_bass_kernel_spmd(build_launch("L2"), in2, core_ids=cores).results
    ex1 = _collect(r2, "_1")
    x1 = [np.ascontiguousarray(np.asarray(r["xT_out_0"])) for r in r2]
    in3 = []
    for c in cores:
        d = inputs_B(1, inp, c, "_1", ex1)
        d.update(inputs_C(1, inp, c, "_1", x1[c]))
        in3.append(d)
    r3 = run_bass_kernel_spmd(build_launch("L3"), in3, core_ids=cores).results
    out = np.empty((2, SEQ, D), np.float32)
    for c in cores:
        out[c // 4, (c % 4) * NT:(c % 4 + 1) * NT, :] = np.asarray(r3[c]["xT_out_1"]).T
    return out
```

```python
import math
from contextlib import ExitStack
import numpy as np
import ml_dtypes
import concourse.bass as bass
import concourse.mybir as mybir
from concourse.bass_utils import run_bass_kernel_spmd

F32 = mybir.dt.float32
BF16 = mybir.dt.bfloat16
AF = mybir.ActivationFunctionType
ALU = mybir.AluOpType

NCORES = 8
D = 1024
SEQ = 8192
NT = 2048
NTB = 4
EPS = 1e-6
DFF = 2816
ROPE_THETA = 500000.0


class Res:
    __slots__ = ("w", "r", "name")

    def __init__(self, name=""):
        self.w = None
        self.r = []
        self.name = name


class Ev:
    __slots__ = ("q", "idx", "needed", "sem", "val")

    def __init__(self, q, idx):
        self.q = q
        self.idx = idx
        self.needed = False
        self.sem = None
        self.val = None


COMPUTE_Q = ("pe", "act", "dve", "pool")
DMA_Q = ("sp", "actq", "poolq")
ENGINE_OF = {"pe": "tensor", "act": "scalar", "dve": "vector", "pool": "gpsimd",
             "sp": "sync", "actq": "scalar", "poolq": "gpsimd"}
NDMASEM = 6


class Prog:
    ARENA_BYTES = 212736

    def __init__(self, nc, internal=()):
        self.nc = nc
        self.stack = ExitStack()
        self.arena = None
        self.off = 0
        self.psum = None
        self.drams = {}
        self.internal = set(internal)
        self.barrier_ev = None
        self.dma_pending = []
        self.last_ev = {}
        self.bscr = None
        self.streams = {"tensor": [], "scalar": [], "vector": [], "gpsimd": [], "sync": []}
        self.evcount = {q: 0 for q in COMPUTE_Q + DMA_Q}
        self.dma_n = {q: 0 for q in DMA_Q}
        self.dma_last = {}
        self.sems = {}
        self.final_events = []

    def sb(self, name, shape, dt):
        if self.arena is None:
            self.arena = self.stack.enter_context(self.nc.sbuf_tensor("arena", [128, self.ARENA_BYTES // 2], BF16))
            self.bscr = self.stack.enter_context(self.nc.sbuf_tensor("bscr", [128, 8], F32))
        nb = 4 if dt == F32 else 2
        n = 1
        for d_ in shape[1:]:
            n *= d_
        nbytes = (n * nb + 63) // 64 * 64
        off = self.off
        assert off + nbytes <= self.ARENA_BYTES, (name, off, nbytes)
        self.off = off + nbytes
        v = self.arena[:, off // 2:off // 2 + n * nb // 2]
        if dt == F32:
            v = v.bitcast(F32)
        if shape[0] < 128:
            v = v[0:shape[0]]
        dims = list(shape[1:])
        if len(dims) == 1:
            return v
        names = "abcd"[:len(dims)]
        pat = "p (%s) -> p %s" % (" ".join(names), " ".join(names))
        return v.rearrange(pat, **{names[i]: dims[i] for i in range(1, len(dims))})

    def phase_begin(self, keep=0):
        self.off = keep

    def ps(self, name, shape, dt=F32):
        if self.psum is None:
            self.psum = self.stack.enter_context(self.nc.psum_tensor(name, list(shape), dt))
        return self.psum

    def dram(self, name, shape, dt, kind):
        if name in self.drams:
            return self.drams[name]
        if name in self.internal:
            kind = "Internal"
        ap = self.nc.dram_tensor(name, list(shape), dt, kind=kind).ap()
        self.drams[name] = ap
        return ap

    def barrier(self):
        deps = list(self.last_ev.values()) + list(self.dma_pending)
        r = Res()
        ev = self.op("pool", "memset", dict(ap=self.bscr[:, 0:2], constant=0.0), wr=[r], extra=deps)
        self.barrier_ev = ev
        self.dma_pending = []

    def op(self, q, name, kw, rd=(), wr=(), extra=()):
        fn = (name, kw)
        deps = list(extra)
        if self.barrier_ev is not None:
            deps.append(self.barrier_ev)
        for r in rd:
            if r.w is not None:
                deps.append(r.w)
        for w in wr:
            if w.w is not None:
                deps.append(w.w)
            deps.extend(w.r)
        self.evcount[q] += 1
        ev = Ev(q, self.evcount[q])
        if q in DMA_Q:
            slot = self.dma_n[q] % NDMASEM
            self.dma_n[q] += 1
            prev = self.dma_last.get((q, slot))
            if prev is not None:
                deps.append(prev)
            self.dma_last[(q, slot)] = ev
            ev.sem = (q, slot)
        else:
            ev.sem = (q, 0)
        best = {}
        dmas = []
        for d in deps:
            if d.q in COMPUTE_Q:
                if d.q == "pe" and q == "pe":
                    continue
                if d.q not in best or best[d.q].idx < d.idx:
                    best[d.q] = d
            else:
                if d not in dmas:
                    dmas.append(d)
        waits = list(best.values()) + dmas
        self.streams[ENGINE_OF[q]].append((q, fn, waits, ev))
        if q in DMA_Q:
            self.dma_pending.append(ev)
        else:
            self.last_ev[q] = ev
        for r in rd:
            r.r.append(ev)
        for w in wr:
            w.w = ev
            w.r = []
        return ev

    def finalize(self, block):
        nc = self.nc
        for eng, stream in self.streams.items():
            seen = {}
            for item in stream:
                q, fn, waits, ev = item
                keep = []
                for d in waits:
                    if d.q in COMPUTE_Q:
                        if seen.get(d.q, 0) >= d.idx:
                            continue
                        seen[d.q] = d.idx
                    else:
                        if seen.get(id(d)):
                            continue
                        seen[id(d)] = True
                    d.needed = True
                    keep.append(d)
                item[2][:] = keep
        for ev in self.final_events:
            ev.needed = True
        counters = {}
        for eng, stream in self.streams.items():
            for q, fn, waits, ev in stream:
                if q in DMA_Q:
                    key = ev.sem
                    counters[key] = counters.get(key, 0) + 16
                    ev.val = counters[key]
                    ev.needed = True
                elif ev.needed:
                    key = ev.sem
                    counters[key] = counters.get(key, 0) + 1
                    ev.val = counters[key]
        for key in counters:
            self.sems[key] = self.stack.enter_context(nc.semaphore("s_%s_%d" % key))
        self.maxcount = dict(counters)

        def emit(engname):
            stream = self.streams[engname]

            def body(eng):
                for q, fn, waits, ev in stream:
                    for d in waits:
                        eng.wait_ge(self.sems[d.sem], d.val)
                    ins = getattr(eng, fn[0])(**fn[1])
                    if ev.needed:
                        ins.then_inc(self.sems[ev.sem], 16 if q in DMA_Q else 1)
                if engname == "sync":
                    for ev in self.final_events:
                        eng.wait_ge(self.sems[ev.sem], ev.val)
            return body

        block.tensor(emit("tensor"))
        block.scalar(emit("scalar"))
        block.vector(emit("vector"))
        block.gpsimd(emit("gpsimd"))
        block.sync(emit("sync"))


def bf(a):
    return np.ascontiguousarray(np.asarray(a, dtype=np.float32)).astype(ml_dtypes.bfloat16)


def const_tables(core):
    a = core % 4
    t = {}
    t["ones_d"] = bf(np.full((128, 128), 1.0 / 1024.0))
    bd = np.zeros((128, 128), np.float32)
    for i in range(4):
        bd[32 * i:32 * i + 32, 32 * i:32 * i + 32] = 1.0 / 32.0
    t["bd32"] = bf(bd)
    rot = np.zeros((128, 128), np.float32)
    for blk in range(4):
        o = 32 * blk
        for i in range(4):
            rot[o + 4 + i, o + i] = -1.0
            rot[o + i, o + 4 + i] = 1.0
    t["rot"] = bf(rot)
    pos = (a * NT + np.arange(NT)).astype(np.float64)
    inv = 1.0 / (ROPE_THETA ** (np.arange(0, 8, 2, dtype=np.float64) / 8.0))
    ang = pos[None, :] * inv[:, None]
    cf = np.ones((128, NT), np.float32)
    sf = np.zeros((128, NT), np.float32)
    for blk in range(4):
        o = 32 * blk
        cf[o:o + 4] = np.cos(ang)
        cf[o + 4:o + 8] = np.cos(ang)
        sf[o:o + 4] = np.sin(ang)
        sf[o + 4:o + 8] = np.sin(ang)
    t["cosf"] = cf
    t["sinf"] = sf
    jc = np.outer(np.arange(128), np.arange(128)).astype(np.float64) * (2 * np.pi / 128.0)
    t["dftg"] = bf(np.concatenate([np.cos(jc), -np.sin(jc)], axis=1))
    return t


WEIGHT_SHAPES = {"w_proj_a": [512, D], "w_proj_b": [512, D], "w_proj_c": [512, D], "w_gate": [D, 3 * D], "w_out": [D, D],
                 "w_ffn_in": [D, 2 * DFF], "w_ffn_out": [DFF, D], "w_in": [D, 3072]}


def cast_weights(P, names_sfx):
    for name, sfx in names_sfx:
        shape = WEIGHT_SHAPES[name]
        src = P.dram(name + sfx, shape, F32, "ExternalInput")
        dst = P.dram(name + sfx + "_bf", shape, BF16, "Internal")
        for r0 in range(0, shape[0], 128):
            P.op("poolq", "dma_start", dict(out=dst[r0:r0 + 128, :], in_=src[r0:r0 + 128, :]), wr=[Res()])


def wsrc(P, name, sfx, bfw):
    shape = WEIGHT_SHAPES[name]
    if bfw:
        return P.dram(name + sfx + "_bf", shape, BF16, "Internal")
    return P.dram(name + sfx, shape, F32, "ExternalInput")


def phase_A(P, sfx, x_res=None, bfw=False):
    nc = P.nc
    if x_res is None:
        xT_d = P.dram("xT" + sfx, [D, NT], F32, "ExternalInput")
    win_d = wsrc(P, "w_in", sfx, bfw)
    g1_d = P.dram("g1" + sfx, [128, 8], F32, "ExternalInput")
    gqk_d = P.dram("gqk" + sfx, [128, 2], F32, "ExternalInput")
    cos_d = P.dram("cosf" + sfx, [128, NT], F32, "ExternalInput")
    sin_d = P.dram("sinf" + sfx, [128, NT], F32, "ExternalInput")
    ones_d = P.dram("ones_d" + sfx, [128, 128], BF16, "ExternalInput")
    bd_d = P.dram("bd32" + sfx, [128, 128], BF16, "ExternalInput")
    rot_d = P.dram("rot" + sfx, [128, 128], BF16, "ExternalInput")
    dftg_d = P.dram("dftg" + sfx, [128, 256], BF16, "ExternalInput")
    qT_o = P.dram("qT" + sfx, [512, NT], BF16, "ExternalOutput")
    kT_o = P.dram("kT" + sfx, [512, NT], BF16, "ExternalOutput")
    vp_o = P.dram("vp" + sfx, [4, NT, 130], BF16, "ExternalOutput")
    gT_o = P.dram("gT" + sfx, [512, NT], BF16, "ExternalOutput")
    zp_o = P.dram("zp" + sfx, [4, NT, 256], BF16, "ExternalOutput")

    xT = P.sb("xT_sb", [128, 8, NT], F32)
    W = P.sb("w_sb", [128, 8, 3072], BF16)
    g1 = P.sb("g1_sb", [128, 8], F32)
    gqk = P.sb("gqk_sb", [128, 2], F32)
    cosf = P.sb("cos_sb", [128, NT], F32)
    sinf = P.sb("sin_sb", [128, NT], F32)
    onesm = P.sb("ones_sb", [128, 128], BF16)
    bdm = P.sb("bd_sb", [128, 128], BF16)
    rotm = P.sb("rot_sb", [128, 128], BF16)
    dftg = P.sb("dftg_sb", [128, 256], BF16)
    hT = P.sb("hT", [128, 8, 512], BF16)
    sq = P.sb("sq", [128, 2, 512], BF16)
    lnv = P.sb("lnv", [128, 512], F32)
    rstd = P.sb("rstd", [128, 512], F32)
    sq2 = P.sb("sq2", [128, 2, 512], BF16)
    ln2 = P.sb("ln2", [128, 2, 512], F32)
    r2 = P.sb("r2", [128, 2, 512], F32)
    qn = P.sb("qn", [128, 2, 512], BF16)
    t1 = P.sb("t1", [128, 2, 512], F32)
    t2 = P.sb("t2", [128, 2, 512], F32)
    qkst = P.sb("qkst", [128, 1, 8, 512], BF16)
    vst = P.sb("vst", [128, 1, 4, 4, 130], BF16)
    sg = P.sb("sg", [128, 2, 512], F32)
    gst = P.sb("gst", [128, 1, 4, 512], BF16)
    fcT = P.sb("fcT", [128, 2, 512], BF16)
    zst = P.sb("zst", [128, 1, 4, 4, 256], BF16)
    PS = P.ps("ps", [128, 8, 512], F32)

    R = lambda n: Res(n)
    r_x = [[R("x%d" % i) for _ in range(NTB)] for i in range(8)]
    r_w = [[R("w%d" % i) for _ in range(8)] for i in range(6)]
    r_c = R("consts")
    r_cs = R("cossin")
    r_h = R("hT")
    r_sq = [R("sq0"), R("sq1")]
    r_ln = R("lnv")
    r_rstd = R("rstd")
    r_bank = [R("bank%d" % i) for i in range(8)]
    r_sq2 = [R("a"), R("b")]
    r_ln2 = [R("a"), R("b")]
    r_r2 = [R("a"), R("b")]
    r_qn = [R("a"), R("b")]
    r_t1 = [R("a"), R("b")]
    r_t2 = [R("a"), R("b")]
    r_qk = [[R("qk%d" % i) for i in range(8)] for _ in range(2)]
    r_v = [R("vst0"), R("vst1")]
    r_sg = [R("a"), R("b")]
    r_g = [[R("g%d" % i) for i in range(4)] for _ in range(2)]
    r_fc = [R("a"), R("b")]
    r_z = [R("zst0"), R("zst1")]

    for s_, d_ in [(g1, g1_d), (gqk, gqk_d), (onesm, ones_d), (bdm, bd_d), (rotm, rot_d), (dftg, dftg_d)]:
        P.op("sp", "dma_start", dict(out=s_[:], in_=d_[:, :]), wr=[r_c])
    if x_res is None:
        for tb in range(NTB):
            for kc in range(8):
                P.op("sp", "dma_start", dict(out=xT[:, kc, tb * 512:(tb + 1) * 512],
                                             in_=xT_d[kc * 128:(kc + 1) * 128, tb * 512:(tb + 1) * 512]), wr=[r_x[kc][tb]])
            if tb == 0:
                for s_, d_ in [(cosf, cos_d), (sinf, sin_d)]:
                    P.op("sp", "dma_start", dict(out=s_[:], in_=d_[:, :]), wr=[r_cs])
    else:
        for s_, d_ in [(cosf, cos_d), (sinf, sin_d)]:
            P.op("sp", "dma_start", dict(out=s_[:], in_=d_[:, :]), wr=[r_cs])
    for wb in range(6):
        for kc in range(8):
            P.op("poolq", "dma_start", dict(
                out=W[:, kc, wb * 512:(wb + 1) * 512],
                in_=win_d[kc * 128:(kc + 1) * 128, wb * 512:(wb + 1) * 512]), wr=[r_w[wb][kc]])
    for sl in range(1):
        P.op("pool", "memset", dict(ap=vst[:, sl, :, :, 64:65], constant=1.0), wr=[r_v[sl]])
        P.op("pool", "memset", dict(ap=vst[:, sl, :, :, 129:130], constant=1.0), wr=[r_v[sl]])

    fin = []
    bank_rr = [0]

    def next_bank():
        b = bank_rr[0] % 4
        bank_rr[0] += 1
        return b

    def proj_chunk(oc, tb, bank):
        wb = (oc * 128) // 512
        for kc in range(8):
            P.op("pe", "matmul", dict(out=PS[:, bank, :], lhsT=W[:, kc, oc * 128:(oc + 1) * 128],
                                                 rhs=hT[:, kc, :], start=(kc == 0), stop=(kc == 7)),
                 rd=[r_w[wb][kc], r_h, r_c], wr=[r_bank[bank]])

    for tb in range(NTB):
        ts_ = slice(tb * 512, (tb + 1) * 512)
        sl = 0
        for kc in range(8):
            s = kc % 2
            P.op("pool", "tensor_tensor", dict(out=sq[:, s, :], in0=xT[:, kc, ts_], in1=xT[:, kc, ts_],
                                                              op=ALU.mult), rd=[r_x[kc][tb]], wr=[r_sq[s]])
            P.op("pe", "matmul", dict(out=PS[:, 4, :], lhsT=onesm[:], rhs=sq[:, s, :],
                                                      start=(kc == 0), stop=(kc == 7)),
                 rd=[r_sq[s], r_c], wr=[r_bank[4]])
        P.op("act", "activation", dict(out=lnv[:], in_=PS[:, 4, :], func=AF.Ln, bias=EPS, scale=1.0),
             rd=[r_bank[4]], wr=[r_ln])
        P.op("act", "activation", dict(out=rstd[:], in_=lnv[:], func=AF.Exp, scale=-0.5),
             rd=[r_ln], wr=[r_rstd])
        for kc in range(8):
            P.op("dve", "scalar_tensor_tensor", dict(out=hT[:, kc, :], in0=xT[:, kc, ts_],
                                                                scalar=g1[:, kc:kc + 1], in1=rstd[:],
                                                                op0=ALU.mult, op1=ALU.mult),
                 rd=[r_x[kc][tb], r_rstd, r_c], wr=[r_h])
        def stage2(oc, bank):
            s = oc % 2
            gi = 0 if oc < 4 else 1
            P.op("pe", "matmul", dict(out=PS[:, 5, :], lhsT=bdm[:], rhs=sq2[:, s, :], start=True, stop=True),
                 rd=[r_sq2[s], r_c], wr=[r_bank[5]])
            P.op("act", "activation", dict(out=ln2[:, s, :], in_=PS[:, 5, :], func=AF.Ln, bias=EPS, scale=1.0),
                 rd=[r_bank[5]], wr=[r_ln2[s]])
            P.op("act", "activation", dict(out=r2[:, s, :], in_=ln2[:, s, :], func=AF.Exp, scale=-0.5),
                 rd=[r_ln2[s]], wr=[r_r2[s]])
            P.op("dve", "scalar_tensor_tensor", dict(
                out=qn[:, s, :], in0=PS[:, bank, :], scalar=gqk[:, gi:gi + 1], in1=r2[:, s, :],
                op0=ALU.mult, op1=ALU.mult), rd=[r_bank[bank], r_r2[s], r_c], wr=[r_qn[s]])

        def stage3(oc):
            s = oc % 2
            P.op("pe", "matmul", dict(out=PS[:, 6, :], lhsT=rotm[:], rhs=qn[:, s, :], start=True, stop=True),
                 rd=[r_qn[s], r_c], wr=[r_bank[6]])
            P.op("pool", "tensor_tensor", dict(out=t1[:, s, :], in0=qn[:, s, :], in1=cosf[:, ts_], op=ALU.mult),
                 rd=[r_qn[s], r_cs], wr=[r_t1[s]])
            P.op("dve", "tensor_tensor", dict(out=t2[:, s, :], in0=PS[:, 6, :], in1=sinf[:, ts_], op=ALU.mult),
                 rd=[r_bank[6], r_cs], wr=[r_t2[s]])
            P.op("pool", "tensor_tensor", dict(out=qkst[:, sl, oc, :], in0=t1[:, s, :], in1=t2[:, s, :],
                                               op=ALU.add), rd=[r_t1[s], r_t2[s]], wr=[r_qk[sl][oc]])

        hist = []
        for oc in range(8):
            s = oc % 2
            bank = next_bank()
            proj_chunk(oc, tb, bank)
            P.op("act", "activation", dict(out=sq2[:, s, :], in_=PS[:, bank, :], func=AF.Square),
                 rd=[r_bank[bank]], wr=[r_sq2[s]])
            hist.append((oc, bank))
            if len(hist) >= 2:
                stage2(*hist[-2])
            if len(hist) >= 3:
                stage3(hist[-3][0])
        stage2(*hist[-1])
        stage3(hist[-2][0])
        stage3(hist[-1][0])
        for tt in range(4):
            tti = tt
            bank = next_bank()
            for kc in range(8):
                P.op("pe", "matmul", dict(
                    out=PS[:, bank, :], lhsT=hT[:, kc, tt * 128:(tt + 1) * 128], rhs=W[:, kc, 1024:1536],
                    start=(kc == 0), stop=(kc == 7)), rd=[r_w[2][kc], r_h], wr=[r_bank[bank]])
            src = PS[:, bank, :].rearrange("p (a b c) -> p a b c", a=4, b=2)
            P.op("act", "activation", dict(out=vst[:, sl, tti, :, 0:64], in_=src[:, :, 0, :], func=AF.Copy),
                 rd=[r_bank[bank]], wr=[r_v[sl]])
            P.op("dve", "tensor_copy", dict(out=vst[:, sl, tti, :, 65:129], in_=src[:, :, 1, :]),
                 rd=[r_bank[bank]], wr=[r_v[sl]])
        for j in range(4):
            s = j % 2
            ba = next_bank()
            proj_chunk(12 + j, tb, ba)
            bb = next_bank()
            proj_chunk(16 + j, tb, bb)
            P.op("act", "activation", dict(out=sg[:, s, :], in_=PS[:, bb, :], func=AF.Sigmoid),
                 rd=[r_bank[bb]], wr=[r_sg[s]])
            P.op("dve", "tensor_tensor", dict(out=gst[:, sl, j, :], in0=PS[:, ba, :], in1=sg[:, s, :],
                                                                  op=ALU.mult),
                 rd=[r_bank[ba], r_sg[s]], wr=[r_g[sl][j]])
        for gr in range(4):
            s = gr % 2
            bank = next_bank()
            proj_chunk(20 + gr, tb, bank)
            P.op("act", "activation", dict(out=fcT[:, s, :], in_=PS[:, bank, :], func=AF.Copy),
                 rd=[r_bank[bank]], wr=[r_fc[s]])
            for tp in range(2):
                for t_ in range(2):
                    tt = tp * 2 + t_
                    P.op("pe", "matmul", dict(
                        out=PS[:, 7, t_ * 256:(t_ + 1) * 256], lhsT=fcT[:, s, tt * 128:(tt + 1) * 128], rhs=dftg[:],
                        start=True, stop=True), rd=[r_fc[s], r_c], wr=[r_bank[7]])
                for t_ in range(2):
                    tti = tp * 2 + t_
                    P.op("dve", "tensor_copy", dict(
                        out=zst[:, sl, tti, gr, :], in_=PS[:, 7, t_ * 256:(t_ + 1) * 256]),
                        rd=[r_bank[7]], wr=[r_z[sl]])

        for oc in range(8):
            dst = qT_o if oc < 4 else kT_o
            o = (oc % 4) * 128
            fin.append(P.op("sp", "dma_start", dict(
                out=dst[o:o + 128, ts_], in_=qkst[:, sl, oc, :]), rd=[r_qk[sl][oc]]))
        for j in range(4):
            fin.append(P.op("sp", "dma_start", dict(
                out=gT_o[j * 128:(j + 1) * 128, ts_], in_=gst[:, sl, j, :]), rd=[r_g[sl][j]]))
        for hp in range(4):
            fin.append(P.op("sp", "dma_start", dict(
                out=vp_o[hp, ts_, :].rearrange("(t p) c -> p t c", p=128), in_=vst[:, sl, :, hp, :]), rd=[r_v[sl]]))
        for gr in range(4):
            fin.append(P.op("sp", "dma_start", dict(
                out=zp_o[gr, ts_, :].rearrange("(t p) c -> p t c", p=128), in_=zst[:, sl, :, gr, :]), rd=[r_z[sl]]))
    return fin


def lam_init_of(l):
    return 0.8 - 0.6 * math.exp(-0.3 * l)


def phase_B1(P, sfx, l, after_loads=None):
    nc = P.nc
    qT_d = P.dram("qT" + sfx, [512, NT], BF16, "ExternalInput")
    kT_d = P.dram("kT_all" + sfx, [4, 512, NT], BF16, "ExternalInput")
    vp_d = P.dram("vp_all" + sfx, [4, 4, NT, 130], BF16, "ExternalInput")
    lam_d = P.dram("lamv" + sfx, [128, 4, 32], F32, "ExternalInput")
    gsub_d = P.dram("gsub" + sfx, [64, 1], F32, "ExternalInput")
    sel_d = P.dram("sel" + sfx, [128, 64], F32, "ExternalInput")
    o64_d = P.dram("ones64" + sfx, [64, 64], BF16, "ExternalInput")
    qm_d = P.dram("qmask" + sfx, [128, 4], F32, "ExternalInput")
    aT_o = P.dram("aT" + sfx, [8, 64, NT], BF16, "ExternalOutput")

    qT = P.sb("qT_sb", [128, 4, NT], BF16)
    qTm = P.sb("qTm_sb", [128, 2, 4, NT], BF16)
    qm = P.sb("qm_sb", [128, 4], F32)
    KT = P.sb("KT_sb", [128, 2, SEQ], BF16)
    VP = P.sb("VP_sb", [128, 2, 64, 130], BF16)
    lamv = P.sb("lamv_sb", [128, 4, 32], F32)
    lprod = P.sb("lprod", [128, 2, 32], F32)
    lsum = P.sb("lsum", [128, 2], F32)
    lexp = P.sb("lexp", [128, 2], F32)
    neglam = P.sb("neglam", [128, 1], F32)
    gsub = P.sb("gsub_sb", [64, 1], F32)
    gsub2 = P.sb("gsub2_sb", [64, 1], F32)
    sel = P.sb("sel_sb", [128, 64], F32)
    o64 = P.sb("o64_sb", [64, 64], BF16)
    E = P.sb("E_sb", [128, 3, 2, 512], BF16)
    Osb = P.sb("Osb", [65, 2, 512], F32)
    Rr = P.sb("Rr", [64, 2, 512], F32)
    tt0 = P.sb("tt0", [64, 2, 512], F32)
    att = P.sb("att", [64, 512], F32)
    sqa = P.sb("sqa", [64, 512], BF16)
    lna = P.sb("lna", [64, 512], F32)
    rsa = P.sb("rsa", [64, 512], F32)
    aT = P.sb("aT_sb", [64, 8, NT], BF16)
    PS = P.ps("ps", [128, 8, 512], F32)

    R = Res
    r_q = [R() for _ in range(4)]
    r_qm = [[R() for _ in range(4)] for _ in range(2)]
    r_kt = [[R() for _ in range(4)] for _ in range(2)]
    r_vp = [[R() for _ in range(4)] for _ in range(2)]
    r_c = R()
    r_lam = R()
    r_bank = [R() for _ in range(8)]
    r_E = [R() for _ in range(3)]
    r_O = [R(), R()]
    r_R = [R(), R()]
    r_t = [R(), R()]
    r_att, r_sq, r_ln, r_rs = R(), R(), R(), R()
    r_aT = [[R() for _ in range(4)] for _ in range(8)]

    lam_init = lam_init_of(l)
    for hp in range(4):
        P.op("sp", "dma_start", dict(out=qT[:, hp, :], in_=qT_d[hp * 128:(hp + 1) * 128, :]), wr=[r_q[hp]])
    P.op("sp", "dma_start", dict(out=lamv[:], in_=lam_d[:, :, :]), wr=[r_lam])
    P.op("sp", "dma_start", dict(out=gsub[:], in_=gsub_d[:, :]), wr=[r_c])
    P.op("sp", "dma_start", dict(out=sel[:], in_=sel_d[:, :]), wr=[r_c])
    P.op("sp", "dma_start", dict(out=o64[:], in_=o64_d[:, :]), wr=[r_c])
    P.op("sp", "dma_start", dict(out=qm[:], in_=qm_d[:, :]), wr=[r_c])

    def load_kv(hp):
        s = hp % 2
        for r in range(4):
            P.op("sp", "dma_start", dict(out=KT[:, s, r * NT:(r + 1) * NT], in_=kT_d[r, hp * 128:(hp + 1) * 128, :]),
                 wr=[r_kt[s][r]])
            P.op("sp", "dma_start", dict(out=VP[:, s, 16 * r:16 * r + 16, :],
                                         in_=vp_d[r, hp].rearrange("(k p) c -> p k c", p=128)), wr=[r_vp[s][r]])

    load_kv(0)
    if after_loads is not None:
        after_loads()
    P.op("dve", "tensor_tensor", dict(out=lprod[:, 0, :], in0=lamv[:, 0, :], in1=lamv[:, 1, :], op=ALU.mult),
         rd=[r_lam], wr=[r_att])
    P.op("dve", "tensor_tensor", dict(out=lprod[:, 1, :], in0=lamv[:, 2, :], in1=lamv[:, 3, :], op=ALU.mult),
         rd=[r_lam], wr=[r_att])
    P.op("dve", "tensor_reduce", dict(out=lsum[:], in_=lprod[:], axis=mybir.AxisListType.X, op=ALU.add),
         rd=[r_att], wr=[r_sq])
    P.op("act", "activation", dict(out=lexp[:], in_=lsum[:], func=AF.Exp), rd=[r_sq], wr=[r_ln])
    P.op("dve", "tensor_tensor", dict(out=neglam[:], in0=lexp[:, 1:2], in1=lexp[:, 0:1], op=ALU.subtract),
         rd=[r_ln], wr=[r_rs])
    P.op("dve", "tensor_scalar", dict(out=neglam[:], in0=neglam[:], scalar1=-lam_init, scalar2=None, op0=ALU.add),
         rd=[r_rs], wr=[r_rs])
    r_neglam = r_rs
    r_neglam_ev_holder = R()
    P.op("dve", "tensor_scalar", dict(out=gsub2[:], in0=gsub[:], scalar1=1.0 - lam_init, scalar2=None, op0=ALU.mult),
         rd=[r_c], wr=[r_neglam_ev_holder])
    r_g2 = r_neglam_ev_holder
    r_att, r_sq, r_ln = R(), R(), R()
    r_rs2 = R()

    scale = 32.0 ** -0.5
    ecnt = [0]

    def qk(hp, s, h2, qs, kt):
        sb0 = (kt % 2) * 2
        for c in range(2):
            i = 2 * h2 + c
            P.op("pe", "matmul", dict(out=PS[:, sb0 + c, :], lhsT=KT[:, s, kt * 128:(kt + 1) * 128],
                                      rhs=qTm[:, s, i, qs], start=True, stop=True),
                 rd=[r_qm[s][i]] + r_kt[s], wr=[r_bank[sb0 + c]])

    def post_copy():
        for c in range(2):
            P.op("dve", "tensor_copy", dict(out=Osb[:, c, :], in_=PS[0:65, 4 + c, :]), rd=[r_bank[4 + c]], wr=[r_O[c]])

    def post_rest(h, qs, qb):
        for c in range(2):
            P.op("pe", "matmul", dict(out=PS[0:64, 6 + c, :], lhsT=sel[0:65, :], rhs=Osb[:, c, :], start=True, stop=True),
                 rd=[r_O[c], r_c], wr=[r_bank[6 + c]])
            P.op("dve", "reciprocal", dict(out=Rr[:, c, :], in_=PS[0:64, 6 + c, :]), rd=[r_bank[6 + c]], wr=[r_R[c]])
            P.op("pool", "tensor_tensor", dict(out=tt0[:, c, :], in0=Osb[0:64, c, :], in1=Rr[:, c, :], op=ALU.mult),
                 rd=[r_O[c], r_R[c]], wr=[r_t[c]])
        P.op("dve", "scalar_tensor_tensor", dict(out=att[:], in0=tt0[:, 1, :], scalar=neglam[0:64, 0:1], in1=tt0[:, 0, :],
                                                 op0=ALU.mult, op1=ALU.add), rd=[r_t[0], r_t[1], r_neglam], wr=[r_att])
        P.op("pool", "tensor_tensor", dict(out=sqa[:], in0=att[:], in1=att[:], op=ALU.mult), rd=[r_att], wr=[r_sq])
        P.op("pe", "matmul", dict(out=PS[0:64, 6, :], lhsT=o64[:], rhs=sqa[:], start=True, stop=True),
             rd=[r_sq, r_c], wr=[r_bank[6]])
        P.op("act", "activation", dict(out=lna[:], in_=PS[0:64, 6, :], func=AF.Ln, bias=EPS, scale=1.0),
             rd=[r_bank[6]], wr=[r_ln])
        P.op("act", "activation", dict(out=rsa[:], in_=lna[:], func=AF.Exp, scale=-0.5), rd=[r_ln], wr=[r_rs2])
        P.op("dve", "scalar_tensor_tensor", dict(out=aT[:, h, qs], in0=att[:], scalar=gsub2[:, 0:1], in1=rsa[:],
                                                 op0=ALU.mult, op1=ALU.mult), rd=[r_att, r_rs2, r_g2], wr=[r_aT[h][qb]])

    def mask_q(hp):
        s_ = hp % 2
        for i in range(4):
            P.op("dve", "tensor_scalar", dict(out=qTm[:, s_, i, :], in0=qT[:, hp, :], scalar1=qm[:, i:i + 1], scalar2=None,
                                               op0=ALU.mult), rd=[r_q[hp], r_c], wr=[r_qm[s_][i]])

    pending = None
    mask_q(0)
    for hp in range(4):
        s = hp % 2
        if hp + 1 < 4:
            load_kv(hp + 1)
            mask_q(hp + 1)
        for h2 in range(2):
            h = hp * 2 + h2
            for qb in range(4):
                qs = slice(qb * 512, (qb + 1) * 512)
                qk(hp, s, h2, qs, 0)
                qk(hp, s, h2, qs, 1)
                def pv(kt_, eb_):
                    for c in range(2):
                        P.op("pe", "matmul", dict(out=PS[0:65, 4 + c, :], lhsT=VP[:, s, kt_, h2 * 65:(h2 + 1) * 65],
                                                  rhs=E[:, eb_, c, :], start=(kt_ == 0), stop=(kt_ == 63)),
                             rd=[r_E[eb_]] + r_vp[s], wr=[r_bank[4 + c]])

                prev_eb = None
                for kt in range(64):
                    sb0 = (kt % 2) * 2
                    eb = ecnt[0] % 3
                    ecnt[0] += 1
                    P.op("act", "activation", dict(out=E[:, eb, :, :], in_=PS[:, sb0:sb0 + 2, :], func=AF.Exp, scale=scale),
                         rd=[r_bank[sb0], r_bank[sb0 + 1]], wr=[r_E[eb]])
                    if kt >= 1:
                        pv(kt - 1, prev_eb)
                    if kt + 2 < 64:
                        qk(hp, s, h2, qs, kt + 2)
                    prev_eb = eb
                    if kt == 6 and pending is not None:
                        post_rest(*pending)
                        pending = None
                pv(63, prev_eb)
                post_copy()
                pending = (h, qs, qb)
    post_rest(*pending)
    fin = []
    for h in range(8):
        fin.append(P.op("sp", "dma_start", dict(out=aT_o[h], in_=aT[:, h, :]), rd=r_aT[h]))
    return fin


def consts_B1():
    sel = np.zeros((128, 64), np.float32)
    sel[64, :] = 1.0
    qm = np.zeros((128, 4), np.float32)
    for i in range(4):
        qm[32 * i:32 * i + 32, i] = 1.0
    return {"sel": sel, "ones64": bf(np.full((64, 64), 1.0 / 64.0)), "qmask": qm}


def phase_B2(P, sfx, after_loads=None):
    nc = P.nc
    g_d = P.dram("gT" + sfx, [512, NT], BF16, "ExternalInput")
    gall_d = P.dram("g_all" + sfx, [4, 512, NT], BF16, "ExternalInput")
    cw_d = P.dram("conv_w" + sfx, [128, 4, 31], F32, "ExternalInput")
    cv4_d = P.dram("cvec" + sfx, [128, 3, 4], F32, "ExternalInput")
    cm_d = P.dram("cmask" + sfx, [128, 8], F32, "ExternalInput")
    id_d = P.dram("ident" + sfx, [128, 128], BF16, "ExternalInput")
    o512_d = P.dram("ones512" + sfx, [128, 128], F32, "ExternalInput")
    bT_o = P.dram("bT" + sfx, [512, NT], BF16, "ExternalOutput")

    gp = P.sb("gpad", [128, 4, NT + 30], BF16)
    hal = P.sb("hal", [128, 4, 4, 2, 15], BF16)
    cw = P.sb("cw", [128, 4, 31], F32)
    cv4 = P.sb("cv4", [128, 3, 4], F32)
    cm = P.sb("cm", [128, 8], F32)
    ident = P.sb("ident_sb", [128, 128], BF16)
    o512 = P.sb("o512", [128, 128], F32)
    Dg = P.sb("Dg", [128, 4, 31, 128], BF16)
    cv = P.sb("cv", [128, 4, 512], F32)
    sqv = P.sb("sqv", [128, 4, 512], F32)
    m2 = P.sb("m2", [128, 512], F32)
    var = P.sb("var", [128, 512], F32)
    lnv = P.sb("lnv", [128, 512], F32)
    rstd = P.sb("rstd", [128, 512], F32)
    nmr = P.sb("nmr", [128, 512], F32)
    yv = P.sb("yv", [128, 2, 512], F32)
    bst = P.sb("bst", [128, 4, NT], BF16)
    PS = P.ps("ps", [128, 8, 512], F32)

    R = Res
    r_g = [R() for _ in range(4)]
    r_hal, r_c, r_D = R(), R(), [R() for _ in range(4)]
    r_bank = [R() for _ in range(8)]
    r_cv = [R() for _ in range(4)]
    r_sq = [R() for _ in range(4)]
    r_m2, r_var, r_ln, r_rstd, r_nmr = R(), R(), R(), R(), R()
    r_y = [R(), R()]
    r_b = [R() for _ in range(4)]

    for j in range(4):
        P.op("sp", "dma_start", dict(out=gp[:, j, 15:15 + NT], in_=g_d[j * 128:(j + 1) * 128, :]), wr=[r_g[j]])
    for i, (s_, d_) in enumerate([(cw, cw_d), (cv4, cv4_d), (cm, cm_d), (ident, id_d), (o512, o512_d)]):
        P.op("sp", "dma_start", dict(out=s_[:], in_=d_), wr=[r_c])
    rh = [R() for _ in range(8)]
    for r in range(4):
        gv = gall_d[r].rearrange("(j p) t -> p j t", p=128)
        P.op("sp", "dma_start", dict(out=hal[:, :, r, 0, :], in_=gv[:, :, NT - 15:NT]), wr=[rh[2 * r]])
        P.op("sp", "dma_start", dict(out=hal[:, :, r, 1, :], in_=gv[:, :, 0:15]), wr=[rh[2 * r + 1]])
    if after_loads is not None:
        after_loads()
    for side in range(2):
        dst = gp[:, :, 0:15] if side == 0 else gp[:, :, 15 + NT:30 + NT]
        for r in range(4):
            mk = cm[:, side * 4 + r:side * 4 + r + 1]
            if r == 0:
                P.op("dve", "tensor_scalar", dict(out=dst, in0=hal[:, :, r, side, :], scalar1=mk, scalar2=None, op0=ALU.mult),
                     rd=[rh[2 * r + side], r_c], wr=r_g)
            else:
                P.op("dve", "scalar_tensor_tensor", dict(out=dst, in0=hal[:, :, r, side, :], scalar=mk, in1=dst,
                                                         op0=ALU.mult, op1=ALU.add), rd=[rh[2 * r + side], r_c], wr=r_g)
    n = 0
    for j in range(4):
        for tau in range(31):
            n += 1
            if n % 3 != 0:
                P.op("dve", "tensor_scalar", dict(out=Dg[:, j, tau, :], in0=ident[:], scalar1=cw[:, j, tau:tau + 1], scalar2=None,
                                                  op0=ALU.mult), rd=[r_c], wr=[r_D[j]])
            else:
                P.op("act", "activation", dict(out=Dg[:, j, tau, :], in_=ident[:], func=AF.Copy, scale=cw[:, j, tau:tau + 1]),
                     rd=[r_c], wr=[r_D[j]])
    fin = []
    for tb in range(NTB):
        ts_ = slice(tb * 512, (tb + 1) * 512)
        for j in range(4):
            bank = j
            for tau in range(31):
                P.op("pe", "matmul", dict(out=PS[:, bank, :], lhsT=Dg[:, j, tau, :],
                                          rhs=gp[:, j, tb * 512 + tau:tb * 512 + tau + 512], start=(tau == 0), stop=(tau == 30)),
                     rd=[r_D[j], r_g[j]], wr=[r_bank[bank]])
            P.op("act", "activation", dict(out=cv[:, j, :], in_=PS[:, bank, :], func=AF.Identity, bias=cv4[:, 0, j:j + 1], scale=1.0),
                 rd=[r_bank[bank], r_c], wr=[r_cv[j]])
            P.op("pool", "tensor_tensor", dict(out=sqv[:, j, :], in0=cv[:, j, :], in1=cv[:, j, :], op=ALU.mult),
                 rd=[r_cv[j]], wr=[r_sq[j]])
        for j in range(4):
            P.op("pe", "matmul", dict(out=PS[:, 4, :], lhsT=o512[:], rhs=cv[:, j, :], start=(j == 0), stop=(j == 3)),
                 rd=[r_cv[j], r_c], wr=[r_bank[4]])
        for j in range(4):
            P.op("pe", "matmul", dict(out=PS[:, 5, :], lhsT=o512[:], rhs=sqv[:, j, :], start=(j == 0), stop=(j == 3)),
                 rd=[r_sq[j], r_c], wr=[r_bank[5]])
        P.op("act", "activation", dict(out=m2[:], in_=PS[:, 4, :], func=AF.Square), rd=[r_bank[4]], wr=[r_m2])
        P.op("dve", "tensor_tensor", dict(out=var[:], in0=PS[:, 5, :], in1=m2[:], op=ALU.subtract), rd=[r_bank[5], r_m2], wr=[r_var])
        P.op("act", "activation", dict(out=lnv[:], in_=var[:], func=AF.Ln, bias=EPS, scale=1.0), rd=[r_var], wr=[r_ln])
        P.op("act", "activation", dict(out=rstd[:], in_=lnv[:], func=AF.Exp, scale=-0.5), rd=[r_ln], wr=[r_rstd])
        P.op("dve", "scalar_tensor_tensor", dict(out=nmr[:], in0=PS[:, 4, :], scalar=-1.0, in1=rstd[:], op0=ALU.mult, op1=ALU.mult),
             rd=[r_bank[4], r_rstd], wr=[r_nmr])
        for j in range(4):
            s = j % 2
            P.op("pool", "tensor_tensor", dict(out=yv[:, s, :], in0=cv[:, j, :], in1=rstd[:], op=ALU.mult),
                 rd=[r_cv[j], r_rstd], wr=[r_y[s]])
            P.op("dve", "tensor_tensor", dict(out=yv[:, s, :], in0=yv[:, s, :], in1=nmr[:], op=ALU.add),
                 rd=[r_nmr], wr=[r_y[s]])
            P.op("act", "activation", dict(out=bst[:, j, ts_], in_=yv[:, s, :], func=AF.Silu, bias=cv4[:, 2, j:j + 1],
                                           scale=cv4[:, 1, j:j + 1]), rd=[r_y[s], r_c], wr=[r_b[j]])
    for j in range(4):
        fin.append(P.op("sp", "dma_start", dict(out=bT_o[j * 128:(j + 1) * 128, :], in_=bst[:, j, :]), rd=[r_b[j]]))
    return fin


def b3_prefetch(P, sfx):
    z_d = P.dram("z_all" + sfx, [4, 4, NT, 256], BF16, "ExternalInput")
    Zt = P.sb("Zt", [128, 1, 128, 256], BF16)

    def issue():
        for g2 in range(2):
            for r in range(4):
                P.op("sp", "dma_start", dict(out=Zt[64 * g2 + 16 * r:64 * g2 + 16 * r + 16, 0, :, :],
                                             in_=z_d[r, g2].rearrange("(p s) c -> p s c", s=128)), wr=[Res()])
    return Zt, issue


def phase_B3(P, sfx, pre=None):
    nc = P.nc
    z_d = P.dram("z_all" + sfx, [4, 4, NT, 256], BF16, "ExternalInput")
    w64_d = P.dram("w64" + sfx, [128, 2, 128], BF16, "ExternalInput")
    tt_d = P.dram("ttab" + sfx, [128, 64, 2, 32], BF16, "ExternalInput")
    fT_o = P.dram("fT" + sfx, [512, NT], BF16, "ExternalOutput")

    Zt = P.sb("Zt", [128, 1, 128, 256], BF16)
    w64 = P.sb("w64_sb", [128, 2, 128], BF16)
    ttab = P.sb("ttab_sb", [128, 64, 2, 32], BF16)
    A = P.sb("A_sb", [128, 2, 128, 2, 64], BF16)
    fst = P.sb("fst", [128, 4, NT], BF16)
    PS = P.ps("ps", [128, 8, 512], F32)

    R = Res
    r_z = [[R() for _ in range(4)] for _ in range(2)]
    r_c = R()
    r_A = [[R() for _ in range(32)] for _ in range(2)]
    r_bank = [R() for _ in range(8)]
    r_f = [R() for _ in range(4)]

    P.op("sp", "dma_start", dict(out=w64[:], in_=w64_d), wr=[r_c])
    P.op("sp", "dma_start", dict(out=ttab[:], in_=tt_d), wr=[r_c])

    def load_group(gp_, g2):
        gr = gp_ * 2 + g2
        for r in range(4):
            P.op("sp", "dma_start", dict(out=Zt[64 * g2 + 16 * r:64 * g2 + 16 * r + 16, 0, :, :],
                                         in_=z_d[r, gr].rearrange("(p s) c -> p s c", s=128)), wr=[r_z[g2][r]])

    if pre is None:
        load_group(0, 0)
        load_group(0, 1)
    ev_n = 0
    bank_n = 0
    fin = []
    for gp_ in range(2):
        sl = 0
        for g2 in range(2):
            gr = gp_ * 2 + g2
            asl = gr % 2
            rows = slice(64 * g2, 64 * g2 + 64)
            for j0 in range(0, 128, 4):
                bank = bank_n % 4
                bank_n += 1
                for jj in range(4):
                    j = j0 + jj
                    for c in range(2):
                        P.op("pe", "matmul", dict(out=PS[:, bank, jj * 128:(jj + 1) * 128], lhsT=Zt[rows, sl, :, c * 128 + j],
                                                  rhs=w64[rows, c, :], start=(c == 0), stop=(c == 1)),
                             rd=r_z[g2] + [r_c], wr=[r_bank[bank]])
                q = "act" if ev_n % 2 == 0 else "dve"
                ev_n += 1
                if q == "act":
                    P.op("act", "activation", dict(out=A[:, asl, j0:j0 + 4, :, :], in_=PS[:, bank, :], func=AF.Copy),
                         rd=[r_bank[bank]], wr=[r_A[asl][j0 // 4]])
                else:
                    P.op("dve", "tensor_copy", dict(out=A[:, asl, j0:j0 + 4, :, :], in_=PS[:, bank, :]),
                         rd=[r_bank[bank]], wr=[r_A[asl][j0 // 4]])
            if gp_ == 0:
                load_group(1, g2)
            for kb in range(4):
                bank = 4 + (kb % 2)
                for kk in range(16):
                    k2 = kb * 16 + kk
                    for c in range(2):
                        P.op("pe", "matmul", dict(out=PS[:, bank, kk * 32:(kk + 1) * 32], lhsT=A[:, asl, :, c, k2],
                                                  rhs=ttab[:, k2, c, :], start=(c == 0), stop=(c == 1)),
                             rd=r_A[asl] + [r_c], wr=[r_bank[bank]])
                dst = fst[:, gr, :].rearrange("p (a b) -> p a b", b=64)[:, :, kb * 16:(kb + 1) * 16]
                src = PS[:, bank, :].rearrange("p (b a) -> p a b", a=32)
                P.op("dve", "tensor_copy", dict(out=dst, in_=src), rd=[r_bank[bank]], wr=[r_f[gr]])
    for gr in range(4):
        fin.append(P.op("sp", "dma_start", dict(out=fT_o[gr * 128:(gr + 1) * 128, :], in_=fst[:, gr, :]), rd=[r_f[gr]]))
    return fin


def consts_B3(core):
    a = core % 4
    s2 = np.arange(64, dtype=np.float64)
    k2 = np.arange(64, dtype=np.float64)
    th = 2 * np.pi * np.outer(s2, k2) / 64.0
    C, S = np.cos(th), np.sin(th)
    w = np.stack([np.concatenate([C, -S], 1), np.concatenate([S, C], 1)], 1)
    w64 = bf(np.concatenate([w, w], 0))
    s1 = np.arange(128, dtype=np.float64)[:, None, None]
    k2_ = np.arange(64, dtype=np.float64)[None, :, None]
    k1 = (32 * a + np.arange(32, dtype=np.float64))[None, None, :]
    ph = 2 * np.pi * s1 * (64 * k1 + k2_) / 8192.0
    sc = 2.0 ** -10
    tt = np.stack([np.cos(ph) * sc, np.sin(ph) * sc], 2)
    return {"w64": w64, "ttab": bf(tt)}


DEBUG_C = False


def phase_C(P, sfx, bfw=False):
    nc = P.nc
    xT_d = P.dram("xT" + sfx, [D, NT], F32, "ExternalInput")
    aT_d = P.dram("aT" + sfx, [8, 64, NT], BF16, "ExternalInput")
    bT_d = P.dram("bT" + sfx, [512, NT], BF16, "ExternalInput")
    fT_d = P.dram("fT" + sfx, [512, NT], BF16, "ExternalInput")
    wpa_d = wsrc(P, "w_proj_a", sfx, bfw)
    wpb_d = wsrc(P, "w_proj_b", sfx, bfw)
    wpc_d = wsrc(P, "w_proj_c", sfx, bfw)
    wg_d = wsrc(P, "w_gate", sfx, bfw)
    bg_d = P.dram("b_gate" + sfx, [128, 24], F32, "ExternalInput")
    wo_d = wsrc(P, "w_out", sfx, bfw)
    g12_d = P.dram("g12" + sfx, [128, 2, 8], F32, "ExternalInput")
    ones_d = P.dram("ones_d" + sfx, [128, 128], BF16, "ExternalInput")
    w1_d = wsrc(P, "w_ffn_in", sfx, bfw)
    w2_d = wsrc(P, "w_ffn_out", sfx, bfw)
    xo_d = P.dram("xT_out" + sfx, [D, NT], F32, "ExternalOutput")

    xT = P.sb("xT_sb", [128, 8, NT], F32)
    ARENA_BYTES = 137216
    arena = P.sb("arena_c", [128, ARENA_BYTES // 2], BF16)

    def carve(off, shape, dt):
        nb = (4 if dt == F32 else 2)
        n = 1
        for d_ in shape[1:]:
            n *= d_
        v = arena[:, off // 2:off // 2 + n * nb // 2]
        if dt == F32:
            v = v.bitcast(F32)
        if len(shape) == 2:
            return v
        if len(shape) == 3:
            return v.rearrange("p (a b) -> p a b", b=shape[2])
        raise ValueError

    WA = carve(0, [128, 8, 3072], BF16)
    WB = carve(49152, [128, 20, 1024], BF16)
    hT = carve(90112, [128, 8, 512], BF16)
    brA = carve(98304, [128, 4, 512], BF16)
    brB = carve(102400, [128, 4, 512], BF16)
    brC = carve(106496, [128, 4, 512], BF16)
    sig = carve(110592, [128, 3, 512], F32)
    tm = carve(116736, [128, 3, 512], F32)
    mT = carve(122880, [128, 8, 512], BF16)
    sq = carve(131072, [128, 2, 512], BF16)
    lnv = carve(133120, [128, 512], F32)
    rstd = carve(135168, [128, 512], F32)
    bg = P.sb("bg", [128, 24], F32)
    g12 = P.sb("g12_sb", [128, 2, 8], F32)
    onesm = P.sb("ones_sb", [128, 128], BF16)
    PS = P.ps("ps", [128, 8, 512], F32)

    R = Res
    r_x = [[R() for _ in range(NTB)] for _ in range(8)]
    r_wa = [[R() for _ in range(3)] for _ in range(8)]
    r_wb = [R() for _ in range(20)]
    r_wpa = R()
    r_c = R()
    r_h = R()
    r_sq = [R(), R()]
    r_ln, r_rstd = R(), R()
    r_br = [R(), R(), R()]
    r_sig = [R(), R(), R()]
    r_tm = [R(), R(), R()]
    r_m = [R() for _ in range(8)]
    r_bank = [R() for _ in range(8)]
    r_act = [R() for _ in range(11)]
    r_sgf = [R(), R()]

    for s_, d_ in [(bg, bg_d), (g12, g12_d), (onesm, ones_d)]:
        P.op("sp", "dma_start", dict(out=s_[:], in_=d_), wr=[r_c])

    def load_x(tb):
        for kc in range(8):
            P.op("sp", "dma_start", dict(out=xT[:, kc, tb * 512:(tb + 1) * 512],
                                         in_=xT_d[kc * 128:(kc + 1) * 128, tb * 512:(tb + 1) * 512]), wr=[r_x[kc][tb]])

    load_x(0)
    for j in range(4):
        P.op("poolq", "dma_start", dict(out=WB[:, 16 + j, :], in_=wpa_d[j * 128:(j + 1) * 128, :]), wr=[r_wb[16 + j]])
    for j in range(4):
        P.op("poolq", "dma_start", dict(out=WB[:, 8 + j, :], in_=wpb_d[j * 128:(j + 1) * 128, :]), wr=[r_wb[8 + j]])
    for j in range(4):
        P.op("poolq", "dma_start", dict(out=WB[:, 12 + j, :], in_=wpc_d[j * 128:(j + 1) * 128, :]), wr=[r_wb[12 + j]])
    for cb in range(3):
        for kc in range(8):
            P.op("poolq", "dma_start", dict(out=WA[:, kc, cb * 1024:(cb + 1) * 1024],
                                            in_=wg_d[kc * 128:(kc + 1) * 128, cb * 1024:(cb + 1) * 1024]), wr=[r_wa[kc][cb]])
    for kc in range(8):
        P.op("poolq", "dma_start", dict(out=WB[:, kc, :], in_=wo_d[kc * 128:(kc + 1) * 128, :]), wr=[r_wb[kc]])

    def rmsnorm_block(tb, gi, hdst):
        ts_ = slice(tb * 512, (tb + 1) * 512)
        for kc in range(8):
            s = kc % 2
            P.op("pool", "tensor_tensor", dict(out=sq[:, s, :], in0=xT[:, kc, ts_], in1=xT[:, kc, ts_], op=ALU.mult),
                 rd=[r_x[kc][tb]], wr=[r_sq[s]])
            P.op("pe", "matmul", dict(out=PS[:, 7, :], lhsT=onesm[:], rhs=sq[:, s, :], start=(kc == 0), stop=(kc == 7)),
                 rd=[r_sq[s], r_c], wr=[r_bank[7]])
        P.op("act", "activation", dict(out=lnv[:], in_=PS[:, 7, :], func=AF.Ln, bias=EPS, scale=1.0), rd=[r_bank[7]], wr=[r_ln])
        P.op("act", "activation", dict(out=rstd[:], in_=lnv[:], func=AF.Exp, scale=-0.5), rd=[r_ln], wr=[r_rstd])
        for kc in range(8):
            P.op("dve", "scalar_tensor_tensor", dict(out=hdst[:, kc, :], in0=xT[:, kc, ts_], scalar=g12[:, gi, kc:kc + 1],
                                                     in1=rstd[:], op0=ALU.mult, op1=ALU.mult),
                 rd=[r_x[kc][tb], r_rstd, r_c], wr=[r_h])

    for tb in range(NTB):
        ts_ = slice(tb * 512, (tb + 1) * 512)
        P.op("sp", "dma_start", dict(out=brA[:], in_=aT_d[:, :, ts_].rearrange("(j h) p t -> (h p) j t", h=2)), wr=[r_br[0]])
        P.op("sp", "dma_start", dict(out=brB[:], in_=bT_d[:, ts_].rearrange("(j p) t -> p j t", p=128)), wr=[r_br[1]])
        P.op("sp", "dma_start", dict(out=brC[:], in_=fT_d[:, ts_].rearrange("(j p) t -> p j t", p=128)), wr=[r_br[2]])
        if tb + 1 < NTB:
            load_x(tb + 1)
        rmsnorm_block(tb, 0, hT[:, :, 0:512])
        for oc in range(8):
            ocs = slice(oc * 128, (oc + 1) * 128)
            for j in range(4):
                P.op("pe", "matmul", dict(out=PS[:, 0, :], lhsT=WB[:, 16 + j, ocs], rhs=brA[:, j, :], start=(j == 0), stop=(j == 3)),
                     rd=[r_wb[16 + j], r_br[0]], wr=[r_bank[0]])
            for j in range(4):
                P.op("pe", "matmul", dict(out=PS[:, 1, :], lhsT=WB[:, 8 + j, ocs], rhs=brB[:, j, :], start=(j == 0), stop=(j == 3)),
                     rd=[r_wb[8 + j], r_br[1]], wr=[r_bank[1]])
            for j in range(4):
                P.op("pe", "matmul", dict(out=PS[:, 2, :], lhsT=WB[:, 12 + j, ocs], rhs=brC[:, j, :], start=(j == 0), stop=(j == 3)),
                     rd=[r_wb[12 + j], r_br[2]], wr=[r_bank[2]])
            for i in range(3):
                for kc in range(8):
                    P.op("pe", "matmul", dict(out=PS[:, 3 + i, :], lhsT=WA[:, kc, i * 1024 + oc * 128:i * 1024 + (oc + 1) * 128],
                                              rhs=hT[:, kc, 0:512], start=(kc == 0), stop=(kc == 7)),
                         rd=[r_wa[kc][i], r_h], wr=[r_bank[3 + i]])
                P.op("act", "activation", dict(out=sig[:, i, :], in_=PS[:, 3 + i, :], func=AF.Sigmoid,
                                               bias=bg[:, i * 8 + oc:i * 8 + oc + 1], scale=1.0),
                     rd=[r_bank[3 + i], r_c], wr=[r_sig[i]])
                P.op("dve", "tensor_tensor", dict(out=tm[:, i, :], in0=PS[:, i, :], in1=sig[:, i, :], op=ALU.mult),
                     rd=[r_bank[i], r_sig[i]], wr=[r_tm[i]])
            P.op("pool", "tensor_tensor", dict(out=tm[:, 0, :], in0=tm[:, 0, :], in1=tm[:, 1, :], op=ALU.add),
                 rd=[r_tm[1]], wr=[r_tm[0]])
            P.op("pool", "tensor_tensor", dict(out=mT[:, oc, :], in0=tm[:, 0, :], in1=tm[:, 2, :], op=ALU.add),
                 rd=[r_tm[0], r_tm[2]], wr=[r_m[oc]])
        for oc2 in range(8):
            bank = 6
            for kc in range(8):
                P.op("pe", "matmul", dict(out=PS[:, bank, :], lhsT=WB[:, kc, oc2 * 128:(oc2 + 1) * 128], rhs=mT[:, kc, :],
                                          start=(kc == 0), stop=(kc == 7)), rd=[r_wb[kc], r_m[kc]], wr=[r_bank[bank]])
            P.op("dve", "tensor_tensor", dict(out=xT[:, oc2, ts_], in0=PS[:, bank, :], in1=xT[:, oc2, ts_], op=ALU.add),
                 rd=[r_bank[bank]], wr=[r_x[oc2][tb]])

    fin = []
    if DEBUG_C:
        x1_d = P.dram("x1_out" + sfx, [D, NT], F32, "ExternalOutput")
        for kc in range(8):
            fin.append(P.op("sp", "dma_start", dict(out=x1_d[kc * 128:(kc + 1) * 128, :], in_=xT[:, kc, :]), rd=r_x[kc]))
    groups = [(0, 6), (6, 6), (12, 5), (17, 5)]
    W1s = [carve(0, [128, 8, 1536], BF16), carve(36864, [128, 8, 1536], BF16)]
    W2s = [carve(24576, [128, 6, 1024], BF16), carve(61440, [128, 6, 1024], BF16)]
    h2T = carve(73728, [128, 8, NT], BF16)
    actT = carve(106496, [128, 6, 512], BF16)
    sgf = carve(112640, [128, 2, 512], F32)
    r_s1 = [[R() for _ in range(8)] for _ in range(2)]
    r_s1u = [[R() for _ in range(8)] for _ in range(2)]
    r_s2 = [[R() for _ in range(6)] for _ in range(2)]
    first_extra = [[x_ for kc in range(8) for x_ in r_wa[kc]],
                   r_wa[6] + r_wa[7] + r_wb[0:12]]
    loaded = [False, False]

    def load_group(g):
        sl_ = g % 2
        j0, n = groups[g]
        extra = [] if loaded[sl_] else first_extra[sl_]
        loaded[sl_] = True
        for kc in range(8):
            P.op("poolq", "dma_start", dict(out=W1s[sl_][:, kc, 0:n * 128],
                                            in_=w1_d[kc * 128:(kc + 1) * 128, j0 * 128:(j0 + n) * 128]), wr=[r_s1[sl_][kc]] + extra)
            P.op("poolq", "dma_start", dict(out=W1s[sl_][:, kc, 768:768 + n * 128],
                                            in_=w1_d[kc * 128:(kc + 1) * 128, DFF + j0 * 128:DFF + (j0 + n) * 128]),
                 wr=[r_s1u[sl_][kc]] + extra)
        for j in range(n):
            P.op("poolq", "dma_start", dict(out=W2s[sl_][:, j, :], in_=w2_d[(j0 + j) * 128:(j0 + j + 1) * 128, :]),
                 wr=[r_s2[sl_][j]] + extra)

    load_group(0)
    P.op("dve", "memset", dict(ap=lnv[:, 0:8], constant=0.0), rd=[], wr=r_sig + r_tm + r_br + r_act + r_sgf + r_m + [r_ln, r_h] + r_wb[12:20])
    for tb in range(NTB):
        rmsnorm_block(tb, 1, h2T[:, :, tb * 512:(tb + 1) * 512])
    load_group(1)
    nb = 0
    for g in range(4):
        sl_ = g % 2
        j0, n = groups[g]
        if g >= 1 and g + 1 < 4:
            load_group(g + 1)
        for tb in range(NTB):
            ts_ = slice(tb * 512, (tb + 1) * 512)
            for j in range(n):
                s = nb % 2
                nb += 1
                bg_, bu_ = (0, 1) if s == 0 else (2, 3)
                for kc in range(8):
                    P.op("pe", "matmul", dict(out=PS[:, bg_, :], lhsT=W1s[sl_][:, kc, j * 128:(j + 1) * 128], rhs=h2T[:, kc, ts_],
                                              start=(kc == 0), stop=(kc == 7)), rd=[r_s1[sl_][kc], r_h], wr=[r_bank[bg_]])
                for kc in range(8):
                    P.op("pe", "matmul", dict(out=PS[:, bu_, :], lhsT=W1s[sl_][:, kc, 768 + j * 128:768 + (j + 1) * 128], rhs=h2T[:, kc, ts_],
                                              start=(kc == 0), stop=(kc == 7)), rd=[r_s1u[sl_][kc], r_h], wr=[r_bank[bu_]])
                P.op("act", "activation", dict(out=sgf[:, s, :], in_=PS[:, bg_, :], func=AF.Silu), rd=[r_bank[bg_]], wr=[r_sgf[s]])
                P.op("dve", "tensor_tensor", dict(out=actT[:, j, :], in0=PS[:, bu_, :], in1=sgf[:, s, :], op=ALU.mult),
                     rd=[r_bank[bu_], r_sgf[s]], wr=[r_act[j]])
            for oc in range(8):
                bank = 4 + (oc % 2)
                for j in range(n):
                    P.op("pe", "matmul", dict(out=PS[:, bank, :], lhsT=W2s[sl_][:, j, oc * 128:(oc + 1) * 128], rhs=actT[:, j, :],
                                              start=(j == 0), stop=(j == n - 1)), rd=[r_s2[sl_][j], r_act[j]], wr=[r_bank[bank]])
                P.op("dve", "tensor_tensor", dict(out=xT[:, oc, ts_], in0=PS[:, bank, :], in1=xT[:, oc, ts_], op=ALU.add),
                     rd=[r_bank[bank]], wr=[r_x[oc][tb]])
    for kc in range(8):
        fin.append(P.op("sp", "dma_start", dict(out=xo_d[kc * 128:(kc + 1) * 128, :], in_=xT[:, kc, :]), rd=r_x[kc]))
    return fin, xT, r_x


def build_launch(kind):
    nc = bass.Bass("TRN2", target_bir_lowering=False)
    if kind == "L1":
        P = Prog(nc)
        fin = phase_A(P, "_0")
    else:
        l = 0 if kind == "L2" else 1
        sfx = "_%d" % l
        P = Prog(nc, internal={"aT" + sfx, "bT" + sfx, "fT" + sfx})
        fin = []
        P.phase_begin()
        wl = [(n_, sfx) for n_ in ("w_proj_a", "w_proj_b", "w_proj_c", "w_gate", "w_out", "w_ffn_in", "w_ffn_out")]
        if kind == "L2":
            wl.append(("w_in", "_1"))
        phase_B1(P, sfx, l)
        P.barrier()
        P.phase_begin()
        pre, issue = b3_prefetch(P, sfx)
        phase_B2(P, sfx, after_loads=issue)
        P.barrier()
        P.phase_begin()
        phase_B3(P, sfx, pre)
        P.barrier()
        P.phase_begin()
        finC, xT, r_x = phase_C(P, sfx)
        fin += finC
        if kind == "L2":
            P.barrier()
            P.phase_begin()
            fin += phase_A(P, "_1", x_res=(xT, r_x))
    P.final_events = fin
    with P.stack:
        with nc.Block() as block:
            P.finalize(block)
    return nc


def inputs_A(l, inp, c, sfx, xT=None):
    ct = const_tables(c)
    g1 = np.ascontiguousarray(inp["norm1_g"][l].reshape(8, 128).T)
    gqk = np.stack([np.tile(inp["qnorm_g"][l], 4), np.tile(inp["knorm_g"][l], 4)], axis=1).astype(np.float32)
    d = {"w_in": np.ascontiguousarray(inp["w_in"][l]), "g1": g1, "gqk": np.ascontiguousarray(gqk),
         "cosf": ct["cosf"], "sinf": ct["sinf"], "ones_d": ct["ones_d"], "bd32": ct["bd32"], "rot": ct["rot"],
         "dftg": ct["dftg"]}
    if xT is not None:
        d["xT"] = xT
    return {k + sfx: v for k, v in d.items()}


def inputs_B(l, inp, c, sfx, ex):
    b, a = c // 4, c % 4
    cb = consts_B1()
    lamv = np.stack([inp["lambda_q1"][l], inp["lambda_k1"][l], inp["lambda_q2"][l], inp["lambda_k2"][l]], 0)
    lamv = np.ascontiguousarray(np.broadcast_to(lamv[None], (128, 4, 32))).astype(np.float32)
    gsub = np.ascontiguousarray(inp["subln_g"][l].reshape(64, 1)).astype(np.float32)
    grp = [b * 4 + r for r in range(4)]
    d = {"qT": ex["qT"][c], "kT_all": np.ascontiguousarray(np.stack([ex["kT"][i] for i in grp], 0)),
         "vp_all": np.ascontiguousarray(np.stack([ex["vp"][i] for i in grp], 0)), "lamv": lamv, "gsub": gsub,
         "sel": cb["sel"], "ones64": cb["ones64"], "qmask": cb["qmask"]}
    cw = np.ascontiguousarray(inp["conv_w"][l].T.reshape(4, 128, 31).transpose(1, 0, 2)).astype(np.float32)
    cvec = np.stack([inp["conv_b"][l].reshape(4, 128).T, inp["conv_ln_g"][l].reshape(4, 128).T,
                     inp["conv_ln_b"][l].reshape(4, 128).T], axis=1).astype(np.float32)
    cm = np.zeros((128, 8), np.float32)
    if a > 0:
        cm[:, a - 1] = 1.0
    if a < 3:
        cm[:, 4 + a + 1] = 1.0
    d.update({"gT": ex["gT"][c], "g_all": np.ascontiguousarray(np.stack([ex["gT"][i] for i in grp], 0)), "conv_w": cw,
              "cvec": np.ascontiguousarray(cvec), "cmask": cm, "ident": bf(np.eye(128)),
              "ones512": np.full((128, 128), 1.0 / 512.0, np.float32)})
    c3 = consts_B3(c)
    d.update({"z_all": np.ascontiguousarray(np.stack([ex["zp"][i] for i in grp], 0)), "w64": c3["w64"], "ttab": c3["ttab"]})
    return {k + sfx: v for k, v in d.items()}


def inputs_C(l, inp, c, sfx, xT):
    ct = const_tables(0)
    g12 = np.stack([inp["norm1_g"][l].reshape(8, 128).T, inp["norm2_g"][l].reshape(8, 128).T], axis=1).astype(np.float32)
    bgate = np.ascontiguousarray(inp["b_gate"][l].reshape(24, 128).T).astype(np.float32)
    d = {"xT": xT, "w_proj_a": np.ascontiguousarray(inp["w_proj_a"][l]), "w_proj_b": np.ascontiguousarray(inp["w_proj_b"][l]),
         "w_proj_c": np.ascontiguousarray(inp["w_proj_c"][l]), "w_gate": np.ascontiguousarray(inp["w_gate"][l]),
         "b_gate": bgate, "w_out": np.ascontiguousarray(inp["w_out"][l]), "g12": np.ascontiguousarray(g12),
         "ones_d": ct["ones_d"], "w_ffn_in": np.ascontiguousarray(inp["w_ffn_in"][l]),
         "w_ffn_out": np.ascontiguousarray(inp["w_ffn_out"][l])}
    return {k + sfx: v for k, v in d.items()}


def _collect(res, sfx):
    return {k: [np.asarray(r[k + sfx]) for r in res] for k in ("qT", "kT", "vp", "gT", "zp")}


def kernel(**inp):
    inp = {k: np.asarray(v) for k, v in inp.items()}
    x = inp["x"]
    xT = [np.ascontiguousarray(x[c // 4, (c % 4) * NT:(c % 4 + 1) * NT, :].T) for c in range(NCORES)]
    cores = list(range(NCORES))
    r1 = run_bass_kernel_spmd(build_launch("L1"), [inputs_A(0, inp, c, "_0", xT[c]) for c in cores], core_ids=cores).results
    ex0 = _collect(r1, "_0")
    in2 = []
    for c in cores:
        d = inputs_B(0, inp, c, "_0", ex0)
        d.update(inputs_C(0, inp, c, "_0", xT[c]))
        d.update(inputs_A(1, inp, c, "_1"))
        in2.append(d)
    r2 = run_bass_kernel_spmd(build_launch("L2"), in2, core_ids=cores).results
    ex1 = _collect(r2, "_1")
    x1 = [np.ascontiguousarray(np.asarray(r["xT_out_0"])) for r in r2]
    in3 = []
    for c in cores:
        d = inputs_B(1, inp, c, "_1", ex1)
        d.update(inputs_C(1, inp, c, "_1", x1[c]))
        in3.append(d)
    r3 = run_bass_kernel_spmd(build_launch("L3"), in3, core_ids=cores).results
    out = np.empty((2, SEQ, D), np.float32)
    for c in cores:
        out[c // 4, (c % 4) * NT:(c % 4 + 1) * NT, :] = np.asarray(r3[c]["xT_out_1"]).T
    return out
```
